# Optimizing a Trainium2 kernel written in Bass

```python
import math
import jax
import jax.numpy as jnp
from jax import lax
import numpy as np

D_MODEL = 1024
BATCH = 32
SEQ = 256
DEPTH = 2
DEC_BATCH = 4
DEC_SEQ = 4096
PAST_LEN = 256

GRID_W = 64
HEAD_DIM = 64
N_MIXERS = 4
GROUP_W = D_MODEL // N_MIXERS
MIX_W = N_MIXERS * GROUP_W
CONV_K = 3
EPS = 1e-6
ROPE_THETA = 10000.0
Q_BLOCK = 128
NEG_INF = -1e30

DN_HEADS = GROUP_W // HEAD_DIM
DN_DK = HEAD_DIM
DN_DV = HEAD_DIM
DN_CHUNK = 64

MLA_HEADS = GROUP_W // HEAD_DIM
MLA_NOPE = 64
MLA_ROPE = 32
MLA_VHD = GROUP_W // MLA_HEADS
MLA_Q_LORA = D_MODEL // 4
MLA_KV_LORA = D_MODEL // 8
MLA_SCALE = (MLA_NOPE + MLA_ROPE) ** -0.5

SSM_HEADS = GROUP_W // 64
SSM_P = GROUP_W // SSM_HEADS
SSM_N = 64
SSM_GROUPS = 2
SSM_CHUNK = 64

SWA_HEADS = GROUP_W // HEAD_DIM
SWA_KV_HEADS = 2
SWA_GQA = SWA_HEADS // SWA_KV_HEADS
SWA_WINDOW = 128
SWA_BLOCK = 128
SWA_SCALE = HEAD_DIM ** -0.5

FF_DIM = -(-(8 * D_MODEL) // (3 * 256)) * 256

DN_IN = 4 * GROUP_W + 4 * DN_HEADS
MLA_IN = MLA_Q_LORA + MLA_KV_LORA + MLA_ROPE
SSM_IN = 2 * GROUP_W + 2 * SSM_GROUPS * SSM_N + 2 * SSM_HEADS
SWA_IN = (SWA_HEADS + 2 * SWA_KV_HEADS) * HEAD_DIM
IN_DIM = DN_IN + MLA_IN + SSM_IN + SWA_IN

kernel_name = 'hybrid_flow_backbone_step'


def split_cols(x, sizes):
    idx = [int(s) for s in np.cumsum(sizes)[:-1]]
    return jnp.split(x, idx, axis=-1)


def rms_norm(x, w):
    xf = x.astype(jnp.float32)
    y = xf * lax.rsqrt(jnp.mean(xf * xf, axis=-1, keepdims=True) + EPS)
    return (y * w.astype(jnp.float32)).astype(x.dtype)


def l2_normalize(x):
    xf = x.astype(jnp.float32)
    return (xf * lax.rsqrt(jnp.sum(xf * xf, axis=-1, keepdims=True) + EPS)).astype(x.dtype)


def modulated_norm(x, w, shift, scale):
    return rms_norm(x, w) * (1 + scale) + shift


def swiglu(h, w_gate_up, w_down):
    gu = h @ w_gate_up
    return (jax.nn.silu(gu[..., :FF_DIM]) * gu[..., FF_DIM:]) @ w_down


def centred_depthwise_conv(u, w):
    pad = (w.shape[0] - 1) // 2
    return lax.conv_general_dilated(u, w[:, None, :].astype(u.dtype), window_strides=(1,),
                                    padding=[(pad, pad)], dimension_numbers=('NWC', 'WIO', 'NWC'),
                                    feature_group_count=u.shape[-1])


def axial_rope(rows, rot_dim):
    row_ids = jnp.broadcast_to(jnp.arange(rows)[:, None], (rows, GRID_W)).reshape(-1).astype(jnp.float32)
    col_ids = jnp.broadcast_to(jnp.arange(GRID_W)[None, :], (rows, GRID_W)).reshape(-1).astype(jnp.float32)
    n_freq = rot_dim // 4
    inv_freq = ROPE_THETA ** (-jnp.arange(n_freq, dtype=jnp.float32) / n_freq)
    ang = jnp.concatenate([row_ids[:, None] * inv_freq, col_ids[:, None] * inv_freq], axis=-1)
    return jnp.cos(ang), jnp.sin(ang)


def apply_rope(x, cos, sin):
    half = x.shape[-1] // 2
    x1, x2 = x[..., :half], x[..., half:]
    c = cos[None, :, None, :].astype(x.dtype)
    s = sin[None, :, None, :].astype(x.dtype)
    return jnp.concatenate([x1 * c - x2 * s, x1 * s + x2 * c], axis=-1)


def gated_delta_chunked(q, k, v, g, beta, s0):
    f32 = jnp.float32
    b, t, h, dk = q.shape
    dv = v.shape[-1]
    n = t // DN_CHUNK

    def chunks(a):
        a = a.astype(f32).reshape((b, n, DN_CHUNK, h) + a.shape[3:])
        return jnp.moveaxis(a, 2, 3).swapaxes(0, 1)

    qc, kc, vc, gc, bc = chunks(q), chunks(k), chunks(v), chunks(g), chunks(beta)
    gcum = jnp.cumsum(gc, axis=-1)
    causal = jnp.tril(jnp.ones((DN_CHUNK, DN_CHUNK), bool))
    strict = jnp.tril(jnp.ones((DN_CHUNK, DN_CHUNK), bool), -1)
    diff = gcum[..., :, None] - gcum[..., None, :]
    decay = jnp.where(causal, jnp.exp(jnp.where(causal, diff, 0.0)), 0.0)
    kb = kc * bc[..., None]
    a_mat = jnp.where(strict, jnp.einsum('nbhid,nbhjd->nbhij', kb, kc) * decay, 0.0)
    tmat = jnp.eye(DN_CHUNK, dtype=f32) + a_mat
    u = lax.linalg.triangular_solve(tmat, vc * bc[..., None], left_side=True, lower=True, unit_diagonal=True)
    w = lax.linalg.triangular_solve(tmat, kb * jnp.exp(gcum)[..., None], left_side=True, lower=True,
                                    unit_diagonal=True)
    qk = jnp.where(causal, jnp.einsum('nbhid,nbhjd->nbhij', qc, kc) * decay, 0.0)

    def step(state, xs):
        q_i, k_i, u_i, w_i, qk_i, g_i = xs
        v_new = u_i - jnp.einsum('bhcd,bhde->bhce', w_i, state)
        o = (jnp.einsum('bhcd,bhde->bhce', q_i * jnp.exp(g_i)[..., None], state)
             + jnp.einsum('bhij,bhje->bhie', qk_i, v_new))
        g_last = g_i[..., -1]
        state = (state * jnp.exp(g_last)[..., None, None]
                 + jnp.einsum('bhcd,bhce->bhde', k_i * jnp.exp(g_last[..., None] - g_i)[..., None], v_new))
        return state, o

    s_fin, o = lax.scan(step, s0.astype(f32), (qc, kc, u, w, qk, gcum))
    o = jnp.moveaxis(o.swapaxes(0, 1), 3, 2).reshape(b, t, h, dv)
    return o, s_fin


def ssd_chunked(x, a, bm, cm, s0):
    f32 = jnp.float32
    b, t, h, pdim = x.shape
    n = t // SSM_CHUNK

    def chunks(arr):
        return arr.astype(f32).reshape((b, n, SSM_CHUNK) + arr.shape[2:]).swapaxes(0, 1)

    xc, bc, cc = chunks(x), chunks(bm), chunks(cm)
    ac = jnp.moveaxis(chunks(a), 2, 3)
    acum = jnp.cumsum(ac, axis=-1)
    causal = jnp.tril(jnp.ones((SSM_CHUNK, SSM_CHUNK), bool))
    diff = acum[..., :, None] - acum[..., None, :]
    lmat = jnp.where(causal, jnp.exp(jnp.where(causal, diff, 0.0)), 0.0)
    scores = jnp.einsum('nbihs,nbjhs->nbhij', cc, bc) * lmat
    y_diag = jnp.einsum('nbhij,nbjhp->nbihp', scores, xc)
    decay_to_end = jnp.exp(acum[..., -1:] - acum)
    chunk_states = jnp.einsum('nbjhs,nbhj,nbjhp->nbhps', bc, decay_to_end, xc)
    chunk_decay = jnp.exp(acum[..., -1])

    def step(state, xs):
        st, dec = xs
        return state * dec[..., None, None] + st, state

    s_fin, s_in = lax.scan(step, s0.astype(f32), (chunk_states, chunk_decay))
    y_off = jnp.einsum('nbihs,nbhps,nbhi->nbihp', cc, s_in, jnp.exp(acum))
    y = (y_diag + y_off).swapaxes(0, 1).reshape(b, t, h, pdim)
    return y, s_fin


def deltanet_mixer(p, lp, s0):
    f32 = jnp.float32
    b, t, _ = p.shape
    qkv, z, beta_raw, alpha_raw = split_cols(p, [3 * GROUP_W, GROUP_W, 2 * DN_HEADS, 2 * DN_HEADS])
    qkv = jax.nn.silu(centred_depthwise_conv(qkv, lp['dn_conv_w']))
    q, k, v = [a.reshape(b, t, DN_HEADS, HEAD_DIM) for a in jnp.split(qkv, 3, axis=-1)]
    q = l2_normalize(q) * (DN_DK ** -0.5)
    k = l2_normalize(k)
    beta = jax.nn.sigmoid(beta_raw.astype(f32)).reshape(b, t, 2, DN_HEADS)
    g = -jnp.exp(lp['dn_a_log'].astype(f32)) * jax.nn.softplus(
        alpha_raw.astype(f32).reshape(b, t, 2, DN_HEADS) + lp['dn_dt_bias'].astype(f32))
    o_f, s_f = gated_delta_chunked(q, k, v, g[:, :, 0], beta[:, :, 0], s0[:, 0])
    o_b, s_b = gated_delta_chunked(jnp.flip(q, 1), jnp.flip(k, 1), jnp.flip(v, 1),
                                   jnp.flip(g[:, :, 1], 1), jnp.flip(beta[:, :, 1], 1), s0[:, 1])
    o = (o_f + jnp.flip(o_b, 1)).astype(p.dtype)
    o = rms_norm(o, lp['dn_norm_w']) * jax.nn.silu(z.reshape(b, t, DN_HEADS, DN_DV))
    return o.reshape(b, t, GROUP_W), jnp.stack([s_f, s_b], axis=1).astype(p.dtype)


def ssd_mixer(p, lp, s0):
    f32 = jnp.float32
    b, t, _ = p.shape
    z, xbc, dt_raw = split_cols(p, [GROUP_W, GROUP_W + 2 * SSM_GROUPS * SSM_N, 2 * SSM_HEADS])
    xbc = jax.nn.silu(centred_depthwise_conv(xbc, lp['ssm_conv_w']) + lp['ssm_conv_b'])
    xs, bm, cm = split_cols(xbc, [GROUP_W, SSM_GROUPS * SSM_N, SSM_GROUPS * SSM_N])
    xs = xs.reshape(b, t, SSM_HEADS, SSM_P)
    rep = SSM_HEADS // SSM_GROUPS
    bm = jnp.repeat(bm.reshape(b, t, SSM_GROUPS, SSM_N), rep, axis=2)
    cm = jnp.repeat(cm.reshape(b, t, SSM_GROUPS, SSM_N), rep, axis=2)
    dt = jax.nn.softplus(dt_raw.astype(f32).reshape(b, t, 2, SSM_HEADS) + lp['ssm_dt_bias'].astype(f32))
    a = -jnp.exp(lp['ssm_a_log'].astype(f32)) * dt
    xdt = xs.astype(f32)[:, :, None] * dt[..., None]
    y_f, s_f = ssd_chunked(xdt[:, :, 0], a[:, :, 0], bm, cm, s0[:, 0])
    y_b, s_b = ssd_chunked(jnp.flip(xdt[:, :, 1], 1), jnp.flip(a[:, :, 1], 1),
                           jnp.flip(bm, 1), jnp.flip(cm, 1), s0[:, 1])
    y = y_f + jnp.flip(y_b, 1) + lp['ssm_d'].astype(f32)[:, None] * xs.astype(f32)
    y = (y.reshape(b, t, GROUP_W) * jax.nn.silu(z.astype(f32))).astype(p.dtype)
    y = rms_norm(y.reshape(b, t, SSM_GROUPS, GROUP_W // SSM_GROUPS),
                 lp['ssm_norm_w'].reshape(SSM_GROUPS, GROUP_W // SSM_GROUPS))
    return y.reshape(b, t, GROUP_W), jnp.stack([s_f, s_b], axis=1).astype(p.dtype)


def mla_project(p, lp):
    b, t, _ = p.shape
    q_lat, kv_lat, k_pe = split_cols(p, [MLA_Q_LORA, MLA_KV_LORA, MLA_ROPE])
    q = (rms_norm(q_lat, lp['mla_q_norm_w']) @ lp['mla_w_uq']).reshape(b, t, MLA_HEADS, MLA_NOPE + MLA_ROPE)
    c_kv = rms_norm(kv_lat, lp['mla_kv_norm_w'])
    return q[..., :MLA_NOPE], q[..., MLA_NOPE:], c_kv, k_pe


def mla_expand(c_kv, k_pe, w_ukv):
    b, t, _ = c_kv.shape
    kv = (c_kv @ w_ukv).reshape(b, t, MLA_HEADS, MLA_NOPE + MLA_VHD)
    k = jnp.concatenate([kv[..., :MLA_NOPE],
                         jnp.broadcast_to(k_pe[:, :, None, :], (b, t, MLA_HEADS, MLA_ROPE))], axis=-1)
    return k, kv[..., MLA_NOPE:]


def swa_project(p):
    b, t, _ = p.shape
    q, k, v = split_cols(p, [SWA_HEADS * HEAD_DIM, SWA_KV_HEADS * HEAD_DIM, SWA_KV_HEADS * HEAD_DIM])
    return (q.reshape(b, t, SWA_KV_HEADS, SWA_GQA, HEAD_DIM),
            k.reshape(b, t, SWA_KV_HEADS, HEAD_DIM), v.reshape(b, t, SWA_KV_HEADS, HEAD_DIM))


def dense_attention(q, k, v, scale, sink=None):
    b, tq, kvh, g, _ = q.shape

    def one_block(i):
        qi = lax.dynamic_slice_in_dim(q, i * Q_BLOCK, Q_BLOCK, axis=1)
        s = jnp.einsum('bqhgd,bkhd->bhgqk', qi, k, preferred_element_type=jnp.float32) * scale
        if sink is not None:
            sk = jnp.broadcast_to(sink.astype(jnp.float32).reshape(1, kvh, g, 1, 1), s.shape[:-1] + (1,))
            prob = jax.nn.softmax(jnp.concatenate([s, sk], axis=-1), axis=-1)[..., :-1]
        else:
            prob = jax.nn.softmax(s, axis=-1)
        return jnp.einsum('bhgqk,bkhd->bqhgd', prob.astype(v.dtype), v)

    o = lax.map(one_block, jnp.arange(tq // Q_BLOCK))
    return jnp.moveaxis(o, 0, 1).reshape(b, tq, kvh, g, v.shape[-1])


def banded_window_attention(q, k, v, k_ctx, v_ctx, sink, scale):
    b, t, kvh, g, _ = q.shape
    n_ctx = k_ctx.shape[1]
    pad = ((0, 0), (SWA_BLOCK, SWA_BLOCK), (0, 0), (0, 0))
    k_pad, v_pad = jnp.pad(k, pad), jnp.pad(v, pad)
    qpos_local = jnp.arange(SWA_BLOCK)
    kpos_local = jnp.arange(3 * SWA_BLOCK)
    in_window = jnp.abs((kpos_local[None, :] - SWA_BLOCK) - qpos_local[:, None]) <= SWA_WINDOW

    def one_block(i):
        start = i * SWA_BLOCK
        qi = lax.dynamic_slice_in_dim(q, start, SWA_BLOCK, axis=1)
        ki = lax.dynamic_slice_in_dim(k_pad, start, 3 * SWA_BLOCK, axis=1)
        vi = lax.dynamic_slice_in_dim(v_pad, start, 3 * SWA_BLOCK, axis=1)
        kpos = start - SWA_BLOCK + kpos_local
        valid = in_window & ((kpos >= 0) & (kpos < t))[None, :]
        s_loc = jnp.einsum('bqhgd,bchd->bhgqc', qi, ki, preferred_element_type=jnp.float32) * scale
        s_loc = jnp.where(valid, s_loc, NEG_INF)
        s_ctx = jnp.einsum('bqhgd,blhd->bhgql', qi, k_ctx, preferred_element_type=jnp.float32) * scale
        sk = jnp.broadcast_to(sink.astype(jnp.float32).reshape(1, kvh, g, 1, 1), s_loc.shape[:-1] + (1,))
        prob = jax.nn.softmax(jnp.concatenate([s_loc, s_ctx, sk], axis=-1), axis=-1)
        p_loc = prob[..., :3 * SWA_BLOCK].astype(v.dtype)
        p_ctx = prob[..., 3 * SWA_BLOCK:3 * SWA_BLOCK + n_ctx].astype(v.dtype)
        return (jnp.einsum('bhgqc,bchd->bqhgd', p_loc, vi)
                + jnp.einsum('bhgql,blhd->bqhgd', p_ctx, v_ctx))

    o = lax.map(one_block, jnp.arange(t // SWA_BLOCK))
    return jnp.moveaxis(o, 0, 1).reshape(b, t, kvh, g, v.shape[-1])


def context_mixers(p, lp):
    b, n_ctx, _ = p.shape
    p_dn, p_mla, p_ssm, p_swa = split_cols(p, [DN_IN, MLA_IN, SSM_IN, SWA_IN])
    o_dn, s_dn = deltanet_mixer(p_dn, lp, jnp.zeros((b, 2, DN_HEADS, DN_DK, DN_DV), p.dtype))
    q_nope, q_pe, c_kv, k_pe = mla_project(p_mla, lp)
    k_m, v_m = mla_expand(c_kv, k_pe, lp['mla_w_ukv'])
    q_m = jnp.concatenate([q_nope, q_pe], axis=-1)[:, :, :, None, :]
    o_mla = dense_attention(q_m, k_m, v_m, MLA_SCALE).reshape(b, n_ctx, GROUP_W)
    o_ssm, s_ssm = ssd_mixer(p_ssm, lp, jnp.zeros((b, 2, SSM_HEADS, SSM_P, SSM_N), p.dtype))
    q_s, k_s, v_s = swa_project(p_swa)
    o_swa = dense_attention(q_s, k_s, v_s, SWA_SCALE, lp['swa_sinks']).reshape(b, n_ctx, GROUP_W)
    mix = jnp.concatenate([o_dn, o_mla, o_ssm, o_swa], axis=-1)
    return mix, (s_dn, c_kv, k_pe, s_ssm, k_s, v_s)


def latent_mixers(p, lp, cache, rope):
    b, t, _ = p.shape
    s0_dn, ckv_ctx, kpe_ctx, s0_ssm, k_ctx, v_ctx = cache
    cos_m, sin_m, cos_s, sin_s = rope
    p_dn, p_mla, p_ssm, p_swa = split_cols(p, [DN_IN, MLA_IN, SSM_IN, SWA_IN])
    o_dn, _ = deltanet_mixer(p_dn, lp, s0_dn)
    q_nope, q_pe, c_kv, k_pe = mla_project(p_mla, lp)
    q_pe = apply_rope(q_pe, cos_m, sin_m)
    k_pe = apply_rope(k_pe[:, :, None, :], cos_m, sin_m)[:, :, 0]
    k_lat, v_lat = mla_expand(c_kv, k_pe, lp['mla_w_ukv'])
    k_c, v_c = mla_expand(ckv_ctx, kpe_ctx, lp['mla_w_ukv'])
    q_m = jnp.concatenate([q_nope, q_pe], axis=-1)[:, :, :, None, :]
    o_mla = dense_attention(q_m, jnp.concatenate([k_c, k_lat], axis=1),
                            jnp.concatenate([v_c, v_lat], axis=1), MLA_SCALE).reshape(b, t, GROUP_W)
    o_ssm, _ = ssd_mixer(p_ssm, lp, s0_ssm)
    q_s, k_s, v_s = swa_project(p_swa)
    q_s = apply_rope(q_s.reshape(b, t, SWA_HEADS, HEAD_DIM), cos_s, sin_s).reshape(
        b, t, SWA_KV_HEADS, SWA_GQA, HEAD_DIM)
    k_s = apply_rope(k_s, cos_s, sin_s)
    o_swa = banded_window_attention(q_s, k_s, v_s, k_ctx, v_ctx, lp['swa_sinks'], SWA_SCALE).reshape(
        b, t, GROUP_W)
    mix = jnp.concatenate([o_dn, o_mla, o_ssm, o_swa], axis=-1)
    return mix, None


def trunk_layer(x, ada, lp, mixer, *mixer_args):
    sh1, sc1, g1, sh2, sc2, g2 = jnp.split(ada, 6, axis=-1)
    p = modulated_norm(x, lp['norm1_w'], sh1, sc1) @ lp['w_in']
    mix, aux = mixer(p, lp, *mixer_args)
    x = x + g1 * (mix @ lp['w_out'])
    x = x + g2 * swiglu(modulated_norm(x, lp['norm2_w'], sh2, sc2), lp['w_gate_up'], lp['w_down'])
    return x, aux


def setup_inputs(seed: int = 0) -> dict:
    key = jax.random.key(seed)
    ks = iter(jax.random.split(key, 40))
    f32 = jnp.float32

    def nrm(shape, scale):
        return jax.random.normal(next(ks), shape, f32) * scale

    def gain(shape):
        return 1.0 + nrm(shape, 0.1)

    def a_log(shape):
        return jnp.log(jax.random.uniform(next(ks), shape, f32, 1.0, 16.0))

    def dt_bias(shape):
        dt = jnp.exp(jax.random.uniform(next(ks), shape, f32, math.log(1e-3), math.log(1e-1)))
        return dt + jnp.log(-jnp.expm1(-dt))

    return {
        'x_prompt': nrm((BATCH, SEQ, D_MODEL), 1.0),
        'x_sample': nrm((DEC_BATCH, DEC_SEQ, D_MODEL), 1.0),
        'c': nrm((DEC_BATCH, D_MODEL), 1.0),
        'state_dn': nrm((DEC_BATCH, DEPTH, 2, DN_HEADS, DN_DK, DN_DV), 0.2),
        'cache_mla_ckv': nrm((DEC_BATCH, DEPTH, PAST_LEN, MLA_KV_LORA), 1.0),
        'cache_mla_kpe': nrm((DEC_BATCH, DEPTH, PAST_LEN, MLA_ROPE), 1.0),
        'state_ssm': nrm((DEC_BATCH, DEPTH, 2, SSM_HEADS, SSM_P, SSM_N), 0.2),
        'cache_swa_k': nrm((DEC_BATCH, DEPTH, PAST_LEN, SWA_KV_HEADS, HEAD_DIM), 1.0),
        'cache_swa_v': nrm((DEC_BATCH, DEPTH, PAST_LEN, SWA_KV_HEADS, HEAD_DIM), 1.0),
        'c_ctx': nrm((D_MODEL,), 1.0),
        'norm1_w': gain((DEPTH, D_MODEL)),
        'norm2_w': gain((DEPTH, D_MODEL)),
        'w_ada': nrm((DEPTH, D_MODEL, 6 * D_MODEL), 0.5 * D_MODEL ** -0.5),
        'b_ada': nrm((DEPTH, 6 * D_MODEL), 0.02),
        'w_in': nrm((DEPTH, D_MODEL, IN_DIM), D_MODEL ** -0.5),
        'w_out': nrm((DEPTH, MIX_W, D_MODEL), MIX_W ** -0.5),
        'dn_conv_w': nrm((DEPTH, CONV_K, 3 * GROUP_W), CONV_K ** -0.5),
        'dn_a_log': a_log((DEPTH, 2, DN_HEADS)),
        'dn_dt_bias': dt_bias((DEPTH, 2, DN_HEADS)),
        'dn_norm_w': gain((DEPTH, DN_DV)),
        'mla_q_norm_w': gain((DEPTH, MLA_Q_LORA)),
        'mla_w_uq': nrm((DEPTH, MLA_Q_LORA, MLA_HEADS * (MLA_NOPE + MLA_ROPE)), MLA_Q_LORA ** -0.5),
        'mla_kv_norm_w': gain((DEPTH, MLA_KV_LORA)),
        'mla_w_ukv': nrm((DEPTH, MLA_KV_LORA, MLA_HEADS * (MLA_NOPE + MLA_VHD)), MLA_KV_LORA ** -0.5),
        'ssm_conv_w': nrm((DEPTH, CONV_K, GROUP_W + 2 * SSM_GROUPS * SSM_N), CONV_K ** -0.5),
        'ssm_conv_b': nrm((DEPTH, GROUP_W + 2 * SSM_GROUPS * SSM_N), 0.02),
        'ssm_a_log': a_log((DEPTH, 2, SSM_HEADS)),
        'ssm_dt_bias': dt_bias((DEPTH, 2, SSM_HEADS)),
        'ssm_d': gain((DEPTH, SSM_HEADS)),
        'ssm_norm_w': gain((DEPTH, GROUP_W)),
        'swa_sinks': nrm((DEPTH, SWA_HEADS), 0.5),
        'w_gate_up': nrm((DEPTH, D_MODEL, 2 * FF_DIM), D_MODEL ** -0.5),
        'w_down': nrm((DEPTH, FF_DIM, D_MODEL), FF_DIM ** -0.5),
        'final_norm_w': gain((D_MODEL,)),
    }


def reference(x_prompt, x_sample, c, state_dn, cache_mla_ckv, cache_mla_kpe, state_ssm, cache_swa_k,
              cache_swa_v, c_ctx, norm1_w, norm2_w, w_ada, b_ada, w_in, w_out, dn_conv_w, dn_a_log,
              dn_dt_bias, dn_norm_w, mla_q_norm_w, mla_w_uq, mla_kv_norm_w, mla_w_ukv, ssm_conv_w,
              ssm_conv_b, ssm_a_log, ssm_dt_bias, ssm_d, ssm_norm_w, swa_sinks, w_gate_up, w_down,
              final_norm_w):
    rows = x_sample.shape[1] // GRID_W
    cos_m, sin_m = axial_rope(rows, MLA_ROPE)
    cos_s, sin_s = axial_rope(rows, HEAD_DIM)
    rope = (cos_m, sin_m, cos_s, sin_s)
    x_ctx, x_lat = x_prompt, x_sample
    st_dn, st_ckv, st_kpe, st_ssm, st_k, st_v = [], [], [], [], [], []
    for l in range(DEPTH):
        lp = {
            'norm1_w': norm1_w[l], 'norm2_w': norm2_w[l], 'w_in': w_in[l], 'w_out': w_out[l],
            'dn_conv_w': dn_conv_w[l], 'dn_a_log': dn_a_log[l], 'dn_dt_bias': dn_dt_bias[l],
            'dn_norm_w': dn_norm_w[l], 'mla_q_norm_w': mla_q_norm_w[l], 'mla_w_uq': mla_w_uq[l],
            'mla_kv_norm_w': mla_kv_norm_w[l], 'mla_w_ukv': mla_w_ukv[l], 'ssm_conv_w': ssm_conv_w[l],
            'ssm_conv_b': ssm_conv_b[l], 'ssm_a_log': ssm_a_log[l], 'ssm_dt_bias': ssm_dt_bias[l],
            'ssm_d': ssm_d[l], 'ssm_norm_w': ssm_norm_w[l], 'swa_sinks': swa_sinks[l],
            'w_gate_up': w_gate_up[l], 'w_down': w_down[l],
        }
        ada_ctx = jax.nn.silu(c_ctx) @ w_ada[l] + b_ada[l]
        x_ctx, (s_dn, ckv, kpe, s_ssm, k_s, v_s) = trunk_layer(x_ctx, ada_ctx, lp, context_mixers)
        st_dn.append(s_dn)
        st_ckv.append(ckv)
        st_kpe.append(kpe)
        st_ssm.append(s_ssm)
        st_k.append(k_s)
        st_v.append(v_s)
        ada_lat = (jax.nn.silu(c) @ w_ada[l] + b_ada[l])[:, None, :]
        cache_l = (state_dn[:, l], cache_mla_ckv[:, l], cache_mla_kpe[:, l], state_ssm[:, l],
                   cache_swa_k[:, l], cache_swa_v[:, l])
        x_lat, _ = trunk_layer(x_lat, ada_lat, lp, latent_mixers, cache_l, rope)
    y_prompt = rms_norm(x_ctx, final_norm_w)
    y_sample = rms_norm(x_lat, final_norm_w)
    new_state_dn = jnp.stack(st_dn, axis=1)
    new_mla_ckv = jnp.stack(st_ckv, axis=1)
    new_mla_kpe = jnp.stack(st_kpe, axis=1)
    new_state_ssm = jnp.stack(st_ssm, axis=1)
    new_swa_k = jnp.stack(st_k, axis=1)
    new_swa_v = jnp.stack(st_v, axis=1)
    return (y_prompt, y_sample, new_state_dn, new_mla_ckv, new_mla_kpe, new_state_ssm, new_swa_k, new_swa_v)
```

```python
import numpy as np
import concourse.bass as bass
import concourse.mybir as mybir
from concourse.bass_utils import run_bass_kernel_spmd
from contextlib import ExitStack

F32 = mybir.dt.float32
BF16 = mybir.dt.bfloat16
AF = mybir.ActivationFunctionType
ALU = mybir.AluOpType
AX = mybir.AxisListType

D = 1024
DEPTH = 2
NCTX = 4
TC = 256
TL = 4096
TTOT = NCTX * TC + TL
LOFF = NCTX * TC
FF = 2816
EPS = 1e-6
NFM = 2496
NTM = 664
R_DNQ, R_DNK, R_DNV = 0, 256, 512
R_SSX, R_SSB, R_SSC = 768, 1024, 1152
R_MQ, R_MKV, R_MKPE = 1280, 1536, 1664
R_SWQ, R_SWQS, R_SWK, R_SWKS = 1728, 1984, 2240, 2368
C_DNZ, C_SSZ, C_BETA, C_ALPHA, C_DT, C_SWV = 0, 256, 512, 520, 528, 536


class Res:
    __slots__ = ("name", "w", "r", "t", "base", "full")

    def __init__(self, name, t=None):
        self.name = name
        self.w = {}
        self.r = {}
        self.base = {}
        self.full = None
        self.t = t

    def __getitem__(self, key):
        return self.t[key]


class Pool:
    def __init__(self, tiles):
        self.tiles = tiles
        self.i = 0

    def next(self):
        t = self.tiles[self.i]
        self.i = (self.i + 1) % len(self.tiles)
        return t


class K:
    def __init__(self, nc, es, ndma=12):
        self.nc = nc
        self.es = es
        self.engs = {"pe": nc.tensor, "act": nc.scalar, "dve": nc.vector,
                     "pool": nc.gpsimd, "sp": nc.sync}
        self.semh = {}
        self.cnt = {}
        self.waited = {e: {} for e in self.engs}
        for e in ["pe", "act", "dve", "pool"]:
            self.semh[e] = es.enter_context(nc.semaphore("s_" + e))
            self.cnt[e] = 0
        self.dq = {}
        for q in ["sp", "pool"]:
            sems = []
            for i in range(ndma):
                key = ("d", q, i)
                self.semh[key] = es.enter_context(nc.semaphore("d_%s_%d" % (q, i)))
                self.cnt[key] = 0
                sems.append(key)
            self.dq[q] = {"sems": sems, "rr": 0}
        self.uid = 0

    def tile(self, name, shape, dtype, es=None):
        self.uid += 1
        t = (es or self.es).enter_context(
            self.nc.sbuf_tensor("%s_%d" % (name, self.uid), list(shape), dtype))
        return Res(name, t)

    def ptile(self, name, shape, dtype=F32, es=None):
        self.uid += 1
        t = (es or self.es).enter_context(
            self.nc.psum_tensor("%s_%d" % (name, self.uid), list(shape), dtype))
        return Res(name, t)

    def dram(self, name, shape, dtype, kind="Internal"):
        if name in getattr(self, "ext", ()):
            kind = "ExternalOutput"
        t = self.nc.dram_tensor(name, list(shape), dtype, kind=kind)
        return Res(name, t.ap())

    def pool(self, name, shape, dtype, n, es=None, psum=False):
        return Pool([(self.ptile if psum else self.tile)("%s%d" % (name, i), shape, dtype, es)
                     for i in range(n)])

    def _wait(self, eng, need):
        for s, v in need.items():
            if self.waited[eng].get(s, 0) < v:
                self.engs[eng].wait_ge(self.semh[s], v)
                self.waited[eng][s] = v

    def _deps(self, eng, reads, writes, acc=False):
        need = {}

        def add(s, v, war=False):
            if s == eng and (eng == "pe" or war):
                return
            if need.get(s, 0) < v:
                need[s] = v

        for t in reads:
            for s, v in t.w.items():
                add(s, v)
        for t in writes:
            if acc:
                if t.full is not None:
                    add(t.full[0], t.full[1])
                for s, v in t.base.items():
                    add(s, v, True)
            else:
                b = {}
                for s, v in t.w.items():
                    add(s, v)
                    b[s] = max(b.get(s, 0), v)
                for s, v in t.r.items():
                    add(s, v, True)
                    b[s] = max(b.get(s, 0), v)
                t.base = b
        self._wait(eng, need)

    def _mark(self, key, val, reads, writes, acc=False):
        for t in reads:
            if t.r.get(key, 0) < val:
                t.r[key] = val
        for t in writes:
            if acc:
                if t.w.get(key, 0) < val:
                    t.w[key] = val
            else:
                t.w = {key: val}
                t.r = {}
                t.full = (key, val)

    def op(self, eng, fn, reads=(), writes=(), inc=True, acc=False):
        self._deps(eng, reads, writes, acc)
        ins = fn(self.engs[eng])
        if inc:
            self.cnt[eng] += 1
            ins.then_inc(self.semh[eng], 1)
            val = self.cnt[eng]
        else:
            val = self.cnt[eng] + 1
        self._mark(eng, val, reads, writes, acc)
        return ins

    def dma(self, q, out, in_, reads=(), writes=(), acc=False, **kw):
        d = self.dq[q]
        key = d["sems"][d["rr"]]
        d["rr"] = (d["rr"] + 1) % len(d["sems"])
        cur = self.cnt[key]
        if cur > 0:
            self._wait(q, {key: cur})
        self._deps(q, reads, writes, acc)
        ins = self.engs[q].dma_start(out=out, in_=in_, **kw)
        ins.then_inc(self.semh[key], 16)
        self.cnt[key] = cur + 16
        self._mark(key, cur + 16, reads, writes, acc)

    def barrier(self):
        need = {s: v for s, v in self.cnt.items() if v > 0}
        for e in self.engs:
            self._wait(e, dict(need))


def build_program(debug=None):
    debug = debug or {}
    nc = bass.Bass("TRN2", target_bir_lowering=False)

    def din(name, shape):
        return nc.dram_tensor(name, list(shape), F32, kind="ExternalInput").ap()

    def dout(name, shape):
        return nc.dram_tensor(name, list(shape), F32, kind="ExternalOutput").ap()

    I = {}
    I["x_ctx"] = din("x_ctx", [NCTX * TC, D])
    I["x_lat"] = din("x_lat", [TL, D])
    I["cvec"] = din("cvec", [2, D])
    I["w_ada"] = din("w_ada", [DEPTH, D, 6 * D])
    I["b_ada"] = din("b_ada", [DEPTH, 6 * D])
    I["norm1_w"] = din("norm1_w", [DEPTH, D])
    I["norm2_w"] = din("norm2_w", [DEPTH, D])
    I["final_norm_w"] = din("final_norm_w", [D])
    I["w_in"] = din("w_in", [DEPTH, D, 2744])
    I["w_out"] = din("w_out", [DEPTH, D, D])
    I["w_gate_up"] = din("w_gate_up", [DEPTH, D, 2 * FF])
    I["w_down"] = din("w_down", [DEPTH, FF, D])
    I["ident"] = din("ident", [128, 128])
    I["mla_q_norm_w"] = din("mla_q_norm_w", [DEPTH, 256])
    I["mla_w_uq"] = din("mla_w_uq", [DEPTH, 256, 384])
    I["mla_kv_norm_w"] = din("mla_kv_norm_w", [DEPTH, 128])
    I["mla_w_ukv"] = din("mla_w_ukv", [DEPTH, 128, 512])
    I["cache_ckv"] = din("cache_ckv", [DEPTH, 256, 128])
    I["cache_kpe"] = din("cache_kpe", [DEPTH, 256, 32])
    I["rope_m"] = din("rope_m", [2, 32, TL])
    I["rope_s"] = din("rope_s", [2, 64, TL])
    I["swa_mask"] = din("swa_mask", [6, 128, 512])
    I["cmask"] = din("cmask", [14, 128, 128])
    I["dn_conv_w"] = din("dn_conv_w", [DEPTH, 3, 768])
    I["dn_a_log"] = din("dn_a_log", [DEPTH, 2, 4])
    I["dn_dt_bias"] = din("dn_dt_bias", [DEPTH, 2, 4])
    I["dn_norm_w"] = din("dn_norm_w", [DEPTH, 64])
    I["state_dn"] = din("state_dn", [DEPTH, 2, 4, 64, 64])
    I["ssm_conv_w"] = din("ssm_conv_w", [DEPTH, 3, 512])
    I["ssm_conv_b"] = din("ssm_conv_b", [DEPTH, 512])
    I["ssm_a_log"] = din("ssm_a_log", [DEPTH, 2, 4])
    I["ssm_dt_bias"] = din("ssm_dt_bias", [DEPTH, 2, 4])
    I["ssm_d"] = din("ssm_d", [DEPTH, 4])
    I["ssm_norm_w"] = din("ssm_norm_w", [DEPTH, 256])
    I["state_ssm"] = din("state_ssm", [DEPTH, 2, 4, 64, 64])
    I["swa_sinks"] = din("swa_sinks", [DEPTH, 4])
    I["cache_swk"] = din("cache_swk", [DEPTH, 256, 2, 64])
    I["cache_swv"] = din("cache_swv", [DEPTH, 256, 2, 64])
    O = {}
    O["y_ctx"] = dout("y_ctx", [NCTX * TC, D])
    O["y_lat"] = dout("y_lat", [TL, D])
    O["new_ckv"] = dout("new_ckv", [NCTX, DEPTH, TC, 128])
    O["new_kpe"] = dout("new_kpe", [NCTX, DEPTH, TC, 32])
    O["new_ssm"] = dout("new_ssm", [NCTX, DEPTH, 2, 4, 64, 64])
    O["new_sdn"] = dout("new_sdn", [NCTX, DEPTH, 2, 4, 64, 64])
    O["new_swk"] = dout("new_swk", [NCTX, DEPTH, TC, 2, 64])
    O["new_swv"] = dout("new_swv", [NCTX, DEPTH, TC, 2, 64])

    with ExitStack() as es:
        k = K(nc, es)
        k.ext = set(debug.get("ext", ()))

        def dump(name, res, ap, shape, dtype=F32):
            if name in debug.get("dump", ()):
                d = nc.dram_tensor("dbg_" + name, list(shape), dtype, kind="ExternalOutput").ap()
                k.dma("sp", d, ap, reads=[res])
        X = [k.dram("xs%d" % i, [D, TTOT], F32) for i in range(DEPTH + 1)]
        XA = k.dram("xa", [D, TTOT], F32)
        XB = k.dram("xb", [D, TTOT], F32)
        PFM = [k.dram("pfm%d" % l, [NFM, TTOT], BF16) for l in range(DEPTH)]
        PTM = [k.dram("ptm%d" % l, [TTOT, NTM], F32) for l in range(DEPTH)]
        MIX = [k.dram("mix%d" % l, [D, TTOT], BF16) for l in range(DEPTH)]

        ident = k.tile("ident", [128, 128], F32)
        k.dma("sp", ident[:], I["ident"][:, :], writes=[ident])
        ones_bf = k.tile("ones_bf", [128, 128], BF16)
        k.op("dve", lambda e: e.memset(ones_bf[:], 1.0), writes=[ones_bf])
        ones_f = k.tile("ones_f", [128, 64], F32)
        k.op("dve", lambda e: e.memset(ones_f[:], 1.0), writes=[ones_f])
        epsb = k.tile("epsb", [128, 1], F32)
        k.op("dve", lambda e: e.memset(epsb[:], EPS), writes=[epsb])
        ps = k.pool("ps", [128, 512], F32, 6, psum=True)
        pacc = k.pool("pacc", [128, 512], F32, 2, psum=True)
        psd = [Pool(ps.tiles[0:4]), Pool(ps.tiles[4:6] + pacc.tiles[0:2])]
        psA = Pool(ps.tiles[0:4])
        psP = Pool(ps.tiles[4:6])
        ps8 = Pool(ps.tiles + pacc.tiles)
        mod = [k.tile("mod%d" % l, [128, 6, 8, 2], F32) for l in range(DEPTH)]
        fnw = k.tile("fnw", [128, 8], F32)
        k.dma("sp", fnw[:], I["final_norm_w"].rearrange("(c p) -> p c", p=128), writes=[fnw],
              allow_slow_non_contiguous=True)

        def pipelined(t0s, load):
            nxt = load(t0s[0])
            for i, t0 in enumerate(t0s):
                cur = nxt
                if i + 1 < len(t0s):
                    nxt = load(t0s[i + 1])
                yield t0, cur

        def pipelined2(t0s, load, prep):
            cur = None
            for t0, ld in pipelined(t0s, load):
                pr = prep(t0, ld)
                if cur is not None:
                    yield cur
                cur = (t0, ld, pr)
            if cur is not None:
                yield cur

        T0S = list(range(0, TTOT, 512))

        def kind_of_tile(tok0):
            return 0 if tok0 < LOFF else 1

        with ExitStack() as ph:
            xin = k.pool("xin", [128, D], F32, 2, ph)
            stg = k.pool("stg", [128, 8, 512], F32, 2, ph)
            for t0 in range(0, TTOT, 512):
                st = stg.next()
                for b in range(4):
                    tok = t0 + b * 128
                    xi = xin.next()
                    src = (I["x_ctx"][tok:tok + 128, :] if tok < LOFF
                           else I["x_lat"][tok - LOFF:tok - LOFF + 128, :])
                    k.dma("sp", xi[:], src, writes=[xi])
                    for c in range(8):
                        p = ps.next()
                        k.op("pe", lambda e: e.transpose(p[:, 0:128], xi[:, c * 128:(c + 1) * 128], ident[:]),
                             reads=[xi, ident], writes=[p])
                        eng = "act" if c % 2 else "dve"
                        if eng == "act":
                            k.op("act", lambda e: e.copy(out=st[:, c, b * 128:(b + 1) * 128], in_=p[:, 0:128]),
                                 reads=[p], writes=[st], acc=not (b == 0 and c == 0))
                        else:
                            k.op("dve", lambda e: e.tensor_copy(out=st[:, c, b * 128:(b + 1) * 128], in_=p[:, 0:128]),
                                 reads=[p], writes=[st], acc=not (b == 0 and c == 0))
                k.dma("pool", X[0][:, t0:t0 + 512].rearrange("(c p) t -> p c t", p=128), st[:],
                      reads=[st], writes=[X[0]], acc=True)
            k.barrier()

        def rms_stats(ph_tiles, xt, ntok, sq, rstd):
            k.op("act", lambda e: e.activation(out=sq[:, :, 0:ntok], in_=xt[:, :, 0:ntok], func=AF.Square),
                 reads=[xt], writes=[sq])
            p = ps.next()
            for c in range(8):
                k.op("pe", lambda e: e.matmul(p[:, 0:ntok], lhsT=ones_bf[:], rhs=sq[:, c, 0:ntok],
                                              start=(c == 0), stop=(c == 7)),
                     reads=[ones_bf, sq], writes=[p], inc=(c == 7))
            k.op("act", lambda e: e.activation(out=rstd[:, 0:ntok], in_=p[:, 0:ntok], func=AF.Sqrt,
                                               scale=1.0 / D, bias=epsb[:, 0:1]),
                 reads=[p, epsb], writes=[rstd])
            k.op("dve", lambda e: e.reciprocal(out=rstd[:, 0:ntok], in_=rstd[:, 0:ntok]),
                 reads=[rstd], writes=[rstd])

        def mod_norm(xt, ntok, rstd, tmpp, hb, modt, ia, ib, kind):
            for c in range(8):
                tmp = tmpp.next()
                k.op("dve", lambda e: e.tensor_tensor(out=tmp[:, 0:ntok], in0=xt[:, c, 0:ntok],
                                                      in1=rstd[:, 0:ntok], op=ALU.mult),
                     reads=[xt, rstd], writes=[tmp])
                k.op("act", lambda e: e.activation(out=hb[:, c, 0:ntok], in_=tmp[:, 0:ntok], func=AF.Identity,
                                                   scale=modt[:, ia, c, kind:kind + 1],
                                                   bias=modt[:, ib, c, kind:kind + 1]),
                     reads=[tmp, modt], writes=[hb], acc=(c > 0))


        SEQS = [(i * TC, TC, False, i) for i in range(NCTX)] + [(LOFF, TL, True, 0)]
        MLA_SCALE = 96 ** -0.5
        SWA_SCALE = 64 ** -0.5

        def evac(i, out, in_, reads, writes, acc=False):
            if i % 2:
                k.op("act", lambda e: e.copy(out=out, in_=in_), reads=reads, writes=writes, acc=acc)
            else:
                k.op("dve", lambda e: e.tensor_copy(out=out, in_=in_), reads=reads, writes=writes, acc=acc)

        def rstd_from_ps(p, n, rstd, dim):
            k.op("act", lambda e: e.activation(out=rstd[:, 0:n], in_=p[:, 0:n], func=AF.Sqrt,
                                               scale=1.0 / dim, bias=epsb[:, 0:1]),
                 reads=[p, epsb], writes=[rstd])
            k.op("dve", lambda e: e.reciprocal(out=rstd[:, 0:n], in_=rstd[:, 0:n]), reads=[rstd], writes=[rstd])

        def attn_core(kT, vt, h_v, qT, NQ, NKB, scale, ptp, masks=None, sink=None, kb_list=None, bg=None):
            po = pacc.next()
            blocks = kb_list if kb_list is not None else [(kb, None) for kb in range(NKB)]
            n = len(blocks)
            LOOK = 3
            pSs = [None] * n

            def issue_s(i):
                kb = blocks[i][0]
                pS = psA.next()
                k.op("pe", lambda e: e.matmul(pS[:, 0:NQ], lhsT=kT[:, kb * 128:(kb + 1) * 128], rhs=qT[:, 0:NQ],
                                              start=True, stop=True), reads=[kT, qT], writes=[pS])
                pSs[i] = pS

            first = True
            if sink is not None:
                e64, srow = sink
                k.op("pe", lambda e: e.matmul(po[0:65, 0:NQ], lhsT=e64[0:1, 0:65], rhs=srow[0:1, 0:NQ],
                                              start=True, stop=False), reads=[e64, srow], writes=[po])
                first = False
            for i in range(min(LOOK, n)):
                issue_s(i)
            for bi, (kb, mk) in enumerate(blocks):
                if bi + LOOK < n:
                    issue_s(bi + LOOK)
                if bg is not None:
                    next(bg, None)
                pS = pSs[bi]
                pt = ptp.next()
                k.op("act", lambda e: e.activation(out=pt[:, 0:NQ], in_=pS[:, 0:NQ], func=AF.Exp, scale=scale),
                     reads=[pS], writes=[pt])
                if mk is not None:
                    k.op("dve", lambda e: e.tensor_tensor(out=pt[:, 0:NQ], in0=pt[:, 0:NQ], in1=mk[:, 0:NQ], op=ALU.mult),
                         reads=[pt, mk], writes=[pt])
                last = (bi == n - 1)
                k.op("pe", lambda e: e.matmul(po[0:65, 0:NQ], lhsT=vt[:, kb, h_v, :], rhs=pt[:, 0:NQ],
                                              start=first, stop=last), reads=[vt, pt], writes=[po])
                first = False
            return po

        def attn_finish(po, NQ, rowbuf, bcs, ostg_p, dst_ap, dst_res):
            k.op("act", lambda e: e.activation(out=rowbuf[64:65, 0:NQ], in_=po[64:65, 0:NQ], func=AF.Ln),
                 reads=[po], writes=[rowbuf])
            k.op("act", lambda e: e.activation(out=rowbuf[64:65, 0:NQ], in_=rowbuf[64:65, 0:NQ], func=AF.Exp, scale=-1.0),
                 reads=[rowbuf], writes=[rowbuf])
            pb = psA.next()
            k.op("pe", lambda e: e.matmul(pb[0:64, 0:NQ], lhsT=ones_f[64:65, 0:64], rhs=rowbuf[64:65, 0:NQ],
                                          start=True, stop=True), reads=[ones_f, rowbuf], writes=[pb])
            k.op("act", lambda e: e.copy(out=bcs[0:64, 0:NQ], in_=pb[0:64, 0:NQ]), reads=[pb], writes=[bcs])
            og = ostg_p.next()
            k.op("dve", lambda e: e.tensor_tensor(out=og[0:64, 0:NQ], in0=po[0:64, 0:NQ], in1=bcs[0:64, 0:NQ],
                                                  op=ALU.mult), reads=[po, bcs], writes=[og])
            k.dma("pool", dst_ap, og[0:64, 0:NQ], reads=[og], writes=[dst_res], acc=True)

        def mla_phase(l):
            with ExitStack() as ph:
                NKMAX = TL + 256
                wuq = k.tile("wuq", [128, 2, 4, 128], BF16, ph)
                wuqs = k.tile("wuqs", [128, 2, 4, 64], BF16, ph)
                wkk = k.tile("wkk", [128, 4, 128], BF16, ph)
                wkv = k.tile("wkv", [128, 256], BF16, ph)
                k.op("dve", lambda e: e.memset(wuq[:], 0.0), writes=[wuq])
                k.op("dve", lambda e: e.memset(wuqs[:], 0.0), writes=[wuqs])
                k.op("dve", lambda e: e.memset(wkk[:], 0.0), writes=[wkk])
                uq = I["mla_w_uq"][l]
                ukv = I["mla_w_ukv"][l]
                for h in range(4):
                    def ld(dst, src):
                        k.dma("pool", dst, src.rearrange("(c p) n -> p c n", p=128), writes=[wuq, wuqs], acc=True)
                    ld(wuq[:, :, h, 64:128], uq[:, 96 * h:96 * h + 64])
                    ld(wuq[:, :, h, 32:64], uq[:, 96 * h + 64:96 * h + 96])
                    ld(wuqs[:, :, h, 32:48], uq[:, 96 * h + 80:96 * h + 96])
                    ld(wuqs[:, :, h, 48:64], uq[:, 96 * h + 64:96 * h + 80])
                    k.dma("pool", wkk[:, h, 64:128], ukv[:, 128 * h:128 * h + 64], writes=[wkk], acc=True)
                    k.dma("pool", wkv[:, 64 * h:64 * h + 64], ukv[:, 128 * h + 64:128 * h + 128], writes=[wkv], acc=True)
                qnw = k.tile("qnw", [128, 2], F32, ph)
                k.dma("sp", qnw[:], I["mla_q_norm_w"][l].rearrange("(c p) -> p c", p=128), writes=[qnw],
                      allow_slow_non_contiguous=True)
                kvnw = k.tile("kvnw", [128, 1], F32, ph)
                k.dma("sp", kvnw[:], I["mla_kv_norm_w"][l].rearrange("(c p) -> p c", p=128), writes=[kvnw],
                      allow_slow_non_contiguous=True)
                CM = k.tile("CM", [64, TL], F32, ph)
                SM = k.tile("SM", [64, TL], F32, ph)
                k.dma("sp", CM[32:64, :], I["rope_m"][0], writes=[CM])
                k.dma("sp", SM[32:64, :], I["rope_m"][1], writes=[SM])
                ckvT = k.tile("ckvT", [128, NKMAX], BF16, ph)
                kpeT = k.tile("kpeT", [64, NKMAX], BF16, ph)
                kTm = [k.tile("kTm%d" % h, [128, NKMAX], BF16, ph) for h in range(4)]
                for h in range(4):
                    k.op("pool", lambda e: e.memset(kTm[h][0:32, :], 0.0), writes=[kTm[h]])
                    k.op("pool", lambda e: e.memset(kTm[h][0:1, :], 1.0), writes=[kTm[h]])
                vm = k.tile("vm", [128, NKMAX // 128, 4, 65], BF16, ph)
                k.op("pool", lambda e: e.memset(vm[:], 1.0), writes=[vm])
                kmx = k.tile("kmx", [1, 4, 16], F32, ph)
                nkmax = k.tile("nkmax", [1, 4], F32, ph)
                kvp = k.pool("kvp", [128, 512], BF16, 2, ph)
                sqp = k.pool("sqm", [128, 2, 512], BF16, 2, ph)
                for t_ in sqp.tiles:
                    k.op("dve", lambda e: e.memset(t_[:], 0.0), writes=[t_])
                sqb = k.tile("sqb", [128, 512], BF16, ph)
                k.op("dve", lambda e: e.memset(sqb[:], 0.0), writes=[sqb])
                rsp = k.pool("rsm", [128, 512], F32, 2, ph)
                f32p = k.pool("f32m", [128, 512], F32, 3, ph)
                kxp = k.pool("kxp", [64, 2, 512], BF16, 2, ph)
                qlp = k.pool("qlp", [128, 2, 512], BF16, 2, ph)
                qnp = k.pool("qnp", [128, 2, 512], BF16, 2, ph)
                qTp = k.pool("qTp", [128, 512], BF16, 8, ph)
                rowp = k.pool("rowpm", [1, 512], F32, 3, ph)
                for t_ in qTp.tiles:
                    k.op("dve", lambda e: e.memset(t_[:], 0.0), writes=[t_])
                ptp = k.pool("ptp", [128, 512], BF16, 6, ph)
                rowbuf = k.tile("rowbuf", [128, 512], F32, ph)
                bcs = k.tile("bcs", [64, 512], F32, ph)
                ogp = k.pool("ogp", [64, 512], BF16, 3, ph)
                tkp = k.pool("tkp", [128, 2, 128], F32, 2, ph)
                otp = k.pool("otp", [128, 128], F32, 2, ph)

                for (off, T, lat, si) in SEQS:
                    k.barrier()
                    TT = min(512, T)
                    koff = 256 if lat else 0
                    NK = T + koff
                    NKB = NK // 128
                    if lat:
                        ck = tkp.next()
                        k.dma("sp", ck[:], I["cache_ckv"][l].rearrange("(b p) f -> p b f", p=128), writes=[ck])
                        for b in range(2):
                            p = ps.next()
                            k.op("pe", lambda e: e.transpose(p[:, 0:128], ck[:, b, :], ident[:]),
                                 reads=[ck, ident], writes=[p])
                            evac(b, ckvT[:, b * 128:(b + 1) * 128], p[:, 0:128], [p], [ckvT], acc=True)
                        kp = tkp.next()
                        k.op("dve", lambda e: e.memset(kp[:], 0.0), writes=[kp])
                        k.dma("sp", kp[:, :, 32:64], I["cache_kpe"][l].rearrange("(b p) f -> p b f", p=128),
                              writes=[kp], acc=True)
                        for b in range(2):
                            p = ps.next()
                            k.op("pe", lambda e: e.transpose(p[:, 0:128], kp[:, b, :], ident[:]),
                                 reads=[kp, ident], writes=[p])
                            evac(b, kpeT[32:64, b * 128:(b + 1) * 128], p[32:64, 0:128], [p], [kpeT], acc=True)
                    for ti, t0 in enumerate(range(0, T, TT)):
                        g0 = off + t0
                        kv = kvp.next()
                        k.dma("sp", kv[:, 0:TT], PFM[l][R_MKV:R_MKV + 128, g0:g0 + TT], reads=[PFM[l]], writes=[kv])
                        sq = sqp.next()
                        k.op("act", lambda e: e.activation(out=sq[:, 0, 0:TT], in_=kv[:, 0:TT], func=AF.Square),
                             reads=[kv], writes=[sq])
                        p = ps.next()
                        k.op("pe", lambda e: e.matmul(p[:, 0:TT], lhsT=ones_bf[:], rhs=sq[:, 0, 0:TT], start=True, stop=True),
                             reads=[ones_bf, sq], writes=[p])
                        rstd = rsp.next()
                        rstd_from_ps(p, TT, rstd, 128)
                        cf = f32p.next()
                        k.op("dve", lambda e: e.scalar_tensor_tensor(out=cf[:, 0:TT], in0=kv[:, 0:TT], scalar=kvnw[:, 0:1],
                                                                     in1=rstd[:, 0:TT], op0=ALU.mult, op1=ALU.mult),
                             reads=[kv, kvnw, rstd], writes=[cf])
                        k.op("act", lambda e: e.copy(out=ckvT[:, koff + t0:koff + t0 + TT], in_=cf[:, 0:TT]),
                             reads=[cf], writes=[ckvT], acc=True)
                        kx = kxp.next()
                        k.dma("sp", kx[32:64, 0, 0:TT], PFM[l][R_MKPE:R_MKPE + 32, g0:g0 + TT], reads=[PFM[l]], writes=[kx])
                        k.dma("sp", kx[32:64, 1, 0:TT], PFM[l][R_MKPE + 32:R_MKPE + 64, g0:g0 + TT], reads=[PFM[l]],
                              writes=[kx], acc=True)
                        if lat:
                            t1 = f32p.next()
                            t2 = f32p.next()
                            k.op("dve", lambda e: e.tensor_tensor(out=t1[32:64, 0:TT], in0=kx[32:64, 0, 0:TT],
                                                                  in1=CM[32:64, t0:t0 + TT], op=ALU.mult),
                                 reads=[kx, CM], writes=[t1])
                            k.op("pool", lambda e: e.tensor_tensor(out=t2[32:64, 0:TT], in0=kx[32:64, 1, 0:TT],
                                                                   in1=SM[32:64, t0:t0 + TT], op=ALU.mult),
                                 reads=[kx, SM], writes=[t2])
                            k.op("dve", lambda e: e.tensor_tensor(out=kpeT[32:64, koff + t0:koff + t0 + TT],
                                                                  in0=t1[32:64, 0:TT], in1=t2[32:64, 0:TT], op=ALU.add),
                                 reads=[t1, t2], writes=[kpeT], acc=True)
                        else:
                            k.op("dve", lambda e: e.tensor_copy(out=kpeT[32:64, t0:t0 + TT], in_=kx[32:64, 0, 0:TT]),
                                 reads=[kx], writes=[kpeT], acc=True)
                            kf = f32p.next()
                            k.op("dve", lambda e: e.memset(kf[:, 0:TT], 0.0), writes=[kf])
                            k.op("act", lambda e: e.copy(out=kf[32:64, 0:TT], in_=kx[32:64, 0, 0:TT]),
                                 reads=[kx], writes=[kf])
                            for b in range(TT // 128):
                                p = ps.next()
                                k.op("pe", lambda e: e.transpose(p[:, 0:128], cf[:, b * 128:(b + 1) * 128], ident[:]),
                                     reads=[cf, ident], writes=[p])
                                ot = otp.next()
                                evac(b, ot[:, :], p[:, 0:128], [p], [ot])
                                k.dma("pool", O["new_ckv"][si, l, t0 + b * 128:t0 + (b + 1) * 128, :], ot[:, :], reads=[ot])
                                p = ps.next()
                                k.op("pe", lambda e: e.transpose(p[:, 0:128], kf[:, b * 128:(b + 1) * 128], ident[:]),
                                     reads=[kf, ident], writes=[p])
                                ot = otp.next()
                                evac(b + 1, ot[:, 0:32], p[:, 32:64], [p], [ot])
                                k.dma("pool", O["new_kpe"][si, l, t0 + b * 128:t0 + (b + 1) * 128, :], ot[:, 0:32], reads=[ot])
                    ntile = (NK + 511) // 512
                    for ti in range(ntile):
                        c0 = ti * 512
                        n = min(512, NK - c0)
                        for h in range(4):
                            p = ps.next()
                            k.op("pe", lambda e: e.matmul(p[:, 0:n], lhsT=wkk[:, h, :], rhs=ckvT[:, c0:c0 + n],
                                                          start=True, stop=True), reads=[wkk, ckvT], writes=[p])
                            evac(h, kTm[h][64:128, c0:c0 + n], p[64:128, 0:n], [p], [kTm[h]], acc=True)
                            k.op("pool", lambda e: e.tensor_copy(out=kTm[h][32:64, c0:c0 + n], in_=kpeT[32:64, c0:c0 + n]),
                                 reads=[kpeT], writes=[kTm[h]], acc=True)
                            for (a0, a1) in ((32, 64), (64, 128)):
                                k.op("act", lambda e: e.activation(out=sqb[a0:a1, 0:n], in_=kTm[h][a0:a1, c0:c0 + n],
                                                                   func=AF.Square), reads=[kTm[h]], writes=[sqb], acc=(a0 == 64))
                            p2 = ps.next()
                            k.op("pe", lambda e: e.matmul(p2[0:1, 0:n], lhsT=ones_bf[:, 0:1], rhs=sqb[:, 0:n],
                                                          start=True, stop=True), reads=[ones_bf, sqb], writes=[p2])
                            k.op("dve", lambda e: e.reduce_max(out=kmx[0:1, h, ti:ti + 1], in_=p2[0:1, 0:n], axis=AX.X),
                                 reads=[p2], writes=[kmx], acc=True)
                    for kb in range(NKB):
                        p = ps.next()
                        k.op("pe", lambda e: e.matmul(p[:, 0:256], lhsT=ckvT[:, kb * 128:(kb + 1) * 128], rhs=wkv[:, :],
                                                      start=True, stop=True), reads=[ckvT, wkv], writes=[p])
                        evac(kb, vm[:, kb, :, 0:64], p[:, 0:256].rearrange("p (h d) -> p h d", h=4), [p], [vm], acc=True)
                    k.op("dve", lambda e: e.reduce_max(out=nkmax[0:1, :], in_=kmx[0:1, :, 0:ntile], axis=AX.X),
                         reads=[kmx], writes=[nkmax])
                    k.op("act", lambda e: e.activation(out=nkmax[0:1, :], in_=nkmax[0:1, :], func=AF.Sqrt),
                         reads=[nkmax], writes=[nkmax])
                    k.op("dve", lambda e: e.tensor_scalar(out=nkmax[0:1, :], in0=nkmax[0:1, :], scalar1=-1.0, scalar2=None,
                                                          op0=ALU.mult), reads=[nkmax], writes=[nkmax])
                    def mla_prep_gen(t0, res):
                        g0 = off + t0
                        ql = qlp.next()
                        k.dma("sp", ql[:, :, 0:TT], PFM[l][R_MQ:R_MQ + 256, g0:g0 + TT].rearrange("(c p) t -> p c t", p=128),
                              reads=[PFM[l]], writes=[ql])
                        sq = sqp.next()
                        k.op("act", lambda e: e.activation(out=sq[:, :, 0:TT], in_=ql[:, :, 0:TT], func=AF.Square),
                             reads=[ql], writes=[sq])
                        p = psP.next()
                        for c in range(2):
                            k.op("pe", lambda e: e.matmul(p[:, 0:TT], lhsT=ones_bf[:], rhs=sq[:, c, 0:TT],
                                                          start=(c == 0), stop=(c == 1)), reads=[ones_bf, sq], writes=[p], inc=(c == 1))
                        yield
                        rstd = rsp.next()
                        rstd_from_ps(p, TT, rstd, 256)
                        yield
                        qn = qnp.next()
                        for c in range(2):
                            k.op("dve", lambda e: e.scalar_tensor_tensor(out=qn[:, c, 0:TT], in0=ql[:, c, 0:TT],
                                                                         scalar=qnw[:, c:c + 1], in1=rstd[:, 0:TT],
                                                                         op0=ALU.mult, op1=ALU.mult),
                                 reads=[ql, qnw, rstd], writes=[qn], acc=(c > 0))
                        yield
                        qTs_h = []
                        for h in range(4):
                            p1 = psP.next()
                            for c in range(2):
                                k.op("pe", lambda e: e.matmul(p1[:, 0:TT], lhsT=wuq[:, c, h, :], rhs=qn[:, c, 0:TT],
                                                              start=(c == 0), stop=(c == 1)), reads=[wuq, qn], writes=[p1], inc=(c == 1))
                            yield
                            qT = qTp.next()
                            k.op("act", lambda e: e.copy(out=qT[64:128, 0:TT], in_=p1[64:128, 0:TT]), reads=[p1], writes=[qT])
                            if lat:
                                p2 = psP.next()
                                for c in range(2):
                                    k.op("pe", lambda e: e.matmul(p2[0:64, 0:TT], lhsT=wuqs[:, c, h, :], rhs=qn[:, c, 0:TT],
                                                                  start=(c == 0), stop=(c == 1)), reads=[wuqs, qn], writes=[p2], inc=(c == 1))
                                t1 = f32p.next()
                                t2 = f32p.next()
                                k.op("dve", lambda e: e.tensor_tensor(out=t1[32:64, 0:TT], in0=p1[32:64, 0:TT],
                                                                      in1=CM[32:64, t0:t0 + TT], op=ALU.mult),
                                     reads=[p1, CM], writes=[t1])
                                k.op("dve", lambda e: e.tensor_tensor(out=t2[32:64, 0:TT], in0=p2[32:64, 0:TT],
                                                                      in1=SM[32:64, t0:t0 + TT], op=ALU.mult),
                                     reads=[p2, SM], writes=[t2])
                                k.op("pool", lambda e: e.tensor_tensor(out=qT[32:64, 0:TT], in0=t1[32:64, 0:TT],
                                                                       in1=t2[32:64, 0:TT], op=ALU.add),
                                     reads=[t1, t2], writes=[qT], acc=True)
                            else:
                                k.op("dve", lambda e: e.tensor_copy(out=qT[32:64, 0:TT], in_=p1[32:64, 0:TT]),
                                     reads=[p1], writes=[qT], acc=True)
                            yield
                            for (a0, a1) in ((32, 64), (64, 128)):
                                k.op("act", lambda e: e.activation(out=sqb[a0:a1, 0:TT], in_=qT[a0:a1, 0:TT], func=AF.Square),
                                     reads=[qT], writes=[sqb], acc=(a0 == 64))
                            pn = psP.next()
                            k.op("pe", lambda e: e.matmul(pn[0:1, 0:TT], lhsT=ones_bf[:, 0:1], rhs=sqb[:, 0:TT],
                                                          start=True, stop=True), reads=[ones_bf, sqb], writes=[pn])
                            yield
                            rw = rowp.next()
                            k.op("act", lambda e: e.activation(out=rw[0:1, 0:TT], in_=pn[0:1, 0:TT], func=AF.Sqrt),
                                 reads=[pn], writes=[rw])
                            k.op("dve", lambda e: e.tensor_scalar(out=qT[0:1, 0:TT], in0=rw[0:1, 0:TT],
                                                                  scalar1=nkmax[0:1, h:h + 1], scalar2=None, op0=ALU.mult),
                                 reads=[rw, nkmax], writes=[qT], acc=True)
                            qTs_h.append(qT)
                            yield

                        res[t0] = (g0, qTs_h)

                    t0s = list(range(0, T, TT))
                    res = {}
                    for _ in mla_prep_gen(t0s[0], res):
                        pass
                    for i_, t0 in enumerate(t0s):
                        g0, qTs_h = res.pop(t0)
                        bg = mla_prep_gen(t0s[i_ + 1], res) if i_ + 1 < len(t0s) else None
                        for h in range(4):
                            po = attn_core(kTm[h], vm, h, qTs_h[h], TT, NKB, MLA_SCALE, ptp, bg=bg)
                            attn_finish(po, TT, rowbuf, bcs, ogp,
                                        MIX[l][256 + 64 * h:256 + 64 * h + 64, g0:g0 + TT], MIX[l])
                        if bg is not None:
                            for _ in bg:
                                pass
                k.barrier()


        def swa_phase(l):
            with ExitStack() as ph:
                NKMAX = TL + 256
                CS = k.tile("CS", [128, TL], F32, ph)
                SS = k.tile("SS", [128, TL], F32, ph)
                k.dma("sp", CS[64:128, :], I["rope_s"][0], writes=[CS])
                k.dma("sp", SS[64:128, :], I["rope_s"][1], writes=[SS])
                mk = k.tile("mk", [128, 6, 512], BF16, ph)
                k.dma("pool", mk[:], I["swa_mask"].rearrange("r p q -> p r q"), writes=[mk])
                sk = k.tile("sk", [1, 4], F32, ph)
                k.dma("sp", sk[:], I["swa_sinks"][l:l + 1, :], writes=[sk])
                e64 = k.tile("e64", [1, 65], BF16, ph)
                k.op("dve", lambda e: e.memset(e64[:], 0.0), writes=[e64])
                k.op("dve", lambda e: e.memset(e64[0:1, 64:65], 1.0), writes=[e64])
                kTs = [k.tile("kTs%d" % h, [128, NKMAX], BF16, ph) for h in range(2)]
                for h in range(2):
                    k.op("pool", lambda e: e.memset(kTs[h][0:64, :], 0.0), writes=[kTs[h]])
                    k.op("pool", lambda e: e.memset(kTs[h][0:1, :], 1.0), writes=[kTs[h]])
                vs = k.tile("vs", [128, NKMAX // 128, 2, 65], BF16, ph)
                k.op("pool", lambda e: e.memset(vs[:], 1.0), writes=[vs])
                kmx = k.tile("kmx", [1, 2, 16], F32, ph)
                nkmax = k.tile("nkmax", [1, 2], F32, ph)
                sqb = k.tile("sqb", [128, 512], BF16, ph)
                k.op("dve", lambda e: e.memset(sqb[:], 0.0), writes=[sqb])
                f32p = k.pool("f32s", [128, 512], F32, 3, ph)
                kxp = k.pool("kxs", [128, 2, 512], BF16, 4, ph)
                qTp = k.pool("qTs", [128, 512], BF16, 6, ph)
                rowp = k.pool("rowps", [1, 512], F32, 3, ph)
                for t_ in qTp.tiles:
                    k.op("dve", lambda e: e.memset(t_[:], 0.0), writes=[t_])
                ptp = k.pool("pts", [128, 512], BF16, 6, ph)
                rowbuf = k.tile("rowbufs", [128, 512], F32, ph)
                srowp = k.pool("srow", [1, 512], BF16, 6, ph)
                bcs = k.tile("bcss", [64, 512], F32, ph)
                ogp = k.pool("ogs", [64, 512], BF16, 3, ph)
                ckp = k.pool("cks", [128, 128], F32, 2, ph)
                for t_ in ckp.tiles:
                    k.op("dve", lambda e: e.memset(t_[:], 0.0), writes=[t_])
                otp = k.pool("ots", [128, 128], F32, 2, ph)

                def rope(dst_ap, dst_res, x, t0, TT, lat, acc=True):
                    if lat:
                        t1 = f32p.next()
                        t2 = f32p.next()
                        k.op("dve", lambda e: e.tensor_tensor(out=t1[64:128, 0:TT], in0=x[64:128, 0, 0:TT],
                                                              in1=CS[64:128, t0:t0 + TT], op=ALU.mult),
                             reads=[x, CS], writes=[t1])
                        k.op("pool", lambda e: e.tensor_tensor(out=t2[64:128, 0:TT], in0=x[64:128, 1, 0:TT],
                                                               in1=SS[64:128, t0:t0 + TT], op=ALU.mult),
                             reads=[x, SS], writes=[t2])
                        k.op("dve", lambda e: e.tensor_tensor(out=dst_ap, in0=t1[64:128, 0:TT], in1=t2[64:128, 0:TT],
                                                              op=ALU.add), reads=[t1, t2], writes=[dst_res], acc=acc)
                    else:
                        k.op("dve", lambda e: e.tensor_copy(out=dst_ap, in_=x[64:128, 0, 0:TT]), reads=[x],
                             writes=[dst_res], acc=acc)

                for (off, T, lat, si) in SEQS:
                    k.barrier()
                    TT = min(512, T)
                    koff = 256 if lat else 0
                    NK = T + koff
                    NKB = NK // 128
                    if lat:
                        for kv in range(2):
                            for b in range(2):
                                ck = ckp.next()
                                k.dma("sp", ck[:, 64:128], I["cache_swk"][l, b * 128:(b + 1) * 128, kv, :], writes=[ck])
                                p = ps.next()
                                k.op("pe", lambda e: e.transpose(p[:, 0:128], ck[:, :], ident[:]), reads=[ck, ident], writes=[p])
                                evac(b, kTs[kv][64:128, b * 128:(b + 1) * 128], p[64:128, 0:128], [p], [kTs[kv]], acc=True)
                        for b in range(2):
                            k.dma("pool", vs[:, b, :, 0:64], I["cache_swv"][l, b * 128:(b + 1) * 128, :, :],
                                  writes=[vs], acc=True)
                    else:
                        k.dma("pool", O["new_swv"][si, l].rearrange("t k d -> t (k d)"),
                              PTM[l][off:off + T, C_SWV:C_SWV + 128], reads=[PTM[l]])
                    for b in range(T // 128):
                        k.dma("pool", vs[:, koff // 128 + b, :, 0:64],
                              PTM[l][off + b * 128:off + (b + 1) * 128, C_SWV:C_SWV + 128].rearrange("p (k d) -> p k d", k=2),
                              reads=[PTM[l]], writes=[vs], acc=True)
                    for kv in range(2):
                        for ti, t0 in enumerate(range(0, T, TT)):
                            g0 = off + t0
                            kx = kxp.next()
                            k.dma("sp", kx[64:128, 0, 0:TT], PFM[l][R_SWK + 64 * kv:R_SWK + 64 * kv + 64, g0:g0 + TT],
                                  reads=[PFM[l]], writes=[kx])
                            k.dma("sp", kx[64:128, 1, 0:TT], PFM[l][R_SWKS + 64 * kv:R_SWKS + 64 * kv + 64, g0:g0 + TT],
                                  reads=[PFM[l]], writes=[kx], acc=True)
                            rope(kTs[kv][64:128, koff + t0:koff + t0 + TT], kTs[kv], kx, t0, TT, lat)
                            if not lat:
                                kf = f32p.next()
                                k.op("dve", lambda e: e.memset(kf[0:64, 0:TT], 0.0), writes=[kf])
                                k.op("act", lambda e: e.copy(out=kf[64:128, 0:TT], in_=kx[64:128, 0, 0:TT]),
                                     reads=[kx], writes=[kf], acc=True)
                                for b in range(TT // 128):
                                    p = ps.next()
                                    k.op("pe", lambda e: e.transpose(p[:, 0:128], kf[:, b * 128:(b + 1) * 128], ident[:]),
                                         reads=[kf, ident], writes=[p])
                                    ot = otp.next()
                                    evac(b, ot[:, 0:64], p[:, 64:128], [p], [ot])
                                    k.dma("pool", O["new_swk"][si, l, t0 + b * 128:t0 + (b + 1) * 128, kv, :], ot[:, 0:64], reads=[ot])
                        ntile = (NK + 511) // 512
                        for ti in range(ntile):
                            c0 = ti * 512
                            n = min(512, NK - c0)
                            k.op("act", lambda e: e.activation(out=sqb[64:128, 0:n], in_=kTs[kv][64:128, c0:c0 + n],
                                                               func=AF.Square), reads=[kTs[kv]], writes=[sqb])
                            p2 = ps.next()
                            k.op("pe", lambda e: e.matmul(p2[0:1, 0:n], lhsT=ones_bf[:, 0:1], rhs=sqb[:, 0:n],
                                                          start=True, stop=True), reads=[ones_bf, sqb], writes=[p2])
                            k.op("dve", lambda e: e.reduce_max(out=kmx[0:1, kv, ti:ti + 1], in_=p2[0:1, 0:n], axis=AX.X),
                                 reads=[p2], writes=[kmx], acc=True)
                    ntile = (NK + 511) // 512
                    k.op("dve", lambda e: e.reduce_max(out=nkmax[0:1, :], in_=kmx[0:1, :, 0:ntile], axis=AX.X),
                         reads=[kmx], writes=[nkmax])
                    k.op("act", lambda e: e.activation(out=nkmax[0:1, :], in_=nkmax[0:1, :], func=AF.Sqrt),
                         reads=[nkmax], writes=[nkmax])
                    k.op("dve", lambda e: e.tensor_scalar(out=nkmax[0:1, :], in0=nkmax[0:1, :], scalar1=-1.0, scalar2=None,
                                                          op0=ALU.mult), reads=[nkmax], writes=[nkmax])
                    for t0 in range(0, T, TT):
                        g0 = off + t0
                        i0 = t0 // 128
                        if lat:
                            kbl = [(0, None), (1, None)]
                            for r in range(6):
                                kbo = i0 - 1 + r
                                if 0 <= kbo < T // 128:
                                    kbl.append((2 + kbo, Res("mkv", mk.t[:, r, :])))
                            for (_, m_) in kbl:
                                if m_ is not None:
                                    m_.w = mk.w
                        else:
                            kbl = [(b, None) for b in range(NKB)]
                        prep_h = []
                        for h in range(4):
                            kv = h // 2
                            qx = kxp.next()
                            k.dma("sp", qx[64:128, 0, 0:TT], PFM[l][R_SWQ + 64 * h:R_SWQ + 64 * h + 64, g0:g0 + TT],
                                  reads=[PFM[l]], writes=[qx])
                            k.dma("sp", qx[64:128, 1, 0:TT], PFM[l][R_SWQS + 64 * h:R_SWQS + 64 * h + 64, g0:g0 + TT],
                                  reads=[PFM[l]], writes=[qx], acc=True)
                            qT = qTp.next()
                            rope(qT[64:128, 0:TT], qT, qx, t0, TT, lat, acc=False)
                            k.op("act", lambda e: e.activation(out=sqb[64:128, 0:TT], in_=qT[64:128, 0:TT], func=AF.Square),
                                 reads=[qT], writes=[sqb])
                            pn = ps.next()
                            k.op("pe", lambda e: e.matmul(pn[0:1, 0:TT], lhsT=ones_bf[:, 0:1], rhs=sqb[:, 0:TT],
                                                          start=True, stop=True), reads=[ones_bf, sqb], writes=[pn])
                            rw = rowp.next()
                            k.op("act", lambda e: e.activation(out=rw[0:1, 0:TT], in_=pn[0:1, 0:TT], func=AF.Sqrt),
                                 reads=[pn], writes=[rw])
                            k.op("dve", lambda e: e.tensor_scalar(out=qT[0:1, 0:TT], in0=rw[0:1, 0:TT],
                                                                  scalar1=nkmax[0:1, kv:kv + 1], scalar2=None, op0=ALU.mult),
                                 reads=[rw, nkmax], writes=[qT], acc=True)
                            srow = srowp.next()
                            k.op("act", lambda e: e.activation(out=srow[0:1, 0:TT], in_=qT[0:1, 0:TT], func=AF.Exp,
                                                               scale=SWA_SCALE, bias=sk[0:1, h:h + 1]),
                                 reads=[qT, sk], writes=[srow])
                            prep_h.append((qT, srow))
                        for h in range(4):
                            kv = h // 2
                            qT, srow = prep_h[h]
                            po = attn_core(kTs[kv], vs, kv, qT, TT, NKB, SWA_SCALE, ptp, sink=(e64, srow), kb_list=kbl)
                            attn_finish(po, TT, rowbuf, bcs, ogp,
                                        MIX[l][768 + 64 * h:768 + 64 * h + 64, g0:g0 + TT], MIX[l])
                k.barrier()


        def ssd_phase(l):
            with ExitStack() as ph:
                cm = k.tile("cm", [128, 5, 128], F32, ph)
                k.dma("sp", cm[:], I["cmask"].rearrange("r p q -> p r q")[:, 0:5, :], writes=[cm])
                onesblk, onesA, onesB = cm[:, 2, :], cm[:, 3, :], cm[:, 4, :]
                cw = k.tile("cw", [128, 4, 3], F32, ph)
                for kk in range(3):
                    k.dma("sp", cw[:, :, kk], I["ssm_conv_w"][l, kk].rearrange("(c p) -> p c", p=128), writes=[cw],
                          acc=(kk > 0), allow_slow_non_contiguous=True)
                cb = k.tile("cb", [128, 4], F32, ph)
                k.dma("sp", cb[:], I["ssm_conv_b"][l].rearrange("(c p) -> p c", p=128), writes=[cb],
                      allow_slow_non_contiguous=True)
                dtb = k.tile("dtb", [128, 8], F32, ph)
                k.dma("sp", dtb[:], I["ssm_dt_bias"][l:l + 1].rearrange("o a b -> o (a b)").partition_broadcast(128), writes=[dtb])
                aneg = k.tile("aneg", [128, 8], F32, ph)
                k.dma("sp", aneg[:], I["ssm_a_log"][l:l + 1].rearrange("o a b -> o (a b)").partition_broadcast(128), writes=[aneg])
                k.op("act", lambda e: e.activation(out=aneg[:], in_=aneg[:], func=AF.Exp), reads=[aneg], writes=[aneg])
                k.op("dve", lambda e: e.tensor_scalar(out=aneg[:], in0=aneg[:], scalar1=-1.0, scalar2=None, op0=ALU.mult),
                     reads=[aneg], writes=[aneg])
                Dt = k.tile("Dt", [128, 4], F32, ph)
                k.dma("sp", Dt[:], I["ssm_d"][l:l + 1, :].partition_broadcast(128), writes=[Dt])
                nwt = k.tile("nwt", [128, 256], F32, ph)
                k.dma("sp", nwt[:], I["ssm_norm_w"][l:l + 1, :].partition_broadcast(128), writes=[nwt])
                onec = k.tile("onec", [128, 1], F32, ph)
                k.op("dve", lambda e: e.memset(onec[:], 1.0), writes=[onec])
                NBM = TL // 128
                BT = k.tile("BT", [128, TL], BF16, ph)
                CT = k.tile("CT", [128, TL], BF16, ph)
                x_tok = k.tile("x_tok", [128, NBM, 256], F32, ph)
                B_tok = k.tile("B_tok", [128, NBM, 128], F32, ph)
                dtr = k.tile("dtr", [128, NBM, 8], F32, ph)
                dtt = k.tile("dtt", [128, NBM, 8], F32, ph)
                at = k.tile("at", [128, NBM, 8], F32, ph)
                yacc = k.tile("yacc", [128, NBM, 256], F32, ph)
                S = [k.tile("S%d" % d, [128, 2, 64], F32, ph) for d in range(2)]
                Sb = [k.tile("Sbs%d" % d, [128, 2, 64], BF16, ph) for d in range(2)]
                xinp = k.pool("xin", [128, 4, 514], BF16, 2, ph)
                xTp = k.pool("xTs", [128, 4, 512], F32, 2, ph)
                tmpp = k.pool("tmps", [128, 512], F32, 4, ph)
                WS = []
                for d_ in range(2):
                    WS.append({"stp": k.pool("stt", [128, 24], F32, 2, ph), "exp": k.pool("exs", [128, 24], F32, 2, ph),
                               "GUp": k.pool("GU", [128, 4, 128], F32, 1, ph), "Lp": k.pool("Lp", [128, 4, 128], F32, 1, ph),
                               "L2p": k.pool("L2p", [128, 4, 128], F32, 1, ph), "scp": k.pool("scT", [128, 4, 128], BF16, 2, ph),
                               "xdp": k.pool("xdt", [128, 4, 64], BF16, 2, ph), "typ": k.pool("tmpy", [128, 4, 64], F32, 3, ph),
                               "Bdp": k.pool("Bd", [128, 4, 128], BF16, 2, ph), "ydp": k.pool("yds", [128, 256], F32, 2, ph)})
                zp = k.pool("zs", [128, 256], F32, 2, ph)
                y2p = k.pool("y2s", [128, 256], F32, 4, ph)
                ssp = k.pool("ssum", [128, 4], F32, 2, ph)
                osp = k.pool("oss", [128, 2, 128], BF16, 2, ph)
                sop = k.pool("sos", [128, 128], F32, 2, ph)

                for (off, T, lat, si) in SEQS:
                    k.barrier()
                    NB = T // 128
                    TT = min(512, T)
                    for t0 in range(0, T, TT):
                        g0 = off + t0
                        xin = xinp.next()
                        first = (t0 == 0)
                        lastt = (t0 + TT >= T)
                        lo = g0 if first else g0 - 1
                        hi = g0 + TT if lastt else g0 + TT + 1
                        c_lo = 1 if first else 0
                        k.dma("sp", xin[:, :, c_lo:c_lo + (hi - lo)],
                              PFM[l][R_SSX:R_SSX + 512, lo:hi].rearrange("(c p) t -> p c t", p=128),
                              reads=[PFM[l]], writes=[xin])
                        if first:
                            k.op("dve", lambda e: e.memset(xin[:, :, 0:1], 0.0), writes=[xin], acc=True)
                        if lastt:
                            k.op("dve", lambda e: e.memset(xin[:, :, TT + 1:TT + 2], 0.0), writes=[xin], acc=True)
                        xT = xTp.next()
                        for c in range(4):
                            ta = tmpp.next()
                            tb = tmpp.next()
                            k.op("dve", lambda e: e.tensor_scalar(out=ta[:, 0:TT], in0=xin[:, c, 0:TT], scalar1=cw[:, c, 0:1],
                                                                  scalar2=None, op0=ALU.mult), reads=[xin, cw], writes=[ta])
                            k.op("dve", lambda e: e.scalar_tensor_tensor(out=tb[:, 0:TT], in0=xin[:, c, 1:TT + 1], scalar=cw[:, c, 1:2],
                                                                         in1=ta[:, 0:TT], op0=ALU.mult, op1=ALU.add),
                                 reads=[xin, cw, ta], writes=[tb])
                            k.op("dve", lambda e: e.scalar_tensor_tensor(out=ta[:, 0:TT], in0=xin[:, c, 2:TT + 2], scalar=cw[:, c, 2:3],
                                                                         in1=tb[:, 0:TT], op0=ALU.mult, op1=ALU.add),
                                 reads=[xin, cw, tb], writes=[ta])
                            k.op("act", lambda e: e.activation(out=xT[:, c, 0:TT], in_=ta[:, 0:TT], func=AF.Silu, bias=cb[:, c:c + 1]),
                                 reads=[ta, cb], writes=[xT], acc=(c > 0))
                            if c >= 2:
                                dres = BT if c == 2 else CT
                                k.op("pool", lambda e: e.tensor_copy(out=dres[:, t0:t0 + TT], in_=xT[:, c, 0:TT]), reads=[xT], writes=[dres], acc=True)
                        for b in range(TT // 128):
                            blk = t0 // 128 + b
                            for c in range(2):
                                p = ps.next()
                                k.op("pe", lambda e: e.transpose(p[:, 0:128], xT[:, c, b * 128:(b + 1) * 128], ident[:]),
                                     reads=[xT, ident], writes=[p])
                                evac(c, x_tok[:, blk, c * 128:(c + 1) * 128], p[:, 0:128], [p], [x_tok], acc=True)
                            p = ps.next()
                            k.op("pe", lambda e: e.transpose(p[:, 0:128], xT[:, 2, b * 128:(b + 1) * 128], ident[:]),
                                 reads=[xT, ident], writes=[p])
                            evac(1, B_tok[:, blk, :], p[:, 0:128], [p], [B_tok], acc=True)
                    if debug.get("ssd_stop", 9) <= 1:
                        continue
                    for b0 in range(0, NB, 2):
                        k.dma("sp", dtr[:, b0:b0 + 2, :],
                              PTM[l][off + b0 * 128:off + (b0 + 2) * 128, C_DT:C_DT + 8].rearrange("(b p) j -> p b j", p=128),
                              reads=[PTM[l]], writes=[dtr], acc=(b0 > 0))
                    if debug.get("ssd_stop", 9) <= 2:
                        continue
                    k.op("dve", lambda e: e.tensor_tensor(out=dtt[:, 0:NB, :], in0=dtr[:, 0:NB, :],
                                                          in1=dtb[:].unsqueeze(1).to_broadcast([128, NB, 8]), op=ALU.add),
                         reads=[dtr, dtb], writes=[dtt])
                    k.op("act", lambda e: e.activation(out=dtr[:, 0:NB, :], in_=dtt[:, 0:NB, :], func=AF.Exp), reads=[dtt], writes=[dtr])
                    k.op("act", lambda e: e.activation(out=dtt[:, 0:NB, :], in_=dtr[:, 0:NB, :], func=AF.Ln, bias=onec[:, 0:1]),
                         reads=[dtr, onec], writes=[dtt])
                    k.op("dve", lambda e: e.tensor_tensor(out=at[:, 0:NB, :], in0=dtt[:, 0:NB, :],
                                                          in1=aneg[:].unsqueeze(1).to_broadcast([128, NB, 8]), op=ALU.mult),
                         reads=[dtt, aneg], writes=[at])
                    for d in range(2):
                        if lat:
                            stin = sop.next()
                            for g in range(2):
                                for hh in range(2):
                                    k.dma("sp", stin[hh * 64:(hh + 1) * 64, g * 64:(g + 1) * 64], I["state_ssm"][l, d, 2 * g + hh, :, :],
                                          writes=[stin], acc=not (g == 0 and hh == 0))
                            p = ps.next()
                            k.op("pe", lambda e: e.transpose(p[:, 0:128], stin[:, :], ident[:]), reads=[stin, ident], writes=[p])
                            k.op("dve", lambda e: e.tensor_copy(out=S[d][:].rearrange("p a b -> p (a b)"), in_=p[:, 0:128]),
                                 reads=[p], writes=[S[d]])
                        else:
                            k.op("dve", lambda e: e.memset(S[d][:], 0.0), writes=[S[d]])
                        k.op("act", lambda e: e.copy(out=Sb[d][:], in_=S[d][:]), reads=[S[d]], writes=[Sb[d]])
                    if debug.get("ssd_stop", 9) <= 3:
                        continue
                    ywritten = set()

                    def ssd_dir_gen(d):
                        stp, exp_, GUp, Lp, L2p, scp, xdp, typ, Bdp, ydp = (WS[d][n_] for n_ in ("stp", "exp", "GUp", "Lp", "L2p", "scp", "xdp", "typ", "Bdp", "ydp"))
                        ps = psd[d]
                        U = cm[:, d, :]
                        order = list(range(NB)) if d == 0 else list(range(NB - 1, -1, -1))
                        halves = (0, 1) if d == 0 else (1, 0)
                        for blk in order:
                            tok0 = blk * 128
                            a_blk = at[:, blk, d * 4:(d + 1) * 4]
                            pc = ps.next()
                            for j, lh in enumerate((U, onesA, onesB)):
                                k.op("pe", lambda e: e.matmul(pc[:, 4 * j:4 * j + 4], lhsT=lh, rhs=a_blk, start=True, stop=True),
                                     reads=[cm, at], writes=[pc], inc=(j == 2))
                            st = stp.next()
                            k.op("act", lambda e: e.copy(out=st[:, 0:12], in_=pc[:, 0:12]), reads=[pc], writes=[st])
                            k.op("dve", lambda e: e.tensor_tensor(out=st[0:64, 16:20], in0=st[0:64, 4:8], in1=st[0:64, 0:4], op=ALU.subtract),
                                 reads=[st], writes=[st])
                            k.op("dve", lambda e: e.tensor_tensor(out=st[64:128, 16:20], in0=st[64:128, 8:12], in1=st[64:128, 0:4], op=ALU.subtract),
                                 reads=[st], writes=[st])
                            ex = exp_.next()
                            k.op("act", lambda e: e.activation(out=ex[:, 0:12], in_=st[:, 0:12], func=AF.Exp), reads=[st], writes=[ex])
                            k.op("act", lambda e: e.activation(out=ex[:, 16:20], in_=st[:, 16:20], func=AF.Exp), reads=[st], writes=[ex], acc=True)
                            if debug.get("scan_stop", 9) <= 1:
                                continue
                            yield
                            GU = GUp.next()
                            for h in range(4):
                                k.op("dve", lambda e: e.tensor_scalar(out=GU[:, h, :], in0=U, scalar1=a_blk[:, h:h + 1], scalar2=None, op0=ALU.mult),
                                     reads=[cm, at], writes=[GU], acc=(h > 0))
                            pa = ps.next()
                            k.op("pe", lambda e: e.matmul(pa[:, :], lhsT=onesblk, rhs=GU[:].rearrange("p h i -> p (h i)"), start=True, stop=True),
                                 reads=[cm, GU], writes=[pa])
                            L = Lp.next()
                            for h in range(4):
                                k.op("dve", lambda e: e.tensor_scalar(out=L[:, h, :], in0=pa[:, h * 128:(h + 1) * 128], scalar1=st[:, h:h + 1],
                                                                      scalar2=0.0, op0=ALU.subtract, op1=ALU.min),
                                     reads=[pa, st], writes=[L], acc=(h > 0))
                            L2 = L2p.next()
                            k.op("act", lambda e: e.activation(out=L2[:], in_=L[:], func=AF.Exp), reads=[L], writes=[L2])
                            k.op("pool", lambda e: e.tensor_tensor(out=L[:], in0=L2[:], in1=U.unsqueeze(1).to_broadcast([128, 4, 128]), op=ALU.mult),
                                 reads=[L2, cm], writes=[L])
                            if debug.get("scan_stop", 9) <= 2:
                                continue
                            yield
                            pcbs = [ps.next(), ps.next()]
                            for g in range(2):
                                k.op("pe", lambda e: e.matmul(pcbs[g][:, 0:128], lhsT=BT[g * 64:(g + 1) * 64, tok0:tok0 + 128],
                                                              rhs=CT[g * 64:(g + 1) * 64, tok0:tok0 + 128], start=True, stop=True),
                                     reads=[BT, CT], writes=[pcbs[g]])
                            scT = scp.next()
                            for g in range(2):
                                k.op("dve", lambda e: e.tensor_tensor(out=scT[:, 2 * g:2 * g + 2, :],
                                                                      in0=pcbs[g][:, 0:128].unsqueeze(1).to_broadcast([128, 2, 128]),
                                                                      in1=L[:, 2 * g:2 * g + 2, :], op=ALU.mult),
                                     reads=[pcbs[g], L], writes=[scT], acc=(g > 0))
                            xdt = xdp.next()
                            k.op("dve", lambda e: e.tensor_tensor(out=xdt[:], in0=x_tok[:, blk, :].rearrange("p (h d) -> p h d", h=4),
                                                                  in1=dtt[:, blk, d * 4:(d + 1) * 4].unsqueeze(2).to_broadcast([128, 4, 64]), op=ALU.mult),
                                 reads=[x_tok, dtt], writes=[xdt])
                            yield
                            pyd = ps.next()
                            for h in range(4):
                                k.op("pe", lambda e: e.matmul(pyd[:, h * 64:(h + 1) * 64], lhsT=scT[:, h, :], rhs=xdt[:, h, :], start=True, stop=True),
                                     reads=[scT, xdt], writes=[pyd], inc=(h == 3))
                            yds = ydp.next()
                            k.op("act", lambda e: e.copy(out=yds[:], in_=pyd[:, 0:256]), reads=[pyd], writes=[yds])
                            if debug.get("scan_stop", 9) <= 3:
                                continue
                            for half in halves:
                                hb = half * 64
                                ec = 4 if half == 0 else 8
                                yield
                                pyos = [ps.next(), ps.next()]
                                for h in range(4):
                                    g, hh = h // 2, h % 2
                                    k.op("pe", lambda e: e.matmul(pyos[g][:, hh * 64:(hh + 1) * 64], lhsT=CT[g * 64:(g + 1) * 64, tok0:tok0 + 128],
                                                                  rhs=Sb[d][g * 64:(g + 1) * 64, hh, :], start=True, stop=True),
                                         reads=[CT, Sb[d]], writes=[pyos[g]])
                                ty = typ.next()
                                for g in range(2):
                                    k.op("dve", lambda e: e.tensor_tensor(out=ty[hb:hb + 64, 2 * g:2 * g + 2, :],
                                                                          in0=pyos[g][hb:hb + 64, 0:128].rearrange("p (h d) -> p h d", h=2),
                                                                          in1=ex[hb:hb + 64, 2 * g:2 * g + 2].unsqueeze(2).to_broadcast([64, 2, 64]), op=ALU.mult),
                                         reads=[pyos[g], ex], writes=[ty], acc=(g > 0))
                                if (blk, half) not in ywritten:
                                    ywritten.add((blk, half))
                                    k.op("dve", lambda e: e.tensor_tensor(out=yacc[hb:hb + 64, blk, :], in0=ty[hb:hb + 64, :, :].rearrange("p h d -> p (h d)"),
                                                                          in1=yds[hb:hb + 64, :], op=ALU.add),
                                         reads=[ty, yds], writes=[yacc], acc=True)
                                else:
                                    ty2 = typ.next()
                                    k.op("dve", lambda e: e.tensor_tensor(out=ty2[hb:hb + 64, :, :].rearrange("p h d -> p (h d)"),
                                                                          in0=ty[hb:hb + 64, :, :].rearrange("p h d -> p (h d)"),
                                                                          in1=yds[hb:hb + 64, :], op=ALU.add),
                                         reads=[ty, yds], writes=[ty2])
                                    k.op("pool", lambda e: e.tensor_tensor(out=yacc[hb:hb + 64, blk, :], in0=yacc[hb:hb + 64, blk, :],
                                                                           in1=ty2[hb:hb + 64, :, :].rearrange("p h d -> p (h d)"), op=ALU.add),
                                         reads=[yacc, ty2], writes=[yacc])
                                if debug.get("scan_stop", 9) <= 4:
                                    continue
                                yield
                                Bd = Bdp.next()
                                for h in range(4):
                                    k.op("dve", lambda e: e.tensor_scalar(out=Bd[hb:hb + 64, h, :], in0=B_tok[hb:hb + 64, blk, :],
                                                                          scalar1=ex[hb:hb + 64, 16 + h:17 + h], scalar2=None, op0=ALU.mult),
                                         reads=[B_tok, ex], writes=[Bd], acc=(h > 0))
                                pst = ps.next()
                                for h in range(4):
                                    k.op("pe", lambda e: e.matmul(pst[:, h * 64:(h + 1) * 64], lhsT=Bd[hb:hb + 64, h, :], rhs=xdt[hb:hb + 64, h, :],
                                                                  start=True, stop=True), reads=[Bd, xdt], writes=[pst], inc=(h == 3))
                                for h in range(4):
                                    g, hh = h // 2, h % 2
                                    k.op("dve", lambda e: e.scalar_tensor_tensor(out=S[d][g * 64:(g + 1) * 64, hh, :], in0=S[d][g * 64:(g + 1) * 64, hh, :],
                                                                                 scalar=ex[g * 64:(g + 1) * 64, ec + h:ec + h + 1],
                                                                                 in1=pst[g * 64:(g + 1) * 64, h * 64:(h + 1) * 64],
                                                                                 op0=ALU.mult, op1=ALU.add),
                                         reads=[S[d], ex, pst], writes=[S[d]])
                                k.op("act", lambda e: e.copy(out=Sb[d][:], in_=S[d][:]), reads=[S[d]], writes=[Sb[d]])
                        if not lat:
                            p = ps.next()
                            k.op("pe", lambda e: e.transpose(p[:, 0:128], S[d][:].rearrange("p a b -> p (a b)"), ident[:]),
                                 reads=[S[d], ident], writes=[p])
                            so = sop.next()
                            k.op("dve", lambda e: e.tensor_copy(out=so[:, :], in_=p[:, 0:128]), reads=[p], writes=[so])
                            for g in range(2):
                                for hh in range(2):
                                    k.dma("pool", O["new_ssm"][si, l, d, 2 * g + hh, :, :], so[hh * 64:(hh + 1) * 64, g * 64:(g + 1) * 64], reads=[so])

                    gens = [ssd_dir_gen(0), ssd_dir_gen(1)]
                    while gens:
                        for g_ in list(gens):
                            try:
                                next(g_)
                            except StopIteration:
                                gens.remove(g_)
                    if debug.get("ssd_stop", 9) <= 4:
                        continue
                    for blk in range(NB):
                        tok0 = blk * 128
                        z = zp.next()
                        k.dma("sp", z[:], PTM[l][off + tok0:off + tok0 + 128, C_SSZ:C_SSZ + 256], reads=[PTM[l]], writes=[z])
                        t1 = y2p.next()
                        k.op("dve", lambda e: e.tensor_tensor(out=t1[:].rearrange("p (h d) -> p h d", h=4),
                                                              in0=x_tok[:, blk, :].rearrange("p (h d) -> p h d", h=4),
                                                              in1=Dt[:, 0:4].unsqueeze(2).to_broadcast([128, 4, 64]), op=ALU.mult),
                             reads=[x_tok, Dt], writes=[t1])
                        t2 = y2p.next()
                        k.op("pool", lambda e: e.tensor_tensor(out=t2[:], in0=t1[:], in1=yacc[:, blk, :], op=ALU.add), reads=[t1, yacc], writes=[t2])
                        sz = y2p.next()
                        k.op("act", lambda e: e.activation(out=sz[:], in_=z[:], func=AF.Silu), reads=[z], writes=[sz])
                        y2 = y2p.next()
                        k.op("dve", lambda e: e.tensor_tensor(out=y2[:], in0=t2[:], in1=sz[:], op=ALU.mult), reads=[t2, sz], writes=[y2])
                        ssum = ssp.next()
                        k.op("dve", lambda e: e.memset(ssum[:], 0.0), writes=[ssum])
                        for g in range(2):
                            k.op("act", lambda e: e.activation(out=t1[:, g * 128:(g + 1) * 128], in_=y2[:, g * 128:(g + 1) * 128], func=AF.Square,
                                                               accum_out=ssum[:, g:g + 1]), reads=[y2], writes=[t1, ssum])
                        k.op("act", lambda e: e.activation(out=ssum[:, 2:4], in_=ssum[:, 0:2], func=AF.Sqrt, scale=1.0 / 128, bias=epsb[:, 0:1]),
                             reads=[ssum, epsb], writes=[ssum])
                        k.op("dve", lambda e: e.reciprocal(out=ssum[:, 0:2], in_=ssum[:, 2:4]), reads=[ssum], writes=[ssum])
                        y3 = sz
                        for g in range(2):
                            k.op("dve", lambda e: e.scalar_tensor_tensor(out=y3[:, g * 128:(g + 1) * 128], in0=y2[:, g * 128:(g + 1) * 128],
                                                                         scalar=ssum[:, g:g + 1], in1=nwt[:, g * 128:(g + 1) * 128],
                                                                         op0=ALU.mult, op1=ALU.mult), reads=[y2, ssum, nwt], writes=[y3])
                        os_ = osp.next()
                        for c in range(2):
                            p = ps.next()
                            k.op("pe", lambda e: e.transpose(p[:, 0:128], y3[:, c * 128:(c + 1) * 128], ident[:]), reads=[y3, ident], writes=[p])
                            evac(c, os_[:, c, :], p[:, 0:128], [p], [os_], acc=(c > 0))
                        k.dma("pool", MIX[l][512:768, off + tok0:off + tok0 + 128].rearrange("(c p) t -> p c t", p=128), os_[:],
                              reads=[os_], writes=[MIX[l]], acc=True)
                k.barrier()


        def dn_phase(l):
            with ExitStack() as ph:
                cm = k.tile("cmd", [128, 14, 128], F32, ph)
                k.dma("sp", cm[:, 0:7, :], I["cmask"].rearrange("r p q -> p r q")[:, 0:7, :], writes=[cm])
                k.dma("sp", cm[:, 7:14, :], I["cmask"].rearrange("r p q -> p r q")[:, 7:14, :], writes=[cm], acc=True)
                onesblk, onesA, onesB, identm = cm[:, 2, :], cm[:, 3, :], cm[:, 4, :], cm[:, 7, :]
                cw = k.tile("cwd", [128, 6, 3], F32, ph)
                for kk in range(3):
                    k.dma("sp", cw[:, :, kk], I["dn_conv_w"][l, kk].rearrange("(c p) -> p c", p=128), writes=[cw],
                          acc=(kk > 0), allow_slow_non_contiguous=True)
                dtb = k.tile("dtbd", [128, 8], F32, ph)
                k.dma("sp", dtb[:], I["dn_dt_bias"][l:l + 1].rearrange("o a b -> o (a b)").partition_broadcast(128), writes=[dtb])
                aneg = k.tile("anegd", [128, 8], F32, ph)
                k.dma("sp", aneg[:], I["dn_a_log"][l:l + 1].rearrange("o a b -> o (a b)").partition_broadcast(128), writes=[aneg])
                k.op("act", lambda e: e.activation(out=aneg[:], in_=aneg[:], func=AF.Exp), reads=[aneg], writes=[aneg])
                k.op("dve", lambda e: e.tensor_scalar(out=aneg[:], in0=aneg[:], scalar1=-1.0, scalar2=None, op0=ALU.mult),
                     reads=[aneg], writes=[aneg])
                nw1 = k.tile("nw1", [128, 64], F32, ph)
                k.dma("sp", nw1[:], I["dn_norm_w"][l:l + 1, :].partition_broadcast(128), writes=[nw1])
                onec = k.tile("onecd", [128, 1], F32, ph)
                k.op("dve", lambda e: e.memset(onec[:], 1.0), writes=[onec])
                NBM = TL // 128
                qT = k.tile("qTd", [128, 2, TL], BF16, ph)
                kT = k.tile("kTd", [128, 2, TL], BF16, ph)
                k_tok = k.tile("k_tok", [128, NBM, 256], BF16, ph)
                v_tok = k.tile("v_tok", [128, NBM, 256], BF16, ph)
                yacc = k.tile("yaccd", [128, NBM, 256], F32, ph)
                braw = k.tile("braw", [128, NBM, 16], F32, ph)
                btmp = k.tile("btmp", [128, NBM, 16], F32, ph)
                lnb = k.tile("lnb", [128, NBM, 8], F32, ph)
                gt = k.tile("gt", [128, NBM, 8], F32, ph)
                S = [k.tile("Sd%d" % d, [128, 2, 64], F32, ph) for d in range(2)]
                Sb = [k.tile("Sbd%d" % d, [128, 2, 64], BF16, ph) for d in range(2)]

                def h4(ap):
                    return ap.rearrange("p (c r) x -> p c r x", c=2)

                for (off, T, lat, si) in SEQS:
                    k.barrier()
                    NB = T // 128
                    TT = min(512, T)
                    with ExitStack() as pre:
                        xinp = k.pool("xind", [128, 6, 514], BF16, 2, pre)
                        cqp = k.pool("cq", [128, 6, 512], F32, 1, pre)
                        tmpp = k.pool("tmpd", [128, 512], F32, 4, pre)
                        knp = k.pool("kn", [128, 2, 512], F32, 1, pre)
                        for t0 in range(0, T, TT):
                            g0 = off + t0
                            xin = xinp.next()
                            first = (t0 == 0)
                            lastt = (t0 + TT >= T)
                            lo = g0 if first else g0 - 1
                            hi = g0 + TT if lastt else g0 + TT + 1
                            c_lo = 1 if first else 0
                            for half3 in range(2):
                                k.dma("sp", xin[:, 3 * half3:3 * half3 + 3, c_lo:c_lo + (hi - lo)],
                                      PFM[l][R_DNQ + 384 * half3:R_DNQ + 384 * half3 + 384, lo:hi].rearrange("(c p) t -> p c t", p=128),
                                      reads=[PFM[l]], writes=[xin], acc=(half3 > 0))
                            if first:
                                k.op("dve", lambda e: e.memset(xin[:, :, 0:1], 0.0), writes=[xin], acc=True)
                            if lastt:
                                k.op("dve", lambda e: e.memset(xin[:, :, TT + 1:TT + 2], 0.0), writes=[xin], acc=True)
                            cq = cqp.next()
                            for c in range(6):
                                ta = tmpp.next()
                                tb = tmpp.next()
                                eng = "dve" if c % 2 == 0 else "pool"
                                k.op("dve", lambda e: e.tensor_scalar(out=ta[:, 0:TT], in0=xin[:, c, 0:TT], scalar1=cw[:, c, 0:1],
                                                                      scalar2=None, op0=ALU.mult), reads=[xin, cw], writes=[ta])
                                k.op("dve", lambda e: e.scalar_tensor_tensor(out=tb[:, 0:TT], in0=xin[:, c, 1:TT + 1], scalar=cw[:, c, 1:2],
                                                                             in1=ta[:, 0:TT], op0=ALU.mult, op1=ALU.add),
                                     reads=[xin, cw, ta], writes=[tb])
                                k.op("dve", lambda e: e.scalar_tensor_tensor(out=ta[:, 0:TT], in0=xin[:, c, 2:TT + 2], scalar=cw[:, c, 2:3],
                                                                             in1=tb[:, 0:TT], op0=ALU.mult, op1=ALU.add),
                                     reads=[xin, cw, tb], writes=[ta])
                                k.op("act", lambda e: e.activation(out=cq[:, c, 0:TT], in_=ta[:, 0:TT], func=AF.Silu),
                                     reads=[ta], writes=[cq], acc=(c > 0))
                            kn = knp.next()
                            for c in range(4):
                                sq = tmpp.next()
                                k.op("act", lambda e: e.activation(out=sq[:, 0:TT], in_=cq[:, c, 0:TT], func=AF.Square), reads=[cq], writes=[sq])
                                p = ps.next()
                                k.op("pe", lambda e: e.matmul(p[:, 0:TT], lhsT=onesblk, rhs=sq[:, 0:TT], start=True, stop=True),
                                     reads=[cm, sq], writes=[p])
                                rs = tmpp.next()
                                k.op("act", lambda e: e.activation(out=rs[:, 0:TT], in_=p[:, 0:TT], func=AF.Sqrt, bias=epsb[:, 0:1]),
                                     reads=[p, epsb], writes=[rs])
                                rs2 = tmpp.next()
                                k.op("dve", lambda e: e.reciprocal(out=rs2[:, 0:TT], in_=rs[:, 0:TT]), reads=[rs], writes=[rs2])
                                if c < 2:
                                    k.op("dve", lambda e: e.scalar_tensor_tensor(out=qT[:, c, t0:t0 + TT], in0=cq[:, c, 0:TT], scalar=0.125,
                                                                                 in1=rs2[:, 0:TT], op0=ALU.mult, op1=ALU.mult),
                                         reads=[cq, rs2], writes=[qT], acc=True)
                                else:
                                    k.op("dve", lambda e: e.tensor_tensor(out=kn[:, c - 2, 0:TT], in0=cq[:, c, 0:TT], in1=rs2[:, 0:TT], op=ALU.mult),
                                         reads=[cq, rs2], writes=[kn], acc=(c > 2))
                                    k.op("act", lambda e: e.copy(out=kT[:, c - 2, t0:t0 + TT], in_=kn[:, c - 2, 0:TT]), reads=[kn], writes=[kT], acc=True)
                            for b in range(TT // 128):
                                blk = t0 // 128 + b
                                for c in range(2):
                                    p = ps.next()
                                    k.op("pe", lambda e: e.transpose(p[:, 0:128], kn[:, c, b * 128:(b + 1) * 128], ident[:]),
                                         reads=[kn, ident], writes=[p])
                                    evac(c, k_tok[:, blk, c * 128:(c + 1) * 128], p[:, 0:128], [p], [k_tok], acc=True)
                                    p = ps.next()
                                    k.op("pe", lambda e: e.transpose(p[:, 0:128], cq[:, 4 + c, b * 128:(b + 1) * 128], ident[:]),
                                         reads=[cq, ident], writes=[p])
                                    evac(c + 1, v_tok[:, blk, c * 128:(c + 1) * 128], p[:, 0:128], [p], [v_tok], acc=True)
                        k.barrier()
                    for b0 in range(0, NB, 2):
                        k.dma("sp", braw[:, b0:b0 + 2, :],
                              PTM[l][off + b0 * 128:off + (b0 + 2) * 128, C_BETA:C_BETA + 16].rearrange("(b p) j -> p b j", p=128),
                              reads=[PTM[l]], writes=[braw], acc=(b0 > 0))
                    k.op("act", lambda e: e.activation(out=btmp[:, 0:NB, 0:8], in_=braw[:, 0:NB, 0:8], func=AF.Exp, scale=-1.0),
                         reads=[braw], writes=[btmp])
                    k.op("act", lambda e: e.activation(out=lnb[:, 0:NB, :], in_=btmp[:, 0:NB, 0:8], func=AF.Ln, bias=onec[:, 0:1]),
                         reads=[btmp, onec], writes=[lnb])
                    k.op("dve", lambda e: e.tensor_scalar(out=lnb[:, 0:NB, :], in0=lnb[:, 0:NB, :], scalar1=-1.0, scalar2=None, op0=ALU.mult),
                         reads=[lnb], writes=[lnb])
                    k.op("dve", lambda e: e.tensor_tensor(out=btmp[:, 0:NB, 8:16], in0=braw[:, 0:NB, 8:16],
                                                          in1=dtb[:].unsqueeze(1).to_broadcast([128, NB, 8]), op=ALU.add),
                         reads=[braw, dtb], writes=[btmp])
                    k.op("act", lambda e: e.activation(out=braw[:, 0:NB, 8:16], in_=btmp[:, 0:NB, 8:16], func=AF.Exp), reads=[btmp], writes=[braw])
                    k.op("act", lambda e: e.activation(out=btmp[:, 0:NB, 8:16], in_=braw[:, 0:NB, 8:16], func=AF.Ln, bias=onec[:, 0:1]),
                         reads=[braw, onec], writes=[btmp])
                    k.op("dve", lambda e: e.tensor_tensor(out=gt[:, 0:NB, :], in0=btmp[:, 0:NB, 8:16],
                                                          in1=aneg[:].unsqueeze(1).to_broadcast([128, NB, 8]), op=ALU.mult),
                         reads=[btmp, aneg], writes=[gt])
                    for d in range(2):
                        if lat:
                            for h in range(4):
                                c_, par = h // 2, h % 2
                                k.dma("sp", S[d][par * 64:(par + 1) * 64, c_, :], I["state_dn"][l, d, h, :, :], writes=[S[d]], acc=(h > 0))
                        else:
                            k.op("dve", lambda e: e.memset(S[d][:], 0.0), writes=[S[d]])
                        k.op("act", lambda e: e.copy(out=Sb[d][:], in_=S[d][:]), reads=[S[d]], writes=[Sb[d]])
                    if debug.get("dn_stop", 9) <= 1:
                        continue
                    with ExitStack() as sc:
                        ywritten = set()
                        WK = []
                        for d_ in range(2):
                            W_ = {"stp": k.pool("std", [128, 24], F32, 2, sc), "exp": k.pool("exd", [128, 24], F32, 4, sc), "w4": {}, "w64": {}}
                            for nm in ("GU", "GUb", "L", "E1", "E2", "E3", "A", "AT", "tm"):
                                W_["w4"][nm] = k.tile("w4" + nm, [128, 4, 128], F32, sc)
                            W_["Xp"] = k.pool("Xp", [128, 4, 128], BF16, 2, sc)
                            W_["XTp"] = k.pool("XTp", [128, 4, 128], BF16, 2, sc)
                            for nm in ("As", "ATs", "Ps", "Ps2"):
                                W_["w4"][nm] = k.tile("w4b" + nm, [128, 4, 128], BF16, sc)
                            for nm in ("ty", "ty2"):
                                W_["w64"][nm] = k.tile("w64" + nm, [128, 4, 64], F32, sc)
                            for nm in ("vb", "kbg", "vn"):
                                W_["w64"][nm] = k.tile("w64" + nm, [128, 4, 64], BF16, sc)
                            W_["slots"] = [{"QK": k.tile("sQK", [128, 4, 128], BF16, sc), "wT": k.tile("swT", [128, 4, 128], BF16, sc),
                                            "u": k.tile("su", [128, 4, 64], F32, sc), "kdec": k.tile("skd", [128, 4, 64], BF16, sc)} for _ in range(2)]
                            WK.append(W_)

                        def dn_intra_gen(d):
                            stp, exp_, w4, w64, Xp, XTp = (WK[d][n_] for n_ in ("stp", "exp", "w4", "w64", "Xp", "XTp"))
                            ps = ps8
                            cnt = 0
                            U = cm[:, d, :]
                            m_incl = cm[:, d, :]
                            m_at = cm[:, 5 + d, :]
                            m_a = cm[:, 6 - d, :]
                            order = list(range(NB)) if d == 0 else list(range(NB - 1, -1, -1))
                            halves = (0, 1) if d == 0 else (1, 0)

                            def bc4(m):
                                return m.unsqueeze(1).to_broadcast([128, 4, 128])

                            for blk in order:
                                while busy[d] >= 2:
                                    yield
                                busy[d] += 1
                                slot = WK[d]["slots"][cnt % 2]
                                cnt += 1
                                tok0 = blk * 128
                                g_blk = gt[:, blk, d * 4:(d + 1) * 4]
                                lb_blk = lnb[:, blk, d * 4:(d + 1) * 4]
                                pc = ps.next()
                                for j, lh in enumerate((U, onesA, onesB)):
                                    k.op("pe", lambda e: e.matmul(pc[:, 4 * j:4 * j + 4], lhsT=lh, rhs=g_blk, start=True, stop=True),
                                         reads=[cm, gt], writes=[pc], inc=(j == 2))
                                st = stp.next()
                                k.op("act", lambda e: e.copy(out=st[:, 0:12], in_=pc[:, 0:12]), reads=[pc], writes=[st])
                                k.op("dve", lambda e: e.tensor_tensor(out=st[:, 12:16], in0=st[:, 0:4], in1=lb_blk, op=ALU.add),
                                     reads=[st, lnb], writes=[st])
                                k.op("dve", lambda e: e.tensor_tensor(out=st[0:64, 16:20], in0=st[0:64, 4:8], in1=st[0:64, 0:4], op=ALU.subtract),
                                     reads=[st], writes=[st])
                                k.op("dve", lambda e: e.tensor_tensor(out=st[64:128, 16:20], in0=st[64:128, 8:12], in1=st[64:128, 0:4], op=ALU.subtract),
                                     reads=[st], writes=[st])
                                ex = exp_.next()
                                k.op("act", lambda e: e.activation(out=ex[:, 0:20], in_=st[:, 0:20], func=AF.Exp), reads=[st], writes=[ex])
                                k.op("act", lambda e: e.activation(out=ex[:, 20:24], in_=lb_blk, func=AF.Exp), reads=[lnb], writes=[ex], acc=True)
                                yield
                                GU, GUb, L = w4["GU"], w4["GUb"], w4["L"]
                                for h in range(4):
                                    k.op("dve", lambda e: e.tensor_scalar(out=GU[:, h, :], in0=U, scalar1=g_blk[:, h:h + 1], scalar2=None, op0=ALU.mult),
                                         reads=[cm, gt], writes=[GU], acc=(h > 0))
                                for h in range(4):
                                    k.op("dve", lambda e: e.scalar_tensor_tensor(out=GUb[:, h, :], in0=identm, scalar=lb_blk[:, h:h + 1],
                                                                                 in1=GU[:, h, :], op0=ALU.mult, op1=ALU.add),
                                         reads=[cm, lnb, GU], writes=[GUb], acc=(h > 0))
                                pa1 = ps.next()
                                k.op("pe", lambda e: e.matmul(pa1[:, :], lhsT=onesblk, rhs=GU[:].rearrange("p h i -> p (h i)"), start=True, stop=True),
                                     reads=[cm, GU], writes=[pa1])
                                pa2 = ps.next()
                                k.op("pe", lambda e: e.matmul(pa2[:, :], lhsT=onesblk, rhs=GUb[:].rearrange("p h i -> p (h i)"), start=True, stop=True),
                                     reads=[cm, GUb], writes=[pa2])
                                for h in range(4):
                                    k.op("dve", lambda e: e.tensor_scalar(out=L[:, h, :], in0=pa1[:, h * 128:(h + 1) * 128], scalar1=st[:, h:h + 1],
                                                                          scalar2=0.0, op0=ALU.subtract, op1=ALU.min),
                                         reads=[pa1, st], writes=[L], acc=(h > 0))
                                k.op("act", lambda e: e.activation(out=w4["tm"][:], in_=L[:], func=AF.Exp), reads=[L], writes=[w4["tm"]])
                                k.op("pool", lambda e: e.tensor_tensor(out=w4["E3"][:], in0=w4["tm"][:], in1=bc4(m_incl), op=ALU.mult),
                                     reads=[w4["tm"], cm], writes=[w4["E3"]])
                                for h in range(4):
                                    k.op("dve", lambda e: e.tensor_scalar(out=L[:, h, :], in0=pa1[:, h * 128:(h + 1) * 128], scalar1=st[:, 12 + h:13 + h],
                                                                          scalar2=0.0, op0=ALU.subtract, op1=ALU.max),
                                         reads=[pa1, st], writes=[L], acc=(h > 0))
                                k.op("act", lambda e: e.activation(out=w4["tm"][:], in_=L[:], func=AF.Exp, scale=-1.0), reads=[L], writes=[w4["tm"]])
                                k.op("pool", lambda e: e.tensor_tensor(out=w4["E1"][:], in0=w4["tm"][:], in1=bc4(m_a), op=ALU.mult),
                                     reads=[w4["tm"], cm], writes=[w4["E1"]])
                                for h in range(4):
                                    k.op("dve", lambda e: e.tensor_scalar(out=L[:, h, :], in0=pa2[:, h * 128:(h + 1) * 128], scalar1=st[:, h:h + 1],
                                                                          scalar2=0.0, op0=ALU.subtract, op1=ALU.min),
                                         reads=[pa2, st], writes=[L], acc=(h > 0))
                                k.op("act", lambda e: e.activation(out=w4["tm"][:], in_=L[:], func=AF.Exp), reads=[L], writes=[w4["tm"]])
                                k.op("pool", lambda e: e.tensor_tensor(out=w4["E2"][:], in0=w4["tm"][:], in1=bc4(m_at), op=ALU.mult),
                                     reads=[w4["tm"], cm], writes=[w4["E2"]])
                                yield
                                pk = [ps.next(), ps.next()]
                                for c_ in range(2):
                                    for par in range(2):
                                        k.op("pe", lambda e: e.matmul(pk[par][:, c_ * 128:(c_ + 1) * 128],
                                                                      lhsT=kT[par * 64:(par + 1) * 64, c_, tok0 + (off - off):tok0 + 128],
                                                                      rhs=kT[par * 64:(par + 1) * 64, c_, tok0:tok0 + 128], start=True, stop=True),
                                             reads=[kT], writes=[pk[par]])
                                A, AT, QK = w4["A"], w4["AT"], slot["QK"]
                                for par in range(2):
                                    k.op("dve", lambda e: e.tensor_tensor(out=h4(A[:])[:, :, par, :], in0=pk[par][:, 0:256].rearrange("p (c i) -> p c i", c=2),
                                                                          in1=h4(w4["E1"][:])[:, :, par, :], op=ALU.mult),
                                         reads=[pk[par], w4["E1"]], writes=[A], acc=(par > 0))
                                    k.op("dve", lambda e: e.tensor_tensor(out=h4(AT[:])[:, :, par, :], in0=pk[par][:, 0:256].rearrange("p (c i) -> p c i", c=2),
                                                                          in1=h4(w4["E2"][:])[:, :, par, :], op=ALU.mult),
                                         reads=[pk[par], w4["E2"]], writes=[AT], acc=(par > 0))
                                pq = [ps.next(), ps.next()]
                                for c_ in range(2):
                                    for par in range(2):
                                        k.op("pe", lambda e: e.matmul(pq[par][:, c_ * 128:(c_ + 1) * 128],
                                                                      lhsT=kT[par * 64:(par + 1) * 64, c_, tok0:tok0 + 128],
                                                                      rhs=qT[par * 64:(par + 1) * 64, c_, tok0:tok0 + 128], start=True, stop=True),
                                             reads=[kT, qT], writes=[pq[par]])
                                for par in range(2):
                                    k.op("dve", lambda e: e.tensor_tensor(out=h4(QK[:])[:, :, par, :], in0=pq[par][:, 0:256].rearrange("p (c i) -> p c i", c=2),
                                                                          in1=h4(w4["E3"][:])[:, :, par, :], op=ALU.mult),
                                         reads=[pq[par], w4["E3"]], writes=[QK], acc=(par > 0))
                                yield
                                X = Xp.next()
                                XT = XTp.next()
                                tm = w4["tm"]
                                k.op("pool", lambda e: e.tensor_tensor(out=tm[:], in0=A[:], in1=bc4(cm[:, 8, :]), op=ALU.mult), reads=[A, cm], writes=[tm])
                                k.op("dve", lambda e: e.scalar_tensor_tensor(out=X[:], in0=tm[:], scalar=-1.0, in1=bc4(identm), op0=ALU.mult, op1=ALU.add),
                                     reads=[tm, cm], writes=[X])
                                k.op("pool", lambda e: e.tensor_tensor(out=L[:], in0=AT[:], in1=bc4(cm[:, 8, :]), op=ALU.mult), reads=[AT, cm], writes=[L])
                                k.op("dve", lambda e: e.scalar_tensor_tensor(out=XT[:], in0=L[:], scalar=-1.0, in1=bc4(identm), op0=ALU.mult, op1=ALU.add),
                                     reads=[L, cm], writes=[XT])
                                As, ATs, Ps, Ps2 = w4["As"], w4["ATs"], w4["Ps"], w4["Ps2"]
                                for lev in range(5):
                                    ms = cm[:, 9 + lev, :]
                                    k.op("pool", lambda e: e.tensor_tensor(out=As[:], in0=A[:], in1=bc4(ms), op=ALU.mult), reads=[A, cm], writes=[As])
                                    k.op("pool", lambda e: e.tensor_tensor(out=ATs[:], in0=AT[:], in1=bc4(ms), op=ALU.mult), reads=[AT, cm], writes=[ATs])
                                    pP = ps.next()
                                    for h in range(4):
                                        k.op("pe", lambda e: e.matmul(pP[:, h * 128:(h + 1) * 128], lhsT=ATs[:, h, :], rhs=X[:, h, :], start=True, stop=True),
                                             reads=[ATs, X], writes=[pP], inc=(h == 3))
                                    k.op("act", lambda e: e.copy(out=Ps[:].rearrange("p h i -> p (h i)"), in_=pP[:, :]), reads=[pP], writes=[Ps])
                                    pP2 = ps.next()
                                    for h in range(4):
                                        k.op("pe", lambda e: e.matmul(pP2[:, h * 128:(h + 1) * 128], lhsT=As[:, h, :], rhs=XT[:, h, :], start=True, stop=True),
                                             reads=[As, XT], writes=[pP2], inc=(h == 3))
                                    k.op("act", lambda e: e.copy(out=Ps2[:].rearrange("p h i -> p (h i)"), in_=pP2[:, :]), reads=[pP2], writes=[Ps2])
                                    yield
                                    pX = ps.next()
                                    for h in range(4):
                                        k.op("pe", lambda e: e.matmul(pX[:, h * 128:(h + 1) * 128], lhsT=XT[:, h, :], rhs=Ps[:, h, :], start=True, stop=True),
                                             reads=[XT, Ps], writes=[pX], inc=(h == 3))
                                    pXT = ps.next()
                                    for h in range(4):
                                        k.op("pe", lambda e: e.matmul(pXT[:, h * 128:(h + 1) * 128], lhsT=X[:, h, :], rhs=Ps2[:, h, :], start=True, stop=True),
                                             reads=[X, Ps2], writes=[pXT], inc=(h == 3))
                                    Xn = Xp.next()
                                    XTn = XTp.next()
                                    k.op("dve", lambda e: e.tensor_tensor(out=Xn[:].rearrange("p h i -> p (h i)"), in0=X[:].rearrange("p h i -> p (h i)"),
                                                                          in1=pX[:, :], op=ALU.subtract), reads=[X, pX], writes=[Xn])
                                    k.op("dve", lambda e: e.tensor_tensor(out=XTn[:].rearrange("p h i -> p (h i)"), in0=XT[:].rearrange("p h i -> p (h i)"),
                                                                          in1=pXT[:, :], op=ALU.subtract), reads=[XT, pXT], writes=[XTn])
                                    X, XT = Xn, XTn
                                    yield
                                yield
                                vb, kbg = w64["vb"], w64["kbg"]
                                kdec, u_sb, wT = slot["kdec"], slot["u"], slot["wT"]

                                def bc64(ap):
                                    return ap.unsqueeze(2).to_broadcast([128, 4, 64])

                                k.op("dve", lambda e: e.tensor_tensor(out=vb[:], in0=v_tok[:, blk, :].rearrange("p (h d) -> p h d", h=4), in1=bc64(ex[:, 20:24]), op=ALU.mult),
                                     reads=[v_tok, ex], writes=[vb])
                                k.op("pool", lambda e: e.tensor_tensor(out=kbg[:], in0=k_tok[:, blk, :].rearrange("p (h d) -> p h d", h=4), in1=bc64(ex[:, 12:16]), op=ALU.mult),
                                     reads=[k_tok, ex], writes=[kbg])
                                k.op("pool", lambda e: e.tensor_tensor(out=kdec[:], in0=k_tok[:, blk, :].rearrange("p (h d) -> p h d", h=4), in1=bc64(ex[:, 16:20]), op=ALU.mult),
                                     reads=[k_tok, ex], writes=[kdec])
                                pu = ps.next()
                                for h in range(4):
                                    k.op("pe", lambda e: e.matmul(pu[:, h * 64:(h + 1) * 64], lhsT=XT[:, h, :], rhs=vb[:, h, :], start=True, stop=True),
                                         reads=[XT, vb], writes=[pu], inc=(h == 3))
                                k.op("act", lambda e: e.copy(out=u_sb[:].rearrange("p h d -> p (h d)"), in_=pu[:, 0:256]), reads=[pu], writes=[u_sb])
                                pwT = ps.next()
                                for h in range(4):
                                    c_ = h // 2
                                    k.op("pe", lambda e: e.matmul(pwT[:, h * 128:(h + 1) * 128], lhsT=kbg[:, 2 * c_:2 * c_ + 2, :].rearrange("p r d -> p (r d)"),
                                                                  rhs=XT[:, h, :], start=True, stop=True),
                                         reads=[kbg, XT], writes=[pwT], inc=(h == 3))
                                for h in range(4):
                                    par = h % 2
                                    evac(h, wT[par * 64:(par + 1) * 64, h, :], pwT[par * 64:(par + 1) * 64, h * 128:(h + 1) * 128], [pwT], [wT], acc=(h > 0))
                                tasks[d].append((blk, ex, slot))
                                yield
                            done[d] = True

                        def dn_rec_gen(d):
                            w64 = WK[d]["w64"]
                            ps = ps8
                            halves = (0, 1) if d == 0 else (1, 0)
                            vn, ty, ty2 = w64["vn"], w64["ty"], w64["ty2"]
                            while True:
                                if not tasks[d]:
                                    if done[d]:
                                        break
                                    yield
                                    continue
                                blk, ex, slot = tasks[d].pop(0)
                                QK, kdec, u_sb, wT = slot["QK"], slot["kdec"], slot["u"], slot["wT"]
                                tok0 = blk * 128
                                for half in halves:
                                    hb = half * 64
                                    ec = 4 if half == 0 else 8
                                    pw = [ps.next(), ps.next()]
                                    for h in range(4):
                                        c_, par = h // 2, h % 2
                                        k.op("pe", lambda e: e.matmul(pw[par][:, c_ * 64:(c_ + 1) * 64], lhsT=wT[par * 64:(par + 1) * 64, h, :],
                                                                      rhs=Sb[d][par * 64:(par + 1) * 64, c_, :], start=True, stop=True),
                                             reads=[wT, Sb[d]], writes=[pw[par]])
                                    for par in range(2):
                                        k.op("dve", lambda e: e.tensor_tensor(out=vn[:].rearrange("p (c r) d -> p c r d", c=2)[:, :, par, :],
                                                                              in0=u_sb[:].rearrange("p (c r) d -> p c r d", c=2)[:, :, par, :],
                                                                              in1=pw[par][:, 0:128].rearrange("p (c d) -> p c d", c=2), op=ALU.subtract),
                                             reads=[u_sb, pw[par]], writes=[vn], acc=(par > 0))
                                    yield
                                    pqs = [ps.next(), ps.next()]
                                    for h in range(4):
                                        c_, par = h // 2, h % 2
                                        k.op("pe", lambda e: e.matmul(pqs[par][:, c_ * 64:(c_ + 1) * 64], lhsT=qT[par * 64:(par + 1) * 64, c_, tok0:tok0 + 128],
                                                                      rhs=Sb[d][par * 64:(par + 1) * 64, c_, :], start=True, stop=True),
                                             reads=[qT, Sb[d]], writes=[pqs[par]])
                                    pqk = ps.next()
                                    for h in range(4):
                                        k.op("pe", lambda e: e.matmul(pqk[:, h * 64:(h + 1) * 64], lhsT=QK[:, h, :], rhs=vn[:, h, :], start=True, stop=True),
                                             reads=[QK, vn], writes=[pqk], inc=(h == 3))
                                    for par in range(2):
                                        k.op("dve", lambda e: e.tensor_tensor(out=ty[hb:hb + 64].rearrange("p (c r) d -> p c r d", c=2)[:, :, par, :],
                                                                              in0=pqs[par][hb:hb + 64, 0:128].rearrange("p (c d) -> p c d", c=2),
                                                                              in1=ex[hb:hb + 64, 0:4].rearrange("p (c r) -> p c r", c=2)[:, :, par].unsqueeze(2).to_broadcast([64, 2, 64]),
                                                                              op=ALU.mult),
                                             reads=[pqs[par], ex], writes=[ty], acc=(par > 0))
                                    if (blk, half) not in ywritten:
                                        ywritten.add((blk, half))
                                        k.op("dve", lambda e: e.tensor_tensor(out=yacc[hb:hb + 64, blk, :], in0=ty[hb:hb + 64].rearrange("p h d -> p (h d)"),
                                                                              in1=pqk[hb:hb + 64, 0:256], op=ALU.add),
                                             reads=[ty, pqk], writes=[yacc], acc=True)
                                    else:
                                        k.op("dve", lambda e: e.tensor_tensor(out=ty2[hb:hb + 64].rearrange("p h d -> p (h d)"),
                                                                              in0=ty[hb:hb + 64].rearrange("p h d -> p (h d)"),
                                                                              in1=pqk[hb:hb + 64, 0:256], op=ALU.add),
                                             reads=[ty, pqk], writes=[ty2])
                                        k.op("pool", lambda e: e.tensor_tensor(out=yacc[hb:hb + 64, blk, :], in0=yacc[hb:hb + 64, blk, :],
                                                                               in1=ty2[hb:hb + 64].rearrange("p h d -> p (h d)"), op=ALU.add),
                                             reads=[yacc, ty2], writes=[yacc])
                                    yield
                                    pst = ps.next()
                                    for h in range(4):
                                        c_ = h // 2
                                        k.op("pe", lambda e: e.matmul(pst[:, h * 64:(h + 1) * 64],
                                                                      lhsT=kdec[hb:hb + 64, 2 * c_:2 * c_ + 2, :].rearrange("p r d -> p (r d)"),
                                                                      rhs=vn[hb:hb + 64, h, :], start=True, stop=True),
                                             reads=[kdec, vn], writes=[pst], inc=(h == 3))
                                    for h in range(4):
                                        c_, par = h // 2, h % 2
                                        k.op("dve", lambda e: e.scalar_tensor_tensor(out=S[d][par * 64:(par + 1) * 64, c_, :], in0=S[d][par * 64:(par + 1) * 64, c_, :],
                                                                                     scalar=ex[par * 64:(par + 1) * 64, ec + h:ec + h + 1],
                                                                                     in1=pst[par * 64:(par + 1) * 64, h * 64:(h + 1) * 64],
                                                                                     op0=ALU.mult, op1=ALU.add),
                                             reads=[S[d], ex, pst], writes=[S[d]])
                                    k.op("act", lambda e: e.copy(out=Sb[d][:], in_=S[d][:]), reads=[S[d]], writes=[Sb[d]])
                                    yield
                                busy[d] -= 1
                            if not lat:
                                for h in range(4):
                                    c_, par = h // 2, h % 2
                                    k.dma("pool", O["new_sdn"][si, l, d, h, :, :], S[d][par * 64:(par + 1) * 64, c_, :], reads=[S[d]])

                        tasks = [[], []]
                        busy = [0, 0]
                        done = [False, False]
                        gens = [dn_intra_gen(0), dn_intra_gen(1), dn_rec_gen(0), dn_rec_gen(1)]
                        while gens:
                            for g_ in list(gens):
                                try:
                                    next(g_)
                                except StopIteration:
                                    gens.remove(g_)
                        k.barrier()
                    if debug.get("dn_stop", 9) <= 3:
                        continue
                    with ExitStack() as fin:
                        zp = k.pool("zd", [128, 256], F32, 2, fin)
                        y2p = k.pool("y2d", [128, 256], F32, 4, fin)
                        ssp = k.pool("ssumd", [128, 8], F32, 2, fin)
                        osp = k.pool("osd", [128, 2, 128], BF16, 2, fin)
                        for blk in range(NB):
                            tok0 = blk * 128
                            z = zp.next()
                            k.dma("sp", z[:], PTM[l][off + tok0:off + tok0 + 128, C_DNZ:C_DNZ + 256], reads=[PTM[l]], writes=[z])
                            ssum = ssp.next()
                            k.op("dve", lambda e: e.memset(ssum[:], 0.0), writes=[ssum])
                            t1 = y2p.next()
                            for h in range(4):
                                k.op("act", lambda e: e.activation(out=t1[:, h * 64:(h + 1) * 64], in_=yacc[:, blk, h * 64:(h + 1) * 64], func=AF.Square,
                                                                   accum_out=ssum[:, h:h + 1]), reads=[yacc], writes=[t1, ssum])
                            k.op("act", lambda e: e.activation(out=ssum[:, 4:8], in_=ssum[:, 0:4], func=AF.Sqrt, scale=1.0 / 64, bias=epsb[:, 0:1]),
                                 reads=[ssum, epsb], writes=[ssum])
                            k.op("dve", lambda e: e.reciprocal(out=ssum[:, 0:4], in_=ssum[:, 4:8]), reads=[ssum], writes=[ssum])
                            t2 = y2p.next()
                            k.op("dve", lambda e: e.tensor_tensor(out=t2[:].rearrange("p (h d) -> p h d", h=4),
                                                                  in0=yacc[:, blk, :].rearrange("p (h d) -> p h d", h=4),
                                                                  in1=ssum[:, 0:4].unsqueeze(2).to_broadcast([128, 4, 64]), op=ALU.mult),
                                 reads=[yacc, ssum], writes=[t2])
                            t3 = y2p.next()
                            k.op("pool", lambda e: e.tensor_tensor(out=t3[:].rearrange("p (h d) -> p h d", h=4),
                                                                   in0=t2[:].rearrange("p (h d) -> p h d", h=4),
                                                                   in1=nw1[:].unsqueeze(1).to_broadcast([128, 4, 64]), op=ALU.mult),
                                 reads=[t2, nw1], writes=[t3])
                            sz = y2p.next()
                            k.op("act", lambda e: e.activation(out=sz[:], in_=z[:], func=AF.Silu), reads=[z], writes=[sz])
                            k.op("dve", lambda e: e.tensor_tensor(out=t1[:], in0=t3[:], in1=sz[:], op=ALU.mult), reads=[t3, sz], writes=[t1])
                            os_ = osp.next()
                            for c in range(2):
                                p = ps.next()
                                k.op("pe", lambda e: e.transpose(p[:, 0:128], t1[:, c * 128:(c + 1) * 128], ident[:]), reads=[t1, ident], writes=[p])
                                evac(c, os_[:, c, :], p[:, 0:128], [p], [os_], acc=(c > 0))
                            k.dma("pool", MIX[l][0:256, off + tok0:off + tok0 + 128].rearrange("(c p) t -> p c t", p=128), os_[:],
                                  reads=[os_], writes=[MIX[l]], acc=True)
                        k.barrier()
                k.barrier()

        for l in range(DEPTH):
            with ExitStack() as ph:
                cs = k.tile("cs", [128, 8, 2], F32, ph)
                for kind in range(2):
                    k.dma("sp", cs[:, :, kind],
                          I["cvec"][kind].rearrange("(c p) -> p c", p=128),
                          writes=[cs], acc=(kind > 0), allow_slow_non_contiguous=True)
                k.op("act", lambda e: e.activation(out=cs[:], in_=cs[:], func=AF.Silu), reads=[cs], writes=[cs])
                bad = k.tile("bad", [128, 48], F32, ph)
                k.dma("sp", bad[:], I["b_ada"][l].rearrange("(j p) -> p j", p=128),
                      writes=[bad], allow_slow_non_contiguous=True)
                nw = k.tile("nw", [128, 2, 8], F32, ph)
                k.dma("sp", nw[:, 0, :], I["norm1_w"][l].rearrange("(c p) -> p c", p=128),
                      writes=[nw], allow_slow_non_contiguous=True)
                k.dma("sp", nw[:, 1, :], I["norm2_w"][l].rearrange("(c p) -> p c", p=128),
                      writes=[nw], acc=True, allow_slow_non_contiguous=True)
                ada = k.tile("ada", [128, 48, 2], F32, ph)
                wap = k.pool("wap", [128, 8, 512], F32, 2, ph)
                for pc in range(12):
                    wa = wap.next()
                    k.dma("sp" if pc % 2 == 0 else "pool", wa[:],
                          I["w_ada"][l][:, pc * 512:(pc + 1) * 512].rearrange("(c p) n -> p c n", p=128),
                          writes=[wa])
                    for jj in range(4):
                        j = pc * 4 + jj
                        p = ps.next()
                        for c in range(8):
                            k.op("pe", lambda e: e.matmul(p[:, 0:2], lhsT=wa[:, c, jj * 128:(jj + 1) * 128],
                                                          rhs=cs[:, c, :], start=(c == 0), stop=(c == 7)),
                                 reads=[wa, cs], writes=[p], inc=(c == 7))
                        k.op("dve", lambda e: e.tensor_scalar(out=ada[:, j, :], in0=p[:, 0:2],
                                                              scalar1=bad[:, j:j + 1], scalar2=None, op0=ALU.add),
                             reads=[p, bad], writes=[ada], acc=(j > 0))
                m = mod[l]
                for (dst, srcj, nwi) in ((0, 8, 0), (3, 32, 1)):
                    for kind in range(2):
                        k.op("dve", lambda e: e.scalar_tensor_tensor(
                            out=m[:, dst, :, kind], in0=ada[:, srcj:srcj + 8, kind], scalar=1.0,
                            in1=nw[:, nwi, :], op0=ALU.add, op1=ALU.mult),
                            reads=[ada, nw], writes=[m], acc=True)
                for (dst, srcj) in ((1, 0), (2, 16), (4, 24), (5, 40)):
                    k.op("dve", lambda e: e.tensor_copy(out=m[:, dst, :, :], in_=ada[:, srcj:srcj + 8, :]),
                         reads=[ada], writes=[m], acc=True)
                dump("mod%d" % l, m, m[:].rearrange("p a c k -> p (a c k)"), [128, 96])
                dump("ada%d" % l, ada, ada[:].rearrange("p a k -> p (a k)"), [128, 96])
                k.barrier()

            with ExitStack() as ph:
                wfm = k.tile("wfm", [128, 8, NFM], BF16, ph)
                wtm = k.tile("wtm", [128, 8, NTM], BF16, ph)
                win = I["w_in"][l]

                def wload(dst, d0, s0, n, first=False):
                    k.dma("pool", dst[:, :, d0:d0 + n], win[:, s0:s0 + n].rearrange("(c p) n -> p c n", p=128),
                          writes=[dst], acc=True)

                wload(wfm, R_DNQ, 0, 768)
                wload(wfm, R_SSX, 1456 + 256, 512)
                wload(wfm, R_MQ, 1040, 384)
                wload(wfm, R_MKPE, 1424, 32)
                wload(wfm, R_MKPE + 32, 1424 + 16, 16)
                wload(wfm, R_MKPE + 48, 1424, 16)
                wload(wfm, R_SWQ, 2232, 256)
                for h in range(4):
                    wload(wfm, R_SWQS + h * 64, 2232 + h * 64 + 32, 32)
                    wload(wfm, R_SWQS + h * 64 + 32, 2232 + h * 64, 32)
                wload(wfm, R_SWK, 2488, 128)
                for h in range(2):
                    wload(wfm, R_SWKS + h * 64, 2488 + h * 64 + 32, 32)
                    wload(wfm, R_SWKS + h * 64 + 32, 2488 + h * 64, 32)
                wload(wtm, C_DNZ, 768, 256)
                wload(wtm, C_SSZ, 1456, 256)
                wload(wtm, C_BETA, 1024, 16)
                wload(wtm, C_DT, 2224, 8)
                wload(wtm, C_SWV, 2616, 128)

                xtp = k.pool("xt", [128, 8, 512], F32, 2, ph)
                sqp = k.pool("sq", [128, 8, 512], BF16, 1, ph)
                hbp = k.pool("hb", [128, 8, 512], BF16, 2, ph)
                rsp = k.pool("rstd", [128, 512], F32, 2, ph)
                tmpp = k.pool("tmp", [128, 512], F32, 3, ph)
                fstg = k.pool("fstg", [128, 512], BF16, 4, ph)
                tstg = k.pool("tstg", [128, NTM], F32, 2, ph)
                fm_chunks = [(r, 128) for r in range(0, R_MKPE, 128)] + [(R_MKPE, 64)] + \
                            [(r, 128) for r in range(R_SWQ, NFM, 128)]
                def loadA(t0):
                    xt = xtp.next()
                    k.dma("sp", xt[:], X[l][:, t0:t0 + 512].rearrange("(c p) t -> p c t", p=128),
                          reads=[X[l]], writes=[xt])
                    return xt

                def prepA(t0, xt):
                    sq = sqp.next()
                    rstd = rsp.next()
                    rms_stats(None, xt, 512, sq, rstd)
                    hb = hbp.next()
                    mod_norm(xt, 512, rstd, tmpp, hb, mod[l], 0, 1, kind_of_tile(t0))
                    return hb

                for t0, xt, hb in pipelined2(T0S, loadA, prepA):
                    kind = kind_of_tile(t0)
                    for ci, (r0, n) in enumerate(fm_chunks):
                        p = ps.next()
                        for c in range(8):
                            k.op("pe", lambda e: e.matmul(p[0:n, :], lhsT=wfm[:, c, r0:r0 + n], rhs=hb[:, c, :],
                                                          start=(c == 0), stop=(c == 7)),
                                 reads=[wfm, hb], writes=[p], inc=(c == 7))
                        fs = fstg.next()
                        if ci % 2:
                            k.op("act", lambda e: e.copy(out=fs[0:n, :], in_=p[0:n, :]), reads=[p], writes=[fs])
                        else:
                            k.op("dve", lambda e: e.tensor_copy(out=fs[0:n, :], in_=p[0:n, :]), reads=[p], writes=[fs])
                        k.dma("sp", PFM[l][r0:r0 + n, t0:t0 + 512], fs[0:n, :], reads=[fs], writes=[PFM[l]], acc=True)
                    for b in range(4):
                        ts_ = tstg.next()
                        for g, (c0, n) in enumerate(((0, 512), (512, NTM - 512))):
                            p = ps.next()
                            for c in range(8):
                                k.op("pe", lambda e: e.matmul(p[:, 0:n], lhsT=hb[:, c, b * 128:(b + 1) * 128],
                                                              rhs=wtm[:, c, c0:c0 + n], start=(c == 0), stop=(c == 7)),
                                     reads=[wtm, hb], writes=[p], inc=(c == 7))
                            if g == 0:
                                k.op("act", lambda e: e.copy(out=ts_[:, c0:c0 + n], in_=p[:, 0:n]),
                                     reads=[p], writes=[ts_])
                            else:
                                k.op("dve", lambda e: e.tensor_copy(out=ts_[:, c0:c0 + n], in_=p[:, 0:n]),
                                     reads=[p], writes=[ts_], acc=True)
                        k.dma("pool", PTM[l][t0 + b * 128:t0 + (b + 1) * 128, :], ts_[:], reads=[ts_],
                              writes=[PTM[l]], acc=True)
                k.barrier()

            if debug.get("zero_mix"):
                with ExitStack() as ph:
                    z = k.tile("z", [128, 8, 512], BF16, ph)
                    k.op("dve", lambda e: e.memset(z[:], 0.0), writes=[z])
                    for t0 in range(0, TTOT, 512):
                        k.dma("sp", MIX[l][:, t0:t0 + 512].rearrange("(c p) t -> p c t", p=128), z[:],
                              reads=[z], writes=[MIX[l]], acc=True)
                    k.barrier()

            if not debug.get("skip_mla"):
                mla_phase(l)
            if not debug.get("skip_swa"):
                swa_phase(l)
            if not debug.get("skip_ssd"):
                ssd_phase(l)
            if not debug.get("skip_dn"):
                dn_phase(l)

            with ExitStack() as ph:
                wo = k.tile("wo", [128, 8, D], BF16, ph)
                k.dma("pool", wo[:], I["w_out"][l].rearrange("(c p) n -> p c n", p=128), writes=[wo])
                xtp = k.pool("xt", [128, 8, 512], F32, 2, ph)
                mxp = k.pool("mx", [128, 8, 512], BF16, 2, ph)
                def loadC1(t0):
                    xt = xtp.next()
                    mx = mxp.next()
                    k.dma("sp", xt[:], X[l][:, t0:t0 + 512].rearrange("(c p) t -> p c t", p=128),
                          reads=[X[l]], writes=[xt])
                    k.dma("sp", mx[:], MIX[l][:, t0:t0 + 512].rearrange("(c p) t -> p c t", p=128),
                          reads=[MIX[l]], writes=[mx])
                    return xt, mx

                for t0, (xt, mx) in pipelined(T0S, loadC1):
                    kind = kind_of_tile(t0)
                    for co in range(8):
                        p = ps.next()
                        for c in range(8):
                            k.op("pe", lambda e: e.matmul(p[:, :], lhsT=wo[:, c, co * 128:(co + 1) * 128],
                                                          rhs=mx[:, c, :], start=(c == 0), stop=(c == 7)),
                                 reads=[wo, mx], writes=[p], inc=(c == 7))
                        k.op("dve", lambda e: e.scalar_tensor_tensor(
                            out=xt[:, co, :], in0=p[:, :], scalar=mod[l][:, 2, co, kind:kind + 1],
                            in1=xt[:, co, :], op0=ALU.mult, op1=ALU.add),
                            reads=[p, mod[l], xt], writes=[xt])
                    k.dma("pool", XA[:, t0:t0 + 512].rearrange("(c p) t -> p c t", p=128), xt[:],
                          reads=[xt], writes=[XA], acc=True)
                k.barrier()

            HJ = 11
            for half in range(2):
                src2 = XA if half == 0 else XB
                dst2 = XB if half == 0 else X[l + 1]
                last = (half == 1 and l == DEPTH - 1)
                with ExitStack() as ph:
                    wg = k.tile("wg", [128, 8, 2, HJ * 128], BF16, ph)
                    wd = k.tile("wd", [128, HJ, D], BF16, ph)
                    j0 = half * HJ * 128
                    for gu in range(2):
                        k.dma("pool", wg[:, :, gu, :],
                              I["w_gate_up"][l][:, gu * FF + j0:gu * FF + j0 + HJ * 128].rearrange(
                                  "(c p) n -> p c n", p=128), writes=[wg], acc=True)
                    k.dma("pool", wd[:], I["w_down"][l][j0:j0 + HJ * 128, :].rearrange("(j p) n -> p j n", p=128),
                          writes=[wd])
                    xtp = k.pool("xt", [128, 8, 512], F32, 3 if half == 0 else 2, ph)
                    x2p = k.pool("x2", [128, 8, 512], F32, 3, ph) if half == 1 else None
                    sqp = k.pool("sq", [128, 8, 512], BF16, 1, ph)
                    hbp = k.pool("hb", [128, 8, 512], BF16, 2, ph)
                    rsp = k.pool("rstd", [128, 512], F32, 2, ph)
                    tmpp = k.pool("tmp", [128, 512], F32, 3, ph)
                    acp = k.pool("act", [128, HJ, 512], BF16, 1, ph)
                    ostg = k.pool("ostg", [128, D], F32, 2, ph) if last else None
                    def loadF(t0):
                        xt = xtp.next()
                        k.dma("sp", xt[:], XA[:, t0:t0 + 512].rearrange("(c p) t -> p c t", p=128),
                              reads=[XA], writes=[xt])
                        if half == 1:
                            x2 = x2p.next()
                            k.dma("sp", x2[:], XB[:, t0:t0 + 512].rearrange("(c p) t -> p c t", p=128),
                                  reads=[XB], writes=[x2])
                        else:
                            x2 = xt
                        return xt, x2

                    def prepF(t0, ld):
                        sq = sqp.next()
                        rstd = rsp.next()
                        rms_stats(None, ld[0], 512, sq, rstd)
                        hb = hbp.next()
                        mod_norm(ld[0], 512, rstd, tmpp, hb, mod[l], 3, 4, kind_of_tile(t0))
                        return hb

                    for t0, (xt, x2), hb in pipelined2(T0S, loadF, prepF):
                        kind = kind_of_tile(t0)
                        sq = sqp.tiles[0]
                        ac = acp.next()
                        for j in range(HJ):
                            pg = ps.next()
                            pu = ps.next()
                            for gu, pp in ((0, pg), (1, pu)):
                                for c in range(8):
                                    k.op("pe", lambda e: e.matmul(pp[:, :], lhsT=wg[:, c, gu, j * 128:(j + 1) * 128],
                                                                  rhs=hb[:, c, :], start=(c == 0), stop=(c == 7)),
                                         reads=[wg, hb], writes=[pp], inc=(c == 7))
                            tmp = tmpp.next()
                            k.op("act", lambda e: e.activation(out=tmp[:], in_=pg[:], func=AF.Silu),
                                 reads=[pg], writes=[tmp])
                            k.op("dve", lambda e: e.tensor_tensor(out=ac[:, j, :], in0=tmp[:], in1=pu[:], op=ALU.mult),
                                 reads=[tmp, pu], writes=[ac], acc=(j > 0))
                        for co in range(8):
                            p = ps.next()
                            for j in range(HJ):
                                k.op("pe", lambda e: e.matmul(p[:, :], lhsT=wd[:, j, co * 128:(co + 1) * 128],
                                                              rhs=ac[:, j, :], start=(j == 0), stop=(j == HJ - 1)),
                                     reads=[wd, ac], writes=[p], inc=(j == HJ - 1))
                            k.op("dve", lambda e: e.scalar_tensor_tensor(
                                out=x2[:, co, :], in0=p[:, :], scalar=mod[l][:, 5, co, kind:kind + 1],
                                in1=x2[:, co, :], op0=ALU.mult, op1=ALU.add),
                                reads=[p, mod[l], x2], writes=[x2])
                        if not last:
                            k.dma("pool", dst2[:, t0:t0 + 512].rearrange("(c p) t -> p c t", p=128), x2[:],
                                  reads=[x2], writes=[dst2], acc=True)
                        else:
                            rstd = rsp.next()
                            rms_stats(None, x2, 512, sq, rstd)
                            for c in range(8):
                                k.op("dve", lambda e: e.scalar_tensor_tensor(
                                    out=x2[:, c, :], in0=x2[:, c, :], scalar=fnw[:, c:c + 1], in1=rstd[:, :],
                                    op0=ALU.mult, op1=ALU.mult), reads=[x2, fnw, rstd], writes=[x2])
                            for b in range(4):
                                os_ = ostg.next()
                                for c in range(8):
                                    p = ps.next()
                                    k.op("pe", lambda e: e.transpose(p[:, 0:128], x2[:, c, b * 128:(b + 1) * 128], ident[:]),
                                         reads=[x2, ident], writes=[p])
                                    if c % 2:
                                        k.op("act", lambda e: e.copy(out=os_[:, c * 128:(c + 1) * 128], in_=p[:, 0:128]),
                                             reads=[p], writes=[os_], acc=(c > 0))
                                    else:
                                        k.op("dve", lambda e: e.tensor_copy(out=os_[:, c * 128:(c + 1) * 128], in_=p[:, 0:128]),
                                             reads=[p], writes=[os_], acc=(c > 0))
                                tok = t0 + b * 128
                                dst = (O["y_ctx"][tok:tok + 128, :] if tok < LOFF
                                       else O["y_lat"][tok - LOFF:tok - LOFF + 128, :])
                                k.dma("pool", dst, os_[:], reads=[os_])
                    k.barrier()
        k.barrier()
    return nc


_CACHE = {}


def _rope_tables():
    out = {}
    pos = np.arange(TL)
    row_ids = (pos // 64).astype(np.float32)
    col_ids = (pos % 64).astype(np.float32)
    for name, rot in (("rope_m", 32), ("rope_s", 64)):
        nf = rot // 4
        inv = (10000.0 ** (-np.arange(nf, dtype=np.float32) / nf)).astype(np.float32)
        ang = np.concatenate([row_ids[:, None] * inv, col_ids[:, None] * inv], axis=-1).astype(np.float32)
        c = np.cos(ang).astype(np.float32).T
        sn = np.sin(ang).astype(np.float32).T
        out[name] = np.ascontiguousarray(np.stack([np.concatenate([c, c], 0), np.concatenate([-sn, sn], 0)]))
    return out


def kernel(**inputs):
    x_prompt = np.ascontiguousarray(inputs["x_prompt"], dtype=np.float32)
    x_sample = np.ascontiguousarray(inputs["x_sample"], dtype=np.float32)
    dbg = inputs.pop("_debug", None) if "_debug" in inputs else None
    if "nc" not in _CACHE or dbg:
        _CACHE["nc"] = build_program(dbg)
    nc = _CACHE["nc"]
    ident = np.eye(128, dtype=np.float32)
    shared = {}
    for name in ["w_ada", "b_ada", "norm1_w", "norm2_w", "final_norm_w", "w_in", "w_out",
                 "w_gate_up", "w_down"]:
        shared[name] = np.ascontiguousarray(inputs[name], dtype=np.float32)
    for name in ["mla_q_norm_w", "mla_w_uq", "mla_kv_norm_w", "mla_w_ukv", "swa_sinks",
                 "dn_conv_w", "dn_a_log", "dn_dt_bias", "dn_norm_w", "ssm_conv_w", "ssm_conv_b", "ssm_a_log", "ssm_dt_bias", "ssm_d", "ssm_norm_w"]:
        shared[name] = np.ascontiguousarray(inputs[name], dtype=np.float32)
    shared.update(_rope_tables())
    kl = np.arange(128)[:, None]
    ql = np.arange(128)[None, :]
    msk = np.zeros((6, 128, 512), np.float32)
    for r in range(6):
        for j in range(4):
            dd = r - 1 - j
            if dd == -1:
                msk[r, :, j * 128:(j + 1) * 128] = (kl >= ql)
            elif dd == 0:
                msk[r, :, j * 128:(j + 1) * 128] = 1.0
            elif dd == 1:
                msk[r, :, j * 128:(j + 1) * 128] = (kl <= ql)
    shared["swa_mask"] = msk
    ii = np.arange(128)
    same = (ii[:, None] // 64) == (ii[None, :] // 64)
    cmask = np.zeros((14, 128, 128), np.float32)
    cmask[5] = same & (ii[:, None] < ii[None, :])
    cmask[6] = same & (ii[:, None] > ii[None, :])
    cmask[7] = np.eye(128, dtype=np.float32)
    for lev, sz_ in enumerate((1, 2, 4, 8, 16, 32)):
        cmask[8 + lev] = ((ii[:, None] // (2 * sz_)) == (ii[None, :] // (2 * sz_))) & ((ii[:, None] // sz_) != (ii[None, :] // sz_))
    cmask[0] = same & (ii[:, None] <= ii[None, :])
    cmask[1] = same & (ii[:, None] >= ii[None, :])
    cmask[2] = same
    cmask[3, 0:64, :] = 1.0
    cmask[4, 64:128, :] = 1.0
    shared["cmask"] = cmask
    in_maps = []
    for core in range(8):
        b = core % 4
        m = dict(shared)
        m["x_ctx"] = x_prompt[core * NCTX:(core + 1) * NCTX].reshape(NCTX * TC, D)
        m["x_lat"] = x_sample[b]
        m["cvec"] = np.stack([inputs["c_ctx"], inputs["c"][b]]).astype(np.float32)
        m["ident"] = ident
        m["cache_ckv"] = np.ascontiguousarray(inputs["cache_mla_ckv"][b], dtype=np.float32)
        m["cache_kpe"] = np.ascontiguousarray(inputs["cache_mla_kpe"][b], dtype=np.float32)
        m["state_ssm"] = np.ascontiguousarray(inputs["state_ssm"][b], dtype=np.float32)
        m["state_dn"] = np.ascontiguousarray(inputs["state_dn"][b], dtype=np.float32)
        m["cache_swk"] = np.ascontiguousarray(inputs["cache_swa_k"][b], dtype=np.float32)
        m["cache_swv"] = np.ascontiguousarray(inputs["cache_swa_v"][b], dtype=np.float32)
        in_maps.append(m)
    res = run_bass_kernel_spmd(nc, in_maps, core_ids=list(range(8)))
    r = res.results
    y_prompt = np.concatenate([r[c]["y_ctx"].reshape(NCTX, TC, D) for c in range(8)], axis=0)
    y_sample = np.stack([r[b]["y_lat"] for b in range(4)], axis=0)
    def gath(name):
        return np.concatenate([np.asarray(r[c][name], dtype=np.float32) for c in range(8)], axis=0)

    new_ckv = gath("new_ckv") if "new_ckv" in r[0] else np.zeros((32, DEPTH, TC, 128), np.float32)
    new_kpe = gath("new_kpe") if "new_kpe" in r[0] else np.zeros((32, DEPTH, TC, 32), np.float32)
    new_swk = gath("new_swk") if "new_swk" in r[0] else np.zeros((32, DEPTH, TC, 2, 64), np.float32)
    new_swv = gath("new_swv") if "new_swv" in r[0] else np.zeros((32, DEPTH, TC, 2, 64), np.float32)
    new_sdn = gath("new_sdn") if "new_sdn" in r[0] else np.zeros((32, DEPTH, 2, 4, 64, 64), np.float32)
    new_ssm = gath("new_ssm") if "new_ssm" in r[0] else np.zeros((32, DEPTH, 2, 4, 64, 64), np.float32)
    outs = (y_prompt, y_sample, new_sdn, new_ckv, new_kpe, new_ssm, new_swk, new_swv)
    if dbg:
        return outs + (r,)
    return outs
```

```python
import numpy as np
import concourse.bass as bass
import concourse.mybir as mybir
from concourse.bass_utils import run_bass_kernel_spmd
from contextlib import ExitStack

F32 = mybir.dt.float32
BF16 = mybir.dt.bfloat16
AF = mybir.ActivationFunctionType
ALU = mybir.AluOpType
AX = mybir.AxisListType

D = 1024
DEPTH = 2
NCTX = 4
TC = 256
TL = 4096
TTOT = NCTX * TC + TL
LOFF = NCTX * TC
FF = 2816
EPS = 1e-6
NFM = 2496
NTM = 664
R_DNQ, R_DNK, R_DNV = 0, 256, 512
R_SSX, R_SSB, R_SSC = 768, 1024, 1152
R_MQ, R_MKV, R_MKPE = 1280, 1536, 1664
R_SWQ, R_SWQS, R_SWK, R_SWKS = 1728, 1984, 2240, 2368
C_DNZ, C_SSZ, C_BETA, C_ALPHA, C_DT, C_SWV = 0, 256, 512, 520, 528, 536


class Res:
    __slots__ = ("name", "w", "r", "t", "base", "full")

    def __init__(self, name, t=None):
        self.name = name
        self.w = {}
        self.r = {}
        self.base = {}
        self.full = None
        self.t = t

    def __getitem__(self, key):
        return self.t[key]


class Pool:
    def __init__(self, tiles):
        self.tiles = tiles
        self.i = 0

    def next(self):
        t = self.tiles[self.i]
        self.i = (self.i + 1) % len(self.tiles)
        return t


class K:
    def __init__(self, nc, es, ndma=12):
        self.nc = nc
        self.es = es
        self.engs = {"pe": nc.tensor, "act": nc.scalar, "dve": nc.vector,
                     "pool": nc.gpsimd, "sp": nc.sync}
        self.semh = {}
        self.cnt = {}
        self.waited = {e: {} for e in self.engs}
        for e in ["pe", "act", "dve", "pool"]:
            self.semh[e] = es.enter_context(nc.semaphore("s_" + e))
            self.cnt[e] = 0
        self.dq = {}
        for q in ["sp", "pool"]:
            sems = []
            for i in range(ndma):
                key = ("d", q, i)
                self.semh[key] = es.enter_context(nc.semaphore("d_%s_%d" % (q, i)))
                self.cnt[key] = 0
                sems.append(key)
            self.dq[q] = {"sems": sems, "rr": 0}
        self.uid = 0

    def tile(self, name, shape, dtype, es=None):
        self.uid += 1
        t = (es or self.es).enter_context(
            self.nc.sbuf_tensor("%s_%d" % (name, self.uid), list(shape), dtype))
        return Res(name, t)

    def ptile(self, name, shape, dtype=F32, es=None):
        self.uid += 1
        t = (es or self.es).enter_context(
            self.nc.psum_tensor("%s_%d" % (name, self.uid), list(shape), dtype))
        return Res(name, t)

    def dram(self, name, shape, dtype, kind="Internal"):
        if name in getattr(self, "ext", ()):
            kind = "ExternalOutput"
        t = self.nc.dram_tensor(name, list(shape), dtype, kind=kind)
        return Res(name, t.ap())

    def pool(self, name, shape, dtype, n, es=None, psum=False):
        return Pool([(self.ptile if psum else self.tile)("%s%d" % (name, i), shape, dtype, es)
                     for i in range(n)])

    def _wait(self, eng, need):
        for s, v in need.items():
            if self.waited[eng].get(s, 0) < v:
                self.engs[eng].wait_ge(self.semh[s], v)
                self.waited[eng][s] = v

    def _deps(self, eng, reads, writes, acc=False):
        need = {}

        def add(s, v, war=False):
            if s == eng and (eng == "pe" or war):
                return
            if need.get(s, 0) < v:
                need[s] = v

        for t in reads:
            for s, v in t.w.items():
                add(s, v)
        for t in writes:
            if acc:
                if t.full is not None:
                    add(t.full[0], t.full[1])
                for s, v in t.base.items():
                    add(s, v, True)
            else:
                b = {}
                for s, v in t.w.items():
                    add(s, v)
                    b[s] = max(b.get(s, 0), v)
                for s, v in t.r.items():
                    add(s, v, True)
                    b[s] = max(b.get(s, 0), v)
                t.base = b
        self._wait(eng, need)

    def _mark(self, key, val, reads, writes, acc=False):
        for t in reads:
            if t.r.get(key, 0) < val:
                t.r[key] = val
        for t in writes:
            if acc:
                if t.w.get(key, 0) < val:
                    t.w[key] = val
            else:
                t.w = {key: val}
                t.r = {}
                t.full = (key, val)

    def op(self, eng, fn, reads=(), writes=(), inc=True, acc=False):
        self._deps(eng, reads, writes, acc)
        ins = fn(self.engs[eng])
        if inc:
            self.cnt[eng] += 1
            ins.then_inc(self.semh[eng], 1)
            val = self.cnt[eng]
        else:
            val = self.cnt[eng] + 1
        self._mark(eng, val, reads, writes, acc)
        return ins

    def dma(self, q, out, in_, reads=(), writes=(), acc=False, **kw):
        d = self.dq[q]
        key = d["sems"][d["rr"]]
        d["rr"] = (d["rr"] + 1) % len(d["sems"])
        cur = self.cnt[key]
        if cur > 0:
            self._wait(q, {key: cur})
        self._deps(q, reads, writes, acc)
        ins = self.engs[q].dma_start(out=out, in_=in_, **kw)
        ins.then_inc(self.semh[key], 16)
        self.cnt[key] = cur + 16
        self._mark(key, cur + 16, reads, writes, acc)

    def barrier(self):
        need = {s: v for s, v in self.cnt.items() if v > 0}
        for e in self.engs:
            self._wait(e, dict(need))


def build_program(debug=None):
    debug = debug or {}
    nc = bass.Bass("TRN2", target_bir_lowering=False)

    def din(name, shape):
        return nc.dram_tensor(name, list(shape), F32, kind="ExternalInput").ap()

    def dout(name, shape):
        return nc.dram_tensor(name, list(shape), F32, kind="ExternalOutput").ap()

    I = {}
    I["x_ctx"] = din("x_ctx", [NCTX * TC, D])
    I["x_lat"] = din("x_lat", [TL, D])
    I["cvec"] = din("cvec", [2, D])
    I["w_ada"] = din("w_ada", [DEPTH, D, 6 * D])
    I["b_ada"] = din("b_ada", [DEPTH, 6 * D])
    I["norm1_w"] = din("norm1_w", [DEPTH, D])
    I["norm2_w"] = din("norm2_w", [DEPTH, D])
    I["final_norm_w"] = din("final_norm_w", [D])
    I["w_in"] = din("w_in", [DEPTH, D, 2744])
    I["w_out"] = din("w_out", [DEPTH, D, D])
    I["w_gate_up"] = din("w_gate_up", [DEPTH, D, 2 * FF])
    I["w_down"] = din("w_down", [DEPTH, FF, D])
    I["ident"] = din("ident", [128, 128])
    I["mla_q_norm_w"] = din("mla_q_norm_w", [DEPTH, 256])
    I["mla_w_uq"] = din("mla_w_uq", [DEPTH, 256, 384])
    I["mla_kv_norm_w"] = din("mla_kv_norm_w", [DEPTH, 128])
    I["mla_w_ukv"] = din("mla_w_ukv", [DEPTH, 128, 512])
    I["cache_ckv"] = din("cache_ckv", [DEPTH, 256, 128])
    I["cache_kpe"] = din("cache_kpe", [DEPTH, 256, 32])
    I["rope_m"] = din("rope_m", [2, 32, TL])
    I["rope_s"] = din("rope_s", [2, 64, TL])
    I["swa_mask"] = din("swa_mask", [6, 128, 512])
    I["cmask"] = din("cmask", [14, 128, 128])
    I["dn_conv_w"] = din("dn_conv_w", [DEPTH, 3, 768])
    I["dn_a_log"] = din("dn_a_log", [DEPTH, 2, 4])
    I["dn_dt_bias"] = din("dn_dt_bias", [DEPTH, 2, 4])
    I["dn_norm_w"] = din("dn_norm_w", [DEPTH, 64])
    I["state_dn"] = din("state_dn", [DEPTH, 2, 4, 64, 64])
    I["ssm_conv_w"] = din("ssm_conv_w", [DEPTH, 3, 512])
    I["ssm_conv_b"] = din("ssm_conv_b", [DEPTH, 512])
    I["ssm_a_log"] = din("ssm_a_log", [DEPTH, 2, 4])
    I["ssm_dt_bias"] = din("ssm_dt_bias", [DEPTH, 2, 4])
    I["ssm_d"] = din("ssm_d", [DEPTH, 4])
    I["ssm_norm_w"] = din("ssm_norm_w", [DEPTH, 256])
    I["state_ssm"] = din("state_ssm", [DEPTH, 2, 4, 64, 64])
    I["swa_sinks"] = din("swa_sinks", [DEPTH, 4])
    I["cache_swk"] = din("cache_swk", [DEPTH, 256, 2, 64])
    I["cache_swv"] = din("cache_swv", [DEPTH, 256, 2, 64])
    O = {}
    O["y_ctx"] = dout("y_ctx", [NCTX * TC, D])
    O["y_lat"] = dout("y_lat", [TL, D])
    O["new_ckv"] = dout("new_ckv", [NCTX, DEPTH, TC, 128])
    O["new_kpe"] = dout("new_kpe", [NCTX, DEPTH, TC, 32])
    O["new_ssm"] = dout("new_ssm", [NCTX, DEPTH, 2, 4, 64, 64])
    O["new_sdn"] = dout("new_sdn", [NCTX, DEPTH, 2, 4, 64, 64])
    O["new_swk"] = dout("new_swk", [NCTX, DEPTH, TC, 2, 64])
    O["new_swv"] = dout("new_swv", [NCTX, DEPTH, TC, 2, 64])

    with ExitStack() as es:
        k = K(nc, es)
        k.ext = set(debug.get("ext", ()))

        def dump(name, res, ap, shape, dtype=F32):
            if name in debug.get("dump", ()):
                d = nc.dram_tensor("dbg_" + name, list(shape), dtype, kind="ExternalOutput").ap()
                k.dma("sp", d, ap, reads=[res])
        X = [k.dram("xs%d" % i, [D, TTOT], F32) for i in range(DEPTH + 1)]
        XA = k.dram("xa", [D, TTOT], F32)
        XB = k.dram("xb", [D, TTOT], F32)
        PFM = [k.dram("pfm%d" % l, [NFM, TTOT], BF16) for l in range(DEPTH)]
        PTM = [k.dram("ptm%d" % l, [TTOT, NTM], F32) for l in range(DEPTH)]
        MIX = [k.dram("mix%d" % l, [D, TTOT], BF16) for l in range(DEPTH)]

        ident = k.tile("ident", [128, 128], F32)
        k.dma("sp", ident[:], I["ident"][:, :], writes=[ident])
        ones_bf = k.tile("ones_bf", [128, 128], BF16)
        k.op("dve", lambda e: e.memset(ones_bf[:], 1.0), writes=[ones_bf])
        ones_f = k.tile("ones_f", [128, 64], F32)
        k.op("dve", lambda e: e.memset(ones_f[:], 1.0), writes=[ones_f])
        epsb = k.tile("epsb", [128, 1], F32)
        k.op("dve", lambda e: e.memset(epsb[:], EPS), writes=[epsb])
        ps = k.pool("ps", [128, 512], F32, 6, psum=True)
        pacc = k.pool("pacc", [128, 512], F32, 2, psum=True)
        psd = [Pool(ps.tiles[0:4]), Pool(ps.tiles[4:6] + pacc.tiles[0:2])]
        ps8 = Pool(ps.tiles + pacc.tiles)
        mod = [k.tile("mod%d" % l, [128, 6, 8, 2], F32) for l in range(DEPTH)]
        fnw = k.tile("fnw", [128, 8], F32)
        k.dma("sp", fnw[:], I["final_norm_w"].rearrange("(c p) -> p c", p=128), writes=[fnw],
              allow_slow_non_contiguous=True)

        def pipelined(t0s, load):
            nxt = load(t0s[0])
            for i, t0 in enumerate(t0s):
                cur = nxt
                if i + 1 < len(t0s):
                    nxt = load(t0s[i + 1])
                yield t0, cur

        def pipelined2(t0s, load, prep):
            cur = None
            for t0, ld in pipelined(t0s, load):
                pr = prep(t0, ld)
                if cur is not None:
                    yield cur
                cur = (t0, ld, pr)
            if cur is not None:
                yield cur

        T0S = list(range(0, TTOT, 512))

        def kind_of_tile(tok0):
            return 0 if tok0 < LOFF else 1

        with ExitStack() as ph:
            xin = k.pool("xin", [128, D], F32, 2, ph)
            stg = k.pool("stg", [128, 8, 512], F32, 2, ph)
            for t0 in range(0, TTOT, 512):
                st = stg.next()
                for b in range(4):
                    tok = t0 + b * 128
                    xi = xin.next()
                    src = (I["x_ctx"][tok:tok + 128, :] if tok < LOFF
                           else I["x_lat"][tok - LOFF:tok - LOFF + 128, :])
                    k.dma("sp", xi[:], src, writes=[xi])
                    for c in range(8):
                        p = ps.next()
                        k.op("pe", lambda e: e.transpose(p[:, 0:128], xi[:, c * 128:(c + 1) * 128], ident[:]),
                             reads=[xi, ident], writes=[p])
                        eng = "act" if c % 2 else "dve"
                        if eng == "act":
                            k.op("act", lambda e: e.copy(out=st[:, c, b * 128:(b + 1) * 128], in_=p[:, 0:128]),
                                 reads=[p], writes=[st], acc=not (b == 0 and c == 0))
                        else:
                            k.op("dve", lambda e: e.tensor_copy(out=st[:, c, b * 128:(b + 1) * 128], in_=p[:, 0:128]),
                                 reads=[p], writes=[st], acc=not (b == 0 and c == 0))
                k.dma("pool", X[0][:, t0:t0 + 512].rearrange("(c p) t -> p c t", p=128), st[:],
                      reads=[st], writes=[X[0]], acc=True)
            k.barrier()

        def rms_stats(ph_tiles, xt, ntok, sq, rstd):
            k.op("act", lambda e: e.activation(out=sq[:, :, 0:ntok], in_=xt[:, :, 0:ntok], func=AF.Square),
                 reads=[xt], writes=[sq])
            p = ps.next()
            for c in range(8):
                k.op("pe", lambda e: e.matmul(p[:, 0:ntok], lhsT=ones_bf[:], rhs=sq[:, c, 0:ntok],
                                              start=(c == 0), stop=(c == 7)),
                     reads=[ones_bf, sq], writes=[p], inc=(c == 7))
            k.op("act", lambda e: e.activation(out=rstd[:, 0:ntok], in_=p[:, 0:ntok], func=AF.Sqrt,
                                               scale=1.0 / D, bias=epsb[:, 0:1]),
                 reads=[p, epsb], writes=[rstd])
            k.op("dve", lambda e: e.reciprocal(out=rstd[:, 0:ntok], in_=rstd[:, 0:ntok]),
                 reads=[rstd], writes=[rstd])

        def mod_norm(xt, ntok, rstd, tmpp, hb, modt, ia, ib, kind):
            for c in range(8):
                tmp = tmpp.next()
                k.op("dve", lambda e: e.tensor_tensor(out=tmp[:, 0:ntok], in0=xt[:, c, 0:ntok],
                                                      in1=rstd[:, 0:ntok], op=ALU.mult),
                     reads=[xt, rstd], writes=[tmp])
                k.op("act", lambda e: e.activation(out=hb[:, c, 0:ntok], in_=tmp[:, 0:ntok], func=AF.Identity,
                                                   scale=modt[:, ia, c, kind:kind + 1],
                                                   bias=modt[:, ib, c, kind:kind + 1]),
                     reads=[tmp, modt], writes=[hb], acc=(c > 0))


        SEQS = [(i * TC, TC, False, i) for i in range(NCTX)] + [(LOFF, TL, True, 0)]
        MLA_SCALE = 96 ** -0.5
        SWA_SCALE = 64 ** -0.5

        def evac(i, out, in_, reads, writes, acc=False):
            if i % 2:
                k.op("act", lambda e: e.copy(out=out, in_=in_), reads=reads, writes=writes, acc=acc)
            else:
                k.op("dve", lambda e: e.tensor_copy(out=out, in_=in_), reads=reads, writes=writes, acc=acc)

        def rstd_from_ps(p, n, rstd, dim):
            k.op("act", lambda e: e.activation(out=rstd[:, 0:n], in_=p[:, 0:n], func=AF.Sqrt,
                                               scale=1.0 / dim, bias=epsb[:, 0:1]),
                 reads=[p, epsb], writes=[rstd])
            k.op("dve", lambda e: e.reciprocal(out=rstd[:, 0:n], in_=rstd[:, 0:n]), reads=[rstd], writes=[rstd])

        def attn_core(kT, vt, h_v, qT, NQ, NKB, scale, ptp, masks=None, sink=None, kb_list=None):
            po = pacc.next()
            blocks = kb_list if kb_list is not None else [(kb, None) for kb in range(NKB)]
            n = len(blocks)
            LOOK = 3
            pSs = [None] * n

            def issue_s(i):
                kb = blocks[i][0]
                pS = ps.next()
                k.op("pe", lambda e: e.matmul(pS[:, 0:NQ], lhsT=kT[:, kb * 128:(kb + 1) * 128], rhs=qT[:, 0:NQ],
                                              start=True, stop=True), reads=[kT, qT], writes=[pS])
                pSs[i] = pS

            first = True
            if sink is not None:
                e64, srow = sink
                k.op("pe", lambda e: e.matmul(po[0:65, 0:NQ], lhsT=e64[0:1, 0:65], rhs=srow[0:1, 0:NQ],
                                              start=True, stop=False), reads=[e64, srow], writes=[po])
                first = False
            for i in range(min(LOOK, n)):
                issue_s(i)
            for bi, (kb, mk) in enumerate(blocks):
                if bi + LOOK < n:
                    issue_s(bi + LOOK)
                pS = pSs[bi]
                pt = ptp.next()
                k.op("act", lambda e: e.activation(out=pt[:, 0:NQ], in_=pS[:, 0:NQ], func=AF.Exp, scale=scale),
                     reads=[pS], writes=[pt])
                if mk is not None:
                    k.op("dve", lambda e: e.tensor_tensor(out=pt[:, 0:NQ], in0=pt[:, 0:NQ], in1=mk[:, 0:NQ], op=ALU.mult),
                         reads=[pt, mk], writes=[pt])
                last = (bi == n - 1)
                k.op("pe", lambda e: e.matmul(po[0:65, 0:NQ], lhsT=vt[:, kb, h_v, :], rhs=pt[:, 0:NQ],
                                              start=first, stop=last), reads=[vt, pt], writes=[po])
                first = False
            return po

        def attn_finish(po, NQ, rowbuf, bcs, ostg_p, dst_ap, dst_res):
            k.op("act", lambda e: e.activation(out=rowbuf[64:65, 0:NQ], in_=po[64:65, 0:NQ], func=AF.Ln),
                 reads=[po], writes=[rowbuf])
            k.op("act", lambda e: e.activation(out=rowbuf[64:65, 0:NQ], in_=rowbuf[64:65, 0:NQ], func=AF.Exp, scale=-1.0),
                 reads=[rowbuf], writes=[rowbuf])
            pb = ps.next()
            k.op("pe", lambda e: e.matmul(pb[0:64, 0:NQ], lhsT=ones_f[64:65, 0:64], rhs=rowbuf[64:65, 0:NQ],
                                          start=True, stop=True), reads=[ones_f, rowbuf], writes=[pb])
            k.op("act", lambda e: e.copy(out=bcs[0:64, 0:NQ], in_=pb[0:64, 0:NQ]), reads=[pb], writes=[bcs])
            og = ostg_p.next()
            k.op("dve", lambda e: e.tensor_tensor(out=og[0:64, 0:NQ], in0=po[0:64, 0:NQ], in1=bcs[0:64, 0:NQ],
                                                  op=ALU.mult), reads=[po, bcs], writes=[og])
            k.dma("pool", dst_ap, og[0:64, 0:NQ], reads=[og], writes=[dst_res], acc=True)

        def mla_phase(l):
            with ExitStack() as ph:
                NKMAX = TL + 256
                wuq = k.tile("wuq", [128, 2, 4, 128], BF16, ph)
                wuqs = k.tile("wuqs", [128, 2, 4, 64], BF16, ph)
                wkk = k.tile("wkk", [128, 4, 128], BF16, ph)
                wkv = k.tile("wkv", [128, 256], BF16, ph)
                k.op("dve", lambda e: e.memset(wuq[:], 0.0), writes=[wuq])
                k.op("dve", lambda e: e.memset(wuqs[:], 0.0), writes=[wuqs])
                k.op("dve", lambda e: e.memset(wkk[:], 0.0), writes=[wkk])
                uq = I["mla_w_uq"][l]
                ukv = I["mla_w_ukv"][l]
                for h in range(4):
                    def ld(dst, src):
                        k.dma("pool", dst, src.rearrange("(c p) n -> p c n", p=128), writes=[wuq, wuqs], acc=True)
                    ld(wuq[:, :, h, 64:128], uq[:, 96 * h:96 * h + 64])
                    ld(wuq[:, :, h, 32:64], uq[:, 96 * h + 64:96 * h + 96])
                    ld(wuqs[:, :, h, 32:48], uq[:, 96 * h + 80:96 * h + 96])
                    ld(wuqs[:, :, h, 48:64], uq[:, 96 * h + 64:96 * h + 80])
                    k.dma("pool", wkk[:, h, 64:128], ukv[:, 128 * h:128 * h + 64], writes=[wkk], acc=True)
                    k.dma("pool", wkv[:, 64 * h:64 * h + 64], ukv[:, 128 * h + 64:128 * h + 128], writes=[wkv], acc=True)
                qnw = k.tile("qnw", [128, 2], F32, ph)
                k.dma("sp", qnw[:], I["mla_q_norm_w"][l].rearrange("(c p) -> p c", p=128), writes=[qnw],
                      allow_slow_non_contiguous=True)
                kvnw = k.tile("kvnw", [128, 1], F32, ph)
                k.dma("sp", kvnw[:], I["mla_kv_norm_w"][l].rearrange("(c p) -> p c", p=128), writes=[kvnw],
                      allow_slow_non_contiguous=True)
                CM = k.tile("CM", [64, TL], F32, ph)
                SM = k.tile("SM", [64, TL], F32, ph)
                k.dma("sp", CM[32:64, :], I["rope_m"][0], writes=[CM])
                k.dma("sp", SM[32:64, :], I["rope_m"][1], writes=[SM])
                ckvT = k.tile("ckvT", [128, NKMAX], BF16, ph)
                kpeT = k.tile("kpeT", [64, NKMAX], BF16, ph)
                kTm = [k.tile("kTm%d" % h, [128, NKMAX], BF16, ph) for h in range(4)]
                for h in range(4):
                    k.op("pool", lambda e: e.memset(kTm[h][0:32, :], 0.0), writes=[kTm[h]])
                    k.op("pool", lambda e: e.memset(kTm[h][0:1, :], 1.0), writes=[kTm[h]])
                vm = k.tile("vm", [128, NKMAX // 128, 4, 65], BF16, ph)
                k.op("pool", lambda e: e.memset(vm[:], 1.0), writes=[vm])
                kmx = k.tile("kmx", [1, 4, 16], F32, ph)
                nkmax = k.tile("nkmax", [1, 4], F32, ph)
                kvp = k.pool("kvp", [128, 512], BF16, 2, ph)
                sqp = k.pool("sqm", [128, 2, 512], BF16, 2, ph)
                for t_ in sqp.tiles:
                    k.op("dve", lambda e: e.memset(t_[:], 0.0), writes=[t_])
                sqb = k.tile("sqb", [128, 512], BF16, ph)
                k.op("dve", lambda e: e.memset(sqb[:], 0.0), writes=[sqb])
                rsp = k.pool("rsm", [128, 512], F32, 2, ph)
                f32p = k.pool("f32m", [128, 512], F32, 3, ph)
                kxp = k.pool("kxp", [64, 2, 512], BF16, 2, ph)
                qlp = k.pool("qlp", [128, 2, 512], BF16, 2, ph)
                qnp = k.pool("qnp", [128, 2, 512], BF16, 2, ph)
                qTp = k.pool("qTp", [128, 512], BF16, 6, ph)
                rowp = k.pool("rowpm", [1, 512], F32, 3, ph)
                for t_ in qTp.tiles:
                    k.op("dve", lambda e: e.memset(t_[:], 0.0), writes=[t_])
                ptp = k.pool("ptp", [128, 512], BF16, 6, ph)
                rowbuf = k.tile("rowbuf", [128, 512], F32, ph)
                bcs = k.tile("bcs", [64, 512], F32, ph)
                ogp = k.pool("ogp", [64, 512], BF16, 3, ph)
                tkp = k.pool("tkp", [128, 2, 128], F32, 2, ph)
                otp = k.pool("otp", [128, 128], F32, 2, ph)

                for (off, T, lat, si) in SEQS:
                    k.barrier()
                    TT = min(512, T)
                    koff = 256 if lat else 0
                    NK = T + koff
                    NKB = NK // 128
                    if lat:
                        ck = tkp.next()
                        k.dma("sp", ck[:], I["cache_ckv"][l].rearrange("(b p) f -> p b f", p=128), writes=[ck])
                        for b in range(2):
                            p = ps.next()
                            k.op("pe", lambda e: e.transpose(p[:, 0:128], ck[:, b, :], ident[:]),
                                 reads=[ck, ident], writes=[p])
                            evac(b, ckvT[:, b * 128:(b + 1) * 128], p[:, 0:128], [p], [ckvT], acc=True)
                        kp = tkp.next()
                        k.op("dve", lambda e: e.memset(kp[:], 0.0), writes=[kp])
                        k.dma("sp", kp[:, :, 32:64], I["cache_kpe"][l].rearrange("(b p) f -> p b f", p=128),
                              writes=[kp], acc=True)
                        for b in range(2):
                            p = ps.next()
                            k.op("pe", lambda e: e.transpose(p[:, 0:128], kp[:, b, :], ident[:]),
                                 reads=[kp, ident], writes=[p])
                            evac(b, kpeT[32:64, b * 128:(b + 1) * 128], p[32:64, 0:128], [p], [kpeT], acc=True)
                    for ti, t0 in enumerate(range(0, T, TT)):
                        g0 = off + t0
                        kv = kvp.next()
                        k.dma("sp", kv[:, 0:TT], PFM[l][R_MKV:R_MKV + 128, g0:g0 + TT], reads=[PFM[l]], writes=[kv])
                        sq = sqp.next()
                        k.op("act", lambda e: e.activation(out=sq[:, 0, 0:TT], in_=kv[:, 0:TT], func=AF.Square),
                             reads=[kv], writes=[sq])
                        p = ps.next()
                        k.op("pe", lambda e: e.matmul(p[:, 0:TT], lhsT=ones_bf[:], rhs=sq[:, 0, 0:TT], start=True, stop=True),
                             reads=[ones_bf, sq], writes=[p])
                        rstd = rsp.next()
                        rstd_from_ps(p, TT, rstd, 128)
                        cf = f32p.next()
                        k.op("dve", lambda e: e.scalar_tensor_tensor(out=cf[:, 0:TT], in0=kv[:, 0:TT], scalar=kvnw[:, 0:1],
                                                                     in1=rstd[:, 0:TT], op0=ALU.mult, op1=ALU.mult),
                             reads=[kv, kvnw, rstd], writes=[cf])
                        k.op("act", lambda e: e.copy(out=ckvT[:, koff + t0:koff + t0 + TT], in_=cf[:, 0:TT]),
                             reads=[cf], writes=[ckvT], acc=True)
                        kx = kxp.next()
                        k.dma("sp", kx[32:64, 0, 0:TT], PFM[l][R_MKPE:R_MKPE + 32, g0:g0 + TT], reads=[PFM[l]], writes=[kx])
                        k.dma("sp", kx[32:64, 1, 0:TT], PFM[l][R_MKPE + 32:R_MKPE + 64, g0:g0 + TT], reads=[PFM[l]],
                              writes=[kx], acc=True)
                        if lat:
                            t1 = f32p.next()
                            t2 = f32p.next()
                            k.op("dve", lambda e: e.tensor_tensor(out=t1[32:64, 0:TT], in0=kx[32:64, 0, 0:TT],
                                                                  in1=CM[32:64, t0:t0 + TT], op=ALU.mult),
                                 reads=[kx, CM], writes=[t1])
                            k.op("pool", lambda e: e.tensor_tensor(out=t2[32:64, 0:TT], in0=kx[32:64, 1, 0:TT],
                                                                   in1=SM[32:64, t0:t0 + TT], op=ALU.mult),
                                 reads=[kx, SM], writes=[t2])
                            k.op("dve", lambda e: e.tensor_tensor(out=kpeT[32:64, koff + t0:koff + t0 + TT],
                                                                  in0=t1[32:64, 0:TT], in1=t2[32:64, 0:TT], op=ALU.add),
                                 reads=[t1, t2], writes=[kpeT], acc=True)
                        else:
                            k.op("dve", lambda e: e.tensor_copy(out=kpeT[32:64, t0:t0 + TT], in_=kx[32:64, 0, 0:TT]),
                                 reads=[kx], writes=[kpeT], acc=True)
                            kf = f32p.next()
                            k.op("dve", lambda e: e.memset(kf[:, 0:TT], 0.0), writes=[kf])
                            k.op("act", lambda e: e.copy(out=kf[32:64, 0:TT], in_=kx[32:64, 0, 0:TT]),
                                 reads=[kx], writes=[kf])
                            for b in range(TT // 128):
                                p = ps.next()
                                k.op("pe", lambda e: e.transpose(p[:, 0:128], cf[:, b * 128:(b + 1) * 128], ident[:]),
                                     reads=[cf, ident], writes=[p])
                                ot = otp.next()
                                evac(b, ot[:, :], p[:, 0:128], [p], [ot])
                                k.dma("pool", O["new_ckv"][si, l, t0 + b * 128:t0 + (b + 1) * 128, :], ot[:, :], reads=[ot])
                                p = ps.next()
                                k.op("pe", lambda e: e.transpose(p[:, 0:128], kf[:, b * 128:(b + 1) * 128], ident[:]),
                                     reads=[kf, ident], writes=[p])
                                ot = otp.next()
                                evac(b + 1, ot[:, 0:32], p[:, 32:64], [p], [ot])
                                k.dma("pool", O["new_kpe"][si, l, t0 + b * 128:t0 + (b + 1) * 128, :], ot[:, 0:32], reads=[ot])
                    ntile = (NK + 511) // 512
                    for ti in range(ntile):
                        c0 = ti * 512
                        n = min(512, NK - c0)
                        for h in range(4):
                            p = ps.next()
                            k.op("pe", lambda e: e.matmul(p[:, 0:n], lhsT=wkk[:, h, :], rhs=ckvT[:, c0:c0 + n],
                                                          start=True, stop=True), reads=[wkk, ckvT], writes=[p])
                            evac(h, kTm[h][64:128, c0:c0 + n], p[64:128, 0:n], [p], [kTm[h]], acc=True)
                            k.op("pool", lambda e: e.tensor_copy(out=kTm[h][32:64, c0:c0 + n], in_=kpeT[32:64, c0:c0 + n]),
                                 reads=[kpeT], writes=[kTm[h]], acc=True)
                            for (a0, a1) in ((32, 64), (64, 128)):
                                k.op("act", lambda e: e.activation(out=sqb[a0:a1, 0:n], in_=kTm[h][a0:a1, c0:c0 + n],
                                                                   func=AF.Square), reads=[kTm[h]], writes=[sqb], acc=(a0 == 64))
                            p2 = ps.next()
                            k.op("pe", lambda e: e.matmul(p2[0:1, 0:n], lhsT=ones_bf[:, 0:1], rhs=sqb[:, 0:n],
                                                          start=True, stop=True), reads=[ones_bf, sqb], writes=[p2])
                            k.op("dve", lambda e: e.reduce_max(out=kmx[0:1, h, ti:ti + 1], in_=p2[0:1, 0:n], axis=AX.X),
                                 reads=[p2], writes=[kmx], acc=True)
                    for kb in range(NKB):
                        p = ps.next()
                        k.op("pe", lambda e: e.matmul(p[:, 0:256], lhsT=ckvT[:, kb * 128:(kb + 1) * 128], rhs=wkv[:, :],
                                                      start=True, stop=True), reads=[ckvT, wkv], writes=[p])
                        evac(kb, vm[:, kb, :, 0:64], p[:, 0:256].rearrange("p (h d) -> p h d", h=4), [p], [vm], acc=True)
                    k.op("dve", lambda e: e.reduce_max(out=nkmax[0:1, :], in_=kmx[0:1, :, 0:ntile], axis=AX.X),
                         reads=[kmx], writes=[nkmax])
                    k.op("act", lambda e: e.activation(out=nkmax[0:1, :], in_=nkmax[0:1, :], func=AF.Sqrt),
                         reads=[nkmax], writes=[nkmax])
                    k.op("dve", lambda e: e.tensor_scalar(out=nkmax[0:1, :], in0=nkmax[0:1, :], scalar1=-1.0, scalar2=None,
                                                          op0=ALU.mult), reads=[nkmax], writes=[nkmax])
                    for t0 in range(0, T, TT):
                        g0 = off + t0
                        ql = qlp.next()
                        k.dma("sp", ql[:, :, 0:TT], PFM[l][R_MQ:R_MQ + 256, g0:g0 + TT].rearrange("(c p) t -> p c t", p=128),
                              reads=[PFM[l]], writes=[ql])
                        sq = sqp.next()
                        k.op("act", lambda e: e.activation(out=sq[:, :, 0:TT], in_=ql[:, :, 0:TT], func=AF.Square),
                             reads=[ql], writes=[sq])
                        p = ps.next()
                        for c in range(2):
                            k.op("pe", lambda e: e.matmul(p[:, 0:TT], lhsT=ones_bf[:], rhs=sq[:, c, 0:TT],
                                                          start=(c == 0), stop=(c == 1)), reads=[ones_bf, sq], writes=[p], inc=(c == 1))
                        rstd = rsp.next()
                        rstd_from_ps(p, TT, rstd, 256)
                        qn = qnp.next()
                        for c in range(2):
                            k.op("dve", lambda e: e.scalar_tensor_tensor(out=qn[:, c, 0:TT], in0=ql[:, c, 0:TT],
                                                                         scalar=qnw[:, c:c + 1], in1=rstd[:, 0:TT],
                                                                         op0=ALU.mult, op1=ALU.mult),
                                 reads=[ql, qnw, rstd], writes=[qn], acc=(c > 0))
                        qTs_h = []
                        for h in range(4):
                            p1 = ps.next()
                            for c in range(2):
                                k.op("pe", lambda e: e.matmul(p1[:, 0:TT], lhsT=wuq[:, c, h, :], rhs=qn[:, c, 0:TT],
                                                              start=(c == 0), stop=(c == 1)), reads=[wuq, qn], writes=[p1], inc=(c == 1))
                            qT = qTp.next()
                            k.op("act", lambda e: e.copy(out=qT[64:128, 0:TT], in_=p1[64:128, 0:TT]), reads=[p1], writes=[qT])
                            if lat:
                                p2 = ps.next()
                                for c in range(2):
                                    k.op("pe", lambda e: e.matmul(p2[0:64, 0:TT], lhsT=wuqs[:, c, h, :], rhs=qn[:, c, 0:TT],
                                                                  start=(c == 0), stop=(c == 1)), reads=[wuqs, qn], writes=[p2], inc=(c == 1))
                                t1 = f32p.next()
                                t2 = f32p.next()
                                k.op("dve", lambda e: e.tensor_tensor(out=t1[32:64, 0:TT], in0=p1[32:64, 0:TT],
                                                                      in1=CM[32:64, t0:t0 + TT], op=ALU.mult),
                                     reads=[p1, CM], writes=[t1])
                                k.op("dve", lambda e: e.tensor_tensor(out=t2[32:64, 0:TT], in0=p2[32:64, 0:TT],
                                                                      in1=SM[32:64, t0:t0 + TT], op=ALU.mult),
                                     reads=[p2, SM], writes=[t2])
                                k.op("pool", lambda e: e.tensor_tensor(out=qT[32:64, 0:TT], in0=t1[32:64, 0:TT],
                                                                       in1=t2[32:64, 0:TT], op=ALU.add),
                                     reads=[t1, t2], writes=[qT], acc=True)
                            else:
                                k.op("dve", lambda e: e.tensor_copy(out=qT[32:64, 0:TT], in_=p1[32:64, 0:TT]),
                                     reads=[p1], writes=[qT], acc=True)
                            for (a0, a1) in ((32, 64), (64, 128)):
                                k.op("act", lambda e: e.activation(out=sqb[a0:a1, 0:TT], in_=qT[a0:a1, 0:TT], func=AF.Square),
                                     reads=[qT], writes=[sqb], acc=(a0 == 64))
                            pn = ps.next()
                            k.op("pe", lambda e: e.matmul(pn[0:1, 0:TT], lhsT=ones_bf[:, 0:1], rhs=sqb[:, 0:TT],
                                                          start=True, stop=True), reads=[ones_bf, sqb], writes=[pn])
                            rw = rowp.next()
                            k.op("act", lambda e: e.activation(out=rw[0:1, 0:TT], in_=pn[0:1, 0:TT], func=AF.Sqrt),
                                 reads=[pn], writes=[rw])
                            k.op("dve", lambda e: e.tensor_scalar(out=qT[0:1, 0:TT], in0=rw[0:1, 0:TT],
                                                                  scalar1=nkmax[0:1, h:h + 1], scalar2=None, op0=ALU.mult),
                                 reads=[rw, nkmax], writes=[qT], acc=True)
                            qTs_h.append(qT)
                        for h in range(4):
                            po = attn_core(kTm[h], vm, h, qTs_h[h], TT, NKB, MLA_SCALE, ptp)
                            attn_finish(po, TT, rowbuf, bcs, ogp,
                                        MIX[l][256 + 64 * h:256 + 64 * h + 64, g0:g0 + TT], MIX[l])
                k.barrier()


        def swa_phase(l):
            with ExitStack() as ph:
                NKMAX = TL + 256
                CS = k.tile("CS", [128, TL], F32, ph)
                SS = k.tile("SS", [128, TL], F32, ph)
                k.dma("sp", CS[64:128, :], I["rope_s"][0], writes=[CS])
                k.dma("sp", SS[64:128, :], I["rope_s"][1], writes=[SS])
                mk = k.tile("mk", [128, 6, 512], BF16, ph)
                k.dma("pool", mk[:], I["swa_mask"].rearrange("r p q -> p r q"), writes=[mk])
                sk = k.tile("sk", [1, 4], F32, ph)
                k.dma("sp", sk[:], I["swa_sinks"][l:l + 1, :], writes=[sk])
                e64 = k.tile("e64", [1, 65], BF16, ph)
                k.op("dve", lambda e: e.memset(e64[:], 0.0), writes=[e64])
                k.op("dve", lambda e: e.memset(e64[0:1, 64:65], 1.0), writes=[e64])
                kTs = [k.tile("kTs%d" % h, [128, NKMAX], BF16, ph) for h in range(2)]
                for h in range(2):
                    k.op("pool", lambda e: e.memset(kTs[h][0:64, :], 0.0), writes=[kTs[h]])
                    k.op("pool", lambda e: e.memset(kTs[h][0:1, :], 1.0), writes=[kTs[h]])
                vs = k.tile("vs", [128, NKMAX // 128, 2, 65], BF16, ph)
                k.op("pool", lambda e: e.memset(vs[:], 1.0), writes=[vs])
                kmx = k.tile("kmx", [1, 2, 16], F32, ph)
                nkmax = k.tile("nkmax", [1, 2], F32, ph)
                sqb = k.tile("sqb", [128, 512], BF16, ph)
                k.op("dve", lambda e: e.memset(sqb[:], 0.0), writes=[sqb])
                f32p = k.pool("f32s", [128, 512], F32, 3, ph)
                kxp = k.pool("kxs", [128, 2, 512], BF16, 4, ph)
                qTp = k.pool("qTs", [128, 512], BF16, 6, ph)
                rowp = k.pool("rowps", [1, 512], F32, 3, ph)
                for t_ in qTp.tiles:
                    k.op("dve", lambda e: e.memset(t_[:], 0.0), writes=[t_])
                ptp = k.pool("pts", [128, 512], BF16, 6, ph)
                rowbuf = k.tile("rowbufs", [128, 512], F32, ph)
                srowp = k.pool("srow", [1, 512], BF16, 6, ph)
                bcs = k.tile("bcss", [64, 512], F32, ph)
                ogp = k.pool("ogs", [64, 512], BF16, 3, ph)
                ckp = k.pool("cks", [128, 128], F32, 2, ph)
                for t_ in ckp.tiles:
                    k.op("dve", lambda e: e.memset(t_[:], 0.0), writes=[t_])
                otp = k.pool("ots", [128, 128], F32, 2, ph)

                def rope(dst_ap, dst_res, x, t0, TT, lat, acc=True):
                    if lat:
                        t1 = f32p.next()
                        t2 = f32p.next()
                        k.op("dve", lambda e: e.tensor_tensor(out=t1[64:128, 0:TT], in0=x[64:128, 0, 0:TT],
                                                              in1=CS[64:128, t0:t0 + TT], op=ALU.mult),
                             reads=[x, CS], writes=[t1])
                        k.op("pool", lambda e: e.tensor_tensor(out=t2[64:128, 0:TT], in0=x[64:128, 1, 0:TT],
                                                               in1=SS[64:128, t0:t0 + TT], op=ALU.mult),
                             reads=[x, SS], writes=[t2])
                        k.op("dve", lambda e: e.tensor_tensor(out=dst_ap, in0=t1[64:128, 0:TT], in1=t2[64:128, 0:TT],
                                                              op=ALU.add), reads=[t1, t2], writes=[dst_res], acc=acc)
                    else:
                        k.op("dve", lambda e: e.tensor_copy(out=dst_ap, in_=x[64:128, 0, 0:TT]), reads=[x],
                             writes=[dst_res], acc=acc)

                for (off, T, lat, si) in SEQS:
                    k.barrier()
                    TT = min(512, T)
                    koff = 256 if lat else 0
                    NK = T + koff
                    NKB = NK // 128
                    if lat:
                        for kv in range(2):
                            for b in range(2):
                                ck = ckp.next()
                                k.dma("sp", ck[:, 64:128], I["cache_swk"][l, b * 128:(b + 1) * 128, kv, :], writes=[ck])
                                p = ps.next()
                                k.op("pe", lambda e: e.transpose(p[:, 0:128], ck[:, :], ident[:]), reads=[ck, ident], writes=[p])
                                evac(b, kTs[kv][64:128, b * 128:(b + 1) * 128], p[64:128, 0:128], [p], [kTs[kv]], acc=True)
                        for b in range(2):
                            k.dma("pool", vs[:, b, :, 0:64], I["cache_swv"][l, b * 128:(b + 1) * 128, :, :],
                                  writes=[vs], acc=True)
                    else:
                        k.dma("pool", O["new_swv"][si, l].rearrange("t k d -> t (k d)"),
                              PTM[l][off:off + T, C_SWV:C_SWV + 128], reads=[PTM[l]])
                    for b in range(T // 128):
                        k.dma("pool", vs[:, koff // 128 + b, :, 0:64],
                              PTM[l][off + b * 128:off + (b + 1) * 128, C_SWV:C_SWV + 128].rearrange("p (k d) -> p k d", k=2),
                              reads=[PTM[l]], writes=[vs], acc=True)
                    for kv in range(2):
                        for ti, t0 in enumerate(range(0, T, TT)):
                            g0 = off + t0
                            kx = kxp.next()
                            k.dma("sp", kx[64:128, 0, 0:TT], PFM[l][R_SWK + 64 * kv:R_SWK + 64 * kv + 64, g0:g0 + TT],
                                  reads=[PFM[l]], writes=[kx])
                            k.dma("sp", kx[64:128, 1, 0:TT], PFM[l][R_SWKS + 64 * kv:R_SWKS + 64 * kv + 64, g0:g0 + TT],
                                  reads=[PFM[l]], writes=[kx], acc=True)
                            rope(kTs[kv][64:128, koff + t0:koff + t0 + TT], kTs[kv], kx, t0, TT, lat)
                            if not lat:
                                kf = f32p.next()
                                k.op("dve", lambda e: e.memset(kf[0:64, 0:TT], 0.0), writes=[kf])
                                k.op("act", lambda e: e.copy(out=kf[64:128, 0:TT], in_=kx[64:128, 0, 0:TT]),
                                     reads=[kx], writes=[kf], acc=True)
                                for b in range(TT // 128):
                                    p = ps.next()
                                    k.op("pe", lambda e: e.transpose(p[:, 0:128], kf[:, b * 128:(b + 1) * 128], ident[:]),
                                         reads=[kf, ident], writes=[p])
                                    ot = otp.next()
                                    evac(b, ot[:, 0:64], p[:, 64:128], [p], [ot])
                                    k.dma("pool", O["new_swk"][si, l, t0 + b * 128:t0 + (b + 1) * 128, kv, :], ot[:, 0:64], reads=[ot])
                        ntile = (NK + 511) // 512
                        for ti in range(ntile):
                            c0 = ti * 512
                            n = min(512, NK - c0)
                            k.op("act", lambda e: e.activation(out=sqb[64:128, 0:n], in_=kTs[kv][64:128, c0:c0 + n],
                                                               func=AF.Square), reads=[kTs[kv]], writes=[sqb])
                            p2 = ps.next()
                            k.op("pe", lambda e: e.matmul(p2[0:1, 0:n], lhsT=ones_bf[:, 0:1], rhs=sqb[:, 0:n],
                                                          start=True, stop=True), reads=[ones_bf, sqb], writes=[p2])
                            k.op("dve", lambda e: e.reduce_max(out=kmx[0:1, kv, ti:ti + 1], in_=p2[0:1, 0:n], axis=AX.X),
                                 reads=[p2], writes=[kmx], acc=True)
                    ntile = (NK + 511) // 512
                    k.op("dve", lambda e: e.reduce_max(out=nkmax[0:1, :], in_=kmx[0:1, :, 0:ntile], axis=AX.X),
                         reads=[kmx], writes=[nkmax])
                    k.op("act", lambda e: e.activation(out=nkmax[0:1, :], in_=nkmax[0:1, :], func=AF.Sqrt),
                         reads=[nkmax], writes=[nkmax])
                    k.op("dve", lambda e: e.tensor_scalar(out=nkmax[0:1, :], in0=nkmax[0:1, :], scalar1=-1.0, scalar2=None,
                                                          op0=ALU.mult), reads=[nkmax], writes=[nkmax])
                    for t0 in range(0, T, TT):
                        g0 = off + t0
                        i0 = t0 // 128
                        if lat:
                            kbl = [(0, None), (1, None)]
                            for r in range(6):
                                kbo = i0 - 1 + r
                                if 0 <= kbo < T // 128:
                                    kbl.append((2 + kbo, Res("mkv", mk.t[:, r, :])))
                            for (_, m_) in kbl:
                                if m_ is not None:
                                    m_.w = mk.w
                        else:
                            kbl = [(b, None) for b in range(NKB)]
                        prep_h = []
                        for h in range(4):
                            kv = h // 2
                            qx = kxp.next()
                            k.dma("sp", qx[64:128, 0, 0:TT], PFM[l][R_SWQ + 64 * h:R_SWQ + 64 * h + 64, g0:g0 + TT],
                                  reads=[PFM[l]], writes=[qx])
                            k.dma("sp", qx[64:128, 1, 0:TT], PFM[l][R_SWQS + 64 * h:R_SWQS + 64 * h + 64, g0:g0 + TT],
                                  reads=[PFM[l]], writes=[qx], acc=True)
                            qT = qTp.next()
                            rope(qT[64:128, 0:TT], qT, qx, t0, TT, lat, acc=False)
                            k.op("act", lambda e: e.activation(out=sqb[64:128, 0:TT], in_=qT[64:128, 0:TT], func=AF.Square),
                                 reads=[qT], writes=[sqb])
                            pn = ps.next()
                            k.op("pe", lambda e: e.matmul(pn[0:1, 0:TT], lhsT=ones_bf[:, 0:1], rhs=sqb[:, 0:TT],
                                                          start=True, stop=True), reads=[ones_bf, sqb], writes=[pn])
                            rw = rowp.next()
                            k.op("act", lambda e: e.activation(out=rw[0:1, 0:TT], in_=pn[0:1, 0:TT], func=AF.Sqrt),
                                 reads=[pn], writes=[rw])
                            k.op("dve", lambda e: e.tensor_scalar(out=qT[0:1, 0:TT], in0=rw[0:1, 0:TT],
                                                                  scalar1=nkmax[0:1, kv:kv + 1], scalar2=None, op0=ALU.mult),
                                 reads=[rw, nkmax], writes=[qT], acc=True)
                            srow = srowp.next()
                            k.op("act", lambda e: e.activation(out=srow[0:1, 0:TT], in_=qT[0:1, 0:TT], func=AF.Exp,
                                                               scale=SWA_SCALE, bias=sk[0:1, h:h + 1]),
                                 reads=[qT, sk], writes=[srow])
                            prep_h.append((qT, srow))
                        for h in range(4):
                            kv = h // 2
                            qT, srow = prep_h[h]
                            po = attn_core(kTs[kv], vs, kv, qT, TT, NKB, SWA_SCALE, ptp, sink=(e64, srow), kb_list=kbl)
                            attn_finish(po, TT, rowbuf, bcs, ogp,
                                        MIX[l][768 + 64 * h:768 + 64 * h + 64, g0:g0 + TT], MIX[l])
                k.barrier()


        def ssd_phase(l):
            with ExitStack() as ph:
                cm = k.tile("cm", [128, 5, 128], F32, ph)
                k.dma("sp", cm[:], I["cmask"].rearrange("r p q -> p r q")[:, 0:5, :], writes=[cm])
                onesblk, onesA, onesB = cm[:, 2, :], cm[:, 3, :], cm[:, 4, :]
                cw = k.tile("cw", [128, 4, 3], F32, ph)
                for kk in range(3):
                    k.dma("sp", cw[:, :, kk], I["ssm_conv_w"][l, kk].rearrange("(c p) -> p c", p=128), writes=[cw],
                          acc=(kk > 0), allow_slow_non_contiguous=True)
                cb = k.tile("cb", [128, 4], F32, ph)
                k.dma("sp", cb[:], I["ssm_conv_b"][l].rearrange("(c p) -> p c", p=128), writes=[cb],
                      allow_slow_non_contiguous=True)
                dtb = k.tile("dtb", [128, 8], F32, ph)
                k.dma("sp", dtb[:], I["ssm_dt_bias"][l:l + 1].rearrange("o a b -> o (a b)").partition_broadcast(128), writes=[dtb])
                aneg = k.tile("aneg", [128, 8], F32, ph)
                k.dma("sp", aneg[:], I["ssm_a_log"][l:l + 1].rearrange("o a b -> o (a b)").partition_broadcast(128), writes=[aneg])
                k.op("act", lambda e: e.activation(out=aneg[:], in_=aneg[:], func=AF.Exp), reads=[aneg], writes=[aneg])
                k.op("dve", lambda e: e.tensor_scalar(out=aneg[:], in0=aneg[:], scalar1=-1.0, scalar2=None, op0=ALU.mult),
                     reads=[aneg], writes=[aneg])
                Dt = k.tile("Dt", [128, 4], F32, ph)
                k.dma("sp", Dt[:], I["ssm_d"][l:l + 1, :].partition_broadcast(128), writes=[Dt])
                nwt = k.tile("nwt", [128, 256], F32, ph)
                k.dma("sp", nwt[:], I["ssm_norm_w"][l:l + 1, :].partition_broadcast(128), writes=[nwt])
                onec = k.tile("onec", [128, 1], F32, ph)
                k.op("dve", lambda e: e.memset(onec[:], 1.0), writes=[onec])
                NBM = TL // 128
                BT = k.tile("BT", [128, TL], BF16, ph)
                CT = k.tile("CT", [128, TL], BF16, ph)
                x_tok = k.tile("x_tok", [128, NBM, 256], F32, ph)
                B_tok = k.tile("B_tok", [128, NBM, 128], F32, ph)
                dtr = k.tile("dtr", [128, NBM, 8], F32, ph)
                dtt = k.tile("dtt", [128, NBM, 8], F32, ph)
                at = k.tile("at", [128, NBM, 8], F32, ph)
                yacc = k.tile("yacc", [128, NBM, 256], F32, ph)
                S = [k.tile("S%d" % d, [128, 2, 64], F32, ph) for d in range(2)]
                Sb = [k.tile("Sbs%d" % d, [128, 2, 64], BF16, ph) for d in range(2)]
                xinp = k.pool("xin", [128, 4, 514], BF16, 2, ph)
                xTp = k.pool("xTs", [128, 4, 512], F32, 2, ph)
                tmpp = k.pool("tmps", [128, 512], F32, 4, ph)
                WS = []
                for d_ in range(2):
                    WS.append({"stp": k.pool("stt", [128, 24], F32, 2, ph), "exp": k.pool("exs", [128, 24], F32, 2, ph),
                               "GUp": k.pool("GU", [128, 4, 128], F32, 1, ph), "Lp": k.pool("Lp", [128, 4, 128], F32, 1, ph),
                               "L2p": k.pool("L2p", [128, 4, 128], F32, 1, ph), "scp": k.pool("scT", [128, 4, 128], BF16, 2, ph),
                               "xdp": k.pool("xdt", [128, 4, 64], BF16, 2, ph), "typ": k.pool("tmpy", [128, 4, 64], F32, 3, ph),
                               "Bdp": k.pool("Bd", [128, 4, 128], BF16, 2, ph), "ydp": k.pool("yds", [128, 256], F32, 2, ph)})
                zp = k.pool("zs", [128, 256], F32, 2, ph)
                y2p = k.pool("y2s", [128, 256], F32, 4, ph)
                ssp = k.pool("ssum", [128, 4], F32, 2, ph)
                osp = k.pool("oss", [128, 2, 128], BF16, 2, ph)
                sop = k.pool("sos", [128, 128], F32, 2, ph)

                for (off, T, lat, SEQT) in ((0, NCTX * TC, False, TC), (LOFF, TL, True, TL)):
                    k.barrier()
                    NB = T // 128
                    BPS = SEQT // 128
                    TT = min(512, SEQT)
                    for t0 in range(0, T, TT):
                        g0 = off + t0
                        xin = xinp.next()
                        first = (t0 % SEQT == 0)
                        lastt = ((t0 + TT) % SEQT == 0)
                        lo = g0 if first else g0 - 1
                        hi = g0 + TT if lastt else g0 + TT + 1
                        c_lo = 1 if first else 0
                        k.dma("sp", xin[:, :, c_lo:c_lo + (hi - lo)],
                              PFM[l][R_SSX:R_SSX + 512, lo:hi].rearrange("(c p) t -> p c t", p=128),
                              reads=[PFM[l]], writes=[xin])
                        if first:
                            k.op("dve", lambda e: e.memset(xin[:, :, 0:1], 0.0), writes=[xin], acc=True)
                        if lastt:
                            k.op("dve", lambda e: e.memset(xin[:, :, TT + 1:TT + 2], 0.0), writes=[xin], acc=True)
                        xT = xTp.next()
                        for c in range(4):
                            ta = tmpp.next()
                            tb = tmpp.next()
                            k.op("dve", lambda e: e.tensor_scalar(out=ta[:, 0:TT], in0=xin[:, c, 0:TT], scalar1=cw[:, c, 0:1],
                                                                  scalar2=None, op0=ALU.mult), reads=[xin, cw], writes=[ta])
                            k.op("dve", lambda e: e.scalar_tensor_tensor(out=tb[:, 0:TT], in0=xin[:, c, 1:TT + 1], scalar=cw[:, c, 1:2],
                                                                         in1=ta[:, 0:TT], op0=ALU.mult, op1=ALU.add),
                                 reads=[xin, cw, ta], writes=[tb])
                            k.op("dve", lambda e: e.scalar_tensor_tensor(out=ta[:, 0:TT], in0=xin[:, c, 2:TT + 2], scalar=cw[:, c, 2:3],
                                                                         in1=tb[:, 0:TT], op0=ALU.mult, op1=ALU.add),
                                 reads=[xin, cw, tb], writes=[ta])
                            k.op("act", lambda e: e.activation(out=xT[:, c, 0:TT], in_=ta[:, 0:TT], func=AF.Silu, bias=cb[:, c:c + 1]),
                                 reads=[ta, cb], writes=[xT], acc=(c > 0))
                            if c >= 2:
                                dres = BT if c == 2 else CT
                                k.op("pool", lambda e: e.tensor_copy(out=dres[:, t0:t0 + TT], in_=xT[:, c, 0:TT]), reads=[xT], writes=[dres], acc=True)
                        for b in range(TT // 128):
                            blk = t0 // 128 + b
                            for c in range(2):
                                p = ps.next()
                                k.op("pe", lambda e: e.transpose(p[:, 0:128], xT[:, c, b * 128:(b + 1) * 128], ident[:]),
                                     reads=[xT, ident], writes=[p])
                                evac(c, x_tok[:, blk, c * 128:(c + 1) * 128], p[:, 0:128], [p], [x_tok], acc=True)
                            p = ps.next()
                            k.op("pe", lambda e: e.transpose(p[:, 0:128], xT[:, 2, b * 128:(b + 1) * 128], ident[:]),
                                 reads=[xT, ident], writes=[p])
                            evac(1, B_tok[:, blk, :], p[:, 0:128], [p], [B_tok], acc=True)
                    if debug.get("ssd_stop", 9) <= 1:
                        continue
                    for b0 in range(0, NB, 2):
                        k.dma("sp", dtr[:, b0:b0 + 2, :],
                              PTM[l][off + b0 * 128:off + (b0 + 2) * 128, C_DT:C_DT + 8].rearrange("(b p) j -> p b j", p=128),
                              reads=[PTM[l]], writes=[dtr], acc=(b0 > 0))
                    if debug.get("ssd_stop", 9) <= 2:
                        continue
                    k.op("dve", lambda e: e.tensor_tensor(out=dtt[:, 0:NB, :], in0=dtr[:, 0:NB, :],
                                                          in1=dtb[:].unsqueeze(1).to_broadcast([128, NB, 8]), op=ALU.add),
                         reads=[dtr, dtb], writes=[dtt])
                    k.op("act", lambda e: e.activation(out=dtr[:, 0:NB, :], in_=dtt[:, 0:NB, :], func=AF.Exp), reads=[dtt], writes=[dtr])
                    k.op("act", lambda e: e.activation(out=dtt[:, 0:NB, :], in_=dtr[:, 0:NB, :], func=AF.Ln, bias=onec[:, 0:1]),
                         reads=[dtr, onec], writes=[dtt])
                    k.op("dve", lambda e: e.tensor_tensor(out=at[:, 0:NB, :], in0=dtt[:, 0:NB, :],
                                                          in1=aneg[:].unsqueeze(1).to_broadcast([128, NB, 8]), op=ALU.mult),
                         reads=[dtt, aneg], writes=[at])
                    for d in range(2):
                        if lat:
                            stin = sop.next()
                            for g in range(2):
                                for hh in range(2):
                                    k.dma("sp", stin[hh * 64:(hh + 1) * 64, g * 64:(g + 1) * 64], I["state_ssm"][l, d, 2 * g + hh, :, :],
                                          writes=[stin], acc=not (g == 0 and hh == 0))
                            p = ps.next()
                            k.op("pe", lambda e: e.transpose(p[:, 0:128], stin[:, :], ident[:]), reads=[stin, ident], writes=[p])
                            k.op("dve", lambda e: e.tensor_copy(out=S[d][:].rearrange("p a b -> p (a b)"), in_=p[:, 0:128]),
                                 reads=[p], writes=[S[d]])
                            k.op("act", lambda e: e.copy(out=Sb[d][:], in_=S[d][:]), reads=[S[d]], writes=[Sb[d]])
                    if debug.get("ssd_stop", 9) <= 3:
                        continue
                    ywritten = set()

                    def ssd_dir_gen(d):
                        stp, exp_, GUp, Lp, L2p, scp, xdp, typ, Bdp, ydp = (WS[d][n_] for n_ in ("stp", "exp", "GUp", "Lp", "L2p", "scp", "xdp", "typ", "Bdp", "ydp"))
                        ps = psd[d]
                        U = cm[:, d, :]
                        order = list(range(NB)) if d == 0 else list(range(NB - 1, -1, -1))
                        halves = (0, 1) if d == 0 else (1, 0)
                        for blk in order:
                            tok0 = blk * 128
                            seq_first = (blk % BPS == 0) if d == 0 else (blk % BPS == BPS - 1)
                            seq_last = (blk % BPS == BPS - 1) if d == 0 else (blk % BPS == 0)
                            if seq_first and not lat:
                                k.op("dve", lambda e: e.memset(S[d][:], 0.0), writes=[S[d]])
                                k.op("act", lambda e: e.copy(out=Sb[d][:], in_=S[d][:]), reads=[S[d]], writes=[Sb[d]])
                            a_blk = at[:, blk, d * 4:(d + 1) * 4]
                            pc = ps.next()
                            for j, lh in enumerate((U, onesA, onesB)):
                                k.op("pe", lambda e: e.matmul(pc[:, 4 * j:4 * j + 4], lhsT=lh, rhs=a_blk, start=True, stop=True),
                                     reads=[cm, at], writes=[pc], inc=(j == 2))
                            st = stp.next()
                            k.op("act", lambda e: e.copy(out=st[:, 0:12], in_=pc[:, 0:12]), reads=[pc], writes=[st])
                            k.op("dve", lambda e: e.tensor_tensor(out=st[0:64, 16:20], in0=st[0:64, 4:8], in1=st[0:64, 0:4], op=ALU.subtract),
                                 reads=[st], writes=[st])
                            k.op("dve", lambda e: e.tensor_tensor(out=st[64:128, 16:20], in0=st[64:128, 8:12], in1=st[64:128, 0:4], op=ALU.subtract),
                                 reads=[st], writes=[st])
                            ex = exp_.next()
                            k.op("act", lambda e: e.activation(out=ex[:, 0:12], in_=st[:, 0:12], func=AF.Exp), reads=[st], writes=[ex])
                            k.op("act", lambda e: e.activation(out=ex[:, 16:20], in_=st[:, 16:20], func=AF.Exp), reads=[st], writes=[ex], acc=True)
                            if debug.get("scan_stop", 9) <= 1:
                                continue
                            yield
                            GU = GUp.next()
                            for h in range(4):
                                k.op("dve", lambda e: e.tensor_scalar(out=GU[:, h, :], in0=U, scalar1=a_blk[:, h:h + 1], scalar2=None, op0=ALU.mult),
                                     reads=[cm, at], writes=[GU], acc=(h > 0))
                            pa = ps.next()
                            k.op("pe", lambda e: e.matmul(pa[:, :], lhsT=onesblk, rhs=GU[:].rearrange("p h i -> p (h i)"), start=True, stop=True),
                                 reads=[cm, GU], writes=[pa])
                            L = Lp.next()
                            for h in range(4):
                                k.op("dve", lambda e: e.tensor_scalar(out=L[:, h, :], in0=pa[:, h * 128:(h + 1) * 128], scalar1=st[:, h:h + 1],
                                                                      scalar2=0.0, op0=ALU.subtract, op1=ALU.min),
                                     reads=[pa, st], writes=[L], acc=(h > 0))
                            L2 = L2p.next()
                            k.op("act", lambda e: e.activation(out=L2[:], in_=L[:], func=AF.Exp), reads=[L], writes=[L2])
                            k.op("pool", lambda e: e.tensor_tensor(out=L[:], in0=L2[:], in1=U.unsqueeze(1).to_broadcast([128, 4, 128]), op=ALU.mult),
                                 reads=[L2, cm], writes=[L])
                            if debug.get("scan_stop", 9) <= 2:
                                continue
                            yield
                            pcbs = [ps.next(), ps.next()]
                            for g in range(2):
                                k.op("pe", lambda e: e.matmul(pcbs[g][:, 0:128], lhsT=BT[g * 64:(g + 1) * 64, tok0:tok0 + 128],
                                                              rhs=CT[g * 64:(g + 1) * 64, tok0:tok0 + 128], start=True, stop=True),
                                     reads=[BT, CT], writes=[pcbs[g]])
                            scT = scp.next()
                            for g in range(2):
                                k.op("dve", lambda e: e.tensor_tensor(out=scT[:, 2 * g:2 * g + 2, :],
                                                                      in0=pcbs[g][:, 0:128].unsqueeze(1).to_broadcast([128, 2, 128]),
                                                                      in1=L[:, 2 * g:2 * g + 2, :], op=ALU.mult),
                                     reads=[pcbs[g], L], writes=[scT], acc=(g > 0))
                            xdt = xdp.next()
                            k.op("dve", lambda e: e.tensor_tensor(out=xdt[:], in0=x_tok[:, blk, :].rearrange("p (h d) -> p h d", h=4),
                                                                  in1=dtt[:, blk, d * 4:(d + 1) * 4].unsqueeze(2).to_broadcast([128, 4, 64]), op=ALU.mult),
                                 reads=[x_tok, dtt], writes=[xdt])
                            yield
                            pyd = ps.next()
                            for h in range(4):
                                k.op("pe", lambda e: e.matmul(pyd[:, h * 64:(h + 1) * 64], lhsT=scT[:, h, :], rhs=xdt[:, h, :], start=True, stop=True),
                                     reads=[scT, xdt], writes=[pyd], inc=(h == 3))
                            yds = ydp.next()
                            k.op("act", lambda e: e.copy(out=yds[:], in_=pyd[:, 0:256]), reads=[pyd], writes=[yds])
                            if debug.get("scan_stop", 9) <= 3:
                                continue
                            for half in halves:
                                hb = half * 64
                                ec = 4 if half == 0 else 8
                                yield
                                pyos = [ps.next(), ps.next()]
                                for h in range(4):
                                    g, hh = h // 2, h % 2
                                    k.op("pe", lambda e: e.matmul(pyos[g][:, hh * 64:(hh + 1) * 64], lhsT=CT[g * 64:(g + 1) * 64, tok0:tok0 + 128],
                                                                  rhs=Sb[d][g * 64:(g + 1) * 64, hh, :], start=True, stop=True),
                                         reads=[CT, Sb[d]], writes=[pyos[g]])
                                ty = typ.next()
                                for g in range(2):
                                    k.op("dve", lambda e: e.tensor_tensor(out=ty[hb:hb + 64, 2 * g:2 * g + 2, :],
                                                                          in0=pyos[g][hb:hb + 64, 0:128].rearrange("p (h d) -> p h d", h=2),
                                                                          in1=ex[hb:hb + 64, 2 * g:2 * g + 2].unsqueeze(2).to_broadcast([64, 2, 64]), op=ALU.mult),
                                         reads=[pyos[g], ex], writes=[ty], acc=(g > 0))
                                if (blk, half) not in ywritten:
                                    ywritten.add((blk, half))
                                    k.op("dve", lambda e: e.tensor_tensor(out=yacc[hb:hb + 64, blk, :], in0=ty[hb:hb + 64, :, :].rearrange("p h d -> p (h d)"),
                                                                          in1=yds[hb:hb + 64, :], op=ALU.add),
                                         reads=[ty, yds], writes=[yacc], acc=True)
                                else:
                                    ty2 = typ.next()
                                    k.op("dve", lambda e: e.tensor_tensor(out=ty2[hb:hb + 64, :, :].rearrange("p h d -> p (h d)"),
                                                                          in0=ty[hb:hb + 64, :, :].rearrange("p h d -> p (h d)"),
                                                                          in1=yds[hb:hb + 64, :], op=ALU.add),
                                         reads=[ty, yds], writes=[ty2])
                                    k.op("pool", lambda e: e.tensor_tensor(out=yacc[hb:hb + 64, blk, :], in0=yacc[hb:hb + 64, blk, :],
                                                                           in1=ty2[hb:hb + 64, :, :].rearrange("p h d -> p (h d)"), op=ALU.add),
                                         reads=[yacc, ty2], writes=[yacc])
                                if debug.get("scan_stop", 9) <= 4:
                                    continue
                                yield
                                Bd = Bdp.next()
                                for h in range(4):
                                    k.op("dve", lambda e: e.tensor_scalar(out=Bd[hb:hb + 64, h, :], in0=B_tok[hb:hb + 64, blk, :],
                                                                          scalar1=ex[hb:hb + 64, 16 + h:17 + h], scalar2=None, op0=ALU.mult),
                                         reads=[B_tok, ex], writes=[Bd], acc=(h > 0))
                                pst = ps.next()
                                for h in range(4):
                                    k.op("pe", lambda e: e.matmul(pst[:, h * 64:(h + 1) * 64], lhsT=Bd[hb:hb + 64, h, :], rhs=xdt[hb:hb + 64, h, :],
                                                                  start=True, stop=True), reads=[Bd, xdt], writes=[pst], inc=(h == 3))
                                for h in range(4):
                                    g, hh = h // 2, h % 2
                                    k.op("dve", lambda e: e.scalar_tensor_tensor(out=S[d][g * 64:(g + 1) * 64, hh, :], in0=S[d][g * 64:(g + 1) * 64, hh, :],
                                                                                 scalar=ex[g * 64:(g + 1) * 64, ec + h:ec + h + 1],
                                                                                 in1=pst[g * 64:(g + 1) * 64, h * 64:(h + 1) * 64],
                                                                                 op0=ALU.mult, op1=ALU.add),
                                         reads=[S[d], ex, pst], writes=[S[d]])
                                k.op("act", lambda e: e.copy(out=Sb[d][:], in_=S[d][:]), reads=[S[d]], writes=[Sb[d]])
                            if seq_last and not lat:
                                p = ps.next()
                                k.op("pe", lambda e: e.transpose(p[:, 0:128], S[d][:].rearrange("p a b -> p (a b)"), ident[:]),
                                     reads=[S[d], ident], writes=[p])
                                so = sop.next()
                                k.op("dve", lambda e: e.tensor_copy(out=so[:, :], in_=p[:, 0:128]), reads=[p], writes=[so])
                                for g in range(2):
                                    for hh in range(2):
                                        k.dma("pool", O["new_ssm"][blk // BPS, l, d, 2 * g + hh, :, :], so[hh * 64:(hh + 1) * 64, g * 64:(g + 1) * 64], reads=[so])

                    gens = [ssd_dir_gen(0), ssd_dir_gen(1)]
                    while gens:
                        for g_ in list(gens):
                            try:
                                next(g_)
                            except StopIteration:
                                gens.remove(g_)
                    if debug.get("ssd_stop", 9) <= 4:
                        continue
                    for blk in range(NB):
                        tok0 = blk * 128
                        z = zp.next()
                        k.dma("sp", z[:], PTM[l][off + tok0:off + tok0 + 128, C_SSZ:C_SSZ + 256], reads=[PTM[l]], writes=[z])
                        t1 = y2p.next()
                        k.op("dve", lambda e: e.tensor_tensor(out=t1[:].rearrange("p (h d) -> p h d", h=4),
                                                              in0=x_tok[:, blk, :].rearrange("p (h d) -> p h d", h=4),
                                                              in1=Dt[:, 0:4].unsqueeze(2).to_broadcast([128, 4, 64]), op=ALU.mult),
                             reads=[x_tok, Dt], writes=[t1])
                        t2 = y2p.next()
                        k.op("pool", lambda e: e.tensor_tensor(out=t2[:], in0=t1[:], in1=yacc[:, blk, :], op=ALU.add), reads=[t1, yacc], writes=[t2])
                        sz = y2p.next()
                        k.op("act", lambda e: e.activation(out=sz[:], in_=z[:], func=AF.Silu), reads=[z], writes=[sz])
                        y2 = y2p.next()
                        k.op("dve", lambda e: e.tensor_tensor(out=y2[:], in0=t2[:], in1=sz[:], op=ALU.mult), reads=[t2, sz], writes=[y2])
                        ssum = ssp.next()
                        k.op("dve", lambda e: e.memset(ssum[:], 0.0), writes=[ssum])
                        for g in range(2):
                            k.op("act", lambda e: e.activation(out=t1[:, g * 128:(g + 1) * 128], in_=y2[:, g * 128:(g + 1) * 128], func=AF.Square,
                                                               accum_out=ssum[:, g:g + 1]), reads=[y2], writes=[t1, ssum])
                        k.op("act", lambda e: e.activation(out=ssum[:, 2:4], in_=ssum[:, 0:2], func=AF.Sqrt, scale=1.0 / 128, bias=epsb[:, 0:1]),
                             reads=[ssum, epsb], writes=[ssum])
                        k.op("dve", lambda e: e.reciprocal(out=ssum[:, 0:2], in_=ssum[:, 2:4]), reads=[ssum], writes=[ssum])
                        y3 = sz
                        for g in range(2):
                            k.op("dve", lambda e: e.scalar_tensor_tensor(out=y3[:, g * 128:(g + 1) * 128], in0=y2[:, g * 128:(g + 1) * 128],
                                                                         scalar=ssum[:, g:g + 1], in1=nwt[:, g * 128:(g + 1) * 128],
                                                                         op0=ALU.mult, op1=ALU.mult), reads=[y2, ssum, nwt], writes=[y3])
                        os_ = osp.next()
                        for c in range(2):
                            p = ps.next()
                            k.op("pe", lambda e: e.transpose(p[:, 0:128], y3[:, c * 128:(c + 1) * 128], ident[:]), reads=[y3, ident], writes=[p])
                            evac(c, os_[:, c, :], p[:, 0:128], [p], [os_], acc=(c > 0))
                        k.dma("pool", MIX[l][512:768, off + tok0:off + tok0 + 128].rearrange("(c p) t -> p c t", p=128), os_[:],
                              reads=[os_], writes=[MIX[l]], acc=True)
                k.barrier()


        def dn_phase(l):
            with ExitStack() as ph:
                cm = k.tile("cmd", [128, 14, 128], F32, ph)
                k.dma("sp", cm[:, 0:7, :], I["cmask"].rearrange("r p q -> p r q")[:, 0:7, :], writes=[cm])
                k.dma("sp", cm[:, 7:14, :], I["cmask"].rearrange("r p q -> p r q")[:, 7:14, :], writes=[cm], acc=True)
                onesblk, onesA, onesB, identm = cm[:, 2, :], cm[:, 3, :], cm[:, 4, :], cm[:, 7, :]
                identb = k.tile("identb", [128, 128], BF16, ph)
                k.op("dve", lambda e: e.tensor_copy(out=identb[:], in_=cm[:, 7, :]), reads=[cm], writes=[identb])
                cw = k.tile("cwd", [128, 6, 3], F32, ph)
                for kk in range(3):
                    k.dma("sp", cw[:, :, kk], I["dn_conv_w"][l, kk].rearrange("(c p) -> p c", p=128), writes=[cw],
                          acc=(kk > 0), allow_slow_non_contiguous=True)
                dtb = k.tile("dtbd", [128, 8], F32, ph)
                k.dma("sp", dtb[:], I["dn_dt_bias"][l:l + 1].rearrange("o a b -> o (a b)").partition_broadcast(128), writes=[dtb])
                aneg = k.tile("anegd", [128, 8], F32, ph)
                k.dma("sp", aneg[:], I["dn_a_log"][l:l + 1].rearrange("o a b -> o (a b)").partition_broadcast(128), writes=[aneg])
                k.op("act", lambda e: e.activation(out=aneg[:], in_=aneg[:], func=AF.Exp), reads=[aneg], writes=[aneg])
                k.op("dve", lambda e: e.tensor_scalar(out=aneg[:], in0=aneg[:], scalar1=-1.0, scalar2=None, op0=ALU.mult),
                     reads=[aneg], writes=[aneg])
                nw1 = k.tile("nw1", [128, 64], F32, ph)
                k.dma("sp", nw1[:], I["dn_norm_w"][l:l + 1, :].partition_broadcast(128), writes=[nw1])
                onec = k.tile("onecd", [128, 1], F32, ph)
                k.op("dve", lambda e: e.memset(onec[:], 1.0), writes=[onec])
                NBM = TL // 128
                qT = k.tile("qTd", [128, 2, TL], BF16, ph)
                kT = k.tile("kTd", [128, 2, TL], BF16, ph)
                k_tok = k.tile("k_tok", [128, NBM, 256], BF16, ph)
                v_tok = k.tile("v_tok", [128, NBM, 256], BF16, ph)
                yacc = k.tile("yaccd", [128, NBM, 256], F32, ph)
                braw = k.tile("braw", [128, NBM, 16], F32, ph)
                btmp = k.tile("btmp", [128, NBM, 16], F32, ph)
                lnb = k.tile("lnb", [128, NBM, 8], F32, ph)
                gt = k.tile("gt", [128, NBM, 8], F32, ph)
                S = [k.tile("Sd%d" % d, [128, 2, 64], F32, ph) for d in range(2)]
                Sb = [k.tile("Sbd%d" % d, [128, 2, 64], BF16, ph) for d in range(2)]

                def h4(ap):
                    return ap.rearrange("p (c r) x -> p c r x", c=2)

                for (off, T, lat, SEQT) in ((0, NCTX * TC, False, TC), (LOFF, TL, True, TL)):
                    k.barrier()
                    NB = T // 128
                    BPS = SEQT // 128
                    TT = min(512, SEQT)
                    with ExitStack() as pre:
                        xinp = k.pool("xind", [128, 6, 514], BF16, 2, pre)
                        cqp = k.pool("cq", [128, 6, 512], F32, 1, pre)
                        tmpp = k.pool("tmpd", [128, 512], F32, 4, pre)
                        knp = k.pool("kn", [128, 2, 512], F32, 1, pre)
                        for t0 in range(0, T, TT):
                            g0 = off + t0
                            xin = xinp.next()
                            first = (t0 % SEQT == 0)
                            lastt = ((t0 + TT) % SEQT == 0)
                            lo = g0 if first else g0 - 1
                            hi = g0 + TT if lastt else g0 + TT + 1
                            c_lo = 1 if first else 0
                            for half3 in range(2):
                                k.dma("sp", xin[:, 3 * half3:3 * half3 + 3, c_lo:c_lo + (hi - lo)],
                                      PFM[l][R_DNQ + 384 * half3:R_DNQ + 384 * half3 + 384, lo:hi].rearrange("(c p) t -> p c t", p=128),
                                      reads=[PFM[l]], writes=[xin], acc=(half3 > 0))
                            if first:
                                k.op("dve", lambda e: e.memset(xin[:, :, 0:1], 0.0), writes=[xin], acc=True)
                            if lastt:
                                k.op("dve", lambda e: e.memset(xin[:, :, TT + 1:TT + 2], 0.0), writes=[xin], acc=True)
                            cq = cqp.next()
                            for c in range(6):
                                ta = tmpp.next()
                                tb = tmpp.next()
                                eng = "dve" if c % 2 == 0 else "pool"
                                k.op("dve", lambda e: e.tensor_scalar(out=ta[:, 0:TT], in0=xin[:, c, 0:TT], scalar1=cw[:, c, 0:1],
                                                                      scalar2=None, op0=ALU.mult), reads=[xin, cw], writes=[ta])
                                k.op("dve", lambda e: e.scalar_tensor_tensor(out=tb[:, 0:TT], in0=xin[:, c, 1:TT + 1], scalar=cw[:, c, 1:2],
                                                                             in1=ta[:, 0:TT], op0=ALU.mult, op1=ALU.add),
                                     reads=[xin, cw, ta], writes=[tb])
                                k.op("dve", lambda e: e.scalar_tensor_tensor(out=ta[:, 0:TT], in0=xin[:, c, 2:TT + 2], scalar=cw[:, c, 2:3],
                                                                             in1=tb[:, 0:TT], op0=ALU.mult, op1=ALU.add),
                                     reads=[xin, cw, tb], writes=[ta])
                                k.op("act", lambda e: e.activation(out=cq[:, c, 0:TT], in_=ta[:, 0:TT], func=AF.Silu),
                                     reads=[ta], writes=[cq], acc=(c > 0))
                            kn = knp.next()
                            for c in range(4):
                                sq = tmpp.next()
                                k.op("act", lambda e: e.activation(out=sq[:, 0:TT], in_=cq[:, c, 0:TT], func=AF.Square), reads=[cq], writes=[sq])
                                p = ps.next()
                                k.op("pe", lambda e: e.matmul(p[:, 0:TT], lhsT=onesblk, rhs=sq[:, 0:TT], start=True, stop=True),
                                     reads=[cm, sq], writes=[p])
                                rs = tmpp.next()
                                k.op("act", lambda e: e.activation(out=rs[:, 0:TT], in_=p[:, 0:TT], func=AF.Sqrt, bias=epsb[:, 0:1]),
                                     reads=[p, epsb], writes=[rs])
                                rs2 = tmpp.next()
                                k.op("dve", lambda e: e.reciprocal(out=rs2[:, 0:TT], in_=rs[:, 0:TT]), reads=[rs], writes=[rs2])
                                if c < 2:
                                    k.op("dve", lambda e: e.scalar_tensor_tensor(out=qT[:, c, t0:t0 + TT], in0=cq[:, c, 0:TT], scalar=0.125,
                                                                                 in1=rs2[:, 0:TT], op0=ALU.mult, op1=ALU.mult),
                                         reads=[cq, rs2], writes=[qT], acc=True)
                                else:
                                    k.op("dve", lambda e: e.tensor_tensor(out=kn[:, c - 2, 0:TT], in0=cq[:, c, 0:TT], in1=rs2[:, 0:TT], op=ALU.mult),
                                         reads=[cq, rs2], writes=[kn], acc=(c > 2))
                                    k.op("act", lambda e: e.copy(out=kT[:, c - 2, t0:t0 + TT], in_=kn[:, c - 2, 0:TT]), reads=[kn], writes=[kT], acc=True)
                            for b in range(TT // 128):
                                blk = t0 // 128 + b
                                for c in range(2):
                                    p = ps.next()
                                    k.op("pe", lambda e: e.transpose(p[:, 0:128], kn[:, c, b * 128:(b + 1) * 128], ident[:]),
                                         reads=[kn, ident], writes=[p])
                                    evac(c, k_tok[:, blk, c * 128:(c + 1) * 128], p[:, 0:128], [p], [k_tok], acc=True)
                                    p = ps.next()
                                    k.op("pe", lambda e: e.transpose(p[:, 0:128], cq[:, 4 + c, b * 128:(b + 1) * 128], ident[:]),
                                         reads=[cq, ident], writes=[p])
                                    evac(c + 1, v_tok[:, blk, c * 128:(c + 1) * 128], p[:, 0:128], [p], [v_tok], acc=True)
                        k.barrier()
                    for b0 in range(0, NB, 2):
                        k.dma("sp", braw[:, b0:b0 + 2, :],
                              PTM[l][off + b0 * 128:off + (b0 + 2) * 128, C_BETA:C_BETA + 16].rearrange("(b p) j -> p b j", p=128),
                              reads=[PTM[l]], writes=[braw], acc=(b0 > 0))
                    k.op("act", lambda e: e.activation(out=btmp[:, 0:NB, 0:8], in_=braw[:, 0:NB, 0:8], func=AF.Exp, scale=-1.0),
                         reads=[braw], writes=[btmp])
                    k.op("act", lambda e: e.activation(out=lnb[:, 0:NB, :], in_=btmp[:, 0:NB, 0:8], func=AF.Ln, bias=onec[:, 0:1]),
                         reads=[btmp, onec], writes=[lnb])
                    k.op("dve", lambda e: e.tensor_scalar(out=lnb[:, 0:NB, :], in0=lnb[:, 0:NB, :], scalar1=-1.0, scalar2=None, op0=ALU.mult),
                         reads=[lnb], writes=[lnb])
                    k.op("dve", lambda e: e.tensor_tensor(out=btmp[:, 0:NB, 8:16], in0=braw[:, 0:NB, 8:16],
                                                          in1=dtb[:].unsqueeze(1).to_broadcast([128, NB, 8]), op=ALU.add),
                         reads=[braw, dtb], writes=[btmp])
                    k.op("act", lambda e: e.activation(out=braw[:, 0:NB, 8:16], in_=btmp[:, 0:NB, 8:16], func=AF.Exp), reads=[btmp], writes=[braw])
                    k.op("act", lambda e: e.activation(out=btmp[:, 0:NB, 8:16], in_=braw[:, 0:NB, 8:16], func=AF.Ln, bias=onec[:, 0:1]),
                         reads=[braw, onec], writes=[btmp])
                    k.op("dve", lambda e: e.tensor_tensor(out=gt[:, 0:NB, :], in0=btmp[:, 0:NB, 8:16],
                                                          in1=aneg[:].unsqueeze(1).to_broadcast([128, NB, 8]), op=ALU.mult),
                         reads=[btmp, aneg], writes=[gt])
                    for d in range(2):
                        if lat:
                            for h in range(4):
                                c_, par = h // 2, h % 2
                                k.dma("sp", S[d][par * 64:(par + 1) * 64, c_, :], I["state_dn"][l, d, h, :, :], writes=[S[d]], acc=(h > 0))
                            k.op("act", lambda e: e.copy(out=Sb[d][:], in_=S[d][:]), reads=[S[d]], writes=[Sb[d]])
                    if debug.get("dn_stop", 9) <= 1:
                        continue
                    with ExitStack() as sc:
                        ywritten = set()
                        WK = []
                        for d_ in range(2):
                            W_ = {"stp": k.pool("std", [128, 24], F32, 2, sc), "exp": k.pool("exd", [128, 24], F32, 4, sc), "w4": {}, "w64": {}}
                            for nm in ("GU", "GUb", "L", "E1", "E2", "E3", "A", "AT", "tm"):
                                W_["w4"][nm] = k.tile("w4" + nm, [128, 4, 128], F32, sc)
                            W_["Xp"] = k.pool("Xp", [128, 4, 128], BF16, 2, sc)
                            W_["XTp"] = k.pool("XTp", [128, 4, 128], BF16, 2, sc)
                            for nm in ("As", "ATs", "Ps", "Ps2"):
                                W_["w4"][nm] = k.tile("w4b" + nm, [128, 4, 128], BF16, sc)
                            for nm in ("ty", "ty2"):
                                W_["w64"][nm] = k.tile("w64" + nm, [128, 4, 64], F32, sc)
                            for nm in ("vb", "kbg", "vn"):
                                W_["w64"][nm] = k.tile("w64" + nm, [128, 4, 64], BF16, sc)
                            W_["slots"] = [{"QK": k.tile("sQK", [128, 4, 128], BF16, sc), "wT": k.tile("swT", [128, 4, 128], BF16, sc),
                                            "u": k.tile("su", [128, 4, 64], F32, sc), "kdec": k.tile("skd", [128, 4, 64], BF16, sc)} for _ in range(2)]
                            WK.append(W_)

                        def dn_intra_gen(d):
                            stp, exp_, w4, w64, Xp, XTp = (WK[d][n_] for n_ in ("stp", "exp", "w4", "w64", "Xp", "XTp"))
                            ps = ps8
                            cnt = 0
                            U = cm[:, d, :]
                            m_incl = cm[:, d, :]
                            m_at = cm[:, 5 + d, :]
                            m_a = cm[:, 6 - d, :]
                            order = list(range(NB)) if d == 0 else list(range(NB - 1, -1, -1))
                            halves = (0, 1) if d == 0 else (1, 0)

                            def bc4(m):
                                return m.unsqueeze(1).to_broadcast([128, 4, 128])

                            for blk in order:
                                while busy[d] >= 2:
                                    yield
                                busy[d] += 1
                                slot = WK[d]["slots"][cnt % 2]
                                cnt += 1
                                tok0 = blk * 128
                                g_blk = gt[:, blk, d * 4:(d + 1) * 4]
                                lb_blk = lnb[:, blk, d * 4:(d + 1) * 4]
                                pc = ps.next()
                                for j, lh in enumerate((U, onesA, onesB)):
                                    k.op("pe", lambda e: e.matmul(pc[:, 4 * j:4 * j + 4], lhsT=lh, rhs=g_blk, start=True, stop=True),
                                         reads=[cm, gt], writes=[pc], inc=(j == 2))
                                st = stp.next()
                                k.op("act", lambda e: e.copy(out=st[:, 0:12], in_=pc[:, 0:12]), reads=[pc], writes=[st])
                                k.op("dve", lambda e: e.tensor_tensor(out=st[:, 12:16], in0=st[:, 0:4], in1=lb_blk, op=ALU.add),
                                     reads=[st, lnb], writes=[st])
                                k.op("dve", lambda e: e.tensor_scalar(out=st[:, 20:24], in0=st[:, 0:4], scalar1=-1.0, scalar2=None, op0=ALU.mult),
                                     reads=[st], writes=[st])
                                k.op("dve", lambda e: e.tensor_tensor(out=st[0:64, 16:20], in0=st[0:64, 4:8], in1=st[0:64, 0:4], op=ALU.subtract),
                                     reads=[st], writes=[st])
                                k.op("dve", lambda e: e.tensor_tensor(out=st[64:128, 16:20], in0=st[64:128, 8:12], in1=st[64:128, 0:4], op=ALU.subtract),
                                     reads=[st], writes=[st])
                                ex = exp_.next()
                                k.op("act", lambda e: e.activation(out=ex[:, 0:20], in_=st[:, 0:20], func=AF.Exp), reads=[st], writes=[ex])
                                k.op("act", lambda e: e.activation(out=ex[:, 20:24], in_=lb_blk, func=AF.Exp), reads=[lnb], writes=[ex], acc=True)
                                yield
                                GU, GUb, L = w4["GU"], w4["GUb"], w4["L"]
                                for h in range(4):
                                    k.op("dve", lambda e: e.tensor_scalar(out=GU[:, h, :], in0=U, scalar1=g_blk[:, h:h + 1], scalar2=None, op0=ALU.mult),
                                         reads=[cm, gt], writes=[GU], acc=(h > 0))
                                for h in range(4):
                                    k.op("dve", lambda e: e.scalar_tensor_tensor(out=GUb[:, h, :], in0=identm, scalar=lb_blk[:, h:h + 1],
                                                                                 in1=GU[:, h, :], op0=ALU.mult, op1=ALU.add),
                                         reads=[cm, lnb, GU], writes=[GUb], acc=(h > 0))
                                pa1 = ps.next()
                                k.op("pe", lambda e: e.matmul(pa1[:, :], lhsT=onesblk, rhs=GU[:].rearrange("p h i -> p (h i)"), start=True, stop=True),
                                     reads=[cm, GU], writes=[pa1])
                                pa2 = ps.next()
                                k.op("pe", lambda e: e.matmul(pa2[:, :], lhsT=onesblk, rhs=GUb[:].rearrange("p h i -> p (h i)"), start=True, stop=True),
                                     reads=[cm, GUb], writes=[pa2])
                                tm = w4["tm"]
                                for (pa_, bcol, scl, msk, dst) in ((pa1, 20, 1.0, m_incl, "E3"), (pa1, 12, -1.0, m_a, "E1"), (pa2, 20, 1.0, m_at, "E2")):
                                    for h in range(4):
                                        k.op("act", lambda e: e.activation(out=tm[:, h, :], in_=pa_[:, h * 128:(h + 1) * 128], func=AF.Exp,
                                                                           bias=st[:, bcol + h:bcol + h + 1], scale=scl),
                                             reads=[pa_, st], writes=[tm], acc=(h > 0))
                                    k.op("dve", lambda e: e.tensor_tensor(out=w4[dst][:], in0=tm[:], in1=bc4(msk), op=ALU.min),
                                         reads=[tm, cm], writes=[w4[dst]])
                                yield
                                pk = [ps.next(), ps.next()]
                                for c_ in range(2):
                                    for par in range(2):
                                        k.op("pe", lambda e: e.matmul(pk[par][:, c_ * 128:(c_ + 1) * 128],
                                                                      lhsT=kT[par * 64:(par + 1) * 64, c_, tok0 + (off - off):tok0 + 128],
                                                                      rhs=kT[par * 64:(par + 1) * 64, c_, tok0:tok0 + 128], start=True, stop=True),
                                             reads=[kT], writes=[pk[par]])
                                A, AT, QK = w4["A"], w4["AT"], slot["QK"]
                                for par in range(2):
                                    k.op("dve", lambda e: e.tensor_tensor(out=h4(A[:])[:, :, par, :], in0=pk[par][:, 0:256].rearrange("p (c i) -> p c i", c=2),
                                                                          in1=h4(w4["E1"][:])[:, :, par, :], op=ALU.mult),
                                         reads=[pk[par], w4["E1"]], writes=[A], acc=(par > 0))
                                    k.op("dve", lambda e: e.tensor_tensor(out=h4(AT[:])[:, :, par, :], in0=pk[par][:, 0:256].rearrange("p (c i) -> p c i", c=2),
                                                                          in1=h4(w4["E2"][:])[:, :, par, :], op=ALU.mult),
                                         reads=[pk[par], w4["E2"]], writes=[AT], acc=(par > 0))
                                pq = [ps.next(), ps.next()]
                                for c_ in range(2):
                                    for par in range(2):
                                        k.op("pe", lambda e: e.matmul(pq[par][:, c_ * 128:(c_ + 1) * 128],
                                                                      lhsT=kT[par * 64:(par + 1) * 64, c_, tok0:tok0 + 128],
                                                                      rhs=qT[par * 64:(par + 1) * 64, c_, tok0:tok0 + 128], start=True, stop=True),
                                             reads=[kT, qT], writes=[pq[par]])
                                for par in range(2):
                                    k.op("dve", lambda e: e.tensor_tensor(out=h4(QK[:])[:, :, par, :], in0=pq[par][:, 0:256].rearrange("p (c i) -> p c i", c=2),
                                                                          in1=h4(w4["E3"][:])[:, :, par, :], op=ALU.mult),
                                         reads=[pq[par], w4["E3"]], writes=[QK], acc=(par > 0))
                                yield
                                X = Xp.next()
                                XT = XTp.next()
                                tm = w4["tm"]
                                k.op("pool", lambda e: e.tensor_tensor(out=tm[:], in0=A[:], in1=bc4(cm[:, 8, :]), op=ALU.mult), reads=[A, cm], writes=[tm])
                                k.op("dve", lambda e: e.scalar_tensor_tensor(out=X[:], in0=tm[:], scalar=-1.0, in1=bc4(identm), op0=ALU.mult, op1=ALU.add),
                                     reads=[tm, cm], writes=[X])
                                k.op("pool", lambda e: e.tensor_tensor(out=L[:], in0=AT[:], in1=bc4(cm[:, 8, :]), op=ALU.mult), reads=[AT, cm], writes=[L])
                                k.op("dve", lambda e: e.scalar_tensor_tensor(out=XT[:], in0=L[:], scalar=-1.0, in1=bc4(identm), op0=ALU.mult, op1=ALU.add),
                                     reads=[L, cm], writes=[XT])
                                As, ATs, Ps, Ps2 = w4["As"], w4["ATs"], w4["Ps"], w4["Ps2"]
                                for lev in range(5):
                                    ms = cm[:, 9 + lev, :]
                                    k.op("pool", lambda e: e.tensor_tensor(out=As[:], in0=A[:], in1=bc4(ms), op=ALU.mult), reads=[A, cm], writes=[As])
                                    k.op("pool", lambda e: e.tensor_tensor(out=ATs[:], in0=AT[:], in1=bc4(ms), op=ALU.mult), reads=[AT, cm], writes=[ATs])
                                    pP = ps.next()
                                    for h in range(4):
                                        k.op("pe", lambda e: e.matmul(pP[:, h * 128:(h + 1) * 128], lhsT=ATs[:, h, :], rhs=X[:, h, :], start=True, stop=True),
                                             reads=[ATs, X], writes=[pP], inc=(h == 3))
                                    k.op("act", lambda e: e.activation(out=Ps[:].rearrange("p h i -> p (h i)"), in_=pP[:, :], func=AF.Identity, scale=-1.0),
                                         reads=[pP], writes=[Ps])
                                    pP2 = ps.next()
                                    for h in range(4):
                                        k.op("pe", lambda e: e.matmul(pP2[:, h * 128:(h + 1) * 128], lhsT=As[:, h, :], rhs=XT[:, h, :], start=True, stop=True),
                                             reads=[As, XT], writes=[pP2], inc=(h == 3))
                                    k.op("act", lambda e: e.activation(out=Ps2[:].rearrange("p h i -> p (h i)"), in_=pP2[:, :], func=AF.Identity, scale=-1.0),
                                         reads=[pP2], writes=[Ps2])
                                    yield
                                    pX = ps.next()
                                    for h in range(4):
                                        k.op("pe", lambda e: e.matmul(pX[:, h * 128:(h + 1) * 128], lhsT=identb[:, :], rhs=X[:, h, :], start=True, stop=False),
                                             reads=[identb, X], writes=[pX], inc=False)
                                        k.op("pe", lambda e: e.matmul(pX[:, h * 128:(h + 1) * 128], lhsT=XT[:, h, :], rhs=Ps[:, h, :], start=False, stop=True),
                                             reads=[XT, Ps], writes=[pX], inc=(h == 3))
                                    pXT = ps.next()
                                    for h in range(4):
                                        k.op("pe", lambda e: e.matmul(pXT[:, h * 128:(h + 1) * 128], lhsT=identb[:, :], rhs=XT[:, h, :], start=True, stop=False),
                                             reads=[identb, XT], writes=[pXT], inc=False)
                                        k.op("pe", lambda e: e.matmul(pXT[:, h * 128:(h + 1) * 128], lhsT=X[:, h, :], rhs=Ps2[:, h, :], start=False, stop=True),
                                             reads=[X, Ps2], writes=[pXT], inc=(h == 3))
                                    Xn = Xp.next()
                                    XTn = XTp.next()
                                    k.op("act", lambda e: e.copy(out=Xn[:].rearrange("p h i -> p (h i)"), in_=pX[:, :]), reads=[pX], writes=[Xn])
                                    k.op("act", lambda e: e.copy(out=XTn[:].rearrange("p h i -> p (h i)"), in_=pXT[:, :]), reads=[pXT], writes=[XTn])
                                    X, XT = Xn, XTn
                                    yield
                                yield
                                vb, kbg = w64["vb"], w64["kbg"]
                                kdec, u_sb, wT = slot["kdec"], slot["u"], slot["wT"]

                                def bc64(ap):
                                    return ap.unsqueeze(2).to_broadcast([128, 4, 64])

                                k.op("dve", lambda e: e.tensor_tensor(out=vb[:], in0=v_tok[:, blk, :].rearrange("p (h d) -> p h d", h=4), in1=bc64(ex[:, 20:24]), op=ALU.mult),
                                     reads=[v_tok, ex], writes=[vb])
                                k.op("pool", lambda e: e.tensor_tensor(out=kbg[:], in0=k_tok[:, blk, :].rearrange("p (h d) -> p h d", h=4), in1=bc64(ex[:, 12:16]), op=ALU.mult),
                                     reads=[k_tok, ex], writes=[kbg])
                                k.op("pool", lambda e: e.tensor_tensor(out=kdec[:], in0=k_tok[:, blk, :].rearrange("p (h d) -> p h d", h=4), in1=bc64(ex[:, 16:20]), op=ALU.mult),
                                     reads=[k_tok, ex], writes=[kdec])
                                pu = ps.next()
                                for h in range(4):
                                    k.op("pe", lambda e: e.matmul(pu[:, h * 64:(h + 1) * 64], lhsT=XT[:, h, :], rhs=vb[:, h, :], start=True, stop=True),
                                         reads=[XT, vb], writes=[pu], inc=(h == 3))
                                k.op("act", lambda e: e.copy(out=u_sb[:].rearrange("p h d -> p (h d)"), in_=pu[:, 0:256]), reads=[pu], writes=[u_sb])
                                pwT = ps.next()
                                for h in range(4):
                                    c_ = h // 2
                                    k.op("pe", lambda e: e.matmul(pwT[:, h * 128:(h + 1) * 128], lhsT=kbg[:, 2 * c_:2 * c_ + 2, :].rearrange("p r d -> p (r d)"),
                                                                  rhs=XT[:, h, :], start=True, stop=True),
                                         reads=[kbg, XT], writes=[pwT], inc=(h == 3))
                                for h in range(4):
                                    par = h % 2
                                    evac(h, wT[par * 64:(par + 1) * 64, h, :], pwT[par * 64:(par + 1) * 64, h * 128:(h + 1) * 128], [pwT], [wT], acc=(h > 0))
                                tasks[d].append((blk, ex, slot))
                                yield
                            done[d] = True

                        def dn_rec_gen(d):
                            w64 = WK[d]["w64"]
                            ps = ps8
                            halves = (0, 1) if d == 0 else (1, 0)
                            vn, ty, ty2 = w64["vn"], w64["ty"], w64["ty2"]
                            while True:
                                if not tasks[d]:
                                    if done[d]:
                                        break
                                    yield
                                    continue
                                blk, ex, slot = tasks[d].pop(0)
                                QK, kdec, u_sb, wT = slot["QK"], slot["kdec"], slot["u"], slot["wT"]
                                tok0 = blk * 128
                                seq_first = (blk % BPS == 0) if d == 0 else (blk % BPS == BPS - 1)
                                seq_last = (blk % BPS == BPS - 1) if d == 0 else (blk % BPS == 0)
                                if seq_first and not lat:
                                    k.op("dve", lambda e: e.memset(S[d][:], 0.0), writes=[S[d]])
                                    k.op("act", lambda e: e.copy(out=Sb[d][:], in_=S[d][:]), reads=[S[d]], writes=[Sb[d]])
                                for half in halves:
                                    hb = half * 64
                                    ec = 4 if half == 0 else 8
                                    pw = [ps.next(), ps.next()]
                                    for h in range(4):
                                        c_, par = h // 2, h % 2
                                        k.op("pe", lambda e: e.matmul(pw[par][:, c_ * 64:(c_ + 1) * 64], lhsT=wT[par * 64:(par + 1) * 64, h, :],
                                                                      rhs=Sb[d][par * 64:(par + 1) * 64, c_, :], start=True, stop=True),
                                             reads=[wT, Sb[d]], writes=[pw[par]])
                                    for par in range(2):
                                        k.op("dve", lambda e: e.tensor_tensor(out=vn[:].rearrange("p (c r) d -> p c r d", c=2)[:, :, par, :],
                                                                              in0=u_sb[:].rearrange("p (c r) d -> p c r d", c=2)[:, :, par, :],
                                                                              in1=pw[par][:, 0:128].rearrange("p (c d) -> p c d", c=2), op=ALU.subtract),
                                             reads=[u_sb, pw[par]], writes=[vn], acc=(par > 0))
                                    yield
                                    pqs = [ps.next(), ps.next()]
                                    for h in range(4):
                                        c_, par = h // 2, h % 2
                                        k.op("pe", lambda e: e.matmul(pqs[par][:, c_ * 64:(c_ + 1) * 64], lhsT=qT[par * 64:(par + 1) * 64, c_, tok0:tok0 + 128],
                                                                      rhs=Sb[d][par * 64:(par + 1) * 64, c_, :], start=True, stop=True),
                                             reads=[qT, Sb[d]], writes=[pqs[par]])
                                    pqk = ps.next()
                                    for h in range(4):
                                        k.op("pe", lambda e: e.matmul(pqk[:, h * 64:(h + 1) * 64], lhsT=QK[:, h, :], rhs=vn[:, h, :], start=True, stop=True),
                                             reads=[QK, vn], writes=[pqk], inc=(h == 3))
                                    for par in range(2):
                                        k.op("dve", lambda e: e.tensor_tensor(out=ty[hb:hb + 64].rearrange("p (c r) d -> p c r d", c=2)[:, :, par, :],
                                                                              in0=pqs[par][hb:hb + 64, 0:128].rearrange("p (c d) -> p c d", c=2),
                                                                              in1=ex[hb:hb + 64, 0:4].rearrange("p (c r) -> p c r", c=2)[:, :, par].unsqueeze(2).to_broadcast([64, 2, 64]),
                                                                              op=ALU.mult),
                                             reads=[pqs[par], ex], writes=[ty], acc=(par > 0))
                                    if (blk, half) not in ywritten:
                                        ywritten.add((blk, half))
                                        k.op("dve", lambda e: e.tensor_tensor(out=yacc[hb:hb + 64, blk, :], in0=ty[hb:hb + 64].rearrange("p h d -> p (h d)"),
                                                                              in1=pqk[hb:hb + 64, 0:256], op=ALU.add),
                                             reads=[ty, pqk], writes=[yacc], acc=True)
                                    else:
                                        k.op("dve", lambda e: e.tensor_tensor(out=ty2[hb:hb + 64].rearrange("p h d -> p (h d)"),
                                                                              in0=ty[hb:hb + 64].rearrange("p h d -> p (h d)"),
                                                                              in1=pqk[hb:hb + 64, 0:256], op=ALU.add),
                                             reads=[ty, pqk], writes=[ty2])
                                        k.op("pool", lambda e: e.tensor_tensor(out=yacc[hb:hb + 64, blk, :], in0=yacc[hb:hb + 64, blk, :],
                                                                               in1=ty2[hb:hb + 64].rearrange("p h d -> p (h d)"), op=ALU.add),
                                             reads=[yacc, ty2], writes=[yacc])
                                    yield
                                    pst = ps.next()
                                    for h in range(4):
                                        c_ = h // 2
                                        k.op("pe", lambda e: e.matmul(pst[:, h * 64:(h + 1) * 64],
                                                                      lhsT=kdec[hb:hb + 64, 2 * c_:2 * c_ + 2, :].rearrange("p r d -> p (r d)"),
                                                                      rhs=vn[hb:hb + 64, h, :], start=True, stop=True),
                                             reads=[kdec, vn], writes=[pst], inc=(h == 3))
                                    for h in range(4):
                                        c_, par = h // 2, h % 2
                                        k.op("dve", lambda e: e.scalar_tensor_tensor(out=S[d][par * 64:(par + 1) * 64, c_, :], in0=S[d][par * 64:(par + 1) * 64, c_, :],
                                                                                     scalar=ex[par * 64:(par + 1) * 64, ec + h:ec + h + 1],
                                                                                     in1=pst[par * 64:(par + 1) * 64, h * 64:(h + 1) * 64],
                                                                                     op0=ALU.mult, op1=ALU.add),
                                             reads=[S[d], ex, pst], writes=[S[d]])
                                    k.op("act", lambda e: e.copy(out=Sb[d][:], in_=S[d][:]), reads=[S[d]], writes=[Sb[d]])
                                    yield
                                if seq_last and not lat:
                                    for h in range(4):
                                        c_, par = h // 2, h % 2
                                        k.dma("pool", O["new_sdn"][blk // BPS, l, d, h, :, :], S[d][par * 64:(par + 1) * 64, c_, :], reads=[S[d]])
                                busy[d] -= 1

                        tasks = [[], []]
                        busy = [0, 0]
                        done = [False, False]
                        gens = [dn_intra_gen(0), dn_intra_gen(1), dn_rec_gen(0), dn_rec_gen(1)]
                        while gens:
                            for g_ in list(gens):
                                try:
                                    next(g_)
                                except StopIteration:
                                    gens.remove(g_)
                        k.barrier()
                    if debug.get("dn_stop", 9) <= 3:
                        continue
                    with ExitStack() as fin:
                        zp = k.pool("zd", [128, 256], F32, 2, fin)
                        y2p = k.pool("y2d", [128, 256], F32, 4, fin)
                        ssp = k.pool("ssumd", [128, 8], F32, 2, fin)
                        osp = k.pool("osd", [128, 2, 128], BF16, 2, fin)
                        for blk in range(NB):
                            tok0 = blk * 128
                            z = zp.next()
                            k.dma("sp", z[:], PTM[l][off + tok0:off + tok0 + 128, C_DNZ:C_DNZ + 256], reads=[PTM[l]], writes=[z])
                            ssum = ssp.next()
                            k.op("dve", lambda e: e.memset(ssum[:], 0.0), writes=[ssum])
                            t1 = y2p.next()
                            for h in range(4):
                                k.op("act", lambda e: e.activation(out=t1[:, h * 64:(h + 1) * 64], in_=yacc[:, blk, h * 64:(h + 1) * 64], func=AF.Square,
                                                                   accum_out=ssum[:, h:h + 1]), reads=[yacc], writes=[t1, ssum])
                            k.op("act", lambda e: e.activation(out=ssum[:, 4:8], in_=ssum[:, 0:4], func=AF.Sqrt, scale=1.0 / 64, bias=epsb[:, 0:1]),
                                 reads=[ssum, epsb], writes=[ssum])
                            k.op("dve", lambda e: e.reciprocal(out=ssum[:, 0:4], in_=ssum[:, 4:8]), reads=[ssum], writes=[ssum])
                            t2 = y2p.next()
                            k.op("dve", lambda e: e.tensor_tensor(out=t2[:].rearrange("p (h d) -> p h d", h=4),
                                                                  in0=yacc[:, blk, :].rearrange("p (h d) -> p h d", h=4),
                                                                  in1=ssum[:, 0:4].unsqueeze(2).to_broadcast([128, 4, 64]), op=ALU.mult),
                                 reads=[yacc, ssum], writes=[t2])
                            t3 = y2p.next()
                            k.op("pool", lambda e: e.tensor_tensor(out=t3[:].rearrange("p (h d) -> p h d", h=4),
                                                                   in0=t2[:].rearrange("p (h d) -> p h d", h=4),
                                                                   in1=nw1[:].unsqueeze(1).to_broadcast([128, 4, 64]), op=ALU.mult),
                                 reads=[t2, nw1], writes=[t3])
                            sz = y2p.next()
                            k.op("act", lambda e: e.activation(out=sz[:], in_=z[:], func=AF.Silu), reads=[z], writes=[sz])
                            k.op("dve", lambda e: e.tensor_tensor(out=t1[:], in0=t3[:], in1=sz[:], op=ALU.mult), reads=[t3, sz], writes=[t1])
                            os_ = osp.next()
                            for c in range(2):
                                p = ps.next()
                                k.op("pe", lambda e: e.transpose(p[:, 0:128], t1[:, c * 128:(c + 1) * 128], ident[:]), reads=[t1, ident], writes=[p])
                                evac(c, os_[:, c, :], p[:, 0:128], [p], [os_], acc=(c > 0))
                            k.dma("pool", MIX[l][0:256, off + tok0:off + tok0 + 128].rearrange("(c p) t -> p c t", p=128), os_[:],
                                  reads=[os_], writes=[MIX[l]], acc=True)
                        k.barrier()
                k.barrier()

        for l in range(DEPTH):
            with ExitStack() as ph:
                cs = k.tile("cs", [128, 8, 2], F32, ph)
                for kind in range(2):
                    k.dma("sp", cs[:, :, kind],
                          I["cvec"][kind].rearrange("(c p) -> p c", p=128),
                          writes=[cs], acc=(kind > 0), allow_slow_non_contiguous=True)
                k.op("act", lambda e: e.activation(out=cs[:], in_=cs[:], func=AF.Silu), reads=[cs], writes=[cs])
                bad = k.tile("bad", [128, 48], F32, ph)
                k.dma("sp", bad[:], I["b_ada"][l].rearrange("(j p) -> p j", p=128),
                      writes=[bad], allow_slow_non_contiguous=True)
                nw = k.tile("nw", [128, 2, 8], F32, ph)
                k.dma("sp", nw[:, 0, :], I["norm1_w"][l].rearrange("(c p) -> p c", p=128),
                      writes=[nw], allow_slow_non_contiguous=True)
                k.dma("sp", nw[:, 1, :], I["norm2_w"][l].rearrange("(c p) -> p c", p=128),
                      writes=[nw], acc=True, allow_slow_non_contiguous=True)
                ada = k.tile("ada", [128, 48, 2], F32, ph)
                wap = k.pool("wap", [128, 8, 512], F32, 2, ph)
                for pc in range(12):
                    wa = wap.next()
                    k.dma("sp" if pc % 2 == 0 else "pool", wa[:],
                          I["w_ada"][l][:, pc * 512:(pc + 1) * 512].rearrange("(c p) n -> p c n", p=128),
                          writes=[wa])
                    for jj in range(4):
                        j = pc * 4 + jj
                        p = ps.next()
                        for c in range(8):
                            k.op("pe", lambda e: e.matmul(p[:, 0:2], lhsT=wa[:, c, jj * 128:(jj + 1) * 128],
                                                          rhs=cs[:, c, :], start=(c == 0), stop=(c == 7)),
                                 reads=[wa, cs], writes=[p], inc=(c == 7))
                        k.op("dve", lambda e: e.tensor_scalar(out=ada[:, j, :], in0=p[:, 0:2],
                                                              scalar1=bad[:, j:j + 1], scalar2=None, op0=ALU.add),
                             reads=[p, bad], writes=[ada], acc=(j > 0))
                m = mod[l]
                for (dst, srcj, nwi) in ((0, 8, 0), (3, 32, 1)):
                    for kind in range(2):
                        k.op("dve", lambda e: e.scalar_tensor_tensor(
                            out=m[:, dst, :, kind], in0=ada[:, srcj:srcj + 8, kind], scalar=1.0,
                            in1=nw[:, nwi, :], op0=ALU.add, op1=ALU.mult),
                            reads=[ada, nw], writes=[m], acc=True)
                for (dst, srcj) in ((1, 0), (2, 16), (4, 24), (5, 40)):
                    k.op("dve", lambda e: e.tensor_copy(out=m[:, dst, :, :], in_=ada[:, srcj:srcj + 8, :]),
                         reads=[ada], writes=[m], acc=True)
                dump("mod%d" % l, m, m[:].rearrange("p a c k -> p (a c k)"), [128, 96])
                dump("ada%d" % l, ada, ada[:].rearrange("p a k -> p (a k)"), [128, 96])
                k.barrier()

            with ExitStack() as ph:
                wfm = k.tile("wfm", [128, 8, NFM], BF16, ph)
                wtm = k.tile("wtm", [128, 8, NTM], BF16, ph)
                win = I["w_in"][l]

                def wload(dst, d0, s0, n, first=False):
                    k.dma("pool", dst[:, :, d0:d0 + n], win[:, s0:s0 + n].rearrange("(c p) n -> p c n", p=128),
                          writes=[dst], acc=True)

                wload(wfm, R_DNQ, 0, 768)
                wload(wfm, R_SSX, 1456 + 256, 512)
                wload(wfm, R_MQ, 1040, 384)
                wload(wfm, R_MKPE, 1424, 32)
                wload(wfm, R_MKPE + 32, 1424 + 16, 16)
                wload(wfm, R_MKPE + 48, 1424, 16)
                wload(wfm, R_SWQ, 2232, 256)
                for h in range(4):
                    wload(wfm, R_SWQS + h * 64, 2232 + h * 64 + 32, 32)
                    wload(wfm, R_SWQS + h * 64 + 32, 2232 + h * 64, 32)
                wload(wfm, R_SWK, 2488, 128)
                for h in range(2):
                    wload(wfm, R_SWKS + h * 64, 2488 + h * 64 + 32, 32)
                    wload(wfm, R_SWKS + h * 64 + 32, 2488 + h * 64, 32)
                wload(wtm, C_DNZ, 768, 256)
                wload(wtm, C_SSZ, 1456, 256)
                wload(wtm, C_BETA, 1024, 16)
                wload(wtm, C_DT, 2224, 8)
                wload(wtm, C_SWV, 2616, 128)

                xtp = k.pool("xt", [128, 8, 512], F32, 2, ph)
                sqp = k.pool("sq", [128, 8, 512], BF16, 1, ph)
                hbp = k.pool("hb", [128, 8, 512], BF16, 2, ph)
                rsp = k.pool("rstd", [128, 512], F32, 2, ph)
                tmpp = k.pool("tmp", [128, 512], F32, 3, ph)
                fstg = k.pool("fstg", [128, 512], BF16, 4, ph)
                tstg = k.pool("tstg", [128, NTM], F32, 2, ph)
                fm_chunks = [(r, 128) for r in range(0, R_MKPE, 128)] + [(R_MKPE, 64)] + \
                            [(r, 128) for r in range(R_SWQ, NFM, 128)]
                def loadA(t0):
                    xt = xtp.next()
                    k.dma("sp", xt[:], X[l][:, t0:t0 + 512].rearrange("(c p) t -> p c t", p=128),
                          reads=[X[l]], writes=[xt])
                    return xt

                def prepA(t0, xt):
                    sq = sqp.next()
                    rstd = rsp.next()
                    rms_stats(None, xt, 512, sq, rstd)
                    hb = hbp.next()
                    mod_norm(xt, 512, rstd, tmpp, hb, mod[l], 0, 1, kind_of_tile(t0))
                    return hb

                for t0, xt, hb in pipelined2(T0S, loadA, prepA):
                    kind = kind_of_tile(t0)
                    for ci, (r0, n) in enumerate(fm_chunks):
                        p = ps.next()
                        for c in range(8):
                            k.op("pe", lambda e: e.matmul(p[0:n, :], lhsT=wfm[:, c, r0:r0 + n], rhs=hb[:, c, :],
                                                          start=(c == 0), stop=(c == 7)),
                                 reads=[wfm, hb], writes=[p], inc=(c == 7))
                        fs = fstg.next()
                        if ci % 2:
                            k.op("act", lambda e: e.copy(out=fs[0:n, :], in_=p[0:n, :]), reads=[p], writes=[fs])
                        else:
                            k.op("dve", lambda e: e.tensor_copy(out=fs[0:n, :], in_=p[0:n, :]), reads=[p], writes=[fs])
                        k.dma("sp", PFM[l][r0:r0 + n, t0:t0 + 512], fs[0:n, :], reads=[fs], writes=[PFM[l]], acc=True)
                    for b in range(4):
                        ts_ = tstg.next()
                        for g, (c0, n) in enumerate(((0, 512), (512, NTM - 512))):
                            p = ps.next()
                            for c in range(8):
                                k.op("pe", lambda e: e.matmul(p[:, 0:n], lhsT=hb[:, c, b * 128:(b + 1) * 128],
                                                              rhs=wtm[:, c, c0:c0 + n], start=(c == 0), stop=(c == 7)),
                                     reads=[wtm, hb], writes=[p], inc=(c == 7))
                            if g == 0:
                                k.op("act", lambda e: e.copy(out=ts_[:, c0:c0 + n], in_=p[:, 0:n]),
                                     reads=[p], writes=[ts_])
                            else:
                                k.op("dve", lambda e: e.tensor_copy(out=ts_[:, c0:c0 + n], in_=p[:, 0:n]),
                                     reads=[p], writes=[ts_], acc=True)
                        k.dma("pool", PTM[l][t0 + b * 128:t0 + (b + 1) * 128, :], ts_[:], reads=[ts_],
                              writes=[PTM[l]], acc=True)
                k.barrier()

            if debug.get("zero_mix"):
                with ExitStack() as ph:
                    z = k.tile("z", [128, 8, 512], BF16, ph)
                    k.op("dve", lambda e: e.memset(z[:], 0.0), writes=[z])
                    for t0 in range(0, TTOT, 512):
                        k.dma("sp", MIX[l][:, t0:t0 + 512].rearrange("(c p) t -> p c t", p=128), z[:],
                              reads=[z], writes=[MIX[l]], acc=True)
                    k.barrier()

            if not debug.get("skip_mla"):
                mla_phase(l)
            if not debug.get("skip_swa"):
                swa_phase(l)
            if not debug.get("skip_ssd"):
                ssd_phase(l)
            if not debug.get("skip_dn"):
                dn_phase(l)

            with ExitStack() as ph:
                wo = k.tile("wo", [128, 8, D], BF16, ph)
                k.dma("pool", wo[:], I["w_out"][l].rearrange("(c p) n -> p c n", p=128), writes=[wo])
                xtp = k.pool("xt", [128, 8, 512], F32, 2, ph)
                mxp = k.pool("mx", [128, 8, 512], BF16, 2, ph)
                def loadC1(t0):
                    xt = xtp.next()
                    mx = mxp.next()
                    k.dma("sp", xt[:], X[l][:, t0:t0 + 512].rearrange("(c p) t -> p c t", p=128),
                          reads=[X[l]], writes=[xt])
                    k.dma("sp", mx[:], MIX[l][:, t0:t0 + 512].rearrange("(c p) t -> p c t", p=128),
                          reads=[MIX[l]], writes=[mx])
                    return xt, mx

                for t0, (xt, mx) in pipelined(T0S, loadC1):
                    kind = kind_of_tile(t0)
                    for co in range(8):
                        p = ps.next()
                        for c in range(8):
                            k.op("pe", lambda e: e.matmul(p[:, :], lhsT=wo[:, c, co * 128:(co + 1) * 128],
                                                          rhs=mx[:, c, :], start=(c == 0), stop=(c == 7)),
                                 reads=[wo, mx], writes=[p], inc=(c == 7))
                        k.op("dve", lambda e: e.scalar_tensor_tensor(
                            out=xt[:, co, :], in0=p[:, :], scalar=mod[l][:, 2, co, kind:kind + 1],
                            in1=xt[:, co, :], op0=ALU.mult, op1=ALU.add),
                            reads=[p, mod[l], xt], writes=[xt])
                    k.dma("pool", XA[:, t0:t0 + 512].rearrange("(c p) t -> p c t", p=128), xt[:],
                          reads=[xt], writes=[XA], acc=True)
                k.barrier()

            HJ = 11
            for half in range(2):
                src2 = XA if half == 0 else XB
                dst2 = XB if half == 0 else X[l + 1]
                last = (half == 1 and l == DEPTH - 1)
                with ExitStack() as ph:
                    wg = k.tile("wg", [128, 8, 2, HJ * 128], BF16, ph)
                    wd = k.tile("wd", [128, HJ, D], BF16, ph)
                    j0 = half * HJ * 128
                    for gu in range(2):
                        k.dma("pool", wg[:, :, gu, :],
                              I["w_gate_up"][l][:, gu * FF + j0:gu * FF + j0 + HJ * 128].rearrange(
                                  "(c p) n -> p c n", p=128), writes=[wg], acc=True)
                    k.dma("pool", wd[:], I["w_down"][l][j0:j0 + HJ * 128, :].rearrange("(j p) n -> p j n", p=128),
                          writes=[wd])
                    xtp = k.pool("xt", [128, 8, 512], F32, 3 if half == 0 else 2, ph)
                    x2p = k.pool("x2", [128, 8, 512], F32, 3, ph) if half == 1 else None
                    sqp = k.pool("sq", [128, 8, 512], BF16, 1, ph)
                    hbp = k.pool("hb", [128, 8, 512], BF16, 2, ph)
                    rsp = k.pool("rstd", [128, 512], F32, 2, ph)
                    tmpp = k.pool("tmp", [128, 512], F32, 3, ph)
                    acp = k.pool("act", [128, HJ, 512], BF16, 1, ph)
                    ostg = k.pool("ostg", [128, D], F32, 2, ph) if last else None
                    def loadF(t0):
                        xt = xtp.next()
                        k.dma("sp", xt[:], XA[:, t0:t0 + 512].rearrange("(c p) t -> p c t", p=128),
                              reads=[XA], writes=[xt])
                        if half == 1:
                            x2 = x2p.next()
                            k.dma("sp", x2[:], XB[:, t0:t0 + 512].rearrange("(c p) t -> p c t", p=128),
                                  reads=[XB], writes=[x2])
                        else:
                            x2 = xt
                        return xt, x2

                    def prepF(t0, ld):
                        sq = sqp.next()
                        rstd = rsp.next()
                        rms_stats(None, ld[0], 512, sq, rstd)
                        hb = hbp.next()
                        mod_norm(ld[0], 512, rstd, tmpp, hb, mod[l], 3, 4, kind_of_tile(t0))
                        return hb

                    for t0, (xt, x2), hb in pipelined2(T0S, loadF, prepF):
                        kind = kind_of_tile(t0)
                        sq = sqp.tiles[0]
                        ac = acp.next()
                        for j in range(HJ):
                            pg = ps.next()
                            pu = ps.next()
                            for gu, pp in ((0, pg), (1, pu)):
                                for c in range(8):
                                    k.op("pe", lambda e: e.matmul(pp[:, :], lhsT=wg[:, c, gu, j * 128:(j + 1) * 128],
                                                                  rhs=hb[:, c, :], start=(c == 0), stop=(c == 7)),
                                         reads=[wg, hb], writes=[pp], inc=(c == 7))
                            tmp = tmpp.next()
                            k.op("act", lambda e: e.activation(out=tmp[:], in_=pg[:], func=AF.Silu),
                                 reads=[pg], writes=[tmp])
                            k.op("dve", lambda e: e.tensor_tensor(out=ac[:, j, :], in0=tmp[:], in1=pu[:], op=ALU.mult),
                                 reads=[tmp, pu], writes=[ac], acc=(j > 0))
                        for co in range(8):
                            p = ps.next()
                            for j in range(HJ):
                                k.op("pe", lambda e: e.matmul(p[:, :], lhsT=wd[:, j, co * 128:(co + 1) * 128],
                                                              rhs=ac[:, j, :], start=(j == 0), stop=(j == HJ - 1)),
                                     reads=[wd, ac], writes=[p], inc=(j == HJ - 1))
                            k.op("dve", lambda e: e.scalar_tensor_tensor(
                                out=x2[:, co, :], in0=p[:, :], scalar=mod[l][:, 5, co, kind:kind + 1],
                                in1=x2[:, co, :], op0=ALU.mult, op1=ALU.add),
                                reads=[p, mod[l], x2], writes=[x2])
                        if not last:
                            k.dma("pool", dst2[:, t0:t0 + 512].rearrange("(c p) t -> p c t", p=128), x2[:],
                                  reads=[x2], writes=[dst2], acc=True)
                        else:
                            rstd = rsp.next()
                            rms_stats(None, x2, 512, sq, rstd)
                            for c in range(8):
                                k.op("dve", lambda e: e.scalar_tensor_tensor(
                                    out=x2[:, c, :], in0=x2[:, c, :], scalar=fnw[:, c:c + 1], in1=rstd[:, :],
                                    op0=ALU.mult, op1=ALU.mult), reads=[x2, fnw, rstd], writes=[x2])
                            for b in range(4):
                                os_ = ostg.next()
                                for c in range(8):
                                    p = ps.next()
                                    k.op("pe", lambda e: e.transpose(p[:, 0:128], x2[:, c, b * 128:(b + 1) * 128], ident[:]),
                                         reads=[x2, ident], writes=[p])
                                    if c % 2:
                                        k.op("act", lambda e: e.copy(out=os_[:, c * 128:(c + 1) * 128], in_=p[:, 0:128]),
                                             reads=[p], writes=[os_], acc=(c > 0))
                                    else:
                                        k.op("dve", lambda e: e.tensor_copy(out=os_[:, c * 128:(c + 1) * 128], in_=p[:, 0:128]),
                                             reads=[p], writes=[os_], acc=(c > 0))
                                tok = t0 + b * 128
                                dst = (O["y_ctx"][tok:tok + 128, :] if tok < LOFF
                                       else O["y_lat"][tok - LOFF:tok - LOFF + 128, :])
                                k.dma("pool", dst, os_[:], reads=[os_])
                    k.barrier()
        k.barrier()
    return nc


_CACHE = {}


def _rope_tables():
    out = {}
    pos = np.arange(TL)
    row_ids = (pos // 64).astype(np.float32)
    col_ids = (pos % 64).astype(np.float32)
    for name, rot in (("rope_m", 32), ("rope_s", 64)):
        nf = rot // 4
        inv = (10000.0 ** (-np.arange(nf, dtype=np.float32) / nf)).astype(np.float32)
        ang = np.concatenate([row_ids[:, None] * inv, col_ids[:, None] * inv], axis=-1).astype(np.float32)
        c = np.cos(ang).astype(np.float32).T
        sn = np.sin(ang).astype(np.float32).T
        out[name] = np.ascontiguousarray(np.stack([np.concatenate([c, c], 0), np.concatenate([-sn, sn], 0)]))
    return out


def kernel(**inputs):
    x_prompt = np.ascontiguousarray(inputs["x_prompt"], dtype=np.float32)
    x_sample = np.ascontiguousarray(inputs["x_sample"], dtype=np.float32)
    dbg = inputs.pop("_debug", None) if "_debug" in inputs else None
    if "nc" not in _CACHE or dbg:
        _CACHE["nc"] = build_program(dbg)
    nc = _CACHE["nc"]
    ident = np.eye(128, dtype=np.float32)
    shared = {}
    for name in ["w_ada", "b_ada", "norm1_w", "norm2_w", "final_norm_w", "w_in", "w_out",
                 "w_gate_up", "w_down"]:
        shared[name] = np.ascontiguousarray(inputs[name], dtype=np.float32)
    for name in ["mla_q_norm_w", "mla_w_uq", "mla_kv_norm_w", "mla_w_ukv", "swa_sinks",
                 "dn_conv_w", "dn_a_log", "dn_dt_bias", "dn_norm_w", "ssm_conv_w", "ssm_conv_b", "ssm_a_log", "ssm_dt_bias", "ssm_d", "ssm_norm_w"]:
        shared[name] = np.ascontiguousarray(inputs[name], dtype=np.float32)
    shared.update(_rope_tables())
    kl = np.arange(128)[:, None]
    ql = np.arange(128)[None, :]
    msk = np.zeros((6, 128, 512), np.float32)
    for r in range(6):
        for j in range(4):
            dd = r - 1 - j
            if dd == -1:
                msk[r, :, j * 128:(j + 1) * 128] = (kl >= ql)
            elif dd == 0:
                msk[r, :, j * 128:(j + 1) * 128] = 1.0
            elif dd == 1:
                msk[r, :, j * 128:(j + 1) * 128] = (kl <= ql)
    shared["swa_mask"] = msk
    ii = np.arange(128)
    same = (ii[:, None] // 64) == (ii[None, :] // 64)
    cmask = np.zeros((14, 128, 128), np.float32)
    cmask[5] = same & (ii[:, None] < ii[None, :])
    cmask[6] = same & (ii[:, None] > ii[None, :])
    cmask[7] = np.eye(128, dtype=np.float32)
    for lev, sz_ in enumerate((1, 2, 4, 8, 16, 32)):
        cmask[8 + lev] = ((ii[:, None] // (2 * sz_)) == (ii[None, :] // (2 * sz_))) & ((ii[:, None] // sz_) != (ii[None, :] // sz_))
    cmask[0] = same & (ii[:, None] <= ii[None, :])
    cmask[1] = same & (ii[:, None] >= ii[None, :])
    cmask[2] = same
    cmask[3, 0:64, :] = 1.0
    cmask[4, 64:128, :] = 1.0
    shared["cmask"] = cmask
    in_maps = []
    for core in range(8):
        b = core % 4
        m = dict(shared)
        m["x_ctx"] = x_prompt[core * NCTX:(core + 1) * NCTX].reshape(NCTX * TC, D)
        m["x_lat"] = x_sample[b]
        m["cvec"] = np.stack([inputs["c_ctx"], inputs["c"][b]]).astype(np.float32)
        m["ident"] = ident
        m["cache_ckv"] = np.ascontiguousarray(inputs["cache_mla_ckv"][b], dtype=np.float32)
        m["cache_kpe"] = np.ascontiguousarray(inputs["cache_mla_kpe"][b], dtype=np.float32)
        m["state_ssm"] = np.ascontiguousarray(inputs["state_ssm"][b], dtype=np.float32)
        m["state_dn"] = np.ascontiguousarray(inputs["state_dn"][b], dtype=np.float32)
        m["cache_swk"] = np.ascontiguousarray(inputs["cache_swa_k"][b], dtype=np.float32)
        m["cache_swv"] = np.ascontiguousarray(inputs["cache_swa_v"][b], dtype=np.float32)
        in_maps.append(m)
    res = run_bass_kernel_spmd(nc, in_maps, core_ids=list(range(8)))
    r = res.results
    y_prompt = np.concatenate([r[c]["y_ctx"].reshape(NCTX, TC, D) for c in range(8)], axis=0)
    y_sample = np.stack([r[b]["y_lat"] for b in range(4)], axis=0)
    def gath(name):
        return np.concatenate([np.asarray(r[c][name], dtype=np.float32) for c in range(8)], axis=0)

    new_ckv = gath("new_ckv") if "new_ckv" in r[0] else np.zeros((32, DEPTH, TC, 128), np.float32)
    new_kpe = gath("new_kpe") if "new_kpe" in r[0] else np.zeros((32, DEPTH, TC, 32), np.float32)
    new_swk = gath("new_swk") if "new_swk" in r[0] else np.zeros((32, DEPTH, TC, 2, 64), np.float32)
    new_swv = gath("new_swv") if "new_swv" in r[0] else np.zeros((32, DEPTH, TC, 2, 64), np.float32)
    new_sdn = gath("new_sdn") if "new_sdn" in r[0] else np.zeros((32, DEPTH, 2, 4, 64, 64), np.float32)
    new_ssm = gath("new_ssm") if "new_ssm" in r[0] else np.zeros((32, DEPTH, 2, 4, 64, 64), np.float32)
    outs = (y_prompt, y_sample, new_sdn, new_ckv, new_kpe, new_ssm, new_swk, new_swv)
    if dbg:
        return outs + (r,)
    return outs
```

```python
import numpy as np
import concourse.bass as bass
import concourse.mybir as mybir
from concourse.bass_utils import run_bass_kernel_spmd
from contextlib import ExitStack

F32 = mybir.dt.float32
BF16 = mybir.dt.bfloat16
AF = mybir.ActivationFunctionType
ALU = mybir.AluOpType
AX = mybir.AxisListType

D = 1024
DEPTH = 2
NCTX = 4
TC = 256
TL = 4096
TTOT = NCTX * TC + TL
LOFF = NCTX * TC
FF = 2816
EPS = 1e-6
NFM = 2496
NTM = 664
R_DNQ, R_DNK, R_DNV = 0, 256, 512
R_SSX, R_SSB, R_SSC = 768, 1024, 1152
R_MQ, R_MKV, R_MKPE = 1280, 1536, 1664
R_SWQ, R_SWQS, R_SWK, R_SWKS = 1728, 1984, 2240, 2368
C_DNZ, C_SSZ, C_BETA, C_ALPHA, C_DT, C_SWV = 0, 256, 512, 520, 528, 536


class Res:
    __slots__ = ("name", "w", "r", "t", "base", "full")

    def __init__(self, name, t=None):
        self.name = name
        self.w = {}
        self.r = {}
        self.base = {}
        self.full = None
        self.t = t

    def __getitem__(self, key):
        return self.t[key]


class Pool:
    def __init__(self, tiles):
        self.tiles = tiles
        self.i = 0

    def next(self):
        t = self.tiles[self.i]
        self.i = (self.i + 1) % len(self.tiles)
        return t


class K:
    def __init__(self, nc, es, ndma=12):
        self.nc = nc
        self.es = es
        self.engs = {"pe": nc.tensor, "act": nc.scalar, "dve": nc.vector,
                     "pool": nc.gpsimd, "sp": nc.sync}
        self.semh = {}
        self.cnt = {}
        self.waited = {e: {} for e in self.engs}
        for e in ["pe", "act", "dve", "pool"]:
            self.semh[e] = es.enter_context(nc.semaphore("s_" + e))
            self.cnt[e] = 0
        self.dq = {}
        for q in ["sp", "pool"]:
            sems = []
            for i in range(ndma):
                key = ("d", q, i)
                self.semh[key] = es.enter_context(nc.semaphore("d_%s_%d" % (q, i)))
                self.cnt[key] = 0
                sems.append(key)
            self.dq[q] = {"sems": sems, "rr": 0}
        self.uid = 0

    def tile(self, name, shape, dtype, es=None):
        self.uid += 1
        t = (es or self.es).enter_context(
            self.nc.sbuf_tensor("%s_%d" % (name, self.uid), list(shape), dtype))
        return Res(name, t)

    def ptile(self, name, shape, dtype=F32, es=None):
        self.uid += 1
        t = (es or self.es).enter_context(
            self.nc.psum_tensor("%s_%d" % (name, self.uid), list(shape), dtype))
        return Res(name, t)

    def dram(self, name, shape, dtype, kind="Internal"):
        if name in getattr(self, "ext", ()):
            kind = "ExternalOutput"
        t = self.nc.dram_tensor(name, list(shape), dtype, kind=kind)
        return Res(name, t.ap())

    def pool(self, name, shape, dtype, n, es=None, psum=False):
        return Pool([(self.ptile if psum else self.tile)("%s%d" % (name, i), shape, dtype, es)
                     for i in range(n)])

    def _wait(self, eng, need):
        for s, v in need.items():
            if self.waited[eng].get(s, 0) < v:
                self.engs[eng].wait_ge(self.semh[s], v)
                self.waited[eng][s] = v

    def _deps(self, eng, reads, writes, acc=False):
        need = {}

        def add(s, v, war=False):
            if s == eng and eng == "pe":
                return
            if need.get(s, 0) < v:
                need[s] = v

        for t in reads:
            for s, v in t.w.items():
                add(s, v)
        for t in writes:
            if acc:
                if t.full is not None:
                    add(t.full[0], t.full[1])
                for s, v in t.base.items():
                    add(s, v, True)
            else:
                b = {}
                for s, v in t.w.items():
                    add(s, v)
                    b[s] = max(b.get(s, 0), v)
                for s, v in t.r.items():
                    add(s, v, True)
                    b[s] = max(b.get(s, 0), v)
                t.base = b
        self._wait(eng, need)

    def _mark(self, key, val, reads, writes, acc=False):
        for t in reads:
            if t.r.get(key, 0) < val:
                t.r[key] = val
        for t in writes:
            if acc:
                if t.w.get(key, 0) < val:
                    t.w[key] = val
            else:
                t.w = {key: val}
                t.r = {}
                t.full = (key, val)

    def op(self, eng, fn, reads=(), writes=(), inc=True, acc=False):
        self._deps(eng, reads, writes, acc)
        ins = fn(self.engs[eng])
        if inc:
            self.cnt[eng] += 1
            ins.then_inc(self.semh[eng], 1)
            val = self.cnt[eng]
        else:
            val = self.cnt[eng] + 1
        self._mark(eng, val, reads, writes, acc)
        return ins

    def dma(self, q, out, in_, reads=(), writes=(), acc=False, **kw):
        d = self.dq[q]
        key = d["sems"][d["rr"]]
        d["rr"] = (d["rr"] + 1) % len(d["sems"])
        cur = self.cnt[key]
        if cur > 0:
            self._wait(q, {key: cur})
        self._deps(q, reads, writes, acc)
        ins = self.engs[q].dma_start(out=out, in_=in_, **kw)
        ins.then_inc(self.semh[key], 16)
        self.cnt[key] = cur + 16
        self._mark(key, cur + 16, reads, writes, acc)

    def barrier(self):
        need = {s: v for s, v in self.cnt.items() if v > 0}
        for e in self.engs:
            self._wait(e, dict(need))


def build_program(debug=None):
    debug = debug or {}
    nc = bass.Bass("TRN2", target_bir_lowering=False)

    def din(name, shape):
        return nc.dram_tensor(name, list(shape), F32, kind="ExternalInput").ap()

    def dout(name, shape):
        return nc.dram_tensor(name, list(shape), F32, kind="ExternalOutput").ap()

    I = {}
    I["x_ctx"] = din("x_ctx", [NCTX * TC, D])
    I["x_lat"] = din("x_lat", [TL, D])
    I["cvec"] = din("cvec", [2, D])
    I["w_ada"] = din("w_ada", [DEPTH, D, 6 * D])
    I["b_ada"] = din("b_ada", [DEPTH, 6 * D])
    I["norm1_w"] = din("norm1_w", [DEPTH, D])
    I["norm2_w"] = din("norm2_w", [DEPTH, D])
    I["final_norm_w"] = din("final_norm_w", [D])
    I["w_in"] = din("w_in", [DEPTH, D, 2744])
    I["w_out"] = din("w_out", [DEPTH, D, D])
    I["w_gate_up"] = din("w_gate_up", [DEPTH, D, 2 * FF])
    I["w_down"] = din("w_down", [DEPTH, FF, D])
    I["ident"] = din("ident", [128, 128])
    I["mla_q_norm_w"] = din("mla_q_norm_w", [DEPTH, 256])
    I["mla_w_uq"] = din("mla_w_uq", [DEPTH, 256, 384])
    I["mla_kv_norm_w"] = din("mla_kv_norm_w", [DEPTH, 128])
    I["mla_w_ukv"] = din("mla_w_ukv", [DEPTH, 128, 512])
    I["cache_ckv"] = din("cache_ckv", [DEPTH, 256, 128])
    I["cache_kpe"] = din("cache_kpe", [DEPTH, 256, 32])
    I["rope_m"] = din("rope_m", [2, 32, TL])
    I["rope_s"] = din("rope_s", [2, 64, TL])
    I["swa_mask"] = din("swa_mask", [6, 128, 512])
    I["cmask"] = din("cmask", [14, 128, 128])
    I["dn_conv_w"] = din("dn_conv_w", [DEPTH, 3, 768])
    I["dn_a_log"] = din("dn_a_log", [DEPTH, 2, 4])
    I["dn_dt_bias"] = din("dn_dt_bias", [DEPTH, 2, 4])
    I["dn_norm_w"] = din("dn_norm_w", [DEPTH, 64])
    I["state_dn"] = din("state_dn", [DEPTH, 2, 4, 64, 64])
    I["ssm_conv_w"] = din("ssm_conv_w", [DEPTH, 3, 512])
    I["ssm_conv_b"] = din("ssm_conv_b", [DEPTH, 512])
    I["ssm_a_log"] = din("ssm_a_log", [DEPTH, 2, 4])
    I["ssm_dt_bias"] = din("ssm_dt_bias", [DEPTH, 2, 4])
    I["ssm_d"] = din("ssm_d", [DEPTH, 4])
    I["ssm_norm_w"] = din("ssm_norm_w", [DEPTH, 256])
    I["state_ssm"] = din("state_ssm", [DEPTH, 2, 4, 64, 64])
    I["swa_sinks"] = din("swa_sinks", [DEPTH, 4])
    I["cache_swk"] = din("cache_swk", [DEPTH, 256, 2, 64])
    I["cache_swv"] = din("cache_swv", [DEPTH, 256, 2, 64])
    O = {}
    O["y_ctx"] = dout("y_ctx", [NCTX * TC, D])
    O["y_lat"] = dout("y_lat", [TL, D])
    O["new_ckv"] = dout("new_ckv", [NCTX, DEPTH, TC, 128])
    O["new_kpe"] = dout("new_kpe", [NCTX, DEPTH, TC, 32])
    O["new_ssm"] = dout("new_ssm", [NCTX, DEPTH, 2, 4, 64, 64])
    O["new_sdn"] = dout("new_sdn", [NCTX, DEPTH, 2, 4, 64, 64])
    O["new_swk"] = dout("new_swk", [NCTX, DEPTH, TC, 2, 64])
    O["new_swv"] = dout("new_swv", [NCTX, DEPTH, TC, 2, 64])

    with ExitStack() as es:
        k = K(nc, es)
        k.ext = set(debug.get("ext", ()))

        def dump(name, res, ap, shape, dtype=F32):
            if name in debug.get("dump", ()):
                d = nc.dram_tensor("dbg_" + name, list(shape), dtype, kind="ExternalOutput").ap()
                k.dma("sp", d, ap, reads=[res])
        X = [k.dram("xs%d" % i, [D, TTOT], F32) for i in range(DEPTH + 1)]
        XA = k.dram("xa", [D, TTOT], F32)
        XB = k.dram("xb", [D, TTOT], F32)
        PFM = [k.dram("pfm%d" % l, [NFM, TTOT], BF16) for l in range(DEPTH)]
        PTM = [k.dram("ptm%d" % l, [TTOT, NTM], F32) for l in range(DEPTH)]
        MIX = [k.dram("mix%d" % l, [D, TTOT], BF16) for l in range(DEPTH)]

        ident = k.tile("ident", [128, 128], F32)
        k.dma("sp", ident[:], I["ident"][:, :], writes=[ident])
        ones_bf = k.tile("ones_bf", [128, 128], BF16)
        k.op("dve", lambda e: e.memset(ones_bf[:], 1.0), writes=[ones_bf])
        ones_f = k.tile("ones_f", [128, 64], F32)
        k.op("dve", lambda e: e.memset(ones_f[:], 1.0), writes=[ones_f])
        epsb = k.tile("epsb", [128, 1], F32)
        k.op("dve", lambda e: e.memset(epsb[:], EPS), writes=[epsb])
        ps = k.pool("ps", [128, 512], F32, 6, psum=True)
        pacc = k.pool("pacc", [128, 512], F32, 2, psum=True)
        psd = [Pool(ps.tiles[0:4]), Pool(ps.tiles[4:6] + pacc.tiles[0:2])]
        ps8 = Pool(ps.tiles + pacc.tiles)
        mod = [k.tile("mod%d" % l, [128, 6, 8, 2], F32) for l in range(DEPTH)]
        fnw = k.tile("fnw", [128, 8], F32)
        k.dma("sp", fnw[:], I["final_norm_w"].rearrange("(c p) -> p c", p=128), writes=[fnw],
              allow_slow_non_contiguous=True)

        def pipelined(t0s, load):
            nxt = load(t0s[0])
            for i, t0 in enumerate(t0s):
                cur = nxt
                if i + 1 < len(t0s):
                    nxt = load(t0s[i + 1])
                yield t0, cur

        def pipelined2(t0s, load, prep):
            cur = None
            for t0, ld in pipelined(t0s, load):
                pr = prep(t0, ld)
                if cur is not None:
                    yield cur
                cur = (t0, ld, pr)
            if cur is not None:
                yield cur

        T0S = list(range(0, TTOT, 512))

        def kind_of_tile(tok0):
            return 0 if tok0 < LOFF else 1

        with ExitStack() as ph:
            xin = k.pool("xin", [128, D], F32, 2, ph)
            stg = k.pool("stg", [128, 8, 512], F32, 2, ph)
            for t0 in range(0, TTOT, 512):
                st = stg.next()
                for b in range(4):
                    tok = t0 + b * 128
                    xi = xin.next()
                    src = (I["x_ctx"][tok:tok + 128, :] if tok < LOFF
                           else I["x_lat"][tok - LOFF:tok - LOFF + 128, :])
                    k.dma("sp", xi[:], src, writes=[xi])
                    for c in range(8):
                        p = ps.next()
                        k.op("pe", lambda e: e.transpose(p[:, 0:128], xi[:, c * 128:(c + 1) * 128], ident[:]),
                             reads=[xi, ident], writes=[p])
                        eng = "act" if c % 2 else "dve"
                        if eng == "act":
                            k.op("act", lambda e: e.copy(out=st[:, c, b * 128:(b + 1) * 128], in_=p[:, 0:128]),
                                 reads=[p], writes=[st], acc=not (b == 0 and c == 0))
                        else:
                            k.op("dve", lambda e: e.tensor_copy(out=st[:, c, b * 128:(b + 1) * 128], in_=p[:, 0:128]),
                                 reads=[p], writes=[st], acc=not (b == 0 and c == 0))
                k.dma("pool", X[0][:, t0:t0 + 512].rearrange("(c p) t -> p c t", p=128), st[:],
                      reads=[st], writes=[X[0]], acc=True)
            k.barrier()

        def rms_stats(ph_tiles, xt, ntok, sq, rstd):
            k.op("act", lambda e: e.activation(out=sq[:, :, 0:ntok], in_=xt[:, :, 0:ntok], func=AF.Square),
                 reads=[xt], writes=[sq])
            p = ps.next()
            for c in range(8):
                k.op("pe", lambda e: e.matmul(p[:, 0:ntok], lhsT=ones_bf[:], rhs=sq[:, c, 0:ntok],
                                              start=(c == 0), stop=(c == 7)),
                     reads=[ones_bf, sq], writes=[p], inc=(c == 7))
            k.op("act", lambda e: e.activation(out=rstd[:, 0:ntok], in_=p[:, 0:ntok], func=AF.Sqrt,
                                               scale=1.0 / D, bias=epsb[:, 0:1]),
                 reads=[p, epsb], writes=[rstd])
            k.op("dve", lambda e: e.reciprocal(out=rstd[:, 0:ntok], in_=rstd[:, 0:ntok]),
                 reads=[rstd], writes=[rstd])

        def mod_norm(xt, ntok, rstd, tmpp, hb, modt, ia, ib, kind):
            for c in range(8):
                tmp = tmpp.next()
                k.op("dve", lambda e: e.tensor_tensor(out=tmp[:, 0:ntok], in0=xt[:, c, 0:ntok],
                                                      in1=rstd[:, 0:ntok], op=ALU.mult),
                     reads=[xt, rstd], writes=[tmp])
                k.op("act", lambda e: e.activation(out=hb[:, c, 0:ntok], in_=tmp[:, 0:ntok], func=AF.Identity,
                                                   scale=modt[:, ia, c, kind:kind + 1],
                                                   bias=modt[:, ib, c, kind:kind + 1]),
                     reads=[tmp, modt], writes=[hb], acc=(c > 0))


        SEQS = [(i * TC, TC, False, i) for i in range(NCTX)] + [(LOFF, TL, True, 0)]
        MLA_SCALE = 96 ** -0.5
        SWA_SCALE = 64 ** -0.5

        def evac(i, out, in_, reads, writes, acc=False):
            if i % 2:
                k.op("act", lambda e: e.copy(out=out, in_=in_), reads=reads, writes=writes, acc=acc)
            else:
                k.op("dve", lambda e: e.tensor_copy(out=out, in_=in_), reads=reads, writes=writes, acc=acc)

        def rstd_from_ps(p, n, rstd, dim):
            k.op("act", lambda e: e.activation(out=rstd[:, 0:n], in_=p[:, 0:n], func=AF.Sqrt,
                                               scale=1.0 / dim, bias=epsb[:, 0:1]),
                 reads=[p, epsb], writes=[rstd])
            k.op("dve", lambda e: e.reciprocal(out=rstd[:, 0:n], in_=rstd[:, 0:n]), reads=[rstd], writes=[rstd])

        def attn_core(kT, vt, h_v, qT, NQ, NKB, scale, ptp, masks=None, sink=None, kb_list=None):
            po = pacc.next()
            blocks = kb_list if kb_list is not None else [(kb, None) for kb in range(NKB)]
            n = len(blocks)
            LOOK = 3
            pSs = [None] * n

            def issue_s(i):
                kb = blocks[i][0]
                pS = ps.next()
                k.op("pe", lambda e: e.matmul(pS[:, 0:NQ], lhsT=kT[:, kb * 128:(kb + 1) * 128], rhs=qT[:, 0:NQ],
                                              start=True, stop=True), reads=[kT, qT], writes=[pS])
                pSs[i] = pS

            first = True
            if sink is not None:
                e64, srow = sink
                k.op("pe", lambda e: e.matmul(po[0:65, 0:NQ], lhsT=e64[0:1, 0:65], rhs=srow[0:1, 0:NQ],
                                              start=True, stop=False), reads=[e64, srow], writes=[po])
                first = False
            for i in range(min(LOOK, n)):
                issue_s(i)
            for bi, (kb, mk) in enumerate(blocks):
                if bi + LOOK < n:
                    issue_s(bi + LOOK)
                pS = pSs[bi]
                pt = ptp.next()
                k.op("act", lambda e: e.activation(out=pt[:, 0:NQ], in_=pS[:, 0:NQ], func=AF.Exp, scale=scale),
                     reads=[pS], writes=[pt])
                if mk is not None:
                    k.op("dve", lambda e: e.tensor_tensor(out=pt[:, 0:NQ], in0=pt[:, 0:NQ], in1=mk[:, 0:NQ], op=ALU.mult),
                         reads=[pt, mk], writes=[pt])
                last = (bi == n - 1)
                k.op("pe", lambda e: e.matmul(po[0:65, 0:NQ], lhsT=vt[:, kb, h_v, :], rhs=pt[:, 0:NQ],
                                              start=first, stop=last), reads=[vt, pt], writes=[po])
                first = False
            return po

        def attn_finish(po, NQ, rowbuf, bcs, ostg_p, dst_ap, dst_res):
            k.op("act", lambda e: e.activation(out=rowbuf[64:65, 0:NQ], in_=po[64:65, 0:NQ], func=AF.Ln),
                 reads=[po], writes=[rowbuf])
            k.op("act", lambda e: e.activation(out=rowbuf[64:65, 0:NQ], in_=rowbuf[64:65, 0:NQ], func=AF.Exp, scale=-1.0),
                 reads=[rowbuf], writes=[rowbuf])
            pb = ps.next()
            k.op("pe", lambda e: e.matmul(pb[0:64, 0:NQ], lhsT=ones_f[64:65, 0:64], rhs=rowbuf[64:65, 0:NQ],
                                          start=True, stop=True), reads=[ones_f, rowbuf], writes=[pb])
            k.op("act", lambda e: e.copy(out=bcs[0:64, 0:NQ], in_=pb[0:64, 0:NQ]), reads=[pb], writes=[bcs])
            og = ostg_p.next()
            k.op("dve", lambda e: e.tensor_tensor(out=og[0:64, 0:NQ], in0=po[0:64, 0:NQ], in1=bcs[0:64, 0:NQ],
                                                  op=ALU.mult), reads=[po, bcs], writes=[og])
            k.dma("pool", dst_ap, og[0:64, 0:NQ], reads=[og], writes=[dst_res], acc=True)

        def mla_phase(l):
            with ExitStack() as ph:
                NKMAX = TL + 256
                wuq = k.tile("wuq", [128, 2, 4, 128], BF16, ph)
                wuqs = k.tile("wuqs", [128, 2, 4, 64], BF16, ph)
                wkk = k.tile("wkk", [128, 4, 128], BF16, ph)
                wkv = k.tile("wkv", [128, 256], BF16, ph)
                k.op("dve", lambda e: e.memset(wuq[:], 0.0), writes=[wuq])
                k.op("dve", lambda e: e.memset(wuqs[:], 0.0), writes=[wuqs])
                k.op("dve", lambda e: e.memset(wkk[:], 0.0), writes=[wkk])
                uq = I["mla_w_uq"][l]
                ukv = I["mla_w_ukv"][l]
                for h in range(4):
                    def ld(dst, src):
                        k.dma("pool", dst, src.rearrange("(c p) n -> p c n", p=128), writes=[wuq, wuqs], acc=True)
                    ld(wuq[:, :, h, 64:128], uq[:, 96 * h:96 * h + 64])
                    ld(wuq[:, :, h, 32:64], uq[:, 96 * h + 64:96 * h + 96])
                    ld(wuqs[:, :, h, 32:48], uq[:, 96 * h + 80:96 * h + 96])
                    ld(wuqs[:, :, h, 48:64], uq[:, 96 * h + 64:96 * h + 80])
                    k.dma("pool", wkk[:, h, 64:128], ukv[:, 128 * h:128 * h + 64], writes=[wkk], acc=True)
                    k.dma("pool", wkv[:, 64 * h:64 * h + 64], ukv[:, 128 * h + 64:128 * h + 128], writes=[wkv], acc=True)
                qnw = k.tile("qnw", [128, 2], F32, ph)
                k.dma("sp", qnw[:], I["mla_q_norm_w"][l].rearrange("(c p) -> p c", p=128), writes=[qnw],
                      allow_slow_non_contiguous=True)
                kvnw = k.tile("kvnw", [128, 1], F32, ph)
                k.dma("sp", kvnw[:], I["mla_kv_norm_w"][l].rearrange("(c p) -> p c", p=128), writes=[kvnw],
                      allow_slow_non_contiguous=True)
                CM = k.tile("CM", [64, TL], F32, ph)
                SM = k.tile("SM", [64, TL], F32, ph)
                k.dma("sp", CM[32:64, :], I["rope_m"][0], writes=[CM])
                k.dma("sp", SM[32:64, :], I["rope_m"][1], writes=[SM])
                ckvT = k.tile("ckvT", [128, NKMAX], BF16, ph)
                kpeT = k.tile("kpeT", [64, NKMAX], BF16, ph)
                kTm = [k.tile("kTm%d" % h, [128, NKMAX], BF16, ph) for h in range(4)]
                for h in range(4):
                    k.op("pool", lambda e: e.memset(kTm[h][0:32, :], 0.0), writes=[kTm[h]])
                    k.op("pool", lambda e: e.memset(kTm[h][0:1, :], 1.0), writes=[kTm[h]])
                vm = k.tile("vm", [128, NKMAX // 128, 4, 65], BF16, ph)
                k.op("pool", lambda e: e.memset(vm[:], 1.0), writes=[vm])
                kmx = k.tile("kmx", [1, 4, 16], F32, ph)
                nkmax = k.tile("nkmax", [1, 4], F32, ph)
                kvp = k.pool("kvp", [128, 512], BF16, 2, ph)
                sqp = k.pool("sqm", [128, 2, 512], BF16, 2, ph)
                for t_ in sqp.tiles:
                    k.op("dve", lambda e: e.memset(t_[:], 0.0), writes=[t_])
                sqb = k.tile("sqb", [128, 512], BF16, ph)
                k.op("dve", lambda e: e.memset(sqb[:], 0.0), writes=[sqb])
                rsp = k.pool("rsm", [128, 512], F32, 2, ph)
                f32p = k.pool("f32m", [128, 512], F32, 3, ph)
                kxp = k.pool("kxp", [64, 2, 512], BF16, 2, ph)
                qlp = k.pool("qlp", [128, 2, 512], BF16, 2, ph)
                qnp = k.pool("qnp", [128, 2, 512], BF16, 2, ph)
                qTp = k.pool("qTp", [128, 512], BF16, 6, ph)
                rowp = k.pool("rowpm", [1, 512], F32, 3, ph)
                for t_ in qTp.tiles:
                    k.op("dve", lambda e: e.memset(t_[:], 0.0), writes=[t_])
                ptp = k.pool("ptp", [128, 512], BF16, 6, ph)
                rowbuf = k.tile("rowbuf", [128, 512], F32, ph)
                bcs = k.tile("bcs", [64, 512], F32, ph)
                ogp = k.pool("ogp", [64, 512], BF16, 3, ph)
                tkp = k.pool("tkp", [128, 2, 128], F32, 2, ph)
                otp = k.pool("otp", [128, 128], F32, 2, ph)

                for (off, T, lat, si) in SEQS:
                    k.barrier()
                    TT = min(512, T)
                    koff = 256 if lat else 0
                    NK = T + koff
                    NKB = NK // 128
                    if lat:
                        ck = tkp.next()
                        k.dma("sp", ck[:], I["cache_ckv"][l].rearrange("(b p) f -> p b f", p=128), writes=[ck])
                        for b in range(2):
                            p = ps.next()
                            k.op("pe", lambda e: e.transpose(p[:, 0:128], ck[:, b, :], ident[:]),
                                 reads=[ck, ident], writes=[p])
                            evac(b, ckvT[:, b * 128:(b + 1) * 128], p[:, 0:128], [p], [ckvT], acc=True)
                        kp = tkp.next()
                        k.op("dve", lambda e: e.memset(kp[:], 0.0), writes=[kp])
                        k.dma("sp", kp[:, :, 32:64], I["cache_kpe"][l].rearrange("(b p) f -> p b f", p=128),
                              writes=[kp], acc=True)
                        for b in range(2):
                            p = ps.next()
                            k.op("pe", lambda e: e.transpose(p[:, 0:128], kp[:, b, :], ident[:]),
                                 reads=[kp, ident], writes=[p])
                            evac(b, kpeT[32:64, b * 128:(b + 1) * 128], p[32:64, 0:128], [p], [kpeT], acc=True)
                    for ti, t0 in enumerate(range(0, T, TT)):
                        g0 = off + t0
                        kv = kvp.next()
                        k.dma("sp", kv[:, 0:TT], PFM[l][R_MKV:R_MKV + 128, g0:g0 + TT], reads=[PFM[l]], writes=[kv])
                        sq = sqp.next()
                        k.op("act", lambda e: e.activation(out=sq[:, 0, 0:TT], in_=kv[:, 0:TT], func=AF.Square),
                             reads=[kv], writes=[sq])
                        p = ps.next()
                        k.op("pe", lambda e: e.matmul(p[:, 0:TT], lhsT=ones_bf[:], rhs=sq[:, 0, 0:TT], start=True, stop=True),
                             reads=[ones_bf, sq], writes=[p])
                        rstd = rsp.next()
                        rstd_from_ps(p, TT, rstd, 128)
                        cf = f32p.next()
                        k.op("dve", lambda e: e.scalar_tensor_tensor(out=cf[:, 0:TT], in0=kv[:, 0:TT], scalar=kvnw[:, 0:1],
                                                                     in1=rstd[:, 0:TT], op0=ALU.mult, op1=ALU.mult),
                             reads=[kv, kvnw, rstd], writes=[cf])
                        k.op("act", lambda e: e.copy(out=ckvT[:, koff + t0:koff + t0 + TT], in_=cf[:, 0:TT]),
                             reads=[cf], writes=[ckvT], acc=True)
                        kx = kxp.next()
                        k.dma("sp", kx[32:64, 0, 0:TT], PFM[l][R_MKPE:R_MKPE + 32, g0:g0 + TT], reads=[PFM[l]], writes=[kx])
                        k.dma("sp", kx[32:64, 1, 0:TT], PFM[l][R_MKPE + 32:R_MKPE + 64, g0:g0 + TT], reads=[PFM[l]],
                              writes=[kx], acc=True)
                        if lat:
                            t1 = f32p.next()
                            t2 = f32p.next()
                            k.op("dve", lambda e: e.tensor_tensor(out=t1[32:64, 0:TT], in0=kx[32:64, 0, 0:TT],
                                                                  in1=CM[32:64, t0:t0 + TT], op=ALU.mult),
                                 reads=[kx, CM], writes=[t1])
                            k.op("pool", lambda e: e.tensor_tensor(out=t2[32:64, 0:TT], in0=kx[32:64, 1, 0:TT],
                                                                   in1=SM[32:64, t0:t0 + TT], op=ALU.mult),
                                 reads=[kx, SM], writes=[t2])
                            k.op("dve", lambda e: e.tensor_tensor(out=kpeT[32:64, koff + t0:koff + t0 + TT],
                                                                  in0=t1[32:64, 0:TT], in1=t2[32:64, 0:TT], op=ALU.add),
                                 reads=[t1, t2], writes=[kpeT], acc=True)
                        else:
                            k.op("dve", lambda e: e.tensor_copy(out=kpeT[32:64, t0:t0 + TT], in_=kx[32:64, 0, 0:TT]),
                                 reads=[kx], writes=[kpeT], acc=True)
                            kf = f32p.next()
                            k.op("dve", lambda e: e.memset(kf[:, 0:TT], 0.0), writes=[kf])
                            k.op("act", lambda e: e.copy(out=kf[32:64, 0:TT], in_=kx[32:64, 0, 0:TT]),
                                 reads=[kx], writes=[kf])
                            for b in range(TT // 128):
                                p = ps.next()
                                k.op("pe", lambda e: e.transpose(p[:, 0:128], cf[:, b * 128:(b + 1) * 128], ident[:]),
                                     reads=[cf, ident], writes=[p])
                                ot = otp.next()
                                evac(b, ot[:, :], p[:, 0:128], [p], [ot])
                                k.dma("pool", O["new_ckv"][si, l, t0 + b * 128:t0 + (b + 1) * 128, :], ot[:, :], reads=[ot])
                                p = ps.next()
                                k.op("pe", lambda e: e.transpose(p[:, 0:128], kf[:, b * 128:(b + 1) * 128], ident[:]),
                                     reads=[kf, ident], writes=[p])
                                ot = otp.next()
                                evac(b + 1, ot[:, 0:32], p[:, 32:64], [p], [ot])
                                k.dma("pool", O["new_kpe"][si, l, t0 + b * 128:t0 + (b + 1) * 128, :], ot[:, 0:32], reads=[ot])
                    ntile = (NK + 511) // 512
                    for ti in range(ntile):
                        c0 = ti * 512
                        n = min(512, NK - c0)
                        for h in range(4):
                            p = ps.next()
                            k.op("pe", lambda e: e.matmul(p[:, 0:n], lhsT=wkk[:, h, :], rhs=ckvT[:, c0:c0 + n],
                                                          start=True, stop=True), reads=[wkk, ckvT], writes=[p])
                            evac(h, kTm[h][64:128, c0:c0 + n], p[64:128, 0:n], [p], [kTm[h]], acc=True)
                            k.op("pool", lambda e: e.tensor_copy(out=kTm[h][32:64, c0:c0 + n], in_=kpeT[32:64, c0:c0 + n]),
                                 reads=[kpeT], writes=[kTm[h]], acc=True)
                            for (a0, a1) in ((32, 64), (64, 128)):
                                k.op("act", lambda e: e.activation(out=sqb[a0:a1, 0:n], in_=kTm[h][a0:a1, c0:c0 + n],
                                                                   func=AF.Square), reads=[kTm[h]], writes=[sqb], acc=(a0 == 64))
                            p2 = ps.next()
                            k.op("pe", lambda e: e.matmul(p2[0:1, 0:n], lhsT=ones_bf[:, 0:1], rhs=sqb[:, 0:n],
                                                          start=True, stop=True), reads=[ones_bf, sqb], writes=[p2])
                            k.op("dve", lambda e: e.reduce_max(out=kmx[0:1, h, ti:ti + 1], in_=p2[0:1, 0:n], axis=AX.X),
                                 reads=[p2], writes=[kmx], acc=True)
                    for kb in range(NKB):
                        p = ps.next()
                        k.op("pe", lambda e: e.matmul(p[:, 0:256], lhsT=ckvT[:, kb * 128:(kb + 1) * 128], rhs=wkv[:, :],
                                                      start=True, stop=True), reads=[ckvT, wkv], writes=[p])
                        evac(kb, vm[:, kb, :, 0:64], p[:, 0:256].rearrange("p (h d) -> p h d", h=4), [p], [vm], acc=True)
                    k.op("dve", lambda e: e.reduce_max(out=nkmax[0:1, :], in_=kmx[0:1, :, 0:ntile], axis=AX.X),
                         reads=[kmx], writes=[nkmax])
                    k.op("act", lambda e: e.activation(out=nkmax[0:1, :], in_=nkmax[0:1, :], func=AF.Sqrt),
                         reads=[nkmax], writes=[nkmax])
                    k.op("dve", lambda e: e.tensor_scalar(out=nkmax[0:1, :], in0=nkmax[0:1, :], scalar1=-1.0, scalar2=None,
                                                          op0=ALU.mult), reads=[nkmax], writes=[nkmax])
                    for t0 in range(0, T, TT):
                        g0 = off + t0
                        ql = qlp.next()
                        k.dma("sp", ql[:, :, 0:TT], PFM[l][R_MQ:R_MQ + 256, g0:g0 + TT].rearrange("(c p) t -> p c t", p=128),
                              reads=[PFM[l]], writes=[ql])
                        sq = sqp.next()
                        k.op("act", lambda e: e.activation(out=sq[:, :, 0:TT], in_=ql[:, :, 0:TT], func=AF.Square),
                             reads=[ql], writes=[sq])
                        p = ps.next()
                        for c in range(2):
                            k.op("pe", lambda e: e.matmul(p[:, 0:TT], lhsT=ones_bf[:], rhs=sq[:, c, 0:TT],
                                                          start=(c == 0), stop=(c == 1)), reads=[ones_bf, sq], writes=[p], inc=(c == 1))
                        rstd = rsp.next()
                        rstd_from_ps(p, TT, rstd, 256)
                        qn = qnp.next()
                        for c in range(2):
                            k.op("dve", lambda e: e.scalar_tensor_tensor(out=qn[:, c, 0:TT], in0=ql[:, c, 0:TT],
                                                                         scalar=qnw[:, c:c + 1], in1=rstd[:, 0:TT],
                                                                         op0=ALU.mult, op1=ALU.mult),
                                 reads=[ql, qnw, rstd], writes=[qn], acc=(c > 0))
                        qTs_h = []
                        for h in range(4):
                            p1 = ps.next()
                            for c in range(2):
                                k.op("pe", lambda e: e.matmul(p1[:, 0:TT], lhsT=wuq[:, c, h, :], rhs=qn[:, c, 0:TT],
                                                              start=(c == 0), stop=(c == 1)), reads=[wuq, qn], writes=[p1], inc=(c == 1))
                            qT = qTp.next()
                            k.op("act", lambda e: e.copy(out=qT[64:128, 0:TT], in_=p1[64:128, 0:TT]), reads=[p1], writes=[qT])
                            if lat:
                                p2 = ps.next()
                                for c in range(2):
                                    k.op("pe", lambda e: e.matmul(p2[0:64, 0:TT], lhsT=wuqs[:, c, h, :], rhs=qn[:, c, 0:TT],
                                                                  start=(c == 0), stop=(c == 1)), reads=[wuqs, qn], writes=[p2], inc=(c == 1))
                                t1 = f32p.next()
                                t2 = f32p.next()
                                k.op("dve", lambda e: e.tensor_tensor(out=t1[32:64, 0:TT], in0=p1[32:64, 0:TT],
                                                                      in1=CM[32:64, t0:t0 + TT], op=ALU.mult),
                                     reads=[p1, CM], writes=[t1])
                                k.op("dve", lambda e: e.tensor_tensor(out=t2[32:64, 0:TT], in0=p2[32:64, 0:TT],
                                                                      in1=SM[32:64, t0:t0 + TT], op=ALU.mult),
                                     reads=[p2, SM], writes=[t2])
                                k.op("pool", lambda e: e.tensor_tensor(out=qT[32:64, 0:TT], in0=t1[32:64, 0:TT],
                                                                       in1=t2[32:64, 0:TT], op=ALU.add),
                                     reads=[t1, t2], writes=[qT], acc=True)
                            else:
                                k.op("dve", lambda e: e.tensor_copy(out=qT[32:64, 0:TT], in_=p1[32:64, 0:TT]),
                                     reads=[p1], writes=[qT], acc=True)
                            for (a0, a1) in ((32, 64), (64, 128)):
                                k.op("act", lambda e: e.activation(out=sqb[a0:a1, 0:TT], in_=qT[a0:a1, 0:TT], func=AF.Square),
                                     reads=[qT], writes=[sqb], acc=(a0 == 64))
                            pn = ps.next()
                            k.op("pe", lambda e: e.matmul(pn[0:1, 0:TT], lhsT=ones_bf[:, 0:1], rhs=sqb[:, 0:TT],
                                                          start=True, stop=True), reads=[ones_bf, sqb], writes=[pn])
                            rw = rowp.next()
                            k.op("act", lambda e: e.activation(out=rw[0:1, 0:TT], in_=pn[0:1, 0:TT], func=AF.Sqrt),
                                 reads=[pn], writes=[rw])
                            k.op("dve", lambda e: e.tensor_scalar(out=qT[0:1, 0:TT], in0=rw[0:1, 0:TT],
                                                                  scalar1=nkmax[0:1, h:h + 1], scalar2=None, op0=ALU.mult),
                                 reads=[rw, nkmax], writes=[qT], acc=True)
                            qTs_h.append(qT)
                        for h in range(4):
                            po = attn_core(kTm[h], vm, h, qTs_h[h], TT, NKB, MLA_SCALE, ptp)
                            attn_finish(po, TT, rowbuf, bcs, ogp,
                                        MIX[l][256 + 64 * h:256 + 64 * h + 64, g0:g0 + TT], MIX[l])
                k.barrier()


        def swa_phase(l):
            with ExitStack() as ph:
                NKMAX = TL + 256
                CS = k.tile("CS", [128, TL], F32, ph)
                SS = k.tile("SS", [128, TL], F32, ph)
                k.dma("sp", CS[64:128, :], I["rope_s"][0], writes=[CS])
                k.dma("sp", SS[64:128, :], I["rope_s"][1], writes=[SS])
                mk = k.tile("mk", [128, 6, 512], BF16, ph)
                k.dma("pool", mk[:], I["swa_mask"].rearrange("r p q -> p r q"), writes=[mk])
                sk = k.tile("sk", [1, 4], F32, ph)
                k.dma("sp", sk[:], I["swa_sinks"][l:l + 1, :], writes=[sk])
                e64 = k.tile("e64", [1, 65], BF16, ph)
                k.op("dve", lambda e: e.memset(e64[:], 0.0), writes=[e64])
                k.op("dve", lambda e: e.memset(e64[0:1, 64:65], 1.0), writes=[e64])
                kTs = [k.tile("kTs%d" % h, [128, NKMAX], BF16, ph) for h in range(2)]
                for h in range(2):
                    k.op("pool", lambda e: e.memset(kTs[h][0:64, :], 0.0), writes=[kTs[h]])
                    k.op("pool", lambda e: e.memset(kTs[h][0:1, :], 1.0), writes=[kTs[h]])
                vs = k.tile("vs", [128, NKMAX // 128, 2, 65], BF16, ph)
                k.op("pool", lambda e: e.memset(vs[:], 1.0), writes=[vs])
                kmx = k.tile("kmx", [1, 2, 16], F32, ph)
                nkmax = k.tile("nkmax", [1, 2], F32, ph)
                sqb = k.tile("sqb", [128, 512], BF16, ph)
                k.op("dve", lambda e: e.memset(sqb[:], 0.0), writes=[sqb])
                f32p = k.pool("f32s", [128, 512], F32, 3, ph)
                kxp = k.pool("kxs", [128, 2, 512], BF16, 4, ph)
                qTp = k.pool("qTs", [128, 512], BF16, 6, ph)
                rowp = k.pool("rowps", [1, 512], F32, 3, ph)
                for t_ in qTp.tiles:
                    k.op("dve", lambda e: e.memset(t_[:], 0.0), writes=[t_])
                ptp = k.pool("pts", [128, 512], BF16, 6, ph)
                rowbuf = k.tile("rowbufs", [128, 512], F32, ph)
                srowp = k.pool("srow", [1, 512], BF16, 6, ph)
                bcs = k.tile("bcss", [64, 512], F32, ph)
                ogp = k.pool("ogs", [64, 512], BF16, 3, ph)
                ckp = k.pool("cks", [128, 128], F32, 2, ph)
                for t_ in ckp.tiles:
                    k.op("dve", lambda e: e.memset(t_[:], 0.0), writes=[t_])
                otp = k.pool("ots", [128, 128], F32, 2, ph)

                def rope(dst_ap, dst_res, x, t0, TT, lat, acc=True):
                    if lat:
                        t1 = f32p.next()
                        t2 = f32p.next()
                        k.op("dve", lambda e: e.tensor_tensor(out=t1[64:128, 0:TT], in0=x[64:128, 0, 0:TT],
                                                              in1=CS[64:128, t0:t0 + TT], op=ALU.mult),
                             reads=[x, CS], writes=[t1])
                        k.op("pool", lambda e: e.tensor_tensor(out=t2[64:128, 0:TT], in0=x[64:128, 1, 0:TT],
                                                               in1=SS[64:128, t0:t0 + TT], op=ALU.mult),
                             reads=[x, SS], writes=[t2])
                        k.op("dve", lambda e: e.tensor_tensor(out=dst_ap, in0=t1[64:128, 0:TT], in1=t2[64:128, 0:TT],
                                                              op=ALU.add), reads=[t1, t2], writes=[dst_res], acc=acc)
                    else:
                        k.op("dve", lambda e: e.tensor_copy(out=dst_ap, in_=x[64:128, 0, 0:TT]), reads=[x],
                             writes=[dst_res], acc=acc)

                for (off, T, lat, si) in SEQS:
                    k.barrier()
                    TT = min(512, T)
                    koff = 256 if lat else 0
                    NK = T + koff
                    NKB = NK // 128
                    if lat:
                        for kv in range(2):
                            for b in range(2):
                                ck = ckp.next()
                                k.dma("sp", ck[:, 64:128], I["cache_swk"][l, b * 128:(b + 1) * 128, kv, :], writes=[ck])
                                p = ps.next()
                                k.op("pe", lambda e: e.transpose(p[:, 0:128], ck[:, :], ident[:]), reads=[ck, ident], writes=[p])
                                evac(b, kTs[kv][64:128, b * 128:(b + 1) * 128], p[64:128, 0:128], [p], [kTs[kv]], acc=True)
                        for b in range(2):
                            k.dma("pool", vs[:, b, :, 0:64], I["cache_swv"][l, b * 128:(b + 1) * 128, :, :],
                                  writes=[vs], acc=True)
                    else:
                        k.dma("pool", O["new_swv"][si, l].rearrange("t k d -> t (k d)"),
                              PTM[l][off:off + T, C_SWV:C_SWV + 128], reads=[PTM[l]])
                    for b in range(T // 128):
                        k.dma("pool", vs[:, koff // 128 + b, :, 0:64],
                              PTM[l][off + b * 128:off + (b + 1) * 128, C_SWV:C_SWV + 128].rearrange("p (k d) -> p k d", k=2),
                              reads=[PTM[l]], writes=[vs], acc=True)
                    for kv in range(2):
                        for ti, t0 in enumerate(range(0, T, TT)):
                            g0 = off + t0
                            kx = kxp.next()
                            k.dma("sp", kx[64:128, 0, 0:TT], PFM[l][R_SWK + 64 * kv:R_SWK + 64 * kv + 64, g0:g0 + TT],
                                  reads=[PFM[l]], writes=[kx])
                            k.dma("sp", kx[64:128, 1, 0:TT], PFM[l][R_SWKS + 64 * kv:R_SWKS + 64 * kv + 64, g0:g0 + TT],
                                  reads=[PFM[l]], writes=[kx], acc=True)
                            rope(kTs[kv][64:128, koff + t0:koff + t0 + TT], kTs[kv], kx, t0, TT, lat)
                            if not lat:
                                kf = f32p.next()
                                k.op("dve", lambda e: e.memset(kf[0:64, 0:TT], 0.0), writes=[kf])
                                k.op("act", lambda e: e.copy(out=kf[64:128, 0:TT], in_=kx[64:128, 0, 0:TT]),
                                     reads=[kx], writes=[kf], acc=True)
                                for b in range(TT // 128):
                                    p = ps.next()
                                    k.op("pe", lambda e: e.transpose(p[:, 0:128], kf[:, b * 128:(b + 1) * 128], ident[:]),
                                         reads=[kf, ident], writes=[p])
                                    ot = otp.next()
                                    evac(b, ot[:, 0:64], p[:, 64:128], [p], [ot])
                                    k.dma("pool", O["new_swk"][si, l, t0 + b * 128:t0 + (b + 1) * 128, kv, :], ot[:, 0:64], reads=[ot])
                        ntile = (NK + 511) // 512
                        for ti in range(ntile):
                            c0 = ti * 512
                            n = min(512, NK - c0)
                            k.op("act", lambda e: e.activation(out=sqb[64:128, 0:n], in_=kTs[kv][64:128, c0:c0 + n],
                                                               func=AF.Square), reads=[kTs[kv]], writes=[sqb])
                            p2 = ps.next()
                            k.op("pe", lambda e: e.matmul(p2[0:1, 0:n], lhsT=ones_bf[:, 0:1], rhs=sqb[:, 0:n],
                                                          start=True, stop=True), reads=[ones_bf, sqb], writes=[p2])
                            k.op("dve", lambda e: e.reduce_max(out=kmx[0:1, kv, ti:ti + 1], in_=p2[0:1, 0:n], axis=AX.X),
                                 reads=[p2], writes=[kmx], acc=True)
                    ntile = (NK + 511) // 512
                    k.op("dve", lambda e: e.reduce_max(out=nkmax[0:1, :], in_=kmx[0:1, :, 0:ntile], axis=AX.X),
                         reads=[kmx], writes=[nkmax])
                    k.op("act", lambda e: e.activation(out=nkmax[0:1, :], in_=nkmax[0:1, :], func=AF.Sqrt),
                         reads=[nkmax], writes=[nkmax])
                    k.op("dve", lambda e: e.tensor_scalar(out=nkmax[0:1, :], in0=nkmax[0:1, :], scalar1=-1.0, scalar2=None,
                                                          op0=ALU.mult), reads=[nkmax], writes=[nkmax])
                    for t0 in range(0, T, TT):
                        g0 = off + t0
                        i0 = t0 // 128
                        if lat:
                            kbl = [(0, None), (1, None)]
                            for r in range(6):
                                kbo = i0 - 1 + r
                                if 0 <= kbo < T // 128:
                                    kbl.append((2 + kbo, Res("mkv", mk.t[:, r, :])))
                            for (_, m_) in kbl:
                                if m_ is not None:
                                    m_.w = mk.w
                        else:
                            kbl = [(b, None) for b in range(NKB)]
                        prep_h = []
                        for h in range(4):
                            kv = h // 2
                            qx = kxp.next()
                            k.dma("sp", qx[64:128, 0, 0:TT], PFM[l][R_SWQ + 64 * h:R_SWQ + 64 * h + 64, g0:g0 + TT],
                                  reads=[PFM[l]], writes=[qx])
                            k.dma("sp", qx[64:128, 1, 0:TT], PFM[l][R_SWQS + 64 * h:R_SWQS + 64 * h + 64, g0:g0 + TT],
                                  reads=[PFM[l]], writes=[qx], acc=True)
                            qT = qTp.next()
                            rope(qT[64:128, 0:TT], qT, qx, t0, TT, lat, acc=False)
                            k.op("act", lambda e: e.activation(out=sqb[64:128, 0:TT], in_=qT[64:128, 0:TT], func=AF.Square),
                                 reads=[qT], writes=[sqb])
                            pn = ps.next()
                            k.op("pe", lambda e: e.matmul(pn[0:1, 0:TT], lhsT=ones_bf[:, 0:1], rhs=sqb[:, 0:TT],
                                                          start=True, stop=True), reads=[ones_bf, sqb], writes=[pn])
                            rw = rowp.next()
                            k.op("act", lambda e: e.activation(out=rw[0:1, 0:TT], in_=pn[0:1, 0:TT], func=AF.Sqrt),
                                 reads=[pn], writes=[rw])
                            k.op("dve", lambda e: e.tensor_scalar(out=qT[0:1, 0:TT], in0=rw[0:1, 0:TT],
                                                                  scalar1=nkmax[0:1, kv:kv + 1], scalar2=None, op0=ALU.mult),
                                 reads=[rw, nkmax], writes=[qT], acc=True)
                            srow = srowp.next()
                            k.op("act", lambda e: e.activation(out=srow[0:1, 0:TT], in_=qT[0:1, 0:TT], func=AF.Exp,
                                                               scale=SWA_SCALE, bias=sk[0:1, h:h + 1]),
                                 reads=[qT, sk], writes=[srow])
                            prep_h.append((qT, srow))
                        for h in range(4):
                            kv = h // 2
                            qT, srow = prep_h[h]
                            po = attn_core(kTs[kv], vs, kv, qT, TT, NKB, SWA_SCALE, ptp, sink=(e64, srow), kb_list=kbl)
                            attn_finish(po, TT, rowbuf, bcs, ogp,
                                        MIX[l][768 + 64 * h:768 + 64 * h + 64, g0:g0 + TT], MIX[l])
                k.barrier()


        def ssd_phase(l):
            with ExitStack() as ph:
                cm = k.tile("cm", [128, 5, 128], F32, ph)
                k.dma("sp", cm[:], I["cmask"].rearrange("r p q -> p r q")[:, 0:5, :], writes=[cm])
                onesblk, onesA, onesB = cm[:, 2, :], cm[:, 3, :], cm[:, 4, :]
                cw = k.tile("cw", [128, 4, 3], F32, ph)
                for kk in range(3):
                    k.dma("sp", cw[:, :, kk], I["ssm_conv_w"][l, kk].rearrange("(c p) -> p c", p=128), writes=[cw],
                          acc=(kk > 0), allow_slow_non_contiguous=True)
                cb = k.tile("cb", [128, 4], F32, ph)
                k.dma("sp", cb[:], I["ssm_conv_b"][l].rearrange("(c p) -> p c", p=128), writes=[cb],
                      allow_slow_non_contiguous=True)
                dtb = k.tile("dtb", [128, 8], F32, ph)
                k.dma("sp", dtb[:], I["ssm_dt_bias"][l:l + 1].rearrange("o a b -> o (a b)").partition_broadcast(128), writes=[dtb])
                aneg = k.tile("aneg", [128, 8], F32, ph)
                k.dma("sp", aneg[:], I["ssm_a_log"][l:l + 1].rearrange("o a b -> o (a b)").partition_broadcast(128), writes=[aneg])
                k.op("act", lambda e: e.activation(out=aneg[:], in_=aneg[:], func=AF.Exp), reads=[aneg], writes=[aneg])
                k.op("dve", lambda e: e.tensor_scalar(out=aneg[:], in0=aneg[:], scalar1=-1.0, scalar2=None, op0=ALU.mult),
                     reads=[aneg], writes=[aneg])
                Dt = k.tile("Dt", [128, 4], F32, ph)
                k.dma("sp", Dt[:], I["ssm_d"][l:l + 1, :].partition_broadcast(128), writes=[Dt])
                nwt = k.tile("nwt", [128, 256], F32, ph)
                k.dma("sp", nwt[:], I["ssm_norm_w"][l:l + 1, :].partition_broadcast(128), writes=[nwt])
                onec = k.tile("onec", [128, 1], F32, ph)
                k.op("dve", lambda e: e.memset(onec[:], 1.0), writes=[onec])
                NBM = TL // 128
                BT = k.tile("BT", [128, TL], BF16, ph)
                CT = k.tile("CT", [128, TL], BF16, ph)
                x_tok = k.tile("x_tok", [128, NBM, 256], F32, ph)
                B_tok = k.tile("B_tok", [128, NBM, 128], F32, ph)
                dtr = k.tile("dtr", [128, NBM, 8], F32, ph)
                dtt = k.tile("dtt", [128, NBM, 8], F32, ph)
                at = k.tile("at", [128, NBM, 8], F32, ph)
                yacc = k.tile("yacc", [128, NBM, 256], F32, ph)
                S = [k.tile("S%d" % d, [128, 2, 64], F32, ph) for d in range(2)]
                Sb = [k.tile("Sbs%d" % d, [128, 2, 64], BF16, ph) for d in range(2)]
                xinp = k.pool("xin", [128, 4, 514], BF16, 2, ph)
                xTp = k.pool("xTs", [128, 4, 512], F32, 2, ph)
                tmpp = k.pool("tmps", [128, 512], F32, 4, ph)
                WS = []
                for d_ in range(2):
                    WS.append({"stp": k.pool("stt", [128, 24], F32, 2, ph), "exp": k.pool("exs", [128, 24], F32, 2, ph),
                               "GUp": k.pool("GU", [128, 4, 128], F32, 1, ph), "Lp": k.pool("Lp", [128, 4, 128], F32, 1, ph),
                               "L2p": k.pool("L2p", [128, 4, 128], F32, 1, ph), "scp": k.pool("scT", [128, 4, 128], BF16, 2, ph),
                               "xdp": k.pool("xdt", [128, 4, 64], BF16, 2, ph), "typ": k.pool("tmpy", [128, 4, 64], F32, 3, ph),
                               "Bdp": k.pool("Bd", [128, 4, 128], BF16, 2, ph), "ydp": k.pool("yds", [128, 256], F32, 2, ph)})
                zp = k.pool("zs", [128, 256], F32, 2, ph)
                y2p = k.pool("y2s", [128, 256], F32, 4, ph)
                ssp = k.pool("ssum", [128, 4], F32, 2, ph)
                osp = k.pool("oss", [128, 2, 128], BF16, 2, ph)
                sop = k.pool("sos", [128, 128], F32, 2, ph)

                for (off, T, lat, SEQT) in ((0, NCTX * TC, False, TC), (LOFF, TL, True, TL)):
                    k.barrier()
                    NB = T // 128
                    BPS = SEQT // 128
                    TT = min(512, SEQT)
                    for t0 in range(0, T, TT):
                        g0 = off + t0
                        xin = xinp.next()
                        first = (t0 % SEQT == 0)
                        lastt = ((t0 + TT) % SEQT == 0)
                        lo = g0 if first else g0 - 1
                        hi = g0 + TT if lastt else g0 + TT + 1
                        c_lo = 1 if first else 0
                        k.dma("sp", xin[:, :, c_lo:c_lo + (hi - lo)],
                              PFM[l][R_SSX:R_SSX + 512, lo:hi].rearrange("(c p) t -> p c t", p=128),
                              reads=[PFM[l]], writes=[xin])
                        if first:
                            k.op("dve", lambda e: e.memset(xin[:, :, 0:1], 0.0), writes=[xin], acc=True)
                        if lastt:
                            k.op("dve", lambda e: e.memset(xin[:, :, TT + 1:TT + 2], 0.0), writes=[xin], acc=True)
                        xT = xTp.next()
                        for c in range(4):
                            ta = tmpp.next()
                            tb = tmpp.next()
                            k.op("dve", lambda e: e.tensor_scalar(out=ta[:, 0:TT], in0=xin[:, c, 0:TT], scalar1=cw[:, c, 0:1],
                                                                  scalar2=None, op0=ALU.mult), reads=[xin, cw], writes=[ta])
                            k.op("dve", lambda e: e.scalar_tensor_tensor(out=tb[:, 0:TT], in0=xin[:, c, 1:TT + 1], scalar=cw[:, c, 1:2],
                                                                         in1=ta[:, 0:TT], op0=ALU.mult, op1=ALU.add),
                                 reads=[xin, cw, ta], writes=[tb])
                            k.op("dve", lambda e: e.scalar_tensor_tensor(out=ta[:, 0:TT], in0=xin[:, c, 2:TT + 2], scalar=cw[:, c, 2:3],
                                                                         in1=tb[:, 0:TT], op0=ALU.mult, op1=ALU.add),
                                 reads=[xin, cw, tb], writes=[ta])
                            k.op("act", lambda e: e.activation(out=xT[:, c, 0:TT], in_=ta[:, 0:TT], func=AF.Silu, bias=cb[:, c:c + 1]),
                                 reads=[ta, cb], writes=[xT], acc=(c > 0))
                            if c >= 2:
                                dres = BT if c == 2 else CT
                                k.op("pool", lambda e: e.tensor_copy(out=dres[:, t0:t0 + TT], in_=xT[:, c, 0:TT]), reads=[xT], writes=[dres], acc=True)
                        for b in range(TT // 128):
                            blk = t0 // 128 + b
                            for c in range(2):
                                p = ps.next()
                                k.op("pe", lambda e: e.transpose(p[:, 0:128], xT[:, c, b * 128:(b + 1) * 128], ident[:]),
                                     reads=[xT, ident], writes=[p])
                                evac(c, x_tok[:, blk, c * 128:(c + 1) * 128], p[:, 0:128], [p], [x_tok], acc=True)
                            p = ps.next()
                            k.op("pe", lambda e: e.transpose(p[:, 0:128], xT[:, 2, b * 128:(b + 1) * 128], ident[:]),
                                 reads=[xT, ident], writes=[p])
                            evac(1, B_tok[:, blk, :], p[:, 0:128], [p], [B_tok], acc=True)
                    if debug.get("ssd_stop", 9) <= 1:
                        continue
                    for b0 in range(0, NB, 2):
                        k.dma("sp", dtr[:, b0:b0 + 2, :],
                              PTM[l][off + b0 * 128:off + (b0 + 2) * 128, C_DT:C_DT + 8].rearrange("(b p) j -> p b j", p=128),
                              reads=[PTM[l]], writes=[dtr], acc=(b0 > 0))
                    if debug.get("ssd_stop", 9) <= 2:
                        continue
                    k.op("dve", lambda e: e.tensor_tensor(out=dtt[:, 0:NB, :], in0=dtr[:, 0:NB, :],
                                                          in1=dtb[:].unsqueeze(1).to_broadcast([128, NB, 8]), op=ALU.add),
                         reads=[dtr, dtb], writes=[dtt])
                    k.op("act", lambda e: e.activation(out=dtr[:, 0:NB, :], in_=dtt[:, 0:NB, :], func=AF.Exp), reads=[dtt], writes=[dtr])
                    k.op("act", lambda e: e.activation(out=dtt[:, 0:NB, :], in_=dtr[:, 0:NB, :], func=AF.Ln, bias=onec[:, 0:1]),
                         reads=[dtr, onec], writes=[dtt])
                    k.op("dve", lambda e: e.tensor_tensor(out=at[:, 0:NB, :], in0=dtt[:, 0:NB, :],
                                                          in1=aneg[:].unsqueeze(1).to_broadcast([128, NB, 8]), op=ALU.mult),
                         reads=[dtt, aneg], writes=[at])
                    for d in range(2):
                        if lat:
                            stin = sop.next()
                            for g in range(2):
                                for hh in range(2):
                                    k.dma("sp", stin[hh * 64:(hh + 1) * 64, g * 64:(g + 1) * 64], I["state_ssm"][l, d, 2 * g + hh, :, :],
                                          writes=[stin], acc=not (g == 0 and hh == 0))
                            p = ps.next()
                            k.op("pe", lambda e: e.transpose(p[:, 0:128], stin[:, :], ident[:]), reads=[stin, ident], writes=[p])
                            k.op("dve", lambda e: e.tensor_copy(out=S[d][:].rearrange("p a b -> p (a b)"), in_=p[:, 0:128]),
                                 reads=[p], writes=[S[d]])
                            k.op("act", lambda e: e.copy(out=Sb[d][:], in_=S[d][:]), reads=[S[d]], writes=[Sb[d]])
                    if debug.get("ssd_stop", 9) <= 3:
                        continue
                    ywritten = set()

                    def ssd_dir_gen(d):
                        stp, exp_, GUp, Lp, L2p, scp, xdp, typ, Bdp, ydp = (WS[d][n_] for n_ in ("stp", "exp", "GUp", "Lp", "L2p", "scp", "xdp", "typ", "Bdp", "ydp"))
                        ps = psd[d]
                        U = cm[:, d, :]
                        order = list(range(NB)) if d == 0 else list(range(NB - 1, -1, -1))
                        halves = (0, 1) if d == 0 else (1, 0)
                        for blk in order:
                            tok0 = blk * 128
                            seq_first = (blk % BPS == 0) if d == 0 else (blk % BPS == BPS - 1)
                            seq_last = (blk % BPS == BPS - 1) if d == 0 else (blk % BPS == 0)
                            if seq_first and not lat:
                                k.op("dve", lambda e: e.memset(S[d][:], 0.0), writes=[S[d]])
                                k.op("act", lambda e: e.copy(out=Sb[d][:], in_=S[d][:]), reads=[S[d]], writes=[Sb[d]])
                            a_blk = at[:, blk, d * 4:(d + 1) * 4]
                            pc = ps.next()
                            for j, lh in enumerate((U, onesA, onesB)):
                                k.op("pe", lambda e: e.matmul(pc[:, 4 * j:4 * j + 4], lhsT=lh, rhs=a_blk, start=True, stop=True),
                                     reads=[cm, at], writes=[pc], inc=(j == 2))
                            st = stp.next()
                            k.op("act", lambda e: e.copy(out=st[:, 0:12], in_=pc[:, 0:12]), reads=[pc], writes=[st])
                            k.op("dve", lambda e: e.tensor_tensor(out=st[0:64, 16:20], in0=st[0:64, 4:8], in1=st[0:64, 0:4], op=ALU.subtract),
                                 reads=[st], writes=[st])
                            k.op("dve", lambda e: e.tensor_tensor(out=st[64:128, 16:20], in0=st[64:128, 8:12], in1=st[64:128, 0:4], op=ALU.subtract),
                                 reads=[st], writes=[st])
                            ex = exp_.next()
                            k.op("act", lambda e: e.activation(out=ex[:, 0:12], in_=st[:, 0:12], func=AF.Exp), reads=[st], writes=[ex])
                            k.op("act", lambda e: e.activation(out=ex[:, 16:20], in_=st[:, 16:20], func=AF.Exp), reads=[st], writes=[ex], acc=True)
                            if debug.get("scan_stop", 9) <= 1:
                                continue
                            yield
                            GU = GUp.next()
                            for h in range(4):
                                k.op("dve", lambda e: e.tensor_scalar(out=GU[:, h, :], in0=U, scalar1=a_blk[:, h:h + 1], scalar2=None, op0=ALU.mult),
                                     reads=[cm, at], writes=[GU], acc=(h > 0))
                            pa = ps.next()
                            k.op("pe", lambda e: e.matmul(pa[:, :], lhsT=onesblk, rhs=GU[:].rearrange("p h i -> p (h i)"), start=True, stop=True),
                                 reads=[cm, GU], writes=[pa])
                            L = Lp.next()
                            for h in range(4):
                                k.op("dve", lambda e: e.tensor_scalar(out=L[:, h, :], in0=pa[:, h * 128:(h + 1) * 128], scalar1=st[:, h:h + 1],
                                                                      scalar2=0.0, op0=ALU.subtract, op1=ALU.min),
                                     reads=[pa, st], writes=[L], acc=(h > 0))
                            L2 = L2p.next()
                            k.op("act", lambda e: e.activation(out=L2[:], in_=L[:], func=AF.Exp), reads=[L], writes=[L2])
                            k.op("pool", lambda e: e.tensor_tensor(out=L[:], in0=L2[:], in1=U.unsqueeze(1).to_broadcast([128, 4, 128]), op=ALU.mult),
                                 reads=[L2, cm], writes=[L])
                            if debug.get("scan_stop", 9) <= 2:
                                continue
                            yield
                            pcbs = [ps.next(), ps.next()]
                            for g in range(2):
                                k.op("pe", lambda e: e.matmul(pcbs[g][:, 0:128], lhsT=BT[g * 64:(g + 1) * 64, tok0:tok0 + 128],
                                                              rhs=CT[g * 64:(g + 1) * 64, tok0:tok0 + 128], start=True, stop=True),
                                     reads=[BT, CT], writes=[pcbs[g]])
                            scT = scp.next()
                            for g in range(2):
                                k.op("dve", lambda e: e.tensor_tensor(out=scT[:, 2 * g:2 * g + 2, :],
                                                                      in0=pcbs[g][:, 0:128].unsqueeze(1).to_broadcast([128, 2, 128]),
                                                                      in1=L[:, 2 * g:2 * g + 2, :], op=ALU.mult),
                                     reads=[pcbs[g], L], writes=[scT], acc=(g > 0))
                            xdt = xdp.next()
                            k.op("dve", lambda e: e.tensor_tensor(out=xdt[:], in0=x_tok[:, blk, :].rearrange("p (h d) -> p h d", h=4),
                                                                  in1=dtt[:, blk, d * 4:(d + 1) * 4].unsqueeze(2).to_broadcast([128, 4, 64]), op=ALU.mult),
                                 reads=[x_tok, dtt], writes=[xdt])
                            yield
                            pyd = ps.next()
                            for h in range(4):
                                k.op("pe", lambda e: e.matmul(pyd[:, h * 64:(h + 1) * 64], lhsT=scT[:, h, :], rhs=xdt[:, h, :], start=True, stop=True),
                                     reads=[scT, xdt], writes=[pyd], inc=(h == 3))
                            yds = ydp.next()
                            k.op("act", lambda e: e.copy(out=yds[:], in_=pyd[:, 0:256]), reads=[pyd], writes=[yds])
                            if debug.get("scan_stop", 9) <= 3:
                                continue
                            for half in halves:
                                hb = half * 64
                                ec = 4 if half == 0 else 8
                                yield
                                pyos = [ps.next(), ps.next()]
                                for h in range(4):
                                    g, hh = h // 2, h % 2
                                    k.op("pe", lambda e: e.matmul(pyos[g][:, hh * 64:(hh + 1) * 64], lhsT=CT[g * 64:(g + 1) * 64, tok0:tok0 + 128],
                                                                  rhs=Sb[d][g * 64:(g + 1) * 64, hh, :], start=True, stop=True),
                                         reads=[CT, Sb[d]], writes=[pyos[g]])
                                ty = typ.next()
                                for g in range(2):
                                    k.op("dve", lambda e: e.tensor_tensor(out=ty[hb:hb + 64, 2 * g:2 * g + 2, :],
                                                                          in0=pyos[g][hb:hb + 64, 0:128].rearrange("p (h d) -> p h d", h=2),
                                                                          in1=ex[hb:hb + 64, 2 * g:2 * g + 2].unsqueeze(2).to_broadcast([64, 2, 64]), op=ALU.mult),
                                         reads=[pyos[g], ex], writes=[ty], acc=(g > 0))
                                if (blk, half) not in ywritten:
                                    ywritten.add((blk, half))
                                    k.op("dve", lambda e: e.tensor_tensor(out=yacc[hb:hb + 64, blk, :], in0=ty[hb:hb + 64, :, :].rearrange("p h d -> p (h d)"),
                                                                          in1=yds[hb:hb + 64, :], op=ALU.add),
                                         reads=[ty, yds], writes=[yacc], acc=True)
                                else:
                                    ty2 = typ.next()
                                    k.op("dve", lambda e: e.tensor_tensor(out=ty2[hb:hb + 64, :, :].rearrange("p h d -> p (h d)"),
                                                                          in0=ty[hb:hb + 64, :, :].rearrange("p h d -> p (h d)"),
                                                                          in1=yds[hb:hb + 64, :], op=ALU.add),
                                         reads=[ty, yds], writes=[ty2])
                                    k.op("pool", lambda e: e.tensor_tensor(out=yacc[hb:hb + 64, blk, :], in0=yacc[hb:hb + 64, blk, :],
                                                                           in1=ty2[hb:hb + 64, :, :].rearrange("p h d -> p (h d)"), op=ALU.add),
                                         reads=[yacc, ty2], writes=[yacc])
                                if debug.get("scan_stop", 9) <= 4:
                                    continue
                                yield
                                Bd = Bdp.next()
                                for h in range(4):
                                    k.op("dve", lambda e: e.tensor_scalar(out=Bd[hb:hb + 64, h, :], in0=B_tok[hb:hb + 64, blk, :],
                                                                          scalar1=ex[hb:hb + 64, 16 + h:17 + h], scalar2=None, op0=ALU.mult),
                                         reads=[B_tok, ex], writes=[Bd], acc=(h > 0))
                                pst = ps.next()
                                for h in range(4):
                                    k.op("pe", lambda e: e.matmul(pst[:, h * 64:(h + 1) * 64], lhsT=Bd[hb:hb + 64, h, :], rhs=xdt[hb:hb + 64, h, :],
                                                                  start=True, stop=True), reads=[Bd, xdt], writes=[pst], inc=(h == 3))
                                for h in range(4):
                                    g, hh = h // 2, h % 2
                                    k.op("dve", lambda e: e.scalar_tensor_tensor(out=S[d][g * 64:(g + 1) * 64, hh, :], in0=S[d][g * 64:(g + 1) * 64, hh, :],
                                                                                 scalar=ex[g * 64:(g + 1) * 64, ec + h:ec + h + 1],
                                                                                 in1=pst[g * 64:(g + 1) * 64, h * 64:(h + 1) * 64],
                                                                                 op0=ALU.mult, op1=ALU.add),
                                         reads=[S[d], ex, pst], writes=[S[d]])
                                k.op("act", lambda e: e.copy(out=Sb[d][:], in_=S[d][:]), reads=[S[d]], writes=[Sb[d]])
                            if seq_last and not lat:
                                p = ps.next()
                                k.op("pe", lambda e: e.transpose(p[:, 0:128], S[d][:].rearrange("p a b -> p (a b)"), ident[:]),
                                     reads=[S[d], ident], writes=[p])
                                so = sop.next()
                                k.op("dve", lambda e: e.tensor_copy(out=so[:, :], in_=p[:, 0:128]), reads=[p], writes=[so])
                                for g in range(2):
                                    for hh in range(2):
                                        k.dma("pool", O["new_ssm"][blk // BPS, l, d, 2 * g + hh, :, :], so[hh * 64:(hh + 1) * 64, g * 64:(g + 1) * 64], reads=[so])

                    gens = [ssd_dir_gen(0), ssd_dir_gen(1)]
                    while gens:
                        for g_ in list(gens):
                            try:
                                next(g_)
                            except StopIteration:
                                gens.remove(g_)
                    if debug.get("ssd_stop", 9) <= 4:
                        continue
                    for blk in range(NB):
                        tok0 = blk * 128
                        z = zp.next()
                        k.dma("sp", z[:], PTM[l][off + tok0:off + tok0 + 128, C_SSZ:C_SSZ + 256], reads=[PTM[l]], writes=[z])
                        t1 = y2p.next()
                        k.op("dve", lambda e: e.tensor_tensor(out=t1[:].rearrange("p (h d) -> p h d", h=4),
                                                              in0=x_tok[:, blk, :].rearrange("p (h d) -> p h d", h=4),
                                                              in1=Dt[:, 0:4].unsqueeze(2).to_broadcast([128, 4, 64]), op=ALU.mult),
                             reads=[x_tok, Dt], writes=[t1])
                        t2 = y2p.next()
                        k.op("pool", lambda e: e.tensor_tensor(out=t2[:], in0=t1[:], in1=yacc[:, blk, :], op=ALU.add), reads=[t1, yacc], writes=[t2])
                        sz = y2p.next()
                        k.op("act", lambda e: e.activation(out=sz[:], in_=z[:], func=AF.Silu), reads=[z], writes=[sz])
                        y2 = y2p.next()
                        k.op("dve", lambda e: e.tensor_tensor(out=y2[:], in0=t2[:], in1=sz[:], op=ALU.mult), reads=[t2, sz], writes=[y2])
                        ssum = ssp.next()
                        k.op("dve", lambda e: e.memset(ssum[:], 0.0), writes=[ssum])
                        for g in range(2):
                            k.op("act", lambda e: e.activation(out=t1[:, g * 128:(g + 1) * 128], in_=y2[:, g * 128:(g + 1) * 128], func=AF.Square,
                                                               accum_out=ssum[:, g:g + 1]), reads=[y2], writes=[t1, ssum])
                        k.op("act", lambda e: e.activation(out=ssum[:, 2:4], in_=ssum[:, 0:2], func=AF.Sqrt, scale=1.0 / 128, bias=epsb[:, 0:1]),
                             reads=[ssum, epsb], writes=[ssum])
                        k.op("dve", lambda e: e.reciprocal(out=ssum[:, 0:2], in_=ssum[:, 2:4]), reads=[ssum], writes=[ssum])
                        y3 = sz
                        for g in range(2):
                            k.op("dve", lambda e: e.scalar_tensor_tensor(out=y3[:, g * 128:(g + 1) * 128], in0=y2[:, g * 128:(g + 1) * 128],
                                                                         scalar=ssum[:, g:g + 1], in1=nwt[:, g * 128:(g + 1) * 128],
                                                                         op0=ALU.mult, op1=ALU.mult), reads=[y2, ssum, nwt], writes=[y3])
                        os_ = osp.next()
                        for c in range(2):
                            p = ps.next()
                            k.op("pe", lambda e: e.transpose(p[:, 0:128], y3[:, c * 128:(c + 1) * 128], ident[:]), reads=[y3, ident], writes=[p])
                            evac(c, os_[:, c, :], p[:, 0:128], [p], [os_], acc=(c > 0))
                        k.dma("pool", MIX[l][512:768, off + tok0:off + tok0 + 128].rearrange("(c p) t -> p c t", p=128), os_[:],
                              reads=[os_], writes=[MIX[l]], acc=True)
                k.barrier()


        def dn_phase(l):
            with ExitStack() as ph:
                cm = k.tile("cmd", [128, 14, 128], F32, ph)
                k.dma("sp", cm[:, 0:7, :], I["cmask"].rearrange("r p q -> p r q")[:, 0:7, :], writes=[cm])
                k.dma("sp", cm[:, 7:14, :], I["cmask"].rearrange("r p q -> p r q")[:, 7:14, :], writes=[cm], acc=True)
                onesblk, onesA, onesB, identm = cm[:, 2, :], cm[:, 3, :], cm[:, 4, :], cm[:, 7, :]
                cw = k.tile("cwd", [128, 6, 3], F32, ph)
                for kk in range(3):
                    k.dma("sp", cw[:, :, kk], I["dn_conv_w"][l, kk].rearrange("(c p) -> p c", p=128), writes=[cw],
                          acc=(kk > 0), allow_slow_non_contiguous=True)
                dtb = k.tile("dtbd", [128, 8], F32, ph)
                k.dma("sp", dtb[:], I["dn_dt_bias"][l:l + 1].rearrange("o a b -> o (a b)").partition_broadcast(128), writes=[dtb])
                aneg = k.tile("anegd", [128, 8], F32, ph)
                k.dma("sp", aneg[:], I["dn_a_log"][l:l + 1].rearrange("o a b -> o (a b)").partition_broadcast(128), writes=[aneg])
                k.op("act", lambda e: e.activation(out=aneg[:], in_=aneg[:], func=AF.Exp), reads=[aneg], writes=[aneg])
                k.op("dve", lambda e: e.tensor_scalar(out=aneg[:], in0=aneg[:], scalar1=-1.0, scalar2=None, op0=ALU.mult),
                     reads=[aneg], writes=[aneg])
                nw1 = k.tile("nw1", [128, 64], F32, ph)
                k.dma("sp", nw1[:], I["dn_norm_w"][l:l + 1, :].partition_broadcast(128), writes=[nw1])
                onec = k.tile("onecd", [128, 1], F32, ph)
                k.op("dve", lambda e: e.memset(onec[:], 1.0), writes=[onec])
                NBM = TL // 128
                qT = k.tile("qTd", [128, 2, TL], BF16, ph)
                kT = k.tile("kTd", [128, 2, TL], BF16, ph)
                k_tok = k.tile("k_tok", [128, NBM, 256], BF16, ph)
                v_tok = k.tile("v_tok", [128, NBM, 256], BF16, ph)
                yacc = k.tile("yaccd", [128, NBM, 256], F32, ph)
                braw = k.tile("braw", [128, NBM, 16], F32, ph)
                btmp = k.tile("btmp", [128, NBM, 16], F32, ph)
                lnb = k.tile("lnb", [128, NBM, 8], F32, ph)
                gt = k.tile("gt", [128, NBM, 8], F32, ph)
                S = [k.tile("Sd%d" % d, [128, 2, 64], F32, ph) for d in range(2)]
                Sb = [k.tile("Sbd%d" % d, [128, 2, 64], BF16, ph) for d in range(2)]

                def h4(ap):
                    return ap.rearrange("p (c r) x -> p c r x", c=2)

                for (off, T, lat, SEQT) in ((0, NCTX * TC, False, TC), (LOFF, TL, True, TL)):
                    k.barrier()
                    NB = T // 128
                    BPS = SEQT // 128
                    TT = min(512, SEQT)
                    with ExitStack() as pre:
                        xinp = k.pool("xind", [128, 6, 514], BF16, 2, pre)
                        cqp = k.pool("cq", [128, 6, 512], F32, 1, pre)
                        tmpp = k.pool("tmpd", [128, 512], F32, 4, pre)
                        knp = k.pool("kn", [128, 2, 512], F32, 1, pre)
                        for t0 in range(0, T, TT):
                            g0 = off + t0
                            xin = xinp.next()
                            first = (t0 % SEQT == 0)
                            lastt = ((t0 + TT) % SEQT == 0)
                            lo = g0 if first else g0 - 1
                            hi = g0 + TT if lastt else g0 + TT + 1
                            c_lo = 1 if first else 0
                            for half3 in range(2):
                                k.dma("sp", xin[:, 3 * half3:3 * half3 + 3, c_lo:c_lo + (hi - lo)],
                                      PFM[l][R_DNQ + 384 * half3:R_DNQ + 384 * half3 + 384, lo:hi].rearrange("(c p) t -> p c t", p=128),
                                      reads=[PFM[l]], writes=[xin], acc=(half3 > 0))
                            if first:
                                k.op("dve", lambda e: e.memset(xin[:, :, 0:1], 0.0), writes=[xin], acc=True)
                            if lastt:
                                k.op("dve", lambda e: e.memset(xin[:, :, TT + 1:TT + 2], 0.0), writes=[xin], acc=True)
                            cq = cqp.next()
                            for c in range(6):
                                ta = tmpp.next()
                                tb = tmpp.next()
                                eng = "dve" if c % 2 == 0 else "pool"
                                k.op("dve", lambda e: e.tensor_scalar(out=ta[:, 0:TT], in0=xin[:, c, 0:TT], scalar1=cw[:, c, 0:1],
                                                                      scalar2=None, op0=ALU.mult), reads=[xin, cw], writes=[ta])
                                k.op("dve", lambda e: e.scalar_tensor_tensor(out=tb[:, 0:TT], in0=xin[:, c, 1:TT + 1], scalar=cw[:, c, 1:2],
                                                                             in1=ta[:, 0:TT], op0=ALU.mult, op1=ALU.add),
                                     reads=[xin, cw, ta], writes=[tb])
                                k.op("dve", lambda e: e.scalar_tensor_tensor(out=ta[:, 0:TT], in0=xin[:, c, 2:TT + 2], scalar=cw[:, c, 2:3],
                                                                             in1=tb[:, 0:TT], op0=ALU.mult, op1=ALU.add),
                                     reads=[xin, cw, tb], writes=[ta])
                                k.op("act", lambda e: e.activation(out=cq[:, c, 0:TT], in_=ta[:, 0:TT], func=AF.Silu),
                                     reads=[ta], writes=[cq], acc=(c > 0))
                            kn = knp.next()
                            for c in range(4):
                                sq = tmpp.next()
                                k.op("act", lambda e: e.activation(out=sq[:, 0:TT], in_=cq[:, c, 0:TT], func=AF.Square), reads=[cq], writes=[sq])
                                p = ps.next()
                                k.op("pe", lambda e: e.matmul(p[:, 0:TT], lhsT=onesblk, rhs=sq[:, 0:TT], start=True, stop=True),
                                     reads=[cm, sq], writes=[p])
                                rs = tmpp.next()
                                k.op("act", lambda e: e.activation(out=rs[:, 0:TT], in_=p[:, 0:TT], func=AF.Sqrt, bias=epsb[:, 0:1]),
                                     reads=[p, epsb], writes=[rs])
                                rs2 = tmpp.next()
                                k.op("dve", lambda e: e.reciprocal(out=rs2[:, 0:TT], in_=rs[:, 0:TT]), reads=[rs], writes=[rs2])
                                if c < 2:
                                    k.op("dve", lambda e: e.scalar_tensor_tensor(out=qT[:, c, t0:t0 + TT], in0=cq[:, c, 0:TT], scalar=0.125,
                                                                                 in1=rs2[:, 0:TT], op0=ALU.mult, op1=ALU.mult),
                                         reads=[cq, rs2], writes=[qT], acc=True)
                                else:
                                    k.op("dve", lambda e: e.tensor_tensor(out=kn[:, c - 2, 0:TT], in0=cq[:, c, 0:TT], in1=rs2[:, 0:TT], op=ALU.mult),
                                         reads=[cq, rs2], writes=[kn], acc=(c > 2))
                                    k.op("act", lambda e: e.copy(out=kT[:, c - 2, t0:t0 + TT], in_=kn[:, c - 2, 0:TT]), reads=[kn], writes=[kT], acc=True)
                            for b in range(TT // 128):
                                blk = t0 // 128 + b
                                for c in range(2):
                                    p = ps.next()
                                    k.op("pe", lambda e: e.transpose(p[:, 0:128], kn[:, c, b * 128:(b + 1) * 128], ident[:]),
                                         reads=[kn, ident], writes=[p])
                                    evac(c, k_tok[:, blk, c * 128:(c + 1) * 128], p[:, 0:128], [p], [k_tok], acc=True)
                                    p = ps.next()
                                    k.op("pe", lambda e: e.transpose(p[:, 0:128], cq[:, 4 + c, b * 128:(b + 1) * 128], ident[:]),
                                         reads=[cq, ident], writes=[p])
                                    evac(c + 1, v_tok[:, blk, c * 128:(c + 1) * 128], p[:, 0:128], [p], [v_tok], acc=True)
                        k.barrier()
                    for b0 in range(0, NB, 2):
                        k.dma("sp", braw[:, b0:b0 + 2, :],
                              PTM[l][off + b0 * 128:off + (b0 + 2) * 128, C_BETA:C_BETA + 16].rearrange("(b p) j -> p b j", p=128),
                              reads=[PTM[l]], writes=[braw], acc=(b0 > 0))
                    k.op("act", lambda e: e.activation(out=btmp[:, 0:NB, 0:8], in_=braw[:, 0:NB, 0:8], func=AF.Exp, scale=-1.0),
                         reads=[braw], writes=[btmp])
                    k.op("act", lambda e: e.activation(out=lnb[:, 0:NB, :], in_=btmp[:, 0:NB, 0:8], func=AF.Ln, bias=onec[:, 0:1]),
                         reads=[btmp, onec], writes=[lnb])
                    k.op("dve", lambda e: e.tensor_scalar(out=lnb[:, 0:NB, :], in0=lnb[:, 0:NB, :], scalar1=-1.0, scalar2=None, op0=ALU.mult),
                         reads=[lnb], writes=[lnb])
                    k.op("dve", lambda e: e.tensor_tensor(out=btmp[:, 0:NB, 8:16], in0=braw[:, 0:NB, 8:16],
                                                          in1=dtb[:].unsqueeze(1).to_broadcast([128, NB, 8]), op=ALU.add),
                         reads=[braw, dtb], writes=[btmp])
                    k.op("act", lambda e: e.activation(out=braw[:, 0:NB, 8:16], in_=btmp[:, 0:NB, 8:16], func=AF.Exp), reads=[btmp], writes=[braw])
                    k.op("act", lambda e: e.activation(out=btmp[:, 0:NB, 8:16], in_=braw[:, 0:NB, 8:16], func=AF.Ln, bias=onec[:, 0:1]),
                         reads=[braw, onec], writes=[btmp])
                    k.op("dve", lambda e: e.tensor_tensor(out=gt[:, 0:NB, :], in0=btmp[:, 0:NB, 8:16],
                                                          in1=aneg[:].unsqueeze(1).to_broadcast([128, NB, 8]), op=ALU.mult),
                         reads=[btmp, aneg], writes=[gt])
                    for d in range(2):
                        if lat:
                            for h in range(4):
                                c_, par = h // 2, h % 2
                                k.dma("sp", S[d][par * 64:(par + 1) * 64, c_, :], I["state_dn"][l, d, h, :, :], writes=[S[d]], acc=(h > 0))
                            k.op("act", lambda e: e.copy(out=Sb[d][:], in_=S[d][:]), reads=[S[d]], writes=[Sb[d]])
                    if debug.get("dn_stop", 9) <= 1:
                        continue
                    with ExitStack() as sc:
                        ywritten = set()
                        WK = []
                        for d_ in range(2):
                            W_ = {"stp": k.pool("std", [128, 24], F32, 2, sc), "exp": k.pool("exd", [128, 24], F32, 4, sc), "w4": {}, "w64": {}}
                            for nm in ("GU", "GUb", "L", "E1", "E2", "E3", "A", "AT", "tm"):
                                W_["w4"][nm] = k.tile("w4" + nm, [128, 4, 128], F32, sc)
                            W_["Xp"] = k.pool("Xp", [128, 4, 128], BF16, 2, sc)
                            W_["XTp"] = k.pool("XTp", [128, 4, 128], BF16, 2, sc)
                            for nm in ("As", "ATs", "Ps", "Ps2"):
                                W_["w4"][nm] = k.tile("w4b" + nm, [128, 4, 128], BF16, sc)
                            for nm in ("ty", "ty2"):
                                W_["w64"][nm] = k.tile("w64" + nm, [128, 4, 64], F32, sc)
                            for nm in ("vb", "kbg", "vn"):
                                W_["w64"][nm] = k.tile("w64" + nm, [128, 4, 64], BF16, sc)
                            W_["slots"] = [{"QK": k.tile("sQK", [128, 4, 128], BF16, sc), "wT": k.tile("swT", [128, 4, 128], BF16, sc),
                                            "u": k.tile("su", [128, 4, 64], F32, sc), "kdec": k.tile("skd", [128, 4, 64], BF16, sc)} for _ in range(2)]
                            WK.append(W_)

                        def dn_intra_gen(d):
                            stp, exp_, w4, w64, Xp, XTp = (WK[d][n_] for n_ in ("stp", "exp", "w4", "w64", "Xp", "XTp"))
                            ps = ps8
                            cnt = 0
                            U = cm[:, d, :]
                            m_incl = cm[:, d, :]
                            m_at = cm[:, 5 + d, :]
                            m_a = cm[:, 6 - d, :]
                            order = list(range(NB)) if d == 0 else list(range(NB - 1, -1, -1))
                            halves = (0, 1) if d == 0 else (1, 0)

                            def bc4(m):
                                return m.unsqueeze(1).to_broadcast([128, 4, 128])

                            for blk in order:
                                while busy[d] >= 2:
                                    yield
                                busy[d] += 1
                                slot = WK[d]["slots"][cnt % 2]
                                cnt += 1
                                tok0 = blk * 128
                                g_blk = gt[:, blk, d * 4:(d + 1) * 4]
                                lb_blk = lnb[:, blk, d * 4:(d + 1) * 4]
                                pc = ps.next()
                                for j, lh in enumerate((U, onesA, onesB)):
                                    k.op("pe", lambda e: e.matmul(pc[:, 4 * j:4 * j + 4], lhsT=lh, rhs=g_blk, start=True, stop=True),
                                         reads=[cm, gt], writes=[pc], inc=(j == 2))
                                st = stp.next()
                                k.op("act", lambda e: e.copy(out=st[:, 0:12], in_=pc[:, 0:12]), reads=[pc], writes=[st])
                                k.op("dve", lambda e: e.tensor_tensor(out=st[:, 12:16], in0=st[:, 0:4], in1=lb_blk, op=ALU.add),
                                     reads=[st, lnb], writes=[st])
                                k.op("dve", lambda e: e.tensor_tensor(out=st[0:64, 16:20], in0=st[0:64, 4:8], in1=st[0:64, 0:4], op=ALU.subtract),
                                     reads=[st], writes=[st])
                                k.op("dve", lambda e: e.tensor_tensor(out=st[64:128, 16:20], in0=st[64:128, 8:12], in1=st[64:128, 0:4], op=ALU.subtract),
                                     reads=[st], writes=[st])
                                ex = exp_.next()
                                k.op("act", lambda e: e.activation(out=ex[:, 0:20], in_=st[:, 0:20], func=AF.Exp), reads=[st], writes=[ex])
                                k.op("act", lambda e: e.activation(out=ex[:, 20:24], in_=lb_blk, func=AF.Exp), reads=[lnb], writes=[ex], acc=True)
                                yield
                                GU, GUb, L = w4["GU"], w4["GUb"], w4["L"]
                                for h in range(4):
                                    k.op("dve", lambda e: e.tensor_scalar(out=GU[:, h, :], in0=U, scalar1=g_blk[:, h:h + 1], scalar2=None, op0=ALU.mult),
                                         reads=[cm, gt], writes=[GU], acc=(h > 0))
                                for h in range(4):
                                    k.op("dve", lambda e: e.scalar_tensor_tensor(out=GUb[:, h, :], in0=identm, scalar=lb_blk[:, h:h + 1],
                                                                                 in1=GU[:, h, :], op0=ALU.mult, op1=ALU.add),
                                         reads=[cm, lnb, GU], writes=[GUb], acc=(h > 0))
                                pa1 = ps.next()
                                k.op("pe", lambda e: e.matmul(pa1[:, :], lhsT=onesblk, rhs=GU[:].rearrange("p h i -> p (h i)"), start=True, stop=True),
                                     reads=[cm, GU], writes=[pa1])
                                pa2 = ps.next()
                                k.op("pe", lambda e: e.matmul(pa2[:, :], lhsT=onesblk, rhs=GUb[:].rearrange("p h i -> p (h i)"), start=True, stop=True),
                                     reads=[cm, GUb], writes=[pa2])
                                for h in range(4):
                                    k.op("dve", lambda e: e.tensor_scalar(out=L[:, h, :], in0=pa1[:, h * 128:(h + 1) * 128], scalar1=st[:, h:h + 1],
                                                                          scalar2=0.0, op0=ALU.subtract, op1=ALU.min),
                                         reads=[pa1, st], writes=[L], acc=(h > 0))
                                k.op("act", lambda e: e.activation(out=w4["tm"][:], in_=L[:], func=AF.Exp), reads=[L], writes=[w4["tm"]])
                                k.op("pool", lambda e: e.tensor_tensor(out=w4["E3"][:], in0=w4["tm"][:], in1=bc4(m_incl), op=ALU.mult),
                                     reads=[w4["tm"], cm], writes=[w4["E3"]])
                                for h in range(4):
                                    k.op("dve", lambda e: e.tensor_scalar(out=L[:, h, :], in0=pa1[:, h * 128:(h + 1) * 128], scalar1=st[:, 12 + h:13 + h],
                                                                          scalar2=0.0, op0=ALU.subtract, op1=ALU.max),
                                         reads=[pa1, st], writes=[L], acc=(h > 0))
                                k.op("act", lambda e: e.activation(out=w4["tm"][:], in_=L[:], func=AF.Exp, scale=-1.0), reads=[L], writes=[w4["tm"]])
                                k.op("pool", lambda e: e.tensor_tensor(out=w4["E1"][:], in0=w4["tm"][:], in1=bc4(m_a), op=ALU.mult),
                                     reads=[w4["tm"], cm], writes=[w4["E1"]])
                                for h in range(4):
                                    k.op("dve", lambda e: e.tensor_scalar(out=L[:, h, :], in0=pa2[:, h * 128:(h + 1) * 128], scalar1=st[:, h:h + 1],
                                                                          scalar2=0.0, op0=ALU.subtract, op1=ALU.min),
                                         reads=[pa2, st], writes=[L], acc=(h > 0))
                                k.op("act", lambda e: e.activation(out=w4["tm"][:], in_=L[:], func=AF.Exp), reads=[L], writes=[w4["tm"]])
                                k.op("pool", lambda e: e.tensor_tensor(out=w4["E2"][:], in0=w4["tm"][:], in1=bc4(m_at), op=ALU.mult),
                                     reads=[w4["tm"], cm], writes=[w4["E2"]])
                                yield
                                pk = [ps.next(), ps.next()]
                                for c_ in range(2):
                                    for par in range(2):
                                        k.op("pe", lambda e: e.matmul(pk[par][:, c_ * 128:(c_ + 1) * 128],
                                                                      lhsT=kT[par * 64:(par + 1) * 64, c_, tok0 + (off - off):tok0 + 128],
                                                                      rhs=kT[par * 64:(par + 1) * 64, c_, tok0:tok0 + 128], start=True, stop=True),
                                             reads=[kT], writes=[pk[par]])
                                A, AT, QK = w4["A"], w4["AT"], slot["QK"]
                                for par in range(2):
                                    k.op("dve", lambda e: e.tensor_tensor(out=h4(A[:])[:, :, par, :], in0=pk[par][:, 0:256].rearrange("p (c i) -> p c i", c=2),
                                                                          in1=h4(w4["E1"][:])[:, :, par, :], op=ALU.mult),
                                         reads=[pk[par], w4["E1"]], writes=[A], acc=(par > 0))
                                    k.op("dve", lambda e: e.tensor_tensor(out=h4(AT[:])[:, :, par, :], in0=pk[par][:, 0:256].rearrange("p (c i) -> p c i", c=2),
                                                                          in1=h4(w4["E2"][:])[:, :, par, :], op=ALU.mult),
                                         reads=[pk[par], w4["E2"]], writes=[AT], acc=(par > 0))
                                pq = [ps.next(), ps.next()]
                                for c_ in range(2):
                                    for par in range(2):
                                        k.op("pe", lambda e: e.matmul(pq[par][:, c_ * 128:(c_ + 1) * 128],
                                                                      lhsT=kT[par * 64:(par + 1) * 64, c_, tok0:tok0 + 128],
                                                                      rhs=qT[par * 64:(par + 1) * 64, c_, tok0:tok0 + 128], start=True, stop=True),
                                             reads=[kT, qT], writes=[pq[par]])
                                for par in range(2):
                                    k.op("dve", lambda e: e.tensor_tensor(out=h4(QK[:])[:, :, par, :], in0=pq[par][:, 0:256].rearrange("p (c i) -> p c i", c=2),
                                                                          in1=h4(w4["E3"][:])[:, :, par, :], op=ALU.mult),
                                         reads=[pq[par], w4["E3"]], writes=[QK], acc=(par > 0))
                                yield
                                X = Xp.next()
                                XT = XTp.next()
                                tm = w4["tm"]
                                k.op("pool", lambda e: e.tensor_tensor(out=tm[:], in0=A[:], in1=bc4(cm[:, 8, :]), op=ALU.mult), reads=[A, cm], writes=[tm])
                                k.op("dve", lambda e: e.scalar_tensor_tensor(out=X[:], in0=tm[:], scalar=-1.0, in1=bc4(identm), op0=ALU.mult, op1=ALU.add),
                                     reads=[tm, cm], writes=[X])
                                k.op("pool", lambda e: e.tensor_tensor(out=L[:], in0=AT[:], in1=bc4(cm[:, 8, :]), op=ALU.mult), reads=[AT, cm], writes=[L])
                                k.op("dve", lambda e: e.scalar_tensor_tensor(out=XT[:], in0=L[:], scalar=-1.0, in1=bc4(identm), op0=ALU.mult, op1=ALU.add),
                                     reads=[L, cm], writes=[XT])
                                As, ATs, Ps, Ps2 = w4["As"], w4["ATs"], w4["Ps"], w4["Ps2"]
                                for lev in range(5):
                                    ms = cm[:, 9 + lev, :]
                                    k.op("pool", lambda e: e.tensor_tensor(out=As[:], in0=A[:], in1=bc4(ms), op=ALU.mult), reads=[A, cm], writes=[As])
                                    k.op("pool", lambda e: e.tensor_tensor(out=ATs[:], in0=AT[:], in1=bc4(ms), op=ALU.mult), reads=[AT, cm], writes=[ATs])
                                    pP = ps.next()
                                    for h in range(4):
                                        k.op("pe", lambda e: e.matmul(pP[:, h * 128:(h + 1) * 128], lhsT=ATs[:, h, :], rhs=X[:, h, :], start=True, stop=True),
                                             reads=[ATs, X], writes=[pP], inc=(h == 3))
                                    k.op("act", lambda e: e.copy(out=Ps[:].rearrange("p h i -> p (h i)"), in_=pP[:, :]), reads=[pP], writes=[Ps])
                                    pP2 = ps.next()
                                    for h in range(4):
                                        k.op("pe", lambda e: e.matmul(pP2[:, h * 128:(h + 1) * 128], lhsT=As[:, h, :], rhs=XT[:, h, :], start=True, stop=True),
                                             reads=[As, XT], writes=[pP2], inc=(h == 3))
                                    k.op("act", lambda e: e.copy(out=Ps2[:].rearrange("p h i -> p (h i)"), in_=pP2[:, :]), reads=[pP2], writes=[Ps2])
                                    yield
                                    pX = ps.next()
                                    for h in range(4):
                                        k.op("pe", lambda e: e.matmul(pX[:, h * 128:(h + 1) * 128], lhsT=XT[:, h, :], rhs=Ps[:, h, :], start=True, stop=True),
                                             reads=[XT, Ps], writes=[pX], inc=(h == 3))
                                    pXT = ps.next()
                                    for h in range(4):
                                        k.op("pe", lambda e: e.matmul(pXT[:, h * 128:(h + 1) * 128], lhsT=X[:, h, :], rhs=Ps2[:, h, :], start=True, stop=True),
                                             reads=[X, Ps2], writes=[pXT], inc=(h == 3))
                                    Xn = Xp.next()
                                    XTn = XTp.next()
                                    k.op("dve", lambda e: e.tensor_tensor(out=Xn[:].rearrange("p h i -> p (h i)"), in0=X[:].rearrange("p h i -> p (h i)"),
                                                                          in1=pX[:, :], op=ALU.subtract), reads=[X, pX], writes=[Xn])
                                    k.op("dve", lambda e: e.tensor_tensor(out=XTn[:].rearrange("p h i -> p (h i)"), in0=XT[:].rearrange("p h i -> p (h i)"),
                                                                          in1=pXT[:, :], op=ALU.subtract), reads=[XT, pXT], writes=[XTn])
                                    X, XT = Xn, XTn
                                    yield
                                yield
                                vb, kbg = w64["vb"], w64["kbg"]
                                kdec, u_sb, wT = slot["kdec"], slot["u"], slot["wT"]

                                def bc64(ap):
                                    return ap.unsqueeze(2).to_broadcast([128, 4, 64])

                                k.op("dve", lambda e: e.tensor_tensor(out=vb[:], in0=v_tok[:, blk, :].rearrange("p (h d) -> p h d", h=4), in1=bc64(ex[:, 20:24]), op=ALU.mult),
                                     reads=[v_tok, ex], writes=[vb])
                                k.op("pool", lambda e: e.tensor_tensor(out=kbg[:], in0=k_tok[:, blk, :].rearrange("p (h d) -> p h d", h=4), in1=bc64(ex[:, 12:16]), op=ALU.mult),
                                     reads=[k_tok, ex], writes=[kbg])
                                k.op("pool", lambda e: e.tensor_tensor(out=kdec[:], in0=k_tok[:, blk, :].rearrange("p (h d) -> p h d", h=4), in1=bc64(ex[:, 16:20]), op=ALU.mult),
                                     reads=[k_tok, ex], writes=[kdec])
                                pu = ps.next()
                                for h in range(4):
                                    k.op("pe", lambda e: e.matmul(pu[:, h * 64:(h + 1) * 64], lhsT=XT[:, h, :], rhs=vb[:, h, :], start=True, stop=True),
                                         reads=[XT, vb], writes=[pu], inc=(h == 3))
                                k.op("act", lambda e: e.copy(out=u_sb[:].rearrange("p h d -> p (h d)"), in_=pu[:, 0:256]), reads=[pu], writes=[u_sb])
                                pwT = ps.next()
                                for h in range(4):
                                    c_ = h // 2
                                    k.op("pe", lambda e: e.matmul(pwT[:, h * 128:(h + 1) * 128], lhsT=kbg[:, 2 * c_:2 * c_ + 2, :].rearrange("p r d -> p (r d)"),
                                                                  rhs=XT[:, h, :], start=True, stop=True),
                                         reads=[kbg, XT], writes=[pwT], inc=(h == 3))
                                for h in range(4):
                                    par = h % 2
                                    evac(h, wT[par * 64:(par + 1) * 64, h, :], pwT[par * 64:(par + 1) * 64, h * 128:(h + 1) * 128], [pwT], [wT], acc=(h > 0))
                                tasks[d].append((blk, ex, slot))
                                yield
                            done[d] = True

                        def dn_rec_gen(d):
                            w64 = WK[d]["w64"]
                            ps = ps8
                            halves = (0, 1) if d == 0 else (1, 0)
                            vn, ty, ty2 = w64["vn"], w64["ty"], w64["ty2"]
                            while True:
                                if not tasks[d]:
                                    if done[d]:
                                        break
                                    yield
                                    continue
                                blk, ex, slot = tasks[d].pop(0)
                                QK, kdec, u_sb, wT = slot["QK"], slot["kdec"], slot["u"], slot["wT"]
                                tok0 = blk * 128
                                seq_first = (blk % BPS == 0) if d == 0 else (blk % BPS == BPS - 1)
                                seq_last = (blk % BPS == BPS - 1) if d == 0 else (blk % BPS == 0)
                                if seq_first and not lat:
                                    k.op("dve", lambda e: e.memset(S[d][:], 0.0), writes=[S[d]])
                                    k.op("act", lambda e: e.copy(out=Sb[d][:], in_=S[d][:]), reads=[S[d]], writes=[Sb[d]])
                                for half in halves:
                                    hb = half * 64
                                    ec = 4 if half == 0 else 8
                                    pw = [ps.next(), ps.next()]
                                    for h in range(4):
                                        c_, par = h // 2, h % 2
                                        k.op("pe", lambda e: e.matmul(pw[par][:, c_ * 64:(c_ + 1) * 64], lhsT=wT[par * 64:(par + 1) * 64, h, :],
                                                                      rhs=Sb[d][par * 64:(par + 1) * 64, c_, :], start=True, stop=True),
                                             reads=[wT, Sb[d]], writes=[pw[par]])
                                    for par in range(2):
                                        k.op("dve", lambda e: e.tensor_tensor(out=vn[:].rearrange("p (c r) d -> p c r d", c=2)[:, :, par, :],
                                                                              in0=u_sb[:].rearrange("p (c r) d -> p c r d", c=2)[:, :, par, :],
                                                                              in1=pw[par][:, 0:128].rearrange("p (c d) -> p c d", c=2), op=ALU.subtract),
                                             reads=[u_sb, pw[par]], writes=[vn], acc=(par > 0))
                                    yield
                                    pqs = [ps.next(), ps.next()]
                                    for h in range(4):
                                        c_, par = h // 2, h % 2
                                        k.op("pe", lambda e: e.matmul(pqs[par][:, c_ * 64:(c_ + 1) * 64], lhsT=qT[par * 64:(par + 1) * 64, c_, tok0:tok0 + 128],
                                                                      rhs=Sb[d][par * 64:(par + 1) * 64, c_, :], start=True, stop=True),
                                             reads=[qT, Sb[d]], writes=[pqs[par]])
                                    pqk = ps.next()
                                    for h in range(4):
                                        k.op("pe", lambda e: e.matmul(pqk[:, h * 64:(h + 1) * 64], lhsT=QK[:, h, :], rhs=vn[:, h, :], start=True, stop=True),
                                             reads=[QK, vn], writes=[pqk], inc=(h == 3))
                                    for par in range(2):
                                        k.op("dve", lambda e: e.tensor_tensor(out=ty[hb:hb + 64].rearrange("p (c r) d -> p c r d", c=2)[:, :, par, :],
                                                                              in0=pqs[par][hb:hb + 64, 0:128].rearrange("p (c d) -> p c d", c=2),
                                                                              in1=ex[hb:hb + 64, 0:4].rearrange("p (c r) -> p c r", c=2)[:, :, par].unsqueeze(2).to_broadcast([64, 2, 64]),
                                                                              op=ALU.mult),
                                             reads=[pqs[par], ex], writes=[ty], acc=(par > 0))
                                    if (blk, half) not in ywritten:
                                        ywritten.add((blk, half))
                                        k.op("dve", lambda e: e.tensor_tensor(out=yacc[hb:hb + 64, blk, :], in0=ty[hb:hb + 64].rearrange("p h d -> p (h d)"),
                                                                              in1=pqk[hb:hb + 64, 0:256], op=ALU.add),
                                             reads=[ty, pqk], writes=[yacc], acc=True)
                                    else:
                                        k.op("dve", lambda e: e.tensor_tensor(out=ty2[hb:hb + 64].rearrange("p h d -> p (h d)"),
                                                                              in0=ty[hb:hb + 64].rearrange("p h d -> p (h d)"),
                                                                              in1=pqk[hb:hb + 64, 0:256], op=ALU.add),
                                             reads=[ty, pqk], writes=[ty2])
                                        k.op("pool", lambda e: e.tensor_tensor(out=yacc[hb:hb + 64, blk, :], in0=yacc[hb:hb + 64, blk, :],
                                                                               in1=ty2[hb:hb + 64].rearrange("p h d -> p (h d)"), op=ALU.add),
                                             reads=[yacc, ty2], writes=[yacc])
                                    yield
                                    pst = ps.next()
                                    for h in range(4):
                                        c_ = h // 2
                                        k.op("pe", lambda e: e.matmul(pst[:, h * 64:(h + 1) * 64],
                                                                      lhsT=kdec[hb:hb + 64, 2 * c_:2 * c_ + 2, :].rearrange("p r d -> p (r d)"),
                                                                      rhs=vn[hb:hb + 64, h, :], start=True, stop=True),
                                             reads=[kdec, vn], writes=[pst], inc=(h == 3))
                                    for h in range(4):
                                        c_, par = h // 2, h % 2
                                        k.op("dve", lambda e: e.scalar_tensor_tensor(out=S[d][par * 64:(par + 1) * 64, c_, :], in0=S[d][par * 64:(par + 1) * 64, c_, :],
                                                                                     scalar=ex[par * 64:(par + 1) * 64, ec + h:ec + h + 1],
                                                                                     in1=pst[par * 64:(par + 1) * 64, h * 64:(h + 1) * 64],
                                                                                     op0=ALU.mult, op1=ALU.add),
                                             reads=[S[d], ex, pst], writes=[S[d]])
                                    k.op("act", lambda e: e.copy(out=Sb[d][:], in_=S[d][:]), reads=[S[d]], writes=[Sb[d]])
                                    yield
                                if seq_last and not lat:
                                    for h in range(4):
                                        c_, par = h // 2, h % 2
                                        k.dma("pool", O["new_sdn"][blk // BPS, l, d, h, :, :], S[d][par * 64:(par + 1) * 64, c_, :], reads=[S[d]])
                                busy[d] -= 1

                        tasks = [[], []]
                        busy = [0, 0]
                        done = [False, False]
                        gens = [dn_intra_gen(0), dn_intra_gen(1), dn_rec_gen(0), dn_rec_gen(1)]
                        while gens:
                            for g_ in list(gens):
                                try:
                                    next(g_)
                                except StopIteration:
                                    gens.remove(g_)
                        k.barrier()
                    if debug.get("dn_stop", 9) <= 3:
                        continue
                    with ExitStack() as fin:
                        zp = k.pool("zd", [128, 256], F32, 2, fin)
                        y2p = k.pool("y2d", [128, 256], F32, 4, fin)
                        ssp = k.pool("ssumd", [128, 8], F32, 2, fin)
                        osp = k.pool("osd", [128, 2, 128], BF16, 2, fin)
                        for blk in range(NB):
                            tok0 = blk * 128
                            z = zp.next()
                            k.dma("sp", z[:], PTM[l][off + tok0:off + tok0 + 128, C_DNZ:C_DNZ + 256], reads=[PTM[l]], writes=[z])
                            ssum = ssp.next()
                            k.op("dve", lambda e: e.memset(ssum[:], 0.0), writes=[ssum])
                            t1 = y2p.next()
                            for h in range(4):
                                k.op("act", lambda e: e.activation(out=t1[:, h * 64:(h + 1) * 64], in_=yacc[:, blk, h * 64:(h + 1) * 64], func=AF.Square,
                                                                   accum_out=ssum[:, h:h + 1]), reads=[yacc], writes=[t1, ssum])
                            k.op("act", lambda e: e.activation(out=ssum[:, 4:8], in_=ssum[:, 0:4], func=AF.Sqrt, scale=1.0 / 64, bias=epsb[:, 0:1]),
                                 reads=[ssum, epsb], writes=[ssum])
                            k.op("dve", lambda e: e.reciprocal(out=ssum[:, 0:4], in_=ssum[:, 4:8]), reads=[ssum], writes=[ssum])
                            t2 = y2p.next()
                            k.op("dve", lambda e: e.tensor_tensor(out=t2[:].rearrange("p (h d) -> p h d", h=4),
                                                                  in0=yacc[:, blk, :].rearrange("p (h d) -> p h d", h=4),
                                                                  in1=ssum[:, 0:4].unsqueeze(2).to_broadcast([128, 4, 64]), op=ALU.mult),
                                 reads=[yacc, ssum], writes=[t2])
                            t3 = y2p.next()
                            k.op("pool", lambda e: e.tensor_tensor(out=t3[:].rearrange("p (h d) -> p h d", h=4),
                                                                   in0=t2[:].rearrange("p (h d) -> p h d", h=4),
                                                                   in1=nw1[:].unsqueeze(1).to_broadcast([128, 4, 64]), op=ALU.mult),
                                 reads=[t2, nw1], writes=[t3])
                            sz = y2p.next()
                            k.op("act", lambda e: e.activation(out=sz[:], in_=z[:], func=AF.Silu), reads=[z], writes=[sz])
                            k.op("dve", lambda e: e.tensor_tensor(out=t1[:], in0=t3[:], in1=sz[:], op=ALU.mult), reads=[t3, sz], writes=[t1])
                            os_ = osp.next()
                            for c in range(2):
                                p = ps.next()
                                k.op("pe", lambda e: e.transpose(p[:, 0:128], t1[:, c * 128:(c + 1) * 128], ident[:]), reads=[t1, ident], writes=[p])
                                evac(c, os_[:, c, :], p[:, 0:128], [p], [os_], acc=(c > 0))
                            k.dma("pool", MIX[l][0:256, off + tok0:off + tok0 + 128].rearrange("(c p) t -> p c t", p=128), os_[:],
                                  reads=[os_], writes=[MIX[l]], acc=True)
                        k.barrier()
                k.barrier()

        for l in range(DEPTH):
            with ExitStack() as ph:
                cs = k.tile("cs", [128, 8, 2], F32, ph)
                for kind in range(2):
                    k.dma("sp", cs[:, :, kind],
                          I["cvec"][kind].rearrange("(c p) -> p c", p=128),
                          writes=[cs], acc=(kind > 0), allow_slow_non_contiguous=True)
                k.op("act", lambda e: e.activation(out=cs[:], in_=cs[:], func=AF.Silu), reads=[cs], writes=[cs])
                bad = k.tile("bad", [128, 48], F32, ph)
                k.dma("sp", bad[:], I["b_ada"][l].rearrange("(j p) -> p j", p=128),
                      writes=[bad], allow_slow_non_contiguous=True)
                nw = k.tile("nw", [128, 2, 8], F32, ph)
                k.dma("sp", nw[:, 0, :], I["norm1_w"][l].rearrange("(c p) -> p c", p=128),
                      writes=[nw], allow_slow_non_contiguous=True)
                k.dma("sp", nw[:, 1, :], I["norm2_w"][l].rearrange("(c p) -> p c", p=128),
                      writes=[nw], acc=True, allow_slow_non_contiguous=True)
                ada = k.tile("ada", [128, 48, 2], F32, ph)
                wap = k.pool("wap", [128, 8, 512], F32, 2, ph)
                for pc in range(12):
                    wa = wap.next()
                    k.dma("sp" if pc % 2 == 0 else "pool", wa[:],
                          I["w_ada"][l][:, pc * 512:(pc + 1) * 512].rearrange("(c p) n -> p c n", p=128),
                          writes=[wa])
                    for jj in range(4):
                        j = pc * 4 + jj
                        p = ps.next()
                        for c in range(8):
                            k.op("pe", lambda e: e.matmul(p[:, 0:2], lhsT=wa[:, c, jj * 128:(jj + 1) * 128],
                                                          rhs=cs[:, c, :], start=(c == 0), stop=(c == 7)),
                                 reads=[wa, cs], writes=[p], inc=(c == 7))
                        k.op("dve", lambda e: e.tensor_scalar(out=ada[:, j, :], in0=p[:, 0:2],
                                                              scalar1=bad[:, j:j + 1], scalar2=None, op0=ALU.add),
                             reads=[p, bad], writes=[ada], acc=(j > 0))
                m = mod[l]
                for (dst, srcj, nwi) in ((0, 8, 0), (3, 32, 1)):
                    for kind in range(2):
                        k.op("dve", lambda e: e.scalar_tensor_tensor(
                            out=m[:, dst, :, kind], in0=ada[:, srcj:srcj + 8, kind], scalar=1.0,
                            in1=nw[:, nwi, :], op0=ALU.add, op1=ALU.mult),
                            reads=[ada, nw], writes=[m], acc=True)
                for (dst, srcj) in ((1, 0), (2, 16), (4, 24), (5, 40)):
                    k.op("dve", lambda e: e.tensor_copy(out=m[:, dst, :, :], in_=ada[:, srcj:srcj + 8, :]),
                         reads=[ada], writes=[m], acc=True)
                dump("mod%d" % l, m, m[:].rearrange("p a c k -> p (a c k)"), [128, 96])
                dump("ada%d" % l, ada, ada[:].rearrange("p a k -> p (a k)"), [128, 96])
                k.barrier()

            with ExitStack() as ph:
                wfm = k.tile("wfm", [128, 8, NFM], BF16, ph)
                wtm = k.tile("wtm", [128, 8, NTM], BF16, ph)
                win = I["w_in"][l]

                def wload(dst, d0, s0, n, first=False):
                    k.dma("pool", dst[:, :, d0:d0 + n], win[:, s0:s0 + n].rearrange("(c p) n -> p c n", p=128),
                          writes=[dst], acc=True)

                wload(wfm, R_DNQ, 0, 768)
                wload(wfm, R_SSX, 1456 + 256, 512)
                wload(wfm, R_MQ, 1040, 384)
                wload(wfm, R_MKPE, 1424, 32)
                wload(wfm, R_MKPE + 32, 1424 + 16, 16)
                wload(wfm, R_MKPE + 48, 1424, 16)
                wload(wfm, R_SWQ, 2232, 256)
                for h in range(4):
                    wload(wfm, R_SWQS + h * 64, 2232 + h * 64 + 32, 32)
                    wload(wfm, R_SWQS + h * 64 + 32, 2232 + h * 64, 32)
                wload(wfm, R_SWK, 2488, 128)
                for h in range(2):
                    wload(wfm, R_SWKS + h * 64, 2488 + h * 64 + 32, 32)
                    wload(wfm, R_SWKS + h * 64 + 32, 2488 + h * 64, 32)
                wload(wtm, C_DNZ, 768, 256)
                wload(wtm, C_SSZ, 1456, 256)
                wload(wtm, C_BETA, 1024, 16)
                wload(wtm, C_DT, 2224, 8)
                wload(wtm, C_SWV, 2616, 128)

                xtp = k.pool("xt", [128, 8, 512], F32, 2, ph)
                sqp = k.pool("sq", [128, 8, 512], BF16, 1, ph)
                hbp = k.pool("hb", [128, 8, 512], BF16, 2, ph)
                rsp = k.pool("rstd", [128, 512], F32, 2, ph)
                tmpp = k.pool("tmp", [128, 512], F32, 3, ph)
                fstg = k.pool("fstg", [128, 512], BF16, 4, ph)
                tstg = k.pool("tstg", [128, NTM], F32, 2, ph)
                fm_chunks = [(r, 128) for r in range(0, R_MKPE, 128)] + [(R_MKPE, 64)] + \
                            [(r, 128) for r in range(R_SWQ, NFM, 128)]
                def loadA(t0):
                    xt = xtp.next()
                    k.dma("sp", xt[:], X[l][:, t0:t0 + 512].rearrange("(c p) t -> p c t", p=128),
                          reads=[X[l]], writes=[xt])
                    return xt

                def prepA(t0, xt):
                    sq = sqp.next()
                    rstd = rsp.next()
                    rms_stats(None, xt, 512, sq, rstd)
                    hb = hbp.next()
                    mod_norm(xt, 512, rstd, tmpp, hb, mod[l], 0, 1, kind_of_tile(t0))
                    return hb

                for t0, xt, hb in pipelined2(T0S, loadA, prepA):
                    kind = kind_of_tile(t0)
                    for ci, (r0, n) in enumerate(fm_chunks):
                        p = ps.next()
                        for c in range(8):
                            k.op("pe", lambda e: e.matmul(p[0:n, :], lhsT=wfm[:, c, r0:r0 + n], rhs=hb[:, c, :],
                                                          start=(c == 0), stop=(c == 7)),
                                 reads=[wfm, hb], writes=[p], inc=(c == 7))
                        fs = fstg.next()
                        if ci % 2:
                            k.op("act", lambda e: e.copy(out=fs[0:n, :], in_=p[0:n, :]), reads=[p], writes=[fs])
                        else:
                            k.op("dve", lambda e: e.tensor_copy(out=fs[0:n, :], in_=p[0:n, :]), reads=[p], writes=[fs])
                        k.dma("sp", PFM[l][r0:r0 + n, t0:t0 + 512], fs[0:n, :], reads=[fs], writes=[PFM[l]], acc=True)
                    for b in range(4):
                        ts_ = tstg.next()
                        for g, (c0, n) in enumerate(((0, 512), (512, NTM - 512))):
                            p = ps.next()
                            for c in range(8):
                                k.op("pe", lambda e: e.matmul(p[:, 0:n], lhsT=hb[:, c, b * 128:(b + 1) * 128],
                                                              rhs=wtm[:, c, c0:c0 + n], start=(c == 0), stop=(c == 7)),
                                     reads=[wtm, hb], writes=[p], inc=(c == 7))
                            if g == 0:
                                k.op("act", lambda e: e.copy(out=ts_[:, c0:c0 + n], in_=p[:, 0:n]),
                                     reads=[p], writes=[ts_])
                            else:
                                k.op("dve", lambda e: e.tensor_copy(out=ts_[:, c0:c0 + n], in_=p[:, 0:n]),
                                     reads=[p], writes=[ts_], acc=True)
                        k.dma("pool", PTM[l][t0 + b * 128:t0 + (b + 1) * 128, :], ts_[:], reads=[ts_],
                              writes=[PTM[l]], acc=True)
                k.barrier()

            if debug.get("zero_mix"):
                with ExitStack() as ph:
                    z = k.tile("z", [128, 8, 512], BF16, ph)
                    k.op("dve", lambda e: e.memset(z[:], 0.0), writes=[z])
                    for t0 in range(0, TTOT, 512):
                        k.dma("sp", MIX[l][:, t0:t0 + 512].rearrange("(c p) t -> p c t", p=128), z[:],
                              reads=[z], writes=[MIX[l]], acc=True)
                    k.barrier()

            if not debug.get("skip_mla"):
                mla_phase(l)
            if not debug.get("skip_swa"):
                swa_phase(l)
            if not debug.get("skip_ssd"):
                ssd_phase(l)
            if not debug.get("skip_dn"):
                dn_phase(l)

            with ExitStack() as ph:
                wo = k.tile("wo", [128, 8, D], BF16, ph)
                k.dma("pool", wo[:], I["w_out"][l].rearrange("(c p) n -> p c n", p=128), writes=[wo])
                xtp = k.pool("xt", [128, 8, 512], F32, 2, ph)
                mxp = k.pool("mx", [128, 8, 512], BF16, 2, ph)
                def loadC1(t0):
                    xt = xtp.next()
                    mx = mxp.next()
                    k.dma("sp", xt[:], X[l][:, t0:t0 + 512].rearrange("(c p) t -> p c t", p=128),
                          reads=[X[l]], writes=[xt])
                    k.dma("sp", mx[:], MIX[l][:, t0:t0 + 512].rearrange("(c p) t -> p c t", p=128),
                          reads=[MIX[l]], writes=[mx])
                    return xt, mx

                for t0, (xt, mx) in pipelined(T0S, loadC1):
                    kind = kind_of_tile(t0)
                    for co in range(8):
                        p = ps.next()
                        for c in range(8):
                            k.op("pe", lambda e: e.matmul(p[:, :], lhsT=wo[:, c, co * 128:(co + 1) * 128],
                                                          rhs=mx[:, c, :], start=(c == 0), stop=(c == 7)),
                                 reads=[wo, mx], writes=[p], inc=(c == 7))
                        k.op("dve", lambda e: e.scalar_tensor_tensor(
                            out=xt[:, co, :], in0=p[:, :], scalar=mod[l][:, 2, co, kind:kind + 1],
                            in1=xt[:, co, :], op0=ALU.mult, op1=ALU.add),
                            reads=[p, mod[l], xt], writes=[xt])
                    k.dma("pool", XA[:, t0:t0 + 512].rearrange("(c p) t -> p c t", p=128), xt[:],
                          reads=[xt], writes=[XA], acc=True)
                k.barrier()

            HJ = 11
            for half in range(2):
                src2 = XA if half == 0 else XB
                dst2 = XB if half == 0 else X[l + 1]
                last = (half == 1 and l == DEPTH - 1)
                with ExitStack() as ph:
                    wg = k.tile("wg", [128, 8, 2, HJ * 128], BF16, ph)
                    wd = k.tile("wd", [128, HJ, D], BF16, ph)
                    j0 = half * HJ * 128
                    for gu in range(2):
                        k.dma("pool", wg[:, :, gu, :],
                              I["w_gate_up"][l][:, gu * FF + j0:gu * FF + j0 + HJ * 128].rearrange(
                                  "(c p) n -> p c n", p=128), writes=[wg], acc=True)
                    k.dma("pool", wd[:], I["w_down"][l][j0:j0 + HJ * 128, :].rearrange("(j p) n -> p j n", p=128),
                          writes=[wd])
                    xtp = k.pool("xt", [128, 8, 512], F32, 3 if half == 0 else 2, ph)
                    x2p = k.pool("x2", [128, 8, 512], F32, 3, ph) if half == 1 else None
                    sqp = k.pool("sq", [128, 8, 512], BF16, 1, ph)
                    hbp = k.pool("hb", [128, 8, 512], BF16, 2, ph)
                    rsp = k.pool("rstd", [128, 512], F32, 2, ph)
                    tmpp = k.pool("tmp", [128, 512], F32, 3, ph)
                    acp = k.pool("act", [128, HJ, 512], BF16, 1, ph)
                    ostg = k.pool("ostg", [128, D], F32, 2, ph) if last else None
                    def loadF(t0):
                        xt = xtp.next()
                        k.dma("sp", xt[:], XA[:, t0:t0 + 512].rearrange("(c p) t -> p c t", p=128),
                              reads=[XA], writes=[xt])
                        if half == 1:
                            x2 = x2p.next()
                            k.dma("sp", x2[:], XB[:, t0:t0 + 512].rearrange("(c p) t -> p c t", p=128),
                                  reads=[XB], writes=[x2])
                        else:
                            x2 = xt
                        return xt, x2

                    def prepF(t0, ld):
                        sq = sqp.next()
                        rstd = rsp.next()
                        rms_stats(None, ld[0], 512, sq, rstd)
                        hb = hbp.next()
                        mod_norm(ld[0], 512, rstd, tmpp, hb, mod[l], 3, 4, kind_of_tile(t0))
                        return hb

                    for t0, (xt, x2), hb in pipelined2(T0S, loadF, prepF):
                        kind = kind_of_tile(t0)
                        sq = sqp.tiles[0]
                        ac = acp.next()
                        for j in range(HJ):
                            pg = ps.next()
                            pu = ps.next()
                            for gu, pp in ((0, pg), (1, pu)):
                                for c in range(8):
                                    k.op("pe", lambda e: e.matmul(pp[:, :], lhsT=wg[:, c, gu, j * 128:(j + 1) * 128],
                                                                  rhs=hb[:, c, :], start=(c == 0), stop=(c == 7)),
                                         reads=[wg, hb], writes=[pp], inc=(c == 7))
                            tmp = tmpp.next()
                            k.op("act", lambda e: e.activation(out=tmp[:], in_=pg[:], func=AF.Silu),
                                 reads=[pg], writes=[tmp])
                            k.op("dve", lambda e: e.tensor_tensor(out=ac[:, j, :], in0=tmp[:], in1=pu[:], op=ALU.mult),
                                 reads=[tmp, pu], writes=[ac], acc=(j > 0))
                        for co in range(8):
                            p = ps.next()
                            for j in range(HJ):
                                k.op("pe", lambda e: e.matmul(p[:, :], lhsT=wd[:, j, co * 128:(co + 1) * 128],
                                                              rhs=ac[:, j, :], start=(j == 0), stop=(j == HJ - 1)),
                                     reads=[wd, ac], writes=[p], inc=(j == HJ - 1))
                            k.op("dve", lambda e: e.scalar_tensor_tensor(
                                out=x2[:, co, :], in0=p[:, :], scalar=mod[l][:, 5, co, kind:kind + 1],
                                in1=x2[:, co, :], op0=ALU.mult, op1=ALU.add),
                                reads=[p, mod[l], x2], writes=[x2])
                        if not last:
                            k.dma("pool", dst2[:, t0:t0 + 512].rearrange("(c p) t -> p c t", p=128), x2[:],
                                  reads=[x2], writes=[dst2], acc=True)
                        else:
                            rstd = rsp.next()
                            rms_stats(None, x2, 512, sq, rstd)
                            for c in range(8):
                                k.op("dve", lambda e: e.scalar_tensor_tensor(
                                    out=x2[:, c, :], in0=x2[:, c, :], scalar=fnw[:, c:c + 1], in1=rstd[:, :],
                                    op0=ALU.mult, op1=ALU.mult), reads=[x2, fnw, rstd], writes=[x2])
                            for b in range(4):
                                os_ = ostg.next()
                                for c in range(8):
                                    p = ps.next()
                                    k.op("pe", lambda e: e.transpose(p[:, 0:128], x2[:, c, b * 128:(b + 1) * 128], ident[:]),
                                         reads=[x2, ident], writes=[p])
                                    if c % 2:
                                        k.op("act", lambda e: e.copy(out=os_[:, c * 128:(c + 1) * 128], in_=p[:, 0:128]),
                                             reads=[p], writes=[os_], acc=(c > 0))
                                    else:
                                        k.op("dve", lambda e: e.tensor_copy(out=os_[:, c * 128:(c + 1) * 128], in_=p[:, 0:128]),
                                             reads=[p], writes=[os_], acc=(c > 0))
                                tok = t0 + b * 128
                                dst = (O["y_ctx"][tok:tok + 128, :] if tok < LOFF
                                       else O["y_lat"][tok - LOFF:tok - LOFF + 128, :])
                                k.dma("pool", dst, os_[:], reads=[os_])
                    k.barrier()
        k.barrier()
    return nc


_CACHE = {}


def _rope_tables():
    out = {}
    pos = np.arange(TL)
    row_ids = (pos // 64).astype(np.float32)
    col_ids = (pos % 64).astype(np.float32)
    for name, rot in (("rope_m", 32), ("rope_s", 64)):
        nf = rot // 4
        inv = (10000.0 ** (-np.arange(nf, dtype=np.float32) / nf)).astype(np.float32)
        ang = np.concatenate([row_ids[:, None] * inv, col_ids[:, None] * inv], axis=-1).astype(np.float32)
        c = np.cos(ang).astype(np.float32).T
        sn = np.sin(ang).astype(np.float32).T
        out[name] = np.ascontiguousarray(np.stack([np.concatenate([c, c], 0), np.concatenate([-sn, sn], 0)]))
    return out


def kernel(**inputs):
    x_prompt = np.ascontiguousarray(inputs["x_prompt"], dtype=np.float32)
    x_sample = np.ascontiguousarray(inputs["x_sample"], dtype=np.float32)
    dbg = inputs.pop("_debug", None) if "_debug" in inputs else None
    if "nc" not in _CACHE or dbg:
        _CACHE["nc"] = build_program(dbg)
    nc = _CACHE["nc"]
    ident = np.eye(128, dtype=np.float32)
    shared = {}
    for name in ["w_ada", "b_ada", "norm1_w", "norm2_w", "final_norm_w", "w_in", "w_out",
                 "w_gate_up", "w_down"]:
        shared[name] = np.ascontiguousarray(inputs[name], dtype=np.float32)
    for name in ["mla_q_norm_w", "mla_w_uq", "mla_kv_norm_w", "mla_w_ukv", "swa_sinks",
                 "dn_conv_w", "dn_a_log", "dn_dt_bias", "dn_norm_w", "ssm_conv_w", "ssm_conv_b", "ssm_a_log", "ssm_dt_bias", "ssm_d", "ssm_norm_w"]:
        shared[name] = np.ascontiguousarray(inputs[name], dtype=np.float32)
    shared.update(_rope_tables())
    kl = np.arange(128)[:, None]
    ql = np.arange(128)[None, :]
    msk = np.zeros((6, 128, 512), np.float32)
    for r in range(6):
        for j in range(4):
            dd = r - 1 - j
            if dd == -1:
                msk[r, :, j * 128:(j + 1) * 128] = (kl >= ql)
            elif dd == 0:
                msk[r, :, j * 128:(j + 1) * 128] = 1.0
            elif dd == 1:
                msk[r, :, j * 128:(j + 1) * 128] = (kl <= ql)
    shared["swa_mask"] = msk
    ii = np.arange(128)
    same = (ii[:, None] // 64) == (ii[None, :] // 64)
    cmask = np.zeros((14, 128, 128), np.float32)
    cmask[5] = same & (ii[:, None] < ii[None, :])
    cmask[6] = same & (ii[:, None] > ii[None, :])
    cmask[7] = np.eye(128, dtype=np.float32)
    for lev, sz_ in enumerate((1, 2, 4, 8, 16, 32)):
        cmask[8 + lev] = ((ii[:, None] // (2 * sz_)) == (ii[None, :] // (2 * sz_))) & ((ii[:, None] // sz_) != (ii[None, :] // sz_))
    cmask[0] = same & (ii[:, None] <= ii[None, :])
    cmask[1] = same & (ii[:, None] >= ii[None, :])
    cmask[2] = same
    cmask[3, 0:64, :] = 1.0
    cmask[4, 64:128, :] = 1.0
    shared["cmask"] = cmask
    in_maps = []
    for core in range(8):
        b = core % 4
        m = dict(shared)
        m["x_ctx"] = x_prompt[core * NCTX:(core + 1) * NCTX].reshape(NCTX * TC, D)
        m["x_lat"] = x_sample[b]
        m["cvec"] = np.stack([inputs["c_ctx"], inputs["c"][b]]).astype(np.float32)
        m["ident"] = ident
        m["cache_ckv"] = np.ascontiguousarray(inputs["cache_mla_ckv"][b], dtype=np.float32)
        m["cache_kpe"] = np.ascontiguousarray(inputs["cache_mla_kpe"][b], dtype=np.float32)
        m["state_ssm"] = np.ascontiguousarray(inputs["state_ssm"][b], dtype=np.float32)
        m["state_dn"] = np.ascontiguousarray(inputs["state_dn"][b], dtype=np.float32)
        m["cache_swk"] = np.ascontiguousarray(inputs["cache_swa_k"][b], dtype=np.float32)
        m["cache_swv"] = np.ascontiguousarray(inputs["cache_swa_v"][b], dtype=np.float32)
        in_maps.append(m)
    res = run_bass_kernel_spmd(nc, in_maps, core_ids=list(range(8)))
    r = res.results
    y_prompt = np.concatenate([r[c]["y_ctx"].reshape(NCTX, TC, D) for c in range(8)], axis=0)
    y_sample = np.stack([r[b]["y_lat"] for b in range(4)], axis=0)
    def gath(name):
        return np.concatenate([np.asarray(r[c][name], dtype=np.float32) for c in range(8)], axis=0)

    new_ckv = gath("new_ckv") if "new_ckv" in r[0] else np.zeros((32, DEPTH, TC, 128), np.float32)
    new_kpe = gath("new_kpe") if "new_kpe" in r[0] else np.zeros((32, DEPTH, TC, 32), np.float32)
    new_swk = gath("new_swk") if "new_swk" in r[0] else np.zeros((32, DEPTH, TC, 2, 64), np.float32)
    new_swv = gath("new_swv") if "new_swv" in r[0] else np.zeros((32, DEPTH, TC, 2, 64), np.float32)
    new_sdn = gath("new_sdn") if "new_sdn" in r[0] else np.zeros((32, DEPTH, 2, 4, 64, 64), np.float32)
    new_ssm = gath("new_ssm") if "new_ssm" in r[0] else np.zeros((32, DEPTH, 2, 4, 64, 64), np.float32)
    outs = (y_prompt, y_sample, new_sdn, new_ckv, new_kpe, new_ssm, new_swk, new_swv)
    if dbg:
        return outs + (r,)
    return outs
```

```python
import numpy as np
import concourse.bass as bass
import concourse.mybir as mybir
from concourse.bass_utils import run_bass_kernel_spmd
from contextlib import ExitStack

F32 = mybir.dt.float32
BF16 = mybir.dt.bfloat16
AF = mybir.ActivationFunctionType
ALU = mybir.AluOpType
AX = mybir.AxisListType

D = 1024
DEPTH = 2
NCTX = 4
TC = 256
TL = 4096
TTOT = NCTX * TC + TL
LOFF = NCTX * TC
FF = 2816
EPS = 1e-6
NFM = 2496
NTM = 664
R_DNQ, R_DNK, R_DNV = 0, 256, 512
R_SSX, R_SSB, R_SSC = 768, 1024, 1152
R_MQ, R_MKV, R_MKPE = 1280, 1536, 1664
R_SWQ, R_SWQS, R_SWK, R_SWKS = 1728, 1984, 2240, 2368
C_DNZ, C_SSZ, C_BETA, C_ALPHA, C_DT, C_SWV = 0, 256, 512, 520, 528, 536


class Res:
    __slots__ = ("name", "w", "r", "t", "base", "full")

    def __init__(self, name, t=None):
        self.name = name
        self.w = {}
        self.r = {}
        self.base = {}
        self.full = None
        self.t = t

    def __getitem__(self, key):
        return self.t[key]


class Pool:
    def __init__(self, tiles):
        self.tiles = tiles
        self.i = 0

    def next(self):
        t = self.tiles[self.i]
        self.i = (self.i + 1) % len(self.tiles)
        return t


class K:
    def __init__(self, nc, es, ndma=12):
        self.nc = nc
        self.es = es
        self.engs = {"pe": nc.tensor, "act": nc.scalar, "dve": nc.vector,
                     "pool": nc.gpsimd, "sp": nc.sync}
        self.semh = {}
        self.cnt = {}
        self.waited = {e: {} for e in self.engs}
        for e in ["pe", "act", "dve", "pool"]:
            self.semh[e] = es.enter_context(nc.semaphore("s_" + e))
            self.cnt[e] = 0
        self.dq = {}
        for q in ["sp", "pool"]:
            sems = []
            for i in range(ndma):
                key = ("d", q, i)
                self.semh[key] = es.enter_context(nc.semaphore("d_%s_%d" % (q, i)))
                self.cnt[key] = 0
                sems.append(key)
            self.dq[q] = {"sems": sems, "rr": 0}
        self.uid = 0

    def tile(self, name, shape, dtype, es=None):
        self.uid += 1
        t = (es or self.es).enter_context(
            self.nc.sbuf_tensor("%s_%d" % (name, self.uid), list(shape), dtype))
        return Res(name, t)

    def ptile(self, name, shape, dtype=F32, es=None):
        self.uid += 1
        t = (es or self.es).enter_context(
            self.nc.psum_tensor("%s_%d" % (name, self.uid), list(shape), dtype))
        return Res(name, t)

    def dram(self, name, shape, dtype, kind="Internal"):
        if name in getattr(self, "ext", ()):
            kind = "ExternalOutput"
        t = self.nc.dram_tensor(name, list(shape), dtype, kind=kind)
        return Res(name, t.ap())

    def pool(self, name, shape, dtype, n, es=None, psum=False):
        return Pool([(self.ptile if psum else self.tile)("%s%d" % (name, i), shape, dtype, es)
                     for i in range(n)])

    def _wait(self, eng, need):
        for s, v in need.items():
            if self.waited[eng].get(s, 0) < v:
                self.engs[eng].wait_ge(self.semh[s], v)
                self.waited[eng][s] = v

    def _deps(self, eng, reads, writes, acc=False):
        need = {}

        def add(s, v, war=False):
            if s == eng and eng == "pe":
                return
            if need.get(s, 0) < v:
                need[s] = v

        for t in reads:
            for s, v in t.w.items():
                add(s, v)
        for t in writes:
            if acc:
                if t.full is not None:
                    add(t.full[0], t.full[1])
                for s, v in t.base.items():
                    add(s, v, True)
            else:
                b = {}
                for s, v in t.w.items():
                    add(s, v)
                    b[s] = max(b.get(s, 0), v)
                for s, v in t.r.items():
                    add(s, v, True)
                    b[s] = max(b.get(s, 0), v)
                t.base = b
        self._wait(eng, need)

    def _mark(self, key, val, reads, writes, acc=False):
        for t in reads:
            if t.r.get(key, 0) < val:
                t.r[key] = val
        for t in writes:
            if acc:
                if t.w.get(key, 0) < val:
                    t.w[key] = val
            else:
                t.w = {key: val}
                t.r = {}
                t.full = (key, val)

    def op(self, eng, fn, reads=(), writes=(), inc=True, acc=False):
        self._deps(eng, reads, writes, acc)
        ins = fn(self.engs[eng])
        if inc:
            self.cnt[eng] += 1
            ins.then_inc(self.semh[eng], 1)
            val = self.cnt[eng]
        else:
            val = self.cnt[eng] + 1
        self._mark(eng, val, reads, writes, acc)
        return ins

    def dma(self, q, out, in_, reads=(), writes=(), acc=False, **kw):
        d = self.dq[q]
        key = d["sems"][d["rr"]]
        d["rr"] = (d["rr"] + 1) % len(d["sems"])
        cur = self.cnt[key]
        if cur > 0:
            self._wait(q, {key: cur})
        self._deps(q, reads, writes, acc)
        ins = self.engs[q].dma_start(out=out, in_=in_, **kw)
        ins.then_inc(self.semh[key], 16)
        self.cnt[key] = cur + 16
        self._mark(key, cur + 16, reads, writes, acc)

    def barrier(self):
        need = {s: v for s, v in self.cnt.items() if v > 0}
        for e in self.engs:
            self._wait(e, dict(need))


def build_program(debug=None):
    debug = debug or {}
    nc = bass.Bass("TRN2", target_bir_lowering=False)

    def din(name, shape):
        return nc.dram_tensor(name, list(shape), F32, kind="ExternalInput").ap()

    def dout(name, shape):
        return nc.dram_tensor(name, list(shape), F32, kind="ExternalOutput").ap()

    I = {}
    I["x_ctx"] = din("x_ctx", [NCTX * TC, D])
    I["x_lat"] = din("x_lat", [TL, D])
    I["cvec"] = din("cvec", [2, D])
    I["w_ada"] = din("w_ada", [DEPTH, D, 6 * D])
    I["b_ada"] = din("b_ada", [DEPTH, 6 * D])
    I["norm1_w"] = din("norm1_w", [DEPTH, D])
    I["norm2_w"] = din("norm2_w", [DEPTH, D])
    I["final_norm_w"] = din("final_norm_w", [D])
    I["w_in"] = din("w_in", [DEPTH, D, 2744])
    I["w_out"] = din("w_out", [DEPTH, D, D])
    I["w_gate_up"] = din("w_gate_up", [DEPTH, D, 2 * FF])
    I["w_down"] = din("w_down", [DEPTH, FF, D])
    I["ident"] = din("ident", [128, 128])
    I["mla_q_norm_w"] = din("mla_q_norm_w", [DEPTH, 256])
    I["mla_w_uq"] = din("mla_w_uq", [DEPTH, 256, 384])
    I["mla_kv_norm_w"] = din("mla_kv_norm_w", [DEPTH, 128])
    I["mla_w_ukv"] = din("mla_w_ukv", [DEPTH, 128, 512])
    I["cache_ckv"] = din("cache_ckv", [DEPTH, 256, 128])
    I["cache_kpe"] = din("cache_kpe", [DEPTH, 256, 32])
    I["rope_m"] = din("rope_m", [2, 32, TL])
    I["rope_s"] = din("rope_s", [2, 64, TL])
    I["swa_mask"] = din("swa_mask", [6, 128, 512])
    I["cmask"] = din("cmask", [14, 128, 128])
    I["dn_conv_w"] = din("dn_conv_w", [DEPTH, 3, 768])
    I["dn_a_log"] = din("dn_a_log", [DEPTH, 2, 4])
    I["dn_dt_bias"] = din("dn_dt_bias", [DEPTH, 2, 4])
    I["dn_norm_w"] = din("dn_norm_w", [DEPTH, 64])
    I["state_dn"] = din("state_dn", [DEPTH, 2, 4, 64, 64])
    I["ssm_conv_w"] = din("ssm_conv_w", [DEPTH, 3, 512])
    I["ssm_conv_b"] = din("ssm_conv_b", [DEPTH, 512])
    I["ssm_a_log"] = din("ssm_a_log", [DEPTH, 2, 4])
    I["ssm_dt_bias"] = din("ssm_dt_bias", [DEPTH, 2, 4])
    I["ssm_d"] = din("ssm_d", [DEPTH, 4])
    I["ssm_norm_w"] = din("ssm_norm_w", [DEPTH, 256])
    I["state_ssm"] = din("state_ssm", [DEPTH, 2, 4, 64, 64])
    I["swa_sinks"] = din("swa_sinks", [DEPTH, 4])
    I["cache_swk"] = din("cache_swk", [DEPTH, 256, 2, 64])
    I["cache_swv"] = din("cache_swv", [DEPTH, 256, 2, 64])
    O = {}
    O["y_ctx"] = dout("y_ctx", [NCTX * TC, D])
    O["y_lat"] = dout("y_lat", [TL, D])
    O["new_ckv"] = dout("new_ckv", [NCTX, DEPTH, TC, 128])
    O["new_kpe"] = dout("new_kpe", [NCTX, DEPTH, TC, 32])
    O["new_ssm"] = dout("new_ssm", [NCTX, DEPTH, 2, 4, 64, 64])
    O["new_sdn"] = dout("new_sdn", [NCTX, DEPTH, 2, 4, 64, 64])
    O["new_swk"] = dout("new_swk", [NCTX, DEPTH, TC, 2, 64])
    O["new_swv"] = dout("new_swv", [NCTX, DEPTH, TC, 2, 64])

    with ExitStack() as es:
        k = K(nc, es)
        k.ext = set(debug.get("ext", ()))

        def dump(name, res, ap, shape, dtype=F32):
            if name in debug.get("dump", ()):
                d = nc.dram_tensor("dbg_" + name, list(shape), dtype, kind="ExternalOutput").ap()
                k.dma("sp", d, ap, reads=[res])
        X = [k.dram("xs%d" % i, [D, TTOT], F32) for i in range(DEPTH + 1)]
        XA = k.dram("xa", [D, TTOT], F32)
        XB = k.dram("xb", [D, TTOT], F32)
        PFM = [k.dram("pfm%d" % l, [NFM, TTOT], BF16) for l in range(DEPTH)]
        PTM = [k.dram("ptm%d" % l, [TTOT, NTM], F32) for l in range(DEPTH)]
        MIX = [k.dram("mix%d" % l, [D, TTOT], BF16) for l in range(DEPTH)]

        ident = k.tile("ident", [128, 128], F32)
        k.dma("sp", ident[:], I["ident"][:, :], writes=[ident])
        ones_bf = k.tile("ones_bf", [128, 128], BF16)
        k.op("dve", lambda e: e.memset(ones_bf[:], 1.0), writes=[ones_bf])
        ones_f = k.tile("ones_f", [128, 64], F32)
        k.op("dve", lambda e: e.memset(ones_f[:], 1.0), writes=[ones_f])
        epsb = k.tile("epsb", [128, 1], F32)
        k.op("dve", lambda e: e.memset(epsb[:], EPS), writes=[epsb])
        ps = k.pool("ps", [128, 512], F32, 6, psum=True)
        pacc = k.pool("pacc", [128, 512], F32, 2, psum=True)
        psd = [Pool(ps.tiles[0:4]), Pool(ps.tiles[4:6] + pacc.tiles[0:2])]
        ps8 = Pool(ps.tiles + pacc.tiles)
        mod = [k.tile("mod%d" % l, [128, 6, 8, 2], F32) for l in range(DEPTH)]
        fnw = k.tile("fnw", [128, 8], F32)
        k.dma("sp", fnw[:], I["final_norm_w"].rearrange("(c p) -> p c", p=128), writes=[fnw],
              allow_slow_non_contiguous=True)

        def pipelined(t0s, load):
            nxt = load(t0s[0])
            for i, t0 in enumerate(t0s):
                cur = nxt
                if i + 1 < len(t0s):
                    nxt = load(t0s[i + 1])
                yield t0, cur

        def pipelined2(t0s, load, prep):
            cur = None
            for t0, ld in pipelined(t0s, load):
                pr = prep(t0, ld)
                if cur is not None:
                    yield cur
                cur = (t0, ld, pr)
            if cur is not None:
                yield cur

        T0S = list(range(0, TTOT, 512))

        def kind_of_tile(tok0):
            return 0 if tok0 < LOFF else 1

        with ExitStack() as ph:
            xin = k.pool("xin", [128, D], F32, 2, ph)
            stg = k.pool("stg", [128, 8, 512], F32, 2, ph)
            for t0 in range(0, TTOT, 512):
                st = stg.next()
                for b in range(4):
                    tok = t0 + b * 128
                    xi = xin.next()
                    src = (I["x_ctx"][tok:tok + 128, :] if tok < LOFF
                           else I["x_lat"][tok - LOFF:tok - LOFF + 128, :])
                    k.dma("sp", xi[:], src, writes=[xi])
                    for c in range(8):
                        p = ps.next()
                        k.op("pe", lambda e: e.transpose(p[:, 0:128], xi[:, c * 128:(c + 1) * 128], ident[:]),
                             reads=[xi, ident], writes=[p])
                        eng = "act" if c % 2 else "dve"
                        if eng == "act":
                            k.op("act", lambda e: e.copy(out=st[:, c, b * 128:(b + 1) * 128], in_=p[:, 0:128]),
                                 reads=[p], writes=[st], acc=not (b == 0 and c == 0))
                        else:
                            k.op("dve", lambda e: e.tensor_copy(out=st[:, c, b * 128:(b + 1) * 128], in_=p[:, 0:128]),
                                 reads=[p], writes=[st], acc=not (b == 0 and c == 0))
                k.dma("pool", X[0][:, t0:t0 + 512].rearrange("(c p) t -> p c t", p=128), st[:],
                      reads=[st], writes=[X[0]], acc=True)
            k.barrier()

        def rms_stats(ph_tiles, xt, ntok, sq, rstd):
            k.op("act", lambda e: e.activation(out=sq[:, :, 0:ntok], in_=xt[:, :, 0:ntok], func=AF.Square),
                 reads=[xt], writes=[sq])
            p = ps.next()
            for c in range(8):
                k.op("pe", lambda e: e.matmul(p[:, 0:ntok], lhsT=ones_bf[:], rhs=sq[:, c, 0:ntok],
                                              start=(c == 0), stop=(c == 7)),
                     reads=[ones_bf, sq], writes=[p], inc=(c == 7))
            k.op("act", lambda e: e.activation(out=rstd[:, 0:ntok], in_=p[:, 0:ntok], func=AF.Ln,
                                               scale=1.0 / D, bias=epsb[:, 0:1]),
                 reads=[p, epsb], writes=[rstd])
            k.op("act", lambda e: e.activation(out=rstd[:, 0:ntok], in_=rstd[:, 0:ntok], func=AF.Exp, scale=-0.5),
                 reads=[rstd], writes=[rstd])

        def mod_norm(xt, ntok, rstd, tmpp, hb, modt, ia, ib, kind):
            for c in range(8):
                tmp = tmpp.next()
                k.op("dve", lambda e: e.tensor_tensor(out=tmp[:, 0:ntok], in0=xt[:, c, 0:ntok],
                                                      in1=rstd[:, 0:ntok], op=ALU.mult),
                     reads=[xt, rstd], writes=[tmp])
                k.op("act", lambda e: e.activation(out=hb[:, c, 0:ntok], in_=tmp[:, 0:ntok], func=AF.Identity,
                                                   scale=modt[:, ia, c, kind:kind + 1],
                                                   bias=modt[:, ib, c, kind:kind + 1]),
                     reads=[tmp, modt], writes=[hb], acc=(c > 0))


        SEQS = [(i * TC, TC, False, i) for i in range(NCTX)] + [(LOFF, TL, True, 0)]
        MLA_SCALE = 96 ** -0.5
        SWA_SCALE = 64 ** -0.5

        def evac(i, out, in_, reads, writes, acc=False):
            if i % 2:
                k.op("act", lambda e: e.copy(out=out, in_=in_), reads=reads, writes=writes, acc=acc)
            else:
                k.op("dve", lambda e: e.tensor_copy(out=out, in_=in_), reads=reads, writes=writes, acc=acc)

        def rstd_from_ps(p, n, rstd, dim):
            k.op("act", lambda e: e.activation(out=rstd[:, 0:n], in_=p[:, 0:n], func=AF.Ln,
                                               scale=1.0 / dim, bias=epsb[:, 0:1]),
                 reads=[p, epsb], writes=[rstd])
            k.op("act", lambda e: e.activation(out=rstd[:, 0:n], in_=rstd[:, 0:n], func=AF.Exp, scale=-0.5),
                 reads=[rstd], writes=[rstd])

        def attn_core(kT, vt, h_v, qT, NQ, NKB, scale, ptp, masks=None, sink=None, kb_list=None):
            po = pacc.next()
            blocks = kb_list if kb_list is not None else [(kb, None) for kb in range(NKB)]
            n = len(blocks)
            LOOK = 3
            pSs = [None] * n

            def issue_s(i):
                kb = blocks[i][0]
                pS = ps.next()
                k.op("pe", lambda e: e.matmul(pS[:, 0:NQ], lhsT=kT[:, kb * 128:(kb + 1) * 128], rhs=qT[:, 0:NQ],
                                              start=True, stop=True), reads=[kT, qT], writes=[pS])
                pSs[i] = pS

            first = True
            if sink is not None:
                e64, srow = sink
                k.op("pe", lambda e: e.matmul(po[0:65, 0:NQ], lhsT=e64[0:1, 0:65], rhs=srow[0:1, 0:NQ],
                                              start=True, stop=False), reads=[e64, srow], writes=[po])
                first = False
            for i in range(min(LOOK, n)):
                issue_s(i)
            for bi, (kb, mk) in enumerate(blocks):
                if bi + LOOK < n:
                    issue_s(bi + LOOK)
                pS = pSs[bi]
                pt = ptp.next()
                k.op("act", lambda e: e.activation(out=pt[:, 0:NQ], in_=pS[:, 0:NQ], func=AF.Exp, scale=scale),
                     reads=[pS], writes=[pt])
                if mk is not None:
                    k.op("dve", lambda e: e.tensor_tensor(out=pt[:, 0:NQ], in0=pt[:, 0:NQ], in1=mk[:, 0:NQ], op=ALU.mult),
                         reads=[pt, mk], writes=[pt])
                last = (bi == n - 1)
                k.op("pe", lambda e: e.matmul(po[0:65, 0:NQ], lhsT=vt[:, kb, h_v, :], rhs=pt[:, 0:NQ],
                                              start=first, stop=last), reads=[vt, pt], writes=[po])
                first = False
            return po

        def attn_finish(po, NQ, rowbuf, bcs, ostg_p, dst_ap, dst_res):
            k.op("act", lambda e: e.activation(out=rowbuf[64:65, 0:NQ], in_=po[64:65, 0:NQ], func=AF.Ln),
                 reads=[po], writes=[rowbuf])
            k.op("act", lambda e: e.activation(out=rowbuf[64:65, 0:NQ], in_=rowbuf[64:65, 0:NQ], func=AF.Exp, scale=-1.0),
                 reads=[rowbuf], writes=[rowbuf])
            pb = ps.next()
            k.op("pe", lambda e: e.matmul(pb[0:64, 0:NQ], lhsT=ones_f[64:65, 0:64], rhs=rowbuf[64:65, 0:NQ],
                                          start=True, stop=True), reads=[ones_f, rowbuf], writes=[pb])
            k.op("act", lambda e: e.copy(out=bcs[0:64, 0:NQ], in_=pb[0:64, 0:NQ]), reads=[pb], writes=[bcs])
            og = ostg_p.next()
            k.op("dve", lambda e: e.tensor_tensor(out=og[0:64, 0:NQ], in0=po[0:64, 0:NQ], in1=bcs[0:64, 0:NQ],
                                                  op=ALU.mult), reads=[po, bcs], writes=[og])
            k.dma("pool", dst_ap, og[0:64, 0:NQ], reads=[og], writes=[dst_res], acc=True)

        def mla_phase(l):
            with ExitStack() as ph:
                NKMAX = TL + 256
                wuq = k.tile("wuq", [128, 2, 4, 128], BF16, ph)
                wuqs = k.tile("wuqs", [128, 2, 4, 64], BF16, ph)
                wkk = k.tile("wkk", [128, 4, 128], BF16, ph)
                wkv = k.tile("wkv", [128, 256], BF16, ph)
                k.op("dve", lambda e: e.memset(wuq[:], 0.0), writes=[wuq])
                k.op("dve", lambda e: e.memset(wuqs[:], 0.0), writes=[wuqs])
                k.op("dve", lambda e: e.memset(wkk[:], 0.0), writes=[wkk])
                uq = I["mla_w_uq"][l]
                ukv = I["mla_w_ukv"][l]
                for h in range(4):
                    def ld(dst, src):
                        k.dma("pool", dst, src.rearrange("(c p) n -> p c n", p=128), writes=[wuq, wuqs], acc=True)
                    ld(wuq[:, :, h, 64:128], uq[:, 96 * h:96 * h + 64])
                    ld(wuq[:, :, h, 32:64], uq[:, 96 * h + 64:96 * h + 96])
                    ld(wuqs[:, :, h, 32:48], uq[:, 96 * h + 80:96 * h + 96])
                    ld(wuqs[:, :, h, 48:64], uq[:, 96 * h + 64:96 * h + 80])
                    k.dma("pool", wkk[:, h, 64:128], ukv[:, 128 * h:128 * h + 64], writes=[wkk], acc=True)
                    k.dma("pool", wkv[:, 64 * h:64 * h + 64], ukv[:, 128 * h + 64:128 * h + 128], writes=[wkv], acc=True)
                qnw = k.tile("qnw", [128, 2], F32, ph)
                k.dma("sp", qnw[:], I["mla_q_norm_w"][l].rearrange("(c p) -> p c", p=128), writes=[qnw],
                      allow_slow_non_contiguous=True)
                kvnw = k.tile("kvnw", [128, 1], F32, ph)
                k.dma("sp", kvnw[:], I["mla_kv_norm_w"][l].rearrange("(c p) -> p c", p=128), writes=[kvnw],
                      allow_slow_non_contiguous=True)
                CM = k.tile("CM", [64, TL], F32, ph)
                SM = k.tile("SM", [64, TL], F32, ph)
                k.dma("sp", CM[32:64, :], I["rope_m"][0], writes=[CM])
                k.dma("sp", SM[32:64, :], I["rope_m"][1], writes=[SM])
                ckvT = k.tile("ckvT", [128, NKMAX], BF16, ph)
                kpeT = k.tile("kpeT", [64, NKMAX], BF16, ph)
                kTm = [k.tile("kTm%d" % h, [128, NKMAX], BF16, ph) for h in range(4)]
                for h in range(4):
                    k.op("pool", lambda e: e.memset(kTm[h][0:32, :], 0.0), writes=[kTm[h]])
                    k.op("pool", lambda e: e.memset(kTm[h][0:1, :], 1.0), writes=[kTm[h]])
                vm = k.tile("vm", [128, NKMAX // 128, 4, 65], BF16, ph)
                k.op("pool", lambda e: e.memset(vm[:], 1.0), writes=[vm])
                kmx = k.tile("kmx", [1, 4, 16], F32, ph)
                nkmax = k.tile("nkmax", [1, 4], F32, ph)
                kvp = k.pool("kvp", [128, 512], BF16, 2, ph)
                sqp = k.pool("sqm", [128, 2, 512], BF16, 2, ph)
                for t_ in sqp.tiles:
                    k.op("dve", lambda e: e.memset(t_[:], 0.0), writes=[t_])
                sqb = k.tile("sqb", [128, 512], BF16, ph)
                k.op("dve", lambda e: e.memset(sqb[:], 0.0), writes=[sqb])
                rsp = k.pool("rsm", [128, 512], F32, 2, ph)
                f32p = k.pool("f32m", [128, 512], F32, 3, ph)
                kxp = k.pool("kxp", [64, 2, 512], BF16, 2, ph)
                qlp = k.pool("qlp", [128, 2, 512], BF16, 2, ph)
                qnp = k.pool("qnp", [128, 2, 512], BF16, 2, ph)
                qTp = k.pool("qTp", [128, 512], BF16, 6, ph)
                rowp = k.pool("rowpm", [1, 512], F32, 3, ph)
                for t_ in qTp.tiles:
                    k.op("dve", lambda e: e.memset(t_[:], 0.0), writes=[t_])
                ptp = k.pool("ptp", [128, 512], BF16, 6, ph)
                rowbuf = k.tile("rowbuf", [128, 512], F32, ph)
                bcs = k.tile("bcs", [64, 512], F32, ph)
                ogp = k.pool("ogp", [64, 512], BF16, 3, ph)
                tkp = k.pool("tkp", [128, 2, 128], F32, 2, ph)
                otp = k.pool("otp", [128, 128], F32, 2, ph)

                for (off, T, lat, si) in SEQS:
                    k.barrier()
                    TT = min(512, T)
                    koff = 256 if lat else 0
                    NK = T + koff
                    NKB = NK // 128
                    if lat:
                        ck = tkp.next()
                        k.dma("sp", ck[:], I["cache_ckv"][l].rearrange("(b p) f -> p b f", p=128), writes=[ck])
                        for b in range(2):
                            p = ps.next()
                            k.op("pe", lambda e: e.transpose(p[:, 0:128], ck[:, b, :], ident[:]),
                                 reads=[ck, ident], writes=[p])
                            evac(b, ckvT[:, b * 128:(b + 1) * 128], p[:, 0:128], [p], [ckvT], acc=True)
                        kp = tkp.next()
                        k.op("dve", lambda e: e.memset(kp[:], 0.0), writes=[kp])
                        k.dma("sp", kp[:, :, 32:64], I["cache_kpe"][l].rearrange("(b p) f -> p b f", p=128),
                              writes=[kp], acc=True)
                        for b in range(2):
                            p = ps.next()
                            k.op("pe", lambda e: e.transpose(p[:, 0:128], kp[:, b, :], ident[:]),
                                 reads=[kp, ident], writes=[p])
                            evac(b, kpeT[32:64, b * 128:(b + 1) * 128], p[32:64, 0:128], [p], [kpeT], acc=True)
                    for ti, t0 in enumerate(range(0, T, TT)):
                        g0 = off + t0
                        kv = kvp.next()
                        k.dma("sp", kv[:, 0:TT], PFM[l][R_MKV:R_MKV + 128, g0:g0 + TT], reads=[PFM[l]], writes=[kv])
                        sq = sqp.next()
                        k.op("act", lambda e: e.activation(out=sq[:, 0, 0:TT], in_=kv[:, 0:TT], func=AF.Square),
                             reads=[kv], writes=[sq])
                        p = ps.next()
                        k.op("pe", lambda e: e.matmul(p[:, 0:TT], lhsT=ones_bf[:], rhs=sq[:, 0, 0:TT], start=True, stop=True),
                             reads=[ones_bf, sq], writes=[p])
                        rstd = rsp.next()
                        rstd_from_ps(p, TT, rstd, 128)
                        cf = f32p.next()
                        k.op("dve", lambda e: e.scalar_tensor_tensor(out=cf[:, 0:TT], in0=kv[:, 0:TT], scalar=kvnw[:, 0:1],
                                                                     in1=rstd[:, 0:TT], op0=ALU.mult, op1=ALU.mult),
                             reads=[kv, kvnw, rstd], writes=[cf])
                        k.op("act", lambda e: e.copy(out=ckvT[:, koff + t0:koff + t0 + TT], in_=cf[:, 0:TT]),
                             reads=[cf], writes=[ckvT], acc=True)
                        kx = kxp.next()
                        k.dma("sp", kx[32:64, 0, 0:TT], PFM[l][R_MKPE:R_MKPE + 32, g0:g0 + TT], reads=[PFM[l]], writes=[kx])
                        k.dma("sp", kx[32:64, 1, 0:TT], PFM[l][R_MKPE + 32:R_MKPE + 64, g0:g0 + TT], reads=[PFM[l]],
                              writes=[kx], acc=True)
                        if lat:
                            t1 = f32p.next()
                            t2 = f32p.next()
                            k.op("dve", lambda e: e.tensor_tensor(out=t1[32:64, 0:TT], in0=kx[32:64, 0, 0:TT],
                                                                  in1=CM[32:64, t0:t0 + TT], op=ALU.mult),
                                 reads=[kx, CM], writes=[t1])
                            k.op("pool", lambda e: e.tensor_tensor(out=t2[32:64, 0:TT], in0=kx[32:64, 1, 0:TT],
                                                                   in1=SM[32:64, t0:t0 + TT], op=ALU.mult),
                                 reads=[kx, SM], writes=[t2])
                            k.op("dve", lambda e: e.tensor_tensor(out=kpeT[32:64, koff + t0:koff + t0 + TT],
                                                                  in0=t1[32:64, 0:TT], in1=t2[32:64, 0:TT], op=ALU.add),
                                 reads=[t1, t2], writes=[kpeT], acc=True)
                        else:
                            k.op("dve", lambda e: e.tensor_copy(out=kpeT[32:64, t0:t0 + TT], in_=kx[32:64, 0, 0:TT]),
                                 reads=[kx], writes=[kpeT], acc=True)
                            kf = f32p.next()
                            k.op("dve", lambda e: e.memset(kf[:, 0:TT], 0.0), writes=[kf])
                            k.op("act", lambda e: e.copy(out=kf[32:64, 0:TT], in_=kx[32:64, 0, 0:TT]),
                                 reads=[kx], writes=[kf])
                            for b in range(TT // 128):
                                p = ps.next()
                                k.op("pe", lambda e: e.transpose(p[:, 0:128], cf[:, b * 128:(b + 1) * 128], ident[:]),
                                     reads=[cf, ident], writes=[p])
                                ot = otp.next()
                                evac(b, ot[:, :], p[:, 0:128], [p], [ot])
                                k.dma("pool", O["new_ckv"][si, l, t0 + b * 128:t0 + (b + 1) * 128, :], ot[:, :], reads=[ot])
                                p = ps.next()
                                k.op("pe", lambda e: e.transpose(p[:, 0:128], kf[:, b * 128:(b + 1) * 128], ident[:]),
                                     reads=[kf, ident], writes=[p])
                                ot = otp.next()
                                evac(b + 1, ot[:, 0:32], p[:, 32:64], [p], [ot])
                                k.dma("pool", O["new_kpe"][si, l, t0 + b * 128:t0 + (b + 1) * 128, :], ot[:, 0:32], reads=[ot])
                    ntile = (NK + 511) // 512
                    for ti in range(ntile):
                        c0 = ti * 512
                        n = min(512, NK - c0)
                        for h in range(4):
                            p = ps.next()
                            k.op("pe", lambda e: e.matmul(p[:, 0:n], lhsT=wkk[:, h, :], rhs=ckvT[:, c0:c0 + n],
                                                          start=True, stop=True), reads=[wkk, ckvT], writes=[p])
                            evac(h, kTm[h][64:128, c0:c0 + n], p[64:128, 0:n], [p], [kTm[h]], acc=True)
                            k.op("pool", lambda e: e.tensor_copy(out=kTm[h][32:64, c0:c0 + n], in_=kpeT[32:64, c0:c0 + n]),
                                 reads=[kpeT], writes=[kTm[h]], acc=True)
                            for (a0, a1) in ((32, 64), (64, 128)):
                                k.op("act", lambda e: e.activation(out=sqb[a0:a1, 0:n], in_=kTm[h][a0:a1, c0:c0 + n],
                                                                   func=AF.Square), reads=[kTm[h]], writes=[sqb], acc=(a0 == 64))
                            p2 = ps.next()
                            k.op("pe", lambda e: e.matmul(p2[0:1, 0:n], lhsT=ones_bf[:, 0:1], rhs=sqb[:, 0:n],
                                                          start=True, stop=True), reads=[ones_bf, sqb], writes=[p2])
                            k.op("dve", lambda e: e.reduce_max(out=kmx[0:1, h, ti:ti + 1], in_=p2[0:1, 0:n], axis=AX.X),
                                 reads=[p2], writes=[kmx], acc=True)
                    for kb in range(NKB):
                        p = ps.next()
                        k.op("pe", lambda e: e.matmul(p[:, 0:256], lhsT=ckvT[:, kb * 128:(kb + 1) * 128], rhs=wkv[:, :],
                                                      start=True, stop=True), reads=[ckvT, wkv], writes=[p])
                        evac(kb, vm[:, kb, :, 0:64], p[:, 0:256].rearrange("p (h d) -> p h d", h=4), [p], [vm], acc=True)
                    k.op("dve", lambda e: e.reduce_max(out=nkmax[0:1, :], in_=kmx[0:1, :, 0:ntile], axis=AX.X),
                         reads=[kmx], writes=[nkmax])
                    k.op("act", lambda e: e.activation(out=nkmax[0:1, :], in_=nkmax[0:1, :], func=AF.Sqrt),
                         reads=[nkmax], writes=[nkmax])
                    k.op("dve", lambda e: e.tensor_scalar(out=nkmax[0:1, :], in0=nkmax[0:1, :], scalar1=-1.0, scalar2=None,
                                                          op0=ALU.mult), reads=[nkmax], writes=[nkmax])
                    for t0 in range(0, T, TT):
                        g0 = off + t0
                        ql = qlp.next()
                        k.dma("sp", ql[:, :, 0:TT], PFM[l][R_MQ:R_MQ + 256, g0:g0 + TT].rearrange("(c p) t -> p c t", p=128),
                              reads=[PFM[l]], writes=[ql])
                        sq = sqp.next()
                        k.op("act", lambda e: e.activation(out=sq[:, :, 0:TT], in_=ql[:, :, 0:TT], func=AF.Square),
                             reads=[ql], writes=[sq])
                        p = ps.next()
                        for c in range(2):
                            k.op("pe", lambda e: e.matmul(p[:, 0:TT], lhsT=ones_bf[:], rhs=sq[:, c, 0:TT],
                                                          start=(c == 0), stop=(c == 1)), reads=[ones_bf, sq], writes=[p], inc=(c == 1))
                        rstd = rsp.next()
                        rstd_from_ps(p, TT, rstd, 256)
                        qn = qnp.next()
                        for c in range(2):
                            k.op("dve", lambda e: e.scalar_tensor_tensor(out=qn[:, c, 0:TT], in0=ql[:, c, 0:TT],
                                                                         scalar=qnw[:, c:c + 1], in1=rstd[:, 0:TT],
                                                                         op0=ALU.mult, op1=ALU.mult),
                                 reads=[ql, qnw, rstd], writes=[qn], acc=(c > 0))
                        qTs_h = []
                        for h in range(4):
                            p1 = ps.next()
                            for c in range(2):
                                k.op("pe", lambda e: e.matmul(p1[:, 0:TT], lhsT=wuq[:, c, h, :], rhs=qn[:, c, 0:TT],
                                                              start=(c == 0), stop=(c == 1)), reads=[wuq, qn], writes=[p1], inc=(c == 1))
                            qT = qTp.next()
                            k.op("act", lambda e: e.copy(out=qT[64:128, 0:TT], in_=p1[64:128, 0:TT]), reads=[p1], writes=[qT])
                            if lat:
                                p2 = ps.next()
                                for c in range(2):
                                    k.op("pe", lambda e: e.matmul(p2[0:64, 0:TT], lhsT=wuqs[:, c, h, :], rhs=qn[:, c, 0:TT],
                                                                  start=(c == 0), stop=(c == 1)), reads=[wuqs, qn], writes=[p2], inc=(c == 1))
                                t1 = f32p.next()
                                t2 = f32p.next()
                                k.op("dve", lambda e: e.tensor_tensor(out=t1[32:64, 0:TT], in0=p1[32:64, 0:TT],
                                                                      in1=CM[32:64, t0:t0 + TT], op=ALU.mult),
                                     reads=[p1, CM], writes=[t1])
                                k.op("dve", lambda e: e.tensor_tensor(out=t2[32:64, 0:TT], in0=p2[32:64, 0:TT],
                                                                      in1=SM[32:64, t0:t0 + TT], op=ALU.mult),
                                     reads=[p2, SM], writes=[t2])
                                k.op("pool", lambda e: e.tensor_tensor(out=qT[32:64, 0:TT], in0=t1[32:64, 0:TT],
                                                                       in1=t2[32:64, 0:TT], op=ALU.add),
                                     reads=[t1, t2], writes=[qT], acc=True)
                            else:
                                k.op("dve", lambda e: e.tensor_copy(out=qT[32:64, 0:TT], in_=p1[32:64, 0:TT]),
                                     reads=[p1], writes=[qT], acc=True)
                            for (a0, a1) in ((32, 64), (64, 128)):
                                k.op("act", lambda e: e.activation(out=sqb[a0:a1, 0:TT], in_=qT[a0:a1, 0:TT], func=AF.Square),
                                     reads=[qT], writes=[sqb], acc=(a0 == 64))
                            pn = ps.next()
                            k.op("pe", lambda e: e.matmul(pn[0:1, 0:TT], lhsT=ones_bf[:, 0:1], rhs=sqb[:, 0:TT],
                                                          start=True, stop=True), reads=[ones_bf, sqb], writes=[pn])
                            rw = rowp.next()
                            k.op("act", lambda e: e.activation(out=rw[0:1, 0:TT], in_=pn[0:1, 0:TT], func=AF.Sqrt),
                                 reads=[pn], writes=[rw])
                            k.op("dve", lambda e: e.tensor_scalar(out=qT[0:1, 0:TT], in0=rw[0:1, 0:TT],
                                                                  scalar1=nkmax[0:1, h:h + 1], scalar2=None, op0=ALU.mult),
                                 reads=[rw, nkmax], writes=[qT], acc=True)
                            qTs_h.append(qT)
                        for h in range(4):
                            po = attn_core(kTm[h], vm, h, qTs_h[h], TT, NKB, MLA_SCALE, ptp)
                            attn_finish(po, TT, rowbuf, bcs, ogp,
                                        MIX[l][256 + 64 * h:256 + 64 * h + 64, g0:g0 + TT], MIX[l])
                k.barrier()


        def swa_phase(l):
            with ExitStack() as ph:
                NKMAX = TL + 256
                CS = k.tile("CS", [128, TL], F32, ph)
                SS = k.tile("SS", [128, TL], F32, ph)
                k.dma("sp", CS[64:128, :], I["rope_s"][0], writes=[CS])
                k.dma("sp", SS[64:128, :], I["rope_s"][1], writes=[SS])
                mk = k.tile("mk", [128, 6, 512], BF16, ph)
                k.dma("pool", mk[:], I["swa_mask"].rearrange("r p q -> p r q"), writes=[mk])
                sk = k.tile("sk", [1, 4], F32, ph)
                k.dma("sp", sk[:], I["swa_sinks"][l:l + 1, :], writes=[sk])
                e64 = k.tile("e64", [1, 65], BF16, ph)
                k.op("dve", lambda e: e.memset(e64[:], 0.0), writes=[e64])
                k.op("dve", lambda e: e.memset(e64[0:1, 64:65], 1.0), writes=[e64])
                kTs = [k.tile("kTs%d" % h, [128, NKMAX], BF16, ph) for h in range(2)]
                for h in range(2):
                    k.op("pool", lambda e: e.memset(kTs[h][0:64, :], 0.0), writes=[kTs[h]])
                    k.op("pool", lambda e: e.memset(kTs[h][0:1, :], 1.0), writes=[kTs[h]])
                vs = k.tile("vs", [128, NKMAX // 128, 2, 65], BF16, ph)
                k.op("pool", lambda e: e.memset(vs[:], 1.0), writes=[vs])
                kmx = k.tile("kmx", [1, 2, 16], F32, ph)
                nkmax = k.tile("nkmax", [1, 2], F32, ph)
                sqb = k.tile("sqb", [128, 512], BF16, ph)
                k.op("dve", lambda e: e.memset(sqb[:], 0.0), writes=[sqb])
                f32p = k.pool("f32s", [128, 512], F32, 3, ph)
                kxp = k.pool("kxs", [128, 2, 512], BF16, 4, ph)
                qTp = k.pool("qTs", [128, 512], BF16, 6, ph)
                rowp = k.pool("rowps", [1, 512], F32, 3, ph)
                for t_ in qTp.tiles:
                    k.op("dve", lambda e: e.memset(t_[:], 0.0), writes=[t_])
                ptp = k.pool("pts", [128, 512], BF16, 6, ph)
                rowbuf = k.tile("rowbufs", [128, 512], F32, ph)
                srowp = k.pool("srow", [1, 512], BF16, 6, ph)
                bcs = k.tile("bcss", [64, 512], F32, ph)
                ogp = k.pool("ogs", [64, 512], BF16, 3, ph)
                ckp = k.pool("cks", [128, 128], F32, 2, ph)
                for t_ in ckp.tiles:
                    k.op("dve", lambda e: e.memset(t_[:], 0.0), writes=[t_])
                otp = k.pool("ots", [128, 128], F32, 2, ph)

                def rope(dst_ap, dst_res, x, t0, TT, lat, acc=True):
                    if lat:
                        t1 = f32p.next()
                        t2 = f32p.next()
                        k.op("dve", lambda e: e.tensor_tensor(out=t1[64:128, 0:TT], in0=x[64:128, 0, 0:TT],
                                                              in1=CS[64:128, t0:t0 + TT], op=ALU.mult),
                             reads=[x, CS], writes=[t1])
                        k.op("pool", lambda e: e.tensor_tensor(out=t2[64:128, 0:TT], in0=x[64:128, 1, 0:TT],
                                                               in1=SS[64:128, t0:t0 + TT], op=ALU.mult),
                             reads=[x, SS], writes=[t2])
                        k.op("dve", lambda e: e.tensor_tensor(out=dst_ap, in0=t1[64:128, 0:TT], in1=t2[64:128, 0:TT],
                                                              op=ALU.add), reads=[t1, t2], writes=[dst_res], acc=acc)
                    else:
                        k.op("dve", lambda e: e.tensor_copy(out=dst_ap, in_=x[64:128, 0, 0:TT]), reads=[x],
                             writes=[dst_res], acc=acc)

                for (off, T, lat, si) in SEQS:
                    k.barrier()
                    TT = min(512, T)
                    koff = 256 if lat else 0
                    NK = T + koff
                    NKB = NK // 128
                    if lat:
                        for kv in range(2):
                            for b in range(2):
                                ck = ckp.next()
                                k.dma("sp", ck[:, 64:128], I["cache_swk"][l, b * 128:(b + 1) * 128, kv, :], writes=[ck])
                                p = ps.next()
                                k.op("pe", lambda e: e.transpose(p[:, 0:128], ck[:, :], ident[:]), reads=[ck, ident], writes=[p])
                                evac(b, kTs[kv][64:128, b * 128:(b + 1) * 128], p[64:128, 0:128], [p], [kTs[kv]], acc=True)
                        for b in range(2):
                            k.dma("pool", vs[:, b, :, 0:64], I["cache_swv"][l, b * 128:(b + 1) * 128, :, :],
                                  writes=[vs], acc=True)
                    else:
                        k.dma("pool", O["new_swv"][si, l].rearrange("t k d -> t (k d)"),
                              PTM[l][off:off + T, C_SWV:C_SWV + 128], reads=[PTM[l]])
                    for b in range(T // 128):
                        k.dma("pool", vs[:, koff // 128 + b, :, 0:64],
                              PTM[l][off + b * 128:off + (b + 1) * 128, C_SWV:C_SWV + 128].rearrange("p (k d) -> p k d", k=2),
                              reads=[PTM[l]], writes=[vs], acc=True)
                    for kv in range(2):
                        for ti, t0 in enumerate(range(0, T, TT)):
                            g0 = off + t0
                            kx = kxp.next()
                            k.dma("sp", kx[64:128, 0, 0:TT], PFM[l][R_SWK + 64 * kv:R_SWK + 64 * kv + 64, g0:g0 + TT],
                                  reads=[PFM[l]], writes=[kx])
                            k.dma("sp", kx[64:128, 1, 0:TT], PFM[l][R_SWKS + 64 * kv:R_SWKS + 64 * kv + 64, g0:g0 + TT],
                                  reads=[PFM[l]], writes=[kx], acc=True)
                            rope(kTs[kv][64:128, koff + t0:koff + t0 + TT], kTs[kv], kx, t0, TT, lat)
                            if not lat:
                                kf = f32p.next()
                                k.op("dve", lambda e: e.memset(kf[0:64, 0:TT], 0.0), writes=[kf])
                                k.op("act", lambda e: e.copy(out=kf[64:128, 0:TT], in_=kx[64:128, 0, 0:TT]),
                                     reads=[kx], writes=[kf], acc=True)
                                for b in range(TT // 128):
                                    p = ps.next()
                                    k.op("pe", lambda e: e.transpose(p[:, 0:128], kf[:, b * 128:(b + 1) * 128], ident[:]),
                                         reads=[kf, ident], writes=[p])
                                    ot = otp.next()
                                    evac(b, ot[:, 0:64], p[:, 64:128], [p], [ot])
                                    k.dma("pool", O["new_swk"][si, l, t0 + b * 128:t0 + (b + 1) * 128, kv, :], ot[:, 0:64], reads=[ot])
                        ntile = (NK + 511) // 512
                        for ti in range(ntile):
                            c0 = ti * 512
                            n = min(512, NK - c0)
                            k.op("act", lambda e: e.activation(out=sqb[64:128, 0:n], in_=kTs[kv][64:128, c0:c0 + n],
                                                               func=AF.Square), reads=[kTs[kv]], writes=[sqb])
                            p2 = ps.next()
                            k.op("pe", lambda e: e.matmul(p2[0:1, 0:n], lhsT=ones_bf[:, 0:1], rhs=sqb[:, 0:n],
                                                          start=True, stop=True), reads=[ones_bf, sqb], writes=[p2])
                            k.op("dve", lambda e: e.reduce_max(out=kmx[0:1, kv, ti:ti + 1], in_=p2[0:1, 0:n], axis=AX.X),
                                 reads=[p2], writes=[kmx], acc=True)
                    ntile = (NK + 511) // 512
                    k.op("dve", lambda e: e.reduce_max(out=nkmax[0:1, :], in_=kmx[0:1, :, 0:ntile], axis=AX.X),
                         reads=[kmx], writes=[nkmax])
                    k.op("act", lambda e: e.activation(out=nkmax[0:1, :], in_=nkmax[0:1, :], func=AF.Sqrt),
                         reads=[nkmax], writes=[nkmax])
                    k.op("dve", lambda e: e.tensor_scalar(out=nkmax[0:1, :], in0=nkmax[0:1, :], scalar1=-1.0, scalar2=None,
                                                          op0=ALU.mult), reads=[nkmax], writes=[nkmax])
                    for t0 in range(0, T, TT):
                        g0 = off + t0
                        i0 = t0 // 128
                        if lat:
                            kbl = [(0, None), (1, None)]
                            for r in range(6):
                                kbo = i0 - 1 + r
                                if 0 <= kbo < T // 128:
                                    kbl.append((2 + kbo, Res("mkv", mk.t[:, r, :])))
                            for (_, m_) in kbl:
                                if m_ is not None:
                                    m_.w = mk.w
                        else:
                            kbl = [(b, None) for b in range(NKB)]
                        prep_h = []
                        for h in range(4):
                            kv = h // 2
                            qx = kxp.next()
                            k.dma("sp", qx[64:128, 0, 0:TT], PFM[l][R_SWQ + 64 * h:R_SWQ + 64 * h + 64, g0:g0 + TT],
                                  reads=[PFM[l]], writes=[qx])
                            k.dma("sp", qx[64:128, 1, 0:TT], PFM[l][R_SWQS + 64 * h:R_SWQS + 64 * h + 64, g0:g0 + TT],
                                  reads=[PFM[l]], writes=[qx], acc=True)
                            qT = qTp.next()
                            rope(qT[64:128, 0:TT], qT, qx, t0, TT, lat, acc=False)
                            k.op("act", lambda e: e.activation(out=sqb[64:128, 0:TT], in_=qT[64:128, 0:TT], func=AF.Square),
                                 reads=[qT], writes=[sqb])
                            pn = ps.next()
                            k.op("pe", lambda e: e.matmul(pn[0:1, 0:TT], lhsT=ones_bf[:, 0:1], rhs=sqb[:, 0:TT],
                                                          start=True, stop=True), reads=[ones_bf, sqb], writes=[pn])
                            rw = rowp.next()
                            k.op("act", lambda e: e.activation(out=rw[0:1, 0:TT], in_=pn[0:1, 0:TT], func=AF.Sqrt),
                                 reads=[pn], writes=[rw])
                            k.op("dve", lambda e: e.tensor_scalar(out=qT[0:1, 0:TT], in0=rw[0:1, 0:TT],
                                                                  scalar1=nkmax[0:1, kv:kv + 1], scalar2=None, op0=ALU.mult),
                                 reads=[rw, nkmax], writes=[qT], acc=True)
                            srow = srowp.next()
                            k.op("act", lambda e: e.activation(out=srow[0:1, 0:TT], in_=qT[0:1, 0:TT], func=AF.Exp,
                                                               scale=SWA_SCALE, bias=sk[0:1, h:h + 1]),
                                 reads=[qT, sk], writes=[srow])
                            prep_h.append((qT, srow))
                        for h in range(4):
                            kv = h // 2
                            qT, srow = prep_h[h]
                            po = attn_core(kTs[kv], vs, kv, qT, TT, NKB, SWA_SCALE, ptp, sink=(e64, srow), kb_list=kbl)
                            attn_finish(po, TT, rowbuf, bcs, ogp,
                                        MIX[l][768 + 64 * h:768 + 64 * h + 64, g0:g0 + TT], MIX[l])
                k.barrier()


        def ssd_phase(l):
            with ExitStack() as ph:
                cm = k.tile("cm", [128, 5, 128], F32, ph)
                k.dma("sp", cm[:], I["cmask"].rearrange("r p q -> p r q")[:, 0:5, :], writes=[cm])
                onesblk, onesA, onesB = cm[:, 2, :], cm[:, 3, :], cm[:, 4, :]
                cw = k.tile("cw", [128, 4, 3], F32, ph)
                for kk in range(3):
                    k.dma("sp", cw[:, :, kk], I["ssm_conv_w"][l, kk].rearrange("(c p) -> p c", p=128), writes=[cw],
                          acc=(kk > 0), allow_slow_non_contiguous=True)
                cb = k.tile("cb", [128, 4], F32, ph)
                k.dma("sp", cb[:], I["ssm_conv_b"][l].rearrange("(c p) -> p c", p=128), writes=[cb],
                      allow_slow_non_contiguous=True)
                dtb = k.tile("dtb", [128, 8], F32, ph)
                k.dma("sp", dtb[:], I["ssm_dt_bias"][l:l + 1].rearrange("o a b -> o (a b)").partition_broadcast(128), writes=[dtb])
                aneg = k.tile("aneg", [128, 8], F32, ph)
                k.dma("sp", aneg[:], I["ssm_a_log"][l:l + 1].rearrange("o a b -> o (a b)").partition_broadcast(128), writes=[aneg])
                k.op("act", lambda e: e.activation(out=aneg[:], in_=aneg[:], func=AF.Exp), reads=[aneg], writes=[aneg])
                k.op("dve", lambda e: e.tensor_scalar(out=aneg[:], in0=aneg[:], scalar1=-1.0, scalar2=None, op0=ALU.mult),
                     reads=[aneg], writes=[aneg])
                Dt = k.tile("Dt", [128, 4], F32, ph)
                k.dma("sp", Dt[:], I["ssm_d"][l:l + 1, :].partition_broadcast(128), writes=[Dt])
                nwt = k.tile("nwt", [128, 256], F32, ph)
                k.dma("sp", nwt[:], I["ssm_norm_w"][l:l + 1, :].partition_broadcast(128), writes=[nwt])
                onec = k.tile("onec", [128, 1], F32, ph)
                k.op("dve", lambda e: e.memset(onec[:], 1.0), writes=[onec])
                NBM = TL // 128
                BT = k.tile("BT", [128, TL], BF16, ph)
                CT = k.tile("CT", [128, TL], BF16, ph)
                x_tok = k.tile("x_tok", [128, NBM, 256], F32, ph)
                B_tok = k.tile("B_tok", [128, NBM, 128], F32, ph)
                dtr = k.tile("dtr", [128, NBM, 8], F32, ph)
                dtt = k.tile("dtt", [128, NBM, 8], F32, ph)
                at = k.tile("at", [128, NBM, 8], F32, ph)
                yacc = k.tile("yacc", [128, NBM, 256], F32, ph)
                S = [k.tile("S%d" % d, [128, 2, 64], F32, ph) for d in range(2)]
                Sb = [k.tile("Sbs%d" % d, [128, 2, 64], BF16, ph) for d in range(2)]
                xinp = k.pool("xin", [128, 4, 514], BF16, 2, ph)
                xTp = k.pool("xTs", [128, 4, 512], F32, 2, ph)
                tmpp = k.pool("tmps", [128, 512], F32, 4, ph)
                WS = []
                for d_ in range(2):
                    WS.append({"stp": k.pool("stt", [128, 24], F32, 2, ph), "exp": k.pool("exs", [128, 24], F32, 2, ph),
                               "GUp": k.pool("GU", [128, 4, 128], F32, 1, ph), "Lp": k.pool("Lp", [128, 4, 128], F32, 1, ph),
                               "L2p": k.pool("L2p", [128, 4, 128], F32, 1, ph), "scp": k.pool("scT", [128, 4, 128], BF16, 2, ph),
                               "xdp": k.pool("xdt", [128, 4, 64], BF16, 2, ph), "typ": k.pool("tmpy", [128, 4, 64], F32, 3, ph),
                               "Bdp": k.pool("Bd", [128, 4, 128], BF16, 2, ph), "ydp": k.pool("yds", [128, 256], F32, 2, ph)})
                zp = k.pool("zs", [128, 256], F32, 2, ph)
                y2p = k.pool("y2s", [128, 256], F32, 4, ph)
                ssp = k.pool("ssum", [128, 4], F32, 2, ph)
                osp = k.pool("oss", [128, 2, 128], BF16, 2, ph)
                sop = k.pool("sos", [128, 128], F32, 2, ph)

                for (off, T, lat, SEQT) in ((0, NCTX * TC, False, TC), (LOFF, TL, True, TL)):
                    k.barrier()
                    NB = T // 128
                    BPS = SEQT // 128
                    TT = min(512, SEQT)
                    for t0 in range(0, T, TT):
                        g0 = off + t0
                        xin = xinp.next()
                        first = (t0 % SEQT == 0)
                        lastt = ((t0 + TT) % SEQT == 0)
                        lo = g0 if first else g0 - 1
                        hi = g0 + TT if lastt else g0 + TT + 1
                        c_lo = 1 if first else 0
                        k.dma("sp", xin[:, :, c_lo:c_lo + (hi - lo)],
                              PFM[l][R_SSX:R_SSX + 512, lo:hi].rearrange("(c p) t -> p c t", p=128),
                              reads=[PFM[l]], writes=[xin])
                        if first:
                            k.op("dve", lambda e: e.memset(xin[:, :, 0:1], 0.0), writes=[xin], acc=True)
                        if lastt:
                            k.op("dve", lambda e: e.memset(xin[:, :, TT + 1:TT + 2], 0.0), writes=[xin], acc=True)
                        xT = xTp.next()
                        for c in range(4):
                            ta = tmpp.next()
                            tb = tmpp.next()
                            k.op("dve", lambda e: e.tensor_scalar(out=ta[:, 0:TT], in0=xin[:, c, 0:TT], scalar1=cw[:, c, 0:1],
                                                                  scalar2=None, op0=ALU.mult), reads=[xin, cw], writes=[ta])
                            k.op("dve", lambda e: e.scalar_tensor_tensor(out=tb[:, 0:TT], in0=xin[:, c, 1:TT + 1], scalar=cw[:, c, 1:2],
                                                                         in1=ta[:, 0:TT], op0=ALU.mult, op1=ALU.add),
                                 reads=[xin, cw, ta], writes=[tb])
                            k.op("dve", lambda e: e.scalar_tensor_tensor(out=ta[:, 0:TT], in0=xin[:, c, 2:TT + 2], scalar=cw[:, c, 2:3],
                                                                         in1=tb[:, 0:TT], op0=ALU.mult, op1=ALU.add),
                                 reads=[xin, cw, tb], writes=[ta])
                            k.op("act", lambda e: e.activation(out=xT[:, c, 0:TT], in_=ta[:, 0:TT], func=AF.Silu, bias=cb[:, c:c + 1]),
                                 reads=[ta, cb], writes=[xT], acc=(c > 0))
                            if c >= 2:
                                dres = BT if c == 2 else CT
                                k.op("pool", lambda e: e.tensor_copy(out=dres[:, t0:t0 + TT], in_=xT[:, c, 0:TT]), reads=[xT], writes=[dres], acc=True)
                        for b in range(TT // 128):
                            blk = t0 // 128 + b
                            for c in range(2):
                                p = ps.next()
                                k.op("pe", lambda e: e.transpose(p[:, 0:128], xT[:, c, b * 128:(b + 1) * 128], ident[:]),
                                     reads=[xT, ident], writes=[p])
                                evac(c, x_tok[:, blk, c * 128:(c + 1) * 128], p[:, 0:128], [p], [x_tok], acc=True)
                            p = ps.next()
                            k.op("pe", lambda e: e.transpose(p[:, 0:128], xT[:, 2, b * 128:(b + 1) * 128], ident[:]),
                                 reads=[xT, ident], writes=[p])
                            evac(1, B_tok[:, blk, :], p[:, 0:128], [p], [B_tok], acc=True)
                    if debug.get("ssd_stop", 9) <= 1:
                        continue
                    for b0 in range(0, NB, 2):
                        k.dma("sp", dtr[:, b0:b0 + 2, :],
                              PTM[l][off + b0 * 128:off + (b0 + 2) * 128, C_DT:C_DT + 8].rearrange("(b p) j -> p b j", p=128),
                              reads=[PTM[l]], writes=[dtr], acc=(b0 > 0))
                    if debug.get("ssd_stop", 9) <= 2:
                        continue
                    k.op("dve", lambda e: e.tensor_tensor(out=dtt[:, 0:NB, :], in0=dtr[:, 0:NB, :],
                                                          in1=dtb[:].unsqueeze(1).to_broadcast([128, NB, 8]), op=ALU.add),
                         reads=[dtr, dtb], writes=[dtt])
                    k.op("act", lambda e: e.activation(out=dtr[:, 0:NB, :], in_=dtt[:, 0:NB, :], func=AF.Exp), reads=[dtt], writes=[dtr])
                    k.op("act", lambda e: e.activation(out=dtt[:, 0:NB, :], in_=dtr[:, 0:NB, :], func=AF.Ln, bias=onec[:, 0:1]),
                         reads=[dtr, onec], writes=[dtt])
                    k.op("dve", lambda e: e.tensor_tensor(out=at[:, 0:NB, :], in0=dtt[:, 0:NB, :],
                                                          in1=aneg[:].unsqueeze(1).to_broadcast([128, NB, 8]), op=ALU.mult),
                         reads=[dtt, aneg], writes=[at])
                    for d in range(2):
                        if lat:
                            stin = sop.next()
                            for g in range(2):
                                for hh in range(2):
                                    k.dma("sp", stin[hh * 64:(hh + 1) * 64, g * 64:(g + 1) * 64], I["state_ssm"][l, d, 2 * g + hh, :, :],
                                          writes=[stin], acc=not (g == 0 and hh == 0))
                            p = ps.next()
                            k.op("pe", lambda e: e.transpose(p[:, 0:128], stin[:, :], ident[:]), reads=[stin, ident], writes=[p])
                            k.op("dve", lambda e: e.tensor_copy(out=S[d][:].rearrange("p a b -> p (a b)"), in_=p[:, 0:128]),
                                 reads=[p], writes=[S[d]])
                            k.op("act", lambda e: e.copy(out=Sb[d][:], in_=S[d][:]), reads=[S[d]], writes=[Sb[d]])
                    if debug.get("ssd_stop", 9) <= 3:
                        continue
                    ywritten = set()

                    def ssd_dir_gen(d):
                        stp, exp_, GUp, Lp, L2p, scp, xdp, typ, Bdp, ydp = (WS[d][n_] for n_ in ("stp", "exp", "GUp", "Lp", "L2p", "scp", "xdp", "typ", "Bdp", "ydp"))
                        ps = psd[d]
                        U = cm[:, d, :]
                        order = list(range(NB)) if d == 0 else list(range(NB - 1, -1, -1))
                        halves = (0, 1) if d == 0 else (1, 0)
                        for blk in order:
                            tok0 = blk * 128
                            seq_first = (blk % BPS == 0) if d == 0 else (blk % BPS == BPS - 1)
                            seq_last = (blk % BPS == BPS - 1) if d == 0 else (blk % BPS == 0)
                            if seq_first and not lat:
                                k.op("dve", lambda e: e.memset(S[d][:], 0.0), writes=[S[d]])
                                k.op("act", lambda e: e.copy(out=Sb[d][:], in_=S[d][:]), reads=[S[d]], writes=[Sb[d]])
                            a_blk = at[:, blk, d * 4:(d + 1) * 4]
                            pc = ps.next()
                            for j, lh in enumerate((U, onesA, onesB)):
                                k.op("pe", lambda e: e.matmul(pc[:, 4 * j:4 * j + 4], lhsT=lh, rhs=a_blk, start=True, stop=True),
                                     reads=[cm, at], writes=[pc], inc=(j == 2))
                            st = stp.next()
                            k.op("act", lambda e: e.copy(out=st[:, 0:12], in_=pc[:, 0:12]), reads=[pc], writes=[st])
                            k.op("dve", lambda e: e.tensor_tensor(out=st[0:64, 16:20], in0=st[0:64, 4:8], in1=st[0:64, 0:4], op=ALU.subtract),
                                 reads=[st], writes=[st])
                            k.op("dve", lambda e: e.tensor_tensor(out=st[64:128, 16:20], in0=st[64:128, 8:12], in1=st[64:128, 0:4], op=ALU.subtract),
                                 reads=[st], writes=[st])
                            ex = exp_.next()
                            k.op("act", lambda e: e.activation(out=ex[:, 0:12], in_=st[:, 0:12], func=AF.Exp), reads=[st], writes=[ex])
                            k.op("act", lambda e: e.activation(out=ex[:, 16:20], in_=st[:, 16:20], func=AF.Exp), reads=[st], writes=[ex], acc=True)
                            if debug.get("scan_stop", 9) <= 1:
                                continue
                            yield
                            GU = GUp.next()
                            for h in range(4):
                                k.op("dve", lambda e: e.tensor_scalar(out=GU[:, h, :], in0=U, scalar1=a_blk[:, h:h + 1], scalar2=None, op0=ALU.mult),
                                     reads=[cm, at], writes=[GU], acc=(h > 0))
                            pa = ps.next()
                            k.op("pe", lambda e: e.matmul(pa[:, :], lhsT=onesblk, rhs=GU[:].rearrange("p h i -> p (h i)"), start=True, stop=True),
                                 reads=[cm, GU], writes=[pa])
                            L = Lp.next()
                            for h in range(4):
                                k.op("dve", lambda e: e.tensor_scalar(out=L[:, h, :], in0=pa[:, h * 128:(h + 1) * 128], scalar1=st[:, h:h + 1],
                                                                      scalar2=0.0, op0=ALU.subtract, op1=ALU.min),
                                     reads=[pa, st], writes=[L], acc=(h > 0))
                            L2 = L2p.next()
                            k.op("act", lambda e: e.activation(out=L2[:], in_=L[:], func=AF.Exp), reads=[L], writes=[L2])
                            k.op("pool", lambda e: e.tensor_tensor(out=L[:], in0=L2[:], in1=U.unsqueeze(1).to_broadcast([128, 4, 128]), op=ALU.mult),
                                 reads=[L2, cm], writes=[L])
                            if debug.get("scan_stop", 9) <= 2:
                                continue
                            yield
                            pcbs = [ps.next(), ps.next()]
                            for g in range(2):
                                k.op("pe", lambda e: e.matmul(pcbs[g][:, 0:128], lhsT=BT[g * 64:(g + 1) * 64, tok0:tok0 + 128],
                                                              rhs=CT[g * 64:(g + 1) * 64, tok0:tok0 + 128], start=True, stop=True),
                                     reads=[BT, CT], writes=[pcbs[g]])
                            scT = scp.next()
                            for g in range(2):
                                k.op("dve", lambda e: e.tensor_tensor(out=scT[:, 2 * g:2 * g + 2, :],
                                                                      in0=pcbs[g][:, 0:128].unsqueeze(1).to_broadcast([128, 2, 128]),
                                                                      in1=L[:, 2 * g:2 * g + 2, :], op=ALU.mult),
                                     reads=[pcbs[g], L], writes=[scT], acc=(g > 0))
                            xdt = xdp.next()
                            k.op("dve", lambda e: e.tensor_tensor(out=xdt[:], in0=x_tok[:, blk, :].rearrange("p (h d) -> p h d", h=4),
                                                                  in1=dtt[:, blk, d * 4:(d + 1) * 4].unsqueeze(2).to_broadcast([128, 4, 64]), op=ALU.mult),
                                 reads=[x_tok, dtt], writes=[xdt])
                            yield
                            pyd = ps.next()
                            for h in range(4):
                                k.op("pe", lambda e: e.matmul(pyd[:, h * 64:(h + 1) * 64], lhsT=scT[:, h, :], rhs=xdt[:, h, :], start=True, stop=True),
                                     reads=[scT, xdt], writes=[pyd], inc=(h == 3))
                            yds = ydp.next()
                            k.op("act", lambda e: e.copy(out=yds[:], in_=pyd[:, 0:256]), reads=[pyd], writes=[yds])
                            if debug.get("scan_stop", 9) <= 3:
                                continue
                            for half in halves:
                                hb = half * 64
                                ec = 4 if half == 0 else 8
                                yield
                                pyos = [ps.next(), ps.next()]
                                for h in range(4):
                                    g, hh = h // 2, h % 2
                                    k.op("pe", lambda e: e.matmul(pyos[g][:, hh * 64:(hh + 1) * 64], lhsT=CT[g * 64:(g + 1) * 64, tok0:tok0 + 128],
                                                                  rhs=Sb[d][g * 64:(g + 1) * 64, hh, :], start=True, stop=True),
                                         reads=[CT, Sb[d]], writes=[pyos[g]])
                                ty = typ.next()
                                for g in range(2):
                                    k.op("dve", lambda e: e.tensor_tensor(out=ty[hb:hb + 64, 2 * g:2 * g + 2, :],
                                                                          in0=pyos[g][hb:hb + 64, 0:128].rearrange("p (h d) -> p h d", h=2),
                                                                          in1=ex[hb:hb + 64, 2 * g:2 * g + 2].unsqueeze(2).to_broadcast([64, 2, 64]), op=ALU.mult),
                                         reads=[pyos[g], ex], writes=[ty], acc=(g > 0))
                                if (blk, half) not in ywritten:
                                    ywritten.add((blk, half))
                                    k.op("dve", lambda e: e.tensor_tensor(out=yacc[hb:hb + 64, blk, :], in0=ty[hb:hb + 64, :, :].rearrange("p h d -> p (h d)"),
                                                                          in1=yds[hb:hb + 64, :], op=ALU.add),
                                         reads=[ty, yds], writes=[yacc], acc=True)
                                else:
                                    ty2 = typ.next()
                                    k.op("dve", lambda e: e.tensor_tensor(out=ty2[hb:hb + 64, :, :].rearrange("p h d -> p (h d)"),
                                                                          in0=ty[hb:hb + 64, :, :].rearrange("p h d -> p (h d)"),
                                                                          in1=yds[hb:hb + 64, :], op=ALU.add),
                                         reads=[ty, yds], writes=[ty2])
                                    k.op("pool", lambda e: e.tensor_tensor(out=yacc[hb:hb + 64, blk, :], in0=yacc[hb:hb + 64, blk, :],
                                                                           in1=ty2[hb:hb + 64, :, :].rearrange("p h d -> p (h d)"), op=ALU.add),
                                         reads=[yacc, ty2], writes=[yacc])
                                if debug.get("scan_stop", 9) <= 4:
                                    continue
                                yield
                                Bd = Bdp.next()
                                for h in range(4):
                                    k.op("dve", lambda e: e.tensor_scalar(out=Bd[hb:hb + 64, h, :], in0=B_tok[hb:hb + 64, blk, :],
                                                                          scalar1=ex[hb:hb + 64, 16 + h:17 + h], scalar2=None, op0=ALU.mult),
                                         reads=[B_tok, ex], writes=[Bd], acc=(h > 0))
                                pst = ps.next()
                                for h in range(4):
                                    k.op("pe", lambda e: e.matmul(pst[:, h * 64:(h + 1) * 64], lhsT=Bd[hb:hb + 64, h, :], rhs=xdt[hb:hb + 64, h, :],
                                                                  start=True, stop=True), reads=[Bd, xdt], writes=[pst], inc=(h == 3))
                                for h in range(4):
                                    g, hh = h // 2, h % 2
                                    k.op("dve", lambda e: e.scalar_tensor_tensor(out=S[d][g * 64:(g + 1) * 64, hh, :], in0=S[d][g * 64:(g + 1) * 64, hh, :],
                                                                                 scalar=ex[g * 64:(g + 1) * 64, ec + h:ec + h + 1],
                                                                                 in1=pst[g * 64:(g + 1) * 64, h * 64:(h + 1) * 64],
                                                                                 op0=ALU.mult, op1=ALU.add),
                                         reads=[S[d], ex, pst], writes=[S[d]])
                                k.op("act", lambda e: e.copy(out=Sb[d][:], in_=S[d][:]), reads=[S[d]], writes=[Sb[d]])
                            if seq_last and not lat:
                                p = ps.next()
                                k.op("pe", lambda e: e.transpose(p[:, 0:128], S[d][:].rearrange("p a b -> p (a b)"), ident[:]),
                                     reads=[S[d], ident], writes=[p])
                                so = sop.next()
                                k.op("dve", lambda e: e.tensor_copy(out=so[:, :], in_=p[:, 0:128]), reads=[p], writes=[so])
                                for g in range(2):
                                    for hh in range(2):
                                        k.dma("pool", O["new_ssm"][blk // BPS, l, d, 2 * g + hh, :, :], so[hh * 64:(hh + 1) * 64, g * 64:(g + 1) * 64], reads=[so])

                    gens = [ssd_dir_gen(0), ssd_dir_gen(1)]
                    while gens:
                        for g_ in list(gens):
                            try:
                                next(g_)
                            except StopIteration:
                                gens.remove(g_)
                    if debug.get("ssd_stop", 9) <= 4:
                        continue
                    for blk in range(NB):
                        tok0 = blk * 128
                        z = zp.next()
                        k.dma("sp", z[:], PTM[l][off + tok0:off + tok0 + 128, C_SSZ:C_SSZ + 256], reads=[PTM[l]], writes=[z])
                        t1 = y2p.next()
                        k.op("dve", lambda e: e.tensor_tensor(out=t1[:].rearrange("p (h d) -> p h d", h=4),
                                                              in0=x_tok[:, blk, :].rearrange("p (h d) -> p h d", h=4),
                                                              in1=Dt[:, 0:4].unsqueeze(2).to_broadcast([128, 4, 64]), op=ALU.mult),
                             reads=[x_tok, Dt], writes=[t1])
                        t2 = y2p.next()
                        k.op("pool", lambda e: e.tensor_tensor(out=t2[:], in0=t1[:], in1=yacc[:, blk, :], op=ALU.add), reads=[t1, yacc], writes=[t2])
                        sz = y2p.next()
                        k.op("act", lambda e: e.activation(out=sz[:], in_=z[:], func=AF.Silu), reads=[z], writes=[sz])
                        y2 = y2p.next()
                        k.op("dve", lambda e: e.tensor_tensor(out=y2[:], in0=t2[:], in1=sz[:], op=ALU.mult), reads=[t2, sz], writes=[y2])
                        ssum = ssp.next()
                        k.op("dve", lambda e: e.memset(ssum[:], 0.0), writes=[ssum])
                        for g in range(2):
                            k.op("act", lambda e: e.activation(out=t1[:, g * 128:(g + 1) * 128], in_=y2[:, g * 128:(g + 1) * 128], func=AF.Square,
                                                               accum_out=ssum[:, g:g + 1]), reads=[y2], writes=[t1, ssum])
                        k.op("act", lambda e: e.activation(out=ssum[:, 2:4], in_=ssum[:, 0:2], func=AF.Sqrt, scale=1.0 / 128, bias=epsb[:, 0:1]),
                             reads=[ssum, epsb], writes=[ssum])
                        k.op("dve", lambda e: e.reciprocal(out=ssum[:, 0:2], in_=ssum[:, 2:4]), reads=[ssum], writes=[ssum])
                        y3 = sz
                        for g in range(2):
                            k.op("dve", lambda e: e.scalar_tensor_tensor(out=y3[:, g * 128:(g + 1) * 128], in0=y2[:, g * 128:(g + 1) * 128],
                                                                         scalar=ssum[:, g:g + 1], in1=nwt[:, g * 128:(g + 1) * 128],
                                                                         op0=ALU.mult, op1=ALU.mult), reads=[y2, ssum, nwt], writes=[y3])
                        os_ = osp.next()
                        for c in range(2):
                            p = ps.next()
                            k.op("pe", lambda e: e.transpose(p[:, 0:128], y3[:, c * 128:(c + 1) * 128], ident[:]), reads=[y3, ident], writes=[p])
                            evac(c, os_[:, c, :], p[:, 0:128], [p], [os_], acc=(c > 0))
                        k.dma("pool", MIX[l][512:768, off + tok0:off + tok0 + 128].rearrange("(c p) t -> p c t", p=128), os_[:],
                              reads=[os_], writes=[MIX[l]], acc=True)
                k.barrier()


        def dn_phase(l):
            with ExitStack() as ph:
                cm = k.tile("cmd", [128, 14, 128], F32, ph)
                k.dma("sp", cm[:, 0:7, :], I["cmask"].rearrange("r p q -> p r q")[:, 0:7, :], writes=[cm])
                k.dma("sp", cm[:, 7:14, :], I["cmask"].rearrange("r p q -> p r q")[:, 7:14, :], writes=[cm], acc=True)
                onesblk, onesA, onesB, identm = cm[:, 2, :], cm[:, 3, :], cm[:, 4, :], cm[:, 7, :]
                cw = k.tile("cwd", [128, 6, 3], F32, ph)
                for kk in range(3):
                    k.dma("sp", cw[:, :, kk], I["dn_conv_w"][l, kk].rearrange("(c p) -> p c", p=128), writes=[cw],
                          acc=(kk > 0), allow_slow_non_contiguous=True)
                dtb = k.tile("dtbd", [128, 8], F32, ph)
                k.dma("sp", dtb[:], I["dn_dt_bias"][l:l + 1].rearrange("o a b -> o (a b)").partition_broadcast(128), writes=[dtb])
                aneg = k.tile("anegd", [128, 8], F32, ph)
                k.dma("sp", aneg[:], I["dn_a_log"][l:l + 1].rearrange("o a b -> o (a b)").partition_broadcast(128), writes=[aneg])
                k.op("act", lambda e: e.activation(out=aneg[:], in_=aneg[:], func=AF.Exp), reads=[aneg], writes=[aneg])
                k.op("dve", lambda e: e.tensor_scalar(out=aneg[:], in0=aneg[:], scalar1=-1.0, scalar2=None, op0=ALU.mult),
                     reads=[aneg], writes=[aneg])
                nw1 = k.tile("nw1", [128, 64], F32, ph)
                k.dma("sp", nw1[:], I["dn_norm_w"][l:l + 1, :].partition_broadcast(128), writes=[nw1])
                onec = k.tile("onecd", [128, 1], F32, ph)
                k.op("dve", lambda e: e.memset(onec[:], 1.0), writes=[onec])
                NBM = TL // 128
                qT = k.tile("qTd", [128, 2, TL], BF16, ph)
                kT = k.tile("kTd", [128, 2, TL], BF16, ph)
                k_tok = k.tile("k_tok", [128, NBM, 256], BF16, ph)
                v_tok = k.tile("v_tok", [128, NBM, 256], BF16, ph)
                yacc = k.tile("yaccd", [128, NBM, 256], F32, ph)
                braw = k.tile("braw", [128, NBM, 16], F32, ph)
                btmp = k.tile("btmp", [128, NBM, 16], F32, ph)
                lnb = k.tile("lnb", [128, NBM, 8], F32, ph)
                gt = k.tile("gt", [128, NBM, 8], F32, ph)
                S = [k.tile("Sd%d" % d, [128, 2, 64], F32, ph) for d in range(2)]
                Sb = [k.tile("Sbd%d" % d, [128, 2, 64], BF16, ph) for d in range(2)]

                def h4(ap):
                    return ap.rearrange("p (c r) x -> p c r x", c=2)

                for (off, T, lat, SEQT) in ((0, NCTX * TC, False, TC), (LOFF, TL, True, TL)):
                    k.barrier()
                    NB = T // 128
                    BPS = SEQT // 128
                    TT = min(512, SEQT)
                    with ExitStack() as pre:
                        xinp = k.pool("xind", [128, 6, 514], BF16, 2, pre)
                        cqp = k.pool("cq", [128, 6, 512], F32, 1, pre)
                        tmpp = k.pool("tmpd", [128, 512], F32, 4, pre)
                        knp = k.pool("kn", [128, 2, 512], F32, 1, pre)
                        for t0 in range(0, T, TT):
                            g0 = off + t0
                            xin = xinp.next()
                            first = (t0 % SEQT == 0)
                            lastt = ((t0 + TT) % SEQT == 0)
                            lo = g0 if first else g0 - 1
                            hi = g0 + TT if lastt else g0 + TT + 1
                            c_lo = 1 if first else 0
                            for half3 in range(2):
                                k.dma("sp", xin[:, 3 * half3:3 * half3 + 3, c_lo:c_lo + (hi - lo)],
                                      PFM[l][R_DNQ + 384 * half3:R_DNQ + 384 * half3 + 384, lo:hi].rearrange("(c p) t -> p c t", p=128),
                                      reads=[PFM[l]], writes=[xin], acc=(half3 > 0))
                            if first:
                                k.op("dve", lambda e: e.memset(xin[:, :, 0:1], 0.0), writes=[xin], acc=True)
                            if lastt:
                                k.op("dve", lambda e: e.memset(xin[:, :, TT + 1:TT + 2], 0.0), writes=[xin], acc=True)
                            cq = cqp.next()
                            for c in range(6):
                                ta = tmpp.next()
                                tb = tmpp.next()
                                eng = "dve" if c % 2 == 0 else "pool"
                                k.op("dve", lambda e: e.tensor_scalar(out=ta[:, 0:TT], in0=xin[:, c, 0:TT], scalar1=cw[:, c, 0:1],
                                                                      scalar2=None, op0=ALU.mult), reads=[xin, cw], writes=[ta])
                                k.op("dve", lambda e: e.scalar_tensor_tensor(out=tb[:, 0:TT], in0=xin[:, c, 1:TT + 1], scalar=cw[:, c, 1:2],
                                                                             in1=ta[:, 0:TT], op0=ALU.mult, op1=ALU.add),
                                     reads=[xin, cw, ta], writes=[tb])
                                k.op("dve", lambda e: e.scalar_tensor_tensor(out=ta[:, 0:TT], in0=xin[:, c, 2:TT + 2], scalar=cw[:, c, 2:3],
                                                                             in1=tb[:, 0:TT], op0=ALU.mult, op1=ALU.add),
                                     reads=[xin, cw, tb], writes=[ta])
                                k.op("act", lambda e: e.activation(out=cq[:, c, 0:TT], in_=ta[:, 0:TT], func=AF.Silu),
                                     reads=[ta], writes=[cq], acc=(c > 0))
                            kn = knp.next()
                            for c in range(4):
                                sq = tmpp.next()
                                k.op("act", lambda e: e.activation(out=sq[:, 0:TT], in_=cq[:, c, 0:TT], func=AF.Square), reads=[cq], writes=[sq])
                                p = ps.next()
                                k.op("pe", lambda e: e.matmul(p[:, 0:TT], lhsT=onesblk, rhs=sq[:, 0:TT], start=True, stop=True),
                                     reads=[cm, sq], writes=[p])
                                rs = tmpp.next()
                                k.op("act", lambda e: e.activation(out=rs[:, 0:TT], in_=p[:, 0:TT], func=AF.Ln, bias=epsb[:, 0:1]),
                                     reads=[p, epsb], writes=[rs])
                                rs2 = tmpp.next()
                                k.op("act", lambda e: e.activation(out=rs2[:, 0:TT], in_=rs[:, 0:TT], func=AF.Exp, scale=-0.5),
                                     reads=[rs], writes=[rs2])
                                if c < 2:
                                    k.op("dve", lambda e: e.scalar_tensor_tensor(out=qT[:, c, t0:t0 + TT], in0=cq[:, c, 0:TT], scalar=0.125,
                                                                                 in1=rs2[:, 0:TT], op0=ALU.mult, op1=ALU.mult),
                                         reads=[cq, rs2], writes=[qT], acc=True)
                                else:
                                    k.op("dve", lambda e: e.tensor_tensor(out=kn[:, c - 2, 0:TT], in0=cq[:, c, 0:TT], in1=rs2[:, 0:TT], op=ALU.mult),
                                         reads=[cq, rs2], writes=[kn], acc=(c > 2))
                                    k.op("act", lambda e: e.copy(out=kT[:, c - 2, t0:t0 + TT], in_=kn[:, c - 2, 0:TT]), reads=[kn], writes=[kT], acc=True)
                            for b in range(TT // 128):
                                blk = t0 // 128 + b
                                for c in range(2):
                                    p = ps.next()
                                    k.op("pe", lambda e: e.transpose(p[:, 0:128], kn[:, c, b * 128:(b + 1) * 128], ident[:]),
                                         reads=[kn, ident], writes=[p])
                                    evac(c, k_tok[:, blk, c * 128:(c + 1) * 128], p[:, 0:128], [p], [k_tok], acc=True)
                                    p = ps.next()
                                    k.op("pe", lambda e: e.transpose(p[:, 0:128], cq[:, 4 + c, b * 128:(b + 1) * 128], ident[:]),
                                         reads=[cq, ident], writes=[p])
                                    evac(c + 1, v_tok[:, blk, c * 128:(c + 1) * 128], p[:, 0:128], [p], [v_tok], acc=True)
                        k.barrier()
                    for b0 in range(0, NB, 2):
                        k.dma("sp", braw[:, b0:b0 + 2, :],
                              PTM[l][off + b0 * 128:off + (b0 + 2) * 128, C_BETA:C_BETA + 16].rearrange("(b p) j -> p b j", p=128),
                              reads=[PTM[l]], writes=[braw], acc=(b0 > 0))
                    k.op("act", lambda e: e.activation(out=btmp[:, 0:NB, 0:8], in_=braw[:, 0:NB, 0:8], func=AF.Exp, scale=-1.0),
                         reads=[braw], writes=[btmp])
                    k.op("act", lambda e: e.activation(out=lnb[:, 0:NB, :], in_=btmp[:, 0:NB, 0:8], func=AF.Ln, bias=onec[:, 0:1]),
                         reads=[btmp, onec], writes=[lnb])
                    k.op("dve", lambda e: e.tensor_scalar(out=lnb[:, 0:NB, :], in0=lnb[:, 0:NB, :], scalar1=-1.0, scalar2=None, op0=ALU.mult),
                         reads=[lnb], writes=[lnb])
                    k.op("dve", lambda e: e.tensor_tensor(out=btmp[:, 0:NB, 8:16], in0=braw[:, 0:NB, 8:16],
                                                          in1=dtb[:].unsqueeze(1).to_broadcast([128, NB, 8]), op=ALU.add),
                         reads=[braw, dtb], writes=[btmp])
                    k.op("act", lambda e: e.activation(out=braw[:, 0:NB, 8:16], in_=btmp[:, 0:NB, 8:16], func=AF.Exp), reads=[btmp], writes=[braw])
                    k.op("act", lambda e: e.activation(out=btmp[:, 0:NB, 8:16], in_=braw[:, 0:NB, 8:16], func=AF.Ln, bias=onec[:, 0:1]),
                         reads=[braw, onec], writes=[btmp])
                    k.op("dve", lambda e: e.tensor_tensor(out=gt[:, 0:NB, :], in0=btmp[:, 0:NB, 8:16],
                                                          in1=aneg[:].unsqueeze(1).to_broadcast([128, NB, 8]), op=ALU.mult),
                         reads=[btmp, aneg], writes=[gt])
                    for d in range(2):
                        if lat:
                            for h in range(4):
                                c_, par = h // 2, h % 2
                                k.dma("sp", S[d][par * 64:(par + 1) * 64, c_, :], I["state_dn"][l, d, h, :, :], writes=[S[d]], acc=(h > 0))
                            k.op("act", lambda e: e.copy(out=Sb[d][:], in_=S[d][:]), reads=[S[d]], writes=[Sb[d]])
                    if debug.get("dn_stop", 9) <= 1:
                        continue
                    with ExitStack() as sc:
                        ywritten = set()
                        WK = []
                        for d_ in range(2):
                            W_ = {"stp": k.pool("std", [128, 24], F32, 2, sc), "exp": k.pool("exd", [128, 24], F32, 4, sc), "w4": {}, "w64": {}}
                            for nm in ("GU", "GUb", "L", "E1", "E2", "E3", "A", "AT", "tm"):
                                W_["w4"][nm] = k.tile("w4" + nm, [128, 4, 128], F32, sc)
                            W_["Xp"] = k.pool("Xp", [128, 4, 128], BF16, 2, sc)
                            W_["XTp"] = k.pool("XTp", [128, 4, 128], BF16, 2, sc)
                            for nm in ("As", "ATs", "Ps", "Ps2"):
                                W_["w4"][nm] = k.tile("w4b" + nm, [128, 4, 128], BF16, sc)
                            for nm in ("ty", "ty2"):
                                W_["w64"][nm] = k.tile("w64" + nm, [128, 4, 64], F32, sc)
                            for nm in ("vb", "kbg", "vn"):
                                W_["w64"][nm] = k.tile("w64" + nm, [128, 4, 64], BF16, sc)
                            W_["slots"] = [{"QK": k.tile("sQK", [128, 4, 128], BF16, sc), "wT": k.tile("swT", [128, 4, 128], BF16, sc),
                                            "u": k.tile("su", [128, 4, 64], F32, sc), "kdec": k.tile("skd", [128, 4, 64], BF16, sc)} for _ in range(2)]
                            WK.append(W_)

                        def dn_intra_gen(d):
                            stp, exp_, w4, w64, Xp, XTp = (WK[d][n_] for n_ in ("stp", "exp", "w4", "w64", "Xp", "XTp"))
                            ps = ps8
                            cnt = 0
                            U = cm[:, d, :]
                            m_incl = cm[:, d, :]
                            m_at = cm[:, 5 + d, :]
                            m_a = cm[:, 6 - d, :]
                            order = list(range(NB)) if d == 0 else list(range(NB - 1, -1, -1))
                            halves = (0, 1) if d == 0 else (1, 0)

                            def bc4(m):
                                return m.unsqueeze(1).to_broadcast([128, 4, 128])

                            for blk in order:
                                while busy[d] >= 2:
                                    yield
                                busy[d] += 1
                                slot = WK[d]["slots"][cnt % 2]
                                cnt += 1
                                tok0 = blk * 128
                                g_blk = gt[:, blk, d * 4:(d + 1) * 4]
                                lb_blk = lnb[:, blk, d * 4:(d + 1) * 4]
                                pc = ps.next()
                                for j, lh in enumerate((U, onesA, onesB)):
                                    k.op("pe", lambda e: e.matmul(pc[:, 4 * j:4 * j + 4], lhsT=lh, rhs=g_blk, start=True, stop=True),
                                         reads=[cm, gt], writes=[pc], inc=(j == 2))
                                st = stp.next()
                                k.op("act", lambda e: e.copy(out=st[:, 0:12], in_=pc[:, 0:12]), reads=[pc], writes=[st])
                                k.op("dve", lambda e: e.tensor_tensor(out=st[:, 12:16], in0=st[:, 0:4], in1=lb_blk, op=ALU.add),
                                     reads=[st, lnb], writes=[st])
                                k.op("dve", lambda e: e.tensor_tensor(out=st[0:64, 16:20], in0=st[0:64, 4:8], in1=st[0:64, 0:4], op=ALU.subtract),
                                     reads=[st], writes=[st])
                                k.op("dve", lambda e: e.tensor_tensor(out=st[64:128, 16:20], in0=st[64:128, 8:12], in1=st[64:128, 0:4], op=ALU.subtract),
                                     reads=[st], writes=[st])
                                ex = exp_.next()
                                k.op("act", lambda e: e.activation(out=ex[:, 0:20], in_=st[:, 0:20], func=AF.Exp), reads=[st], writes=[ex])
                                k.op("act", lambda e: e.activation(out=ex[:, 20:24], in_=lb_blk, func=AF.Exp), reads=[lnb], writes=[ex], acc=True)
                                yield
                                GU, GUb, L = w4["GU"], w4["GUb"], w4["L"]
                                for h in range(4):
                                    k.op("dve", lambda e: e.tensor_scalar(out=GU[:, h, :], in0=U, scalar1=g_blk[:, h:h + 1], scalar2=None, op0=ALU.mult),
                                         reads=[cm, gt], writes=[GU], acc=(h > 0))
                                for h in range(4):
                                    k.op("dve", lambda e: e.scalar_tensor_tensor(out=GUb[:, h, :], in0=identm, scalar=lb_blk[:, h:h + 1],
                                                                                 in1=GU[:, h, :], op0=ALU.mult, op1=ALU.add),
                                         reads=[cm, lnb, GU], writes=[GUb], acc=(h > 0))
                                pa1 = ps.next()
                                k.op("pe", lambda e: e.matmul(pa1[:, :], lhsT=onesblk, rhs=GU[:].rearrange("p h i -> p (h i)"), start=True, stop=True),
                                     reads=[cm, GU], writes=[pa1])
                                pa2 = ps.next()
                                k.op("pe", lambda e: e.matmul(pa2[:, :], lhsT=onesblk, rhs=GUb[:].rearrange("p h i -> p (h i)"), start=True, stop=True),
                                     reads=[cm, GUb], writes=[pa2])
                                for h in range(4):
                                    k.op("dve", lambda e: e.tensor_scalar(out=L[:, h, :], in0=pa1[:, h * 128:(h + 1) * 128], scalar1=st[:, h:h + 1],
                                                                          scalar2=0.0, op0=ALU.subtract, op1=ALU.min),
                                         reads=[pa1, st], writes=[L], acc=(h > 0))
                                k.op("act", lambda e: e.activation(out=w4["tm"][:], in_=L[:], func=AF.Exp), reads=[L], writes=[w4["tm"]])
                                k.op("pool", lambda e: e.tensor_tensor(out=w4["E3"][:], in0=w4["tm"][:], in1=bc4(m_incl), op=ALU.mult),
                                     reads=[w4["tm"], cm], writes=[w4["E3"]])
                                for h in range(4):
                                    k.op("dve", lambda e: e.tensor_scalar(out=L[:, h, :], in0=pa1[:, h * 128:(h + 1) * 128], scalar1=st[:, 12 + h:13 + h],
                                                                          scalar2=0.0, op0=ALU.subtract, op1=ALU.max),
                                         reads=[pa1, st], writes=[L], acc=(h > 0))
                                k.op("act", lambda e: e.activation(out=w4["tm"][:], in_=L[:], func=AF.Exp, scale=-1.0), reads=[L], writes=[w4["tm"]])
                                k.op("pool", lambda e: e.tensor_tensor(out=w4["E1"][:], in0=w4["tm"][:], in1=bc4(m_a), op=ALU.mult),
                                     reads=[w4["tm"], cm], writes=[w4["E1"]])
                                for h in range(4):
                                    k.op("dve", lambda e: e.tensor_scalar(out=L[:, h, :], in0=pa2[:, h * 128:(h + 1) * 128], scalar1=st[:, h:h + 1],
                                                                          scalar2=0.0, op0=ALU.subtract, op1=ALU.min),
                                         reads=[pa2, st], writes=[L], acc=(h > 0))
                                k.op("act", lambda e: e.activation(out=w4["tm"][:], in_=L[:], func=AF.Exp), reads=[L], writes=[w4["tm"]])
                                k.op("pool", lambda e: e.tensor_tensor(out=w4["E2"][:], in0=w4["tm"][:], in1=bc4(m_at), op=ALU.mult),
                                     reads=[w4["tm"], cm], writes=[w4["E2"]])
                                yield
                                pk = [ps.next(), ps.next()]
                                for c_ in range(2):
                                    for par in range(2):
                                        k.op("pe", lambda e: e.matmul(pk[par][:, c_ * 128:(c_ + 1) * 128],
                                                                      lhsT=kT[par * 64:(par + 1) * 64, c_, tok0 + (off - off):tok0 + 128],
                                                                      rhs=kT[par * 64:(par + 1) * 64, c_, tok0:tok0 + 128], start=True, stop=True),
                                             reads=[kT], writes=[pk[par]])
                                A, AT, QK = w4["A"], w4["AT"], slot["QK"]
                                for par in range(2):
                                    k.op("dve", lambda e: e.tensor_tensor(out=h4(A[:])[:, :, par, :], in0=pk[par][:, 0:256].rearrange("p (c i) -> p c i", c=2),
                                                                          in1=h4(w4["E1"][:])[:, :, par, :], op=ALU.mult),
                                         reads=[pk[par], w4["E1"]], writes=[A], acc=(par > 0))
                                    k.op("dve", lambda e: e.tensor_tensor(out=h4(AT[:])[:, :, par, :], in0=pk[par][:, 0:256].rearrange("p (c i) -> p c i", c=2),
                                                                          in1=h4(w4["E2"][:])[:, :, par, :], op=ALU.mult),
                                         reads=[pk[par], w4["E2"]], writes=[AT], acc=(par > 0))
                                pq = [ps.next(), ps.next()]
                                for c_ in range(2):
                                    for par in range(2):
                                        k.op("pe", lambda e: e.matmul(pq[par][:, c_ * 128:(c_ + 1) * 128],
                                                                      lhsT=kT[par * 64:(par + 1) * 64, c_, tok0:tok0 + 128],
                                                                      rhs=qT[par * 64:(par + 1) * 64, c_, tok0:tok0 + 128], start=True, stop=True),
                                             reads=[kT, qT], writes=[pq[par]])
                                for par in range(2):
                                    k.op("dve", lambda e: e.tensor_tensor(out=h4(QK[:])[:, :, par, :], in0=pq[par][:, 0:256].rearrange("p (c i) -> p c i", c=2),
                                                                          in1=h4(w4["E3"][:])[:, :, par, :], op=ALU.mult),
                                         reads=[pq[par], w4["E3"]], writes=[QK], acc=(par > 0))
                                yield
                                X = Xp.next()
                                XT = XTp.next()
                                tm = w4["tm"]
                                k.op("pool", lambda e: e.tensor_tensor(out=tm[:], in0=A[:], in1=bc4(cm[:, 8, :]), op=ALU.mult), reads=[A, cm], writes=[tm])
                                k.op("dve", lambda e: e.scalar_tensor_tensor(out=X[:], in0=tm[:], scalar=-1.0, in1=bc4(identm), op0=ALU.mult, op1=ALU.add),
                                     reads=[tm, cm], writes=[X])
                                k.op("pool", lambda e: e.tensor_tensor(out=L[:], in0=AT[:], in1=bc4(cm[:, 8, :]), op=ALU.mult), reads=[AT, cm], writes=[L])
                                k.op("dve", lambda e: e.scalar_tensor_tensor(out=XT[:], in0=L[:], scalar=-1.0, in1=bc4(identm), op0=ALU.mult, op1=ALU.add),
                                     reads=[L, cm], writes=[XT])
                                As, ATs, Ps, Ps2 = w4["As"], w4["ATs"], w4["Ps"], w4["Ps2"]
                                for lev in range(5):
                                    ms = cm[:, 9 + lev, :]
                                    k.op("pool", lambda e: e.tensor_tensor(out=As[:], in0=A[:], in1=bc4(ms), op=ALU.mult), reads=[A, cm], writes=[As])
                                    k.op("pool", lambda e: e.tensor_tensor(out=ATs[:], in0=AT[:], in1=bc4(ms), op=ALU.mult), reads=[AT, cm], writes=[ATs])
                                    pP = ps.next()
                                    for h in range(4):
                                        k.op("pe", lambda e: e.matmul(pP[:, h * 128:(h + 1) * 128], lhsT=ATs[:, h, :], rhs=X[:, h, :], start=True, stop=True),
                                             reads=[ATs, X], writes=[pP], inc=(h == 3))
                                    k.op("act", lambda e: e.copy(out=Ps[:].rearrange("p h i -> p (h i)"), in_=pP[:, :]), reads=[pP], writes=[Ps])
                                    pP2 = ps.next()
                                    for h in range(4):
                                        k.op("pe", lambda e: e.matmul(pP2[:, h * 128:(h + 1) * 128], lhsT=As[:, h, :], rhs=XT[:, h, :], start=True, stop=True),
                                             reads=[As, XT], writes=[pP2], inc=(h == 3))
                                    k.op("act", lambda e: e.copy(out=Ps2[:].rearrange("p h i -> p (h i)"), in_=pP2[:, :]), reads=[pP2], writes=[Ps2])
                                    yield
                                    pX = ps.next()
                                    for h in range(4):
                                        k.op("pe", lambda e: e.matmul(pX[:, h * 128:(h + 1) * 128], lhsT=XT[:, h, :], rhs=Ps[:, h, :], start=True, stop=True),
                                             reads=[XT, Ps], writes=[pX], inc=(h == 3))
                                    pXT = ps.next()
                                    for h in range(4):
                                        k.op("pe", lambda e: e.matmul(pXT[:, h * 128:(h + 1) * 128], lhsT=X[:, h, :], rhs=Ps2[:, h, :], start=True, stop=True),
                                             reads=[X, Ps2], writes=[pXT], inc=(h == 3))
                                    Xn = Xp.next()
                                    XTn = XTp.next()
                                    k.op("dve", lambda e: e.tensor_tensor(out=Xn[:].rearrange("p h i -> p (h i)"), in0=X[:].rearrange("p h i -> p (h i)"),
                                                                          in1=pX[:, :], op=ALU.subtract), reads=[X, pX], writes=[Xn])
                                    k.op("dve", lambda e: e.tensor_tensor(out=XTn[:].rearrange("p h i -> p (h i)"), in0=XT[:].rearrange("p h i -> p (h i)"),
                                                                          in1=pXT[:, :], op=ALU.subtract), reads=[XT, pXT], writes=[XTn])
                                    X, XT = Xn, XTn
                                    yield
                                yield
                                vb, kbg = w64["vb"], w64["kbg"]
                                kdec, u_sb, wT = slot["kdec"], slot["u"], slot["wT"]

                                def bc64(ap):
                                    return ap.unsqueeze(2).to_broadcast([128, 4, 64])

                                k.op("dve", lambda e: e.tensor_tensor(out=vb[:], in0=v_tok[:, blk, :].rearrange("p (h d) -> p h d", h=4), in1=bc64(ex[:, 20:24]), op=ALU.mult),
                                     reads=[v_tok, ex], writes=[vb])
                                k.op("pool", lambda e: e.tensor_tensor(out=kbg[:], in0=k_tok[:, blk, :].rearrange("p (h d) -> p h d", h=4), in1=bc64(ex[:, 12:16]), op=ALU.mult),
                                     reads=[k_tok, ex], writes=[kbg])
                                k.op("pool", lambda e: e.tensor_tensor(out=kdec[:], in0=k_tok[:, blk, :].rearrange("p (h d) -> p h d", h=4), in1=bc64(ex[:, 16:20]), op=ALU.mult),
                                     reads=[k_tok, ex], writes=[kdec])
                                pu = ps.next()
                                for h in range(4):
                                    k.op("pe", lambda e: e.matmul(pu[:, h * 64:(h + 1) * 64], lhsT=XT[:, h, :], rhs=vb[:, h, :], start=True, stop=True),
                                         reads=[XT, vb], writes=[pu], inc=(h == 3))
                                k.op("act", lambda e: e.copy(out=u_sb[:].rearrange("p h d -> p (h d)"), in_=pu[:, 0:256]), reads=[pu], writes=[u_sb])
                                pwT = ps.next()
                                for h in range(4):
                                    c_ = h // 2
                                    k.op("pe", lambda e: e.matmul(pwT[:, h * 128:(h + 1) * 128], lhsT=kbg[:, 2 * c_:2 * c_ + 2, :].rearrange("p r d -> p (r d)"),
                                                                  rhs=XT[:, h, :], start=True, stop=True),
                                         reads=[kbg, XT], writes=[pwT], inc=(h == 3))
                                for h in range(4):
                                    par = h % 2
                                    evac(h, wT[par * 64:(par + 1) * 64, h, :], pwT[par * 64:(par + 1) * 64, h * 128:(h + 1) * 128], [pwT], [wT], acc=(h > 0))
                                tasks[d].append((blk, ex, slot))
                                yield
                            done[d] = True

                        def dn_rec_gen(d):
                            w64 = WK[d]["w64"]
                            ps = ps8
                            halves = (0, 1) if d == 0 else (1, 0)
                            vn, ty, ty2 = w64["vn"], w64["ty"], w64["ty2"]
                            while True:
                                if not tasks[d]:
                                    if done[d]:
                                        break
                                    yield
                                    continue
                                blk, ex, slot = tasks[d].pop(0)
                                QK, kdec, u_sb, wT = slot["QK"], slot["kdec"], slot["u"], slot["wT"]
                                tok0 = blk * 128
                                seq_first = (blk % BPS == 0) if d == 0 else (blk % BPS == BPS - 1)
                                seq_last = (blk % BPS == BPS - 1) if d == 0 else (blk % BPS == 0)
                                if seq_first and not lat:
                                    k.op("dve", lambda e: e.memset(S[d][:], 0.0), writes=[S[d]])
                                    k.op("act", lambda e: e.copy(out=Sb[d][:], in_=S[d][:]), reads=[S[d]], writes=[Sb[d]])
                                for half in halves:
                                    hb = half * 64
                                    ec = 4 if half == 0 else 8
                                    pw = [ps.next(), ps.next()]
                                    for h in range(4):
                                        c_, par = h // 2, h % 2
                                        k.op("pe", lambda e: e.matmul(pw[par][:, c_ * 64:(c_ + 1) * 64], lhsT=wT[par * 64:(par + 1) * 64, h, :],
                                                                      rhs=Sb[d][par * 64:(par + 1) * 64, c_, :], start=True, stop=True),
                                             reads=[wT, Sb[d]], writes=[pw[par]])
                                    for par in range(2):
                                        k.op("dve", lambda e: e.tensor_tensor(out=vn[:].rearrange("p (c r) d -> p c r d", c=2)[:, :, par, :],
                                                                              in0=u_sb[:].rearrange("p (c r) d -> p c r d", c=2)[:, :, par, :],
                                                                              in1=pw[par][:, 0:128].rearrange("p (c d) -> p c d", c=2), op=ALU.subtract),
                                             reads=[u_sb, pw[par]], writes=[vn], acc=(par > 0))
                                    yield
                                    pqs = [ps.next(), ps.next()]
                                    for h in range(4):
                                        c_, par = h // 2, h % 2
                                        k.op("pe", lambda e: e.matmul(pqs[par][:, c_ * 64:(c_ + 1) * 64], lhsT=qT[par * 64:(par + 1) * 64, c_, tok0:tok0 + 128],
                                                                      rhs=Sb[d][par * 64:(par + 1) * 64, c_, :], start=True, stop=True),
                                             reads=[qT, Sb[d]], writes=[pqs[par]])
                                    pqk = ps.next()
                                    for h in range(4):
                                        k.op("pe", lambda e: e.matmul(pqk[:, h * 64:(h + 1) * 64], lhsT=QK[:, h, :], rhs=vn[:, h, :], start=True, stop=True),
                                             reads=[QK, vn], writes=[pqk], inc=(h == 3))
                                    for par in range(2):
                                        k.op("dve", lambda e: e.tensor_tensor(out=ty[hb:hb + 64].rearrange("p (c r) d -> p c r d", c=2)[:, :, par, :],
                                                                              in0=pqs[par][hb:hb + 64, 0:128].rearrange("p (c d) -> p c d", c=2),
                                                                              in1=ex[hb:hb + 64, 0:4].rearrange("p (c r) -> p c r", c=2)[:, :, par].unsqueeze(2).to_broadcast([64, 2, 64]),
                                                                              op=ALU.mult),
                                             reads=[pqs[par], ex], writes=[ty], acc=(par > 0))
                                    if (blk, half) not in ywritten:
                                        ywritten.add((blk, half))
                                        k.op("dve", lambda e: e.tensor_tensor(out=yacc[hb:hb + 64, blk, :], in0=ty[hb:hb + 64].rearrange("p h d -> p (h d)"),
                                                                              in1=pqk[hb:hb + 64, 0:256], op=ALU.add),
                                             reads=[ty, pqk], writes=[yacc], acc=True)
                                    else:
                                        k.op("dve", lambda e: e.tensor_tensor(out=ty2[hb:hb + 64].rearrange("p h d -> p (h d)"),
                                                                              in0=ty[hb:hb + 64].rearrange("p h d -> p (h d)"),
                                                                              in1=pqk[hb:hb + 64, 0:256], op=ALU.add),
                                             reads=[ty, pqk], writes=[ty2])
                                        k.op("pool", lambda e: e.tensor_tensor(out=yacc[hb:hb + 64, blk, :], in0=yacc[hb:hb + 64, blk, :],
                                                                               in1=ty2[hb:hb + 64].rearrange("p h d -> p (h d)"), op=ALU.add),
                                             reads=[yacc, ty2], writes=[yacc])
                                    yield
                                    pst = ps.next()
                                    for h in range(4):
                                        c_ = h // 2
                                        k.op("pe", lambda e: e.matmul(pst[:, h * 64:(h + 1) * 64],
                                                                      lhsT=kdec[hb:hb + 64, 2 * c_:2 * c_ + 2, :].rearrange("p r d -> p (r d)"),
                                                                      rhs=vn[hb:hb + 64, h, :], start=True, stop=True),
                                             reads=[kdec, vn], writes=[pst], inc=(h == 3))
                                    for h in range(4):
                                        c_, par = h // 2, h % 2
                                        k.op("dve", lambda e: e.scalar_tensor_tensor(out=S[d][par * 64:(par + 1) * 64, c_, :], in0=S[d][par * 64:(par + 1) * 64, c_, :],
                                                                                     scalar=ex[par * 64:(par + 1) * 64, ec + h:ec + h + 1],
                                                                                     in1=pst[par * 64:(par + 1) * 64, h * 64:(h + 1) * 64],
                                                                                     op0=ALU.mult, op1=ALU.add),
                                             reads=[S[d], ex, pst], writes=[S[d]])
                                    k.op("act", lambda e: e.copy(out=Sb[d][:], in_=S[d][:]), reads=[S[d]], writes=[Sb[d]])
                                    yield
                                if seq_last and not lat:
                                    for h in range(4):
                                        c_, par = h // 2, h % 2
                                        k.dma("pool", O["new_sdn"][blk // BPS, l, d, h, :, :], S[d][par * 64:(par + 1) * 64, c_, :], reads=[S[d]])
                                busy[d] -= 1

                        tasks = [[], []]
                        busy = [0, 0]
                        done = [False, False]
                        gens = [dn_intra_gen(0), dn_intra_gen(1), dn_rec_gen(0), dn_rec_gen(1)]
                        while gens:
                            for g_ in list(gens):
                                try:
                                    next(g_)
                                except StopIteration:
                                    gens.remove(g_)
                        k.barrier()
                    if debug.get("dn_stop", 9) <= 3:
                        continue
                    with ExitStack() as fin:
                        zp = k.pool("zd", [128, 256], F32, 2, fin)
                        y2p = k.pool("y2d", [128, 256], F32, 4, fin)
                        ssp = k.pool("ssumd", [128, 8], F32, 2, fin)
                        osp = k.pool("osd", [128, 2, 128], BF16, 2, fin)
                        for blk in range(NB):
                            tok0 = blk * 128
                            z = zp.next()
                            k.dma("sp", z[:], PTM[l][off + tok0:off + tok0 + 128, C_DNZ:C_DNZ + 256], reads=[PTM[l]], writes=[z])
                            ssum = ssp.next()
                            k.op("dve", lambda e: e.memset(ssum[:], 0.0), writes=[ssum])
                            t1 = y2p.next()
                            for h in range(4):
                                k.op("act", lambda e: e.activation(out=t1[:, h * 64:(h + 1) * 64], in_=yacc[:, blk, h * 64:(h + 1) * 64], func=AF.Square,
                                                                   accum_out=ssum[:, h:h + 1]), reads=[yacc], writes=[t1, ssum])
                            k.op("act", lambda e: e.activation(out=ssum[:, 4:8], in_=ssum[:, 0:4], func=AF.Sqrt, scale=1.0 / 64, bias=epsb[:, 0:1]),
                                 reads=[ssum, epsb], writes=[ssum])
                            k.op("dve", lambda e: e.reciprocal(out=ssum[:, 0:4], in_=ssum[:, 4:8]), reads=[ssum], writes=[ssum])
                            t2 = y2p.next()
                            k.op("dve", lambda e: e.tensor_tensor(out=t2[:].rearrange("p (h d) -> p h d", h=4),
                                                                  in0=yacc[:, blk, :].rearrange("p (h d) -> p h d", h=4),
                                                                  in1=ssum[:, 0:4].unsqueeze(2).to_broadcast([128, 4, 64]), op=ALU.mult),
                                 reads=[yacc, ssum], writes=[t2])
                            t3 = y2p.next()
                            k.op("pool", lambda e: e.tensor_tensor(out=t3[:].rearrange("p (h d) -> p h d", h=4),
                                                                   in0=t2[:].rearrange("p (h d) -> p h d", h=4),
                                                                   in1=nw1[:].unsqueeze(1).to_broadcast([128, 4, 64]), op=ALU.mult),
                                 reads=[t2, nw1], writes=[t3])
                            sz = y2p.next()
                            k.op("act", lambda e: e.activation(out=sz[:], in_=z[:], func=AF.Silu), reads=[z], writes=[sz])
                            k.op("dve", lambda e: e.tensor_tensor(out=t1[:], in0=t3[:], in1=sz[:], op=ALU.mult), reads=[t3, sz], writes=[t1])
                            os_ = osp.next()
                            for c in range(2):
                                p = ps.next()
                                k.op("pe", lambda e: e.transpose(p[:, 0:128], t1[:, c * 128:(c + 1) * 128], ident[:]), reads=[t1, ident], writes=[p])
                                evac(c, os_[:, c, :], p[:, 0:128], [p], [os_], acc=(c > 0))
                            k.dma("pool", MIX[l][0:256, off + tok0:off + tok0 + 128].rearrange("(c p) t -> p c t", p=128), os_[:],
                                  reads=[os_], writes=[MIX[l]], acc=True)
                        k.barrier()
                k.barrier()

        for l in range(DEPTH):
            with ExitStack() as ph:
                cs = k.tile("cs", [128, 8, 2], F32, ph)
                for kind in range(2):
                    k.dma("sp", cs[:, :, kind],
                          I["cvec"][kind].rearrange("(c p) -> p c", p=128),
                          writes=[cs], acc=(kind > 0), allow_slow_non_contiguous=True)
                k.op("act", lambda e: e.activation(out=cs[:], in_=cs[:], func=AF.Silu), reads=[cs], writes=[cs])
                bad = k.tile("bad", [128, 48], F32, ph)
                k.dma("sp", bad[:], I["b_ada"][l].rearrange("(j p) -> p j", p=128),
                      writes=[bad], allow_slow_non_contiguous=True)
                nw = k.tile("nw", [128, 2, 8], F32, ph)
                k.dma("sp", nw[:, 0, :], I["norm1_w"][l].rearrange("(c p) -> p c", p=128),
                      writes=[nw], allow_slow_non_contiguous=True)
                k.dma("sp", nw[:, 1, :], I["norm2_w"][l].rearrange("(c p) -> p c", p=128),
                      writes=[nw], acc=True, allow_slow_non_contiguous=True)
                ada = k.tile("ada", [128, 48, 2], F32, ph)
                wap = k.pool("wap", [128, 8, 512], F32, 2, ph)
                for pc in range(12):
                    wa = wap.next()
                    k.dma("sp" if pc % 2 == 0 else "pool", wa[:],
                          I["w_ada"][l][:, pc * 512:(pc + 1) * 512].rearrange("(c p) n -> p c n", p=128),
                          writes=[wa])
                    for jj in range(4):
                        j = pc * 4 + jj
                        p = ps.next()
                        for c in range(8):
                            k.op("pe", lambda e: e.matmul(p[:, 0:2], lhsT=wa[:, c, jj * 128:(jj + 1) * 128],
                                                          rhs=cs[:, c, :], start=(c == 0), stop=(c == 7)),
                                 reads=[wa, cs], writes=[p], inc=(c == 7))
                        k.op("dve", lambda e: e.tensor_scalar(out=ada[:, j, :], in0=p[:, 0:2],
                                                              scalar1=bad[:, j:j + 1], scalar2=None, op0=ALU.add),
                             reads=[p, bad], writes=[ada], acc=(j > 0))
                m = mod[l]
                for (dst, srcj, nwi) in ((0, 8, 0), (3, 32, 1)):
                    for kind in range(2):
                        k.op("dve", lambda e: e.scalar_tensor_tensor(
                            out=m[:, dst, :, kind], in0=ada[:, srcj:srcj + 8, kind], scalar=1.0,
                            in1=nw[:, nwi, :], op0=ALU.add, op1=ALU.mult),
                            reads=[ada, nw], writes=[m], acc=True)
                for (dst, srcj) in ((1, 0), (2, 16), (4, 24), (5, 40)):
                    k.op("dve", lambda e: e.tensor_copy(out=m[:, dst, :, :], in_=ada[:, srcj:srcj + 8, :]),
                         reads=[ada], writes=[m], acc=True)
                dump("mod%d" % l, m, m[:].rearrange("p a c k -> p (a c k)"), [128, 96])
                dump("ada%d" % l, ada, ada[:].rearrange("p a k -> p (a k)"), [128, 96])
                k.barrier()

            with ExitStack() as ph:
                wfm = k.tile("wfm", [128, 8, NFM], BF16, ph)
                wtm = k.tile("wtm", [128, 8, NTM], BF16, ph)
                win = I["w_in"][l]

                def wload(dst, d0, s0, n, first=False):
                    k.dma("pool", dst[:, :, d0:d0 + n], win[:, s0:s0 + n].rearrange("(c p) n -> p c n", p=128),
                          writes=[dst], acc=True)

                wload(wfm, R_DNQ, 0, 768)
                wload(wfm, R_SSX, 1456 + 256, 512)
                wload(wfm, R_MQ, 1040, 384)
                wload(wfm, R_MKPE, 1424, 32)
                wload(wfm, R_MKPE + 32, 1424 + 16, 16)
                wload(wfm, R_MKPE + 48, 1424, 16)
                wload(wfm, R_SWQ, 2232, 256)
                for h in range(4):
                    wload(wfm, R_SWQS + h * 64, 2232 + h * 64 + 32, 32)
                    wload(wfm, R_SWQS + h * 64 + 32, 2232 + h * 64, 32)
                wload(wfm, R_SWK, 2488, 128)
                for h in range(2):
                    wload(wfm, R_SWKS + h * 64, 2488 + h * 64 + 32, 32)
                    wload(wfm, R_SWKS + h * 64 + 32, 2488 + h * 64, 32)
                wload(wtm, C_DNZ, 768, 256)
                wload(wtm, C_SSZ, 1456, 256)
                wload(wtm, C_BETA, 1024, 16)
                wload(wtm, C_DT, 2224, 8)
                wload(wtm, C_SWV, 2616, 128)

                xtp = k.pool("xt", [128, 8, 512], F32, 2, ph)
                sqp = k.pool("sq", [128, 8, 512], BF16, 1, ph)
                hbp = k.pool("hb", [128, 8, 512], BF16, 2, ph)
                rsp = k.pool("rstd", [128, 512], F32, 2, ph)
                tmpp = k.pool("tmp", [128, 512], F32, 3, ph)
                fstg = k.pool("fstg", [128, 512], BF16, 4, ph)
                tstg = k.pool("tstg", [128, NTM], F32, 2, ph)
                fm_chunks = [(r, 128) for r in range(0, R_MKPE, 128)] + [(R_MKPE, 64)] + \
                            [(r, 128) for r in range(R_SWQ, NFM, 128)]
                def loadA(t0):
                    xt = xtp.next()
                    k.dma("sp", xt[:], X[l][:, t0:t0 + 512].rearrange("(c p) t -> p c t", p=128),
                          reads=[X[l]], writes=[xt])
                    return xt

                def prepA(t0, xt):
                    sq = sqp.next()
                    rstd = rsp.next()
                    rms_stats(None, xt, 512, sq, rstd)
                    hb = hbp.next()
                    mod_norm(xt, 512, rstd, tmpp, hb, mod[l], 0, 1, kind_of_tile(t0))
                    return hb

                for t0, xt, hb in pipelined2(T0S, loadA, prepA):
                    kind = kind_of_tile(t0)
                    for ci, (r0, n) in enumerate(fm_chunks):
                        p = ps.next()
                        for c in range(8):
                            k.op("pe", lambda e: e.matmul(p[0:n, :], lhsT=wfm[:, c, r0:r0 + n], rhs=hb[:, c, :],
                                                          start=(c == 0), stop=(c == 7)),
                                 reads=[wfm, hb], writes=[p], inc=(c == 7))
                        fs = fstg.next()
                        if ci % 2:
                            k.op("act", lambda e: e.copy(out=fs[0:n, :], in_=p[0:n, :]), reads=[p], writes=[fs])
                        else:
                            k.op("dve", lambda e: e.tensor_copy(out=fs[0:n, :], in_=p[0:n, :]), reads=[p], writes=[fs])
                        k.dma("sp", PFM[l][r0:r0 + n, t0:t0 + 512], fs[0:n, :], reads=[fs], writes=[PFM[l]], acc=True)
                    for b in range(4):
                        ts_ = tstg.next()
                        for g, (c0, n) in enumerate(((0, 512), (512, NTM - 512))):
                            p = ps.next()
                            for c in range(8):
                                k.op("pe", lambda e: e.matmul(p[:, 0:n], lhsT=hb[:, c, b * 128:(b + 1) * 128],
                                                              rhs=wtm[:, c, c0:c0 + n], start=(c == 0), stop=(c == 7)),
                                     reads=[wtm, hb], writes=[p], inc=(c == 7))
                            if g == 0:
                                k.op("act", lambda e: e.copy(out=ts_[:, c0:c0 + n], in_=p[:, 0:n]),
                                     reads=[p], writes=[ts_])
                            else:
                                k.op("dve", lambda e: e.tensor_copy(out=ts_[:, c0:c0 + n], in_=p[:, 0:n]),
                                     reads=[p], writes=[ts_], acc=True)
                        k.dma("pool", PTM[l][t0 + b * 128:t0 + (b + 1) * 128, :], ts_[:], reads=[ts_],
                              writes=[PTM[l]], acc=True)
                k.barrier()

            if debug.get("zero_mix"):
                with ExitStack() as ph:
                    z = k.tile("z", [128, 8, 512], BF16, ph)
                    k.op("dve", lambda e: e.memset(z[:], 0.0), writes=[z])
                    for t0 in range(0, TTOT, 512):
                        k.dma("sp", MIX[l][:, t0:t0 + 512].rearrange("(c p) t -> p c t", p=128), z[:],
                              reads=[z], writes=[MIX[l]], acc=True)
                    k.barrier()

            if not debug.get("skip_mla"):
                mla_phase(l)
            if not debug.get("skip_swa"):
                swa_phase(l)
            if not debug.get("skip_ssd"):
                ssd_phase(l)
            if not debug.get("skip_dn"):
                dn_phase(l)

            with ExitStack() as ph:
                wo = k.tile("wo", [128, 8, D], BF16, ph)
                k.dma("pool", wo[:], I["w_out"][l].rearrange("(c p) n -> p c n", p=128), writes=[wo])
                xtp = k.pool("xt", [128, 8, 512], F32, 2, ph)
                mxp = k.pool("mx", [128, 8, 512], BF16, 2, ph)
                def loadC1(t0):
                    xt = xtp.next()
                    mx = mxp.next()
                    k.dma("sp", xt[:], X[l][:, t0:t0 + 512].rearrange("(c p) t -> p c t", p=128),
                          reads=[X[l]], writes=[xt])
                    k.dma("sp", mx[:], MIX[l][:, t0:t0 + 512].rearrange("(c p) t -> p c t", p=128),
                          reads=[MIX[l]], writes=[mx])
                    return xt, mx

                for t0, (xt, mx) in pipelined(T0S, loadC1):
                    kind = kind_of_tile(t0)
                    for co in range(8):
                        p = ps.next()
                        for c in range(8):
                            k.op("pe", lambda e: e.matmul(p[:, :], lhsT=wo[:, c, co * 128:(co + 1) * 128],
                                                          rhs=mx[:, c, :], start=(c == 0), stop=(c == 7)),
                                 reads=[wo, mx], writes=[p], inc=(c == 7))
                        k.op("dve", lambda e: e.scalar_tensor_tensor(
                            out=xt[:, co, :], in0=p[:, :], scalar=mod[l][:, 2, co, kind:kind + 1],
                            in1=xt[:, co, :], op0=ALU.mult, op1=ALU.add),
                            reads=[p, mod[l], xt], writes=[xt])
                    k.dma("pool", XA[:, t0:t0 + 512].rearrange("(c p) t -> p c t", p=128), xt[:],
                          reads=[xt], writes=[XA], acc=True)
                k.barrier()

            HJ = 11
            for half in range(2):
                src2 = XA if half == 0 else XB
                dst2 = XB if half == 0 else X[l + 1]
                last = (half == 1 and l == DEPTH - 1)
                with ExitStack() as ph:
                    wg = k.tile("wg", [128, 8, 2, HJ * 128], BF16, ph)
                    wd = k.tile("wd", [128, HJ, D], BF16, ph)
                    j0 = half * HJ * 128
                    for gu in range(2):
                        k.dma("pool", wg[:, :, gu, :],
                              I["w_gate_up"][l][:, gu * FF + j0:gu * FF + j0 + HJ * 128].rearrange(
                                  "(c p) n -> p c n", p=128), writes=[wg], acc=True)
                    k.dma("pool", wd[:], I["w_down"][l][j0:j0 + HJ * 128, :].rearrange("(j p) n -> p j n", p=128),
                          writes=[wd])
                    xtp = k.pool("xt", [128, 8, 512], F32, 3 if half == 0 else 2, ph)
                    x2p = k.pool("x2", [128, 8, 512], F32, 3, ph) if half == 1 else None
                    sqp = k.pool("sq", [128, 8, 512], BF16, 1, ph)
                    hbp = k.pool("hb", [128, 8, 512], BF16, 2, ph)
                    rsp = k.pool("rstd", [128, 512], F32, 2, ph)
                    tmpp = k.pool("tmp", [128, 512], F32, 3, ph)
                    acp = k.pool("act", [128, HJ, 512], BF16, 1, ph)
                    ostg = k.pool("ostg", [128, D], F32, 2, ph) if last else None
                    def loadF(t0):
                        xt = xtp.next()
                        k.dma("sp", xt[:], XA[:, t0:t0 + 512].rearrange("(c p) t -> p c t", p=128),
                              reads=[XA], writes=[xt])
                        if half == 1:
                            x2 = x2p.next()
                            k.dma("sp", x2[:], XB[:, t0:t0 + 512].rearrange("(c p) t -> p c t", p=128),
                                  reads=[XB], writes=[x2])
                        else:
                            x2 = xt
                        return xt, x2

                    def prepF(t0, ld):
                        sq = sqp.next()
                        rstd = rsp.next()
                        rms_stats(None, ld[0], 512, sq, rstd)
                        hb = hbp.next()
                        mod_norm(ld[0], 512, rstd, tmpp, hb, mod[l], 3, 4, kind_of_tile(t0))
                        return hb

                    for t0, (xt, x2), hb in pipelined2(T0S, loadF, prepF):
                        kind = kind_of_tile(t0)
                        sq = sqp.tiles[0]
                        ac = acp.next()
                        for j in range(HJ):
                            pg = ps.next()
                            pu = ps.next()
                            for gu, pp in ((0, pg), (1, pu)):
                                for c in range(8):
                                    k.op("pe", lambda e: e.matmul(pp[:, :], lhsT=wg[:, c, gu, j * 128:(j + 1) * 128],
                                                                  rhs=hb[:, c, :], start=(c == 0), stop=(c == 7)),
                                         reads=[wg, hb], writes=[pp], inc=(c == 7))
                            tmp = tmpp.next()
                            k.op("act", lambda e: e.activation(out=tmp[:], in_=pg[:], func=AF.Silu),
                                 reads=[pg], writes=[tmp])
                            k.op("dve", lambda e: e.tensor_tensor(out=ac[:, j, :], in0=tmp[:], in1=pu[:], op=ALU.mult),
                                 reads=[tmp, pu], writes=[ac], acc=(j > 0))
                        for co in range(8):
                            p = ps.next()
                            for j in range(HJ):
                                k.op("pe", lambda e: e.matmul(p[:, :], lhsT=wd[:, j, co * 128:(co + 1) * 128],
                                                              rhs=ac[:, j, :], start=(j == 0), stop=(j == HJ - 1)),
                                     reads=[wd, ac], writes=[p], inc=(j == HJ - 1))
                            k.op("dve", lambda e: e.scalar_tensor_tensor(
                                out=x2[:, co, :], in0=p[:, :], scalar=mod[l][:, 5, co, kind:kind + 1],
                                in1=x2[:, co, :], op0=ALU.mult, op1=ALU.add),
                                reads=[p, mod[l], x2], writes=[x2])
                        if not last:
                            k.dma("pool", dst2[:, t0:t0 + 512].rearrange("(c p) t -> p c t", p=128), x2[:],
                                  reads=[x2], writes=[dst2], acc=True)
                        else:
                            rstd = rsp.next()
                            rms_stats(None, x2, 512, sq, rstd)
                            for c in range(8):
                                k.op("dve", lambda e: e.scalar_tensor_tensor(
                                    out=x2[:, c, :], in0=x2[:, c, :], scalar=fnw[:, c:c + 1], in1=rstd[:, :],
                                    op0=ALU.mult, op1=ALU.mult), reads=[x2, fnw, rstd], writes=[x2])
                            for b in range(4):
                                os_ = ostg.next()
                                for c in range(8):
                                    p = ps.next()
                                    k.op("pe", lambda e: e.transpose(p[:, 0:128], x2[:, c, b * 128:(b + 1) * 128], ident[:]),
                                         reads=[x2, ident], writes=[p])
                                    if c % 2:
                                        k.op("act", lambda e: e.copy(out=os_[:, c * 128:(c + 1) * 128], in_=p[:, 0:128]),
                                             reads=[p], writes=[os_], acc=(c > 0))
                                    else:
                                        k.op("dve", lambda e: e.tensor_copy(out=os_[:, c * 128:(c + 1) * 128], in_=p[:, 0:128]),
                                             reads=[p], writes=[os_], acc=(c > 0))
                                tok = t0 + b * 128
                                dst = (O["y_ctx"][tok:tok + 128, :] if tok < LOFF
                                       else O["y_lat"][tok - LOFF:tok - LOFF + 128, :])
                                k.dma("pool", dst, os_[:], reads=[os_])
                    k.barrier()
        k.barrier()
    return nc


_CACHE = {}


def _rope_tables():
    out = {}
    pos = np.arange(TL)
    row_ids = (pos // 64).astype(np.float32)
    col_ids = (pos % 64).astype(np.float32)
    for name, rot in (("rope_m", 32), ("rope_s", 64)):
        nf = rot // 4
        inv = (10000.0 ** (-np.arange(nf, dtype=np.float32) / nf)).astype(np.float32)
        ang = np.concatenate([row_ids[:, None] * inv, col_ids[:, None] * inv], axis=-1).astype(np.float32)
        c = np.cos(ang).astype(np.float32).T
        sn = np.sin(ang).astype(np.float32).T
        out[name] = np.ascontiguousarray(np.stack([np.concatenate([c, c], 0), np.concatenate([-sn, sn], 0)]))
    return out


def kernel(**inputs):
    x_prompt = np.ascontiguousarray(inputs["x_prompt"], dtype=np.float32)
    x_sample = np.ascontiguousarray(inputs["x_sample"], dtype=np.float32)
    dbg = inputs.pop("_debug", None) if "_debug" in inputs else None
    if "nc" not in _CACHE or dbg:
        _CACHE["nc"] = build_program(dbg)
    nc = _CACHE["nc"]
    ident = np.eye(128, dtype=np.float32)
    shared = {}
    for name in ["w_ada", "b_ada", "norm1_w", "norm2_w", "final_norm_w", "w_in", "w_out",
                 "w_gate_up", "w_down"]:
        shared[name] = np.ascontiguousarray(inputs[name], dtype=np.float32)
    for name in ["mla_q_norm_w", "mla_w_uq", "mla_kv_norm_w", "mla_w_ukv", "swa_sinks",
                 "dn_conv_w", "dn_a_log", "dn_dt_bias", "dn_norm_w", "ssm_conv_w", "ssm_conv_b", "ssm_a_log", "ssm_dt_bias", "ssm_d", "ssm_norm_w"]:
        shared[name] = np.ascontiguousarray(inputs[name], dtype=np.float32)
    shared.update(_rope_tables())
    kl = np.arange(128)[:, None]
    ql = np.arange(128)[None, :]
    msk = np.zeros((6, 128, 512), np.float32)
    for r in range(6):
        for j in range(4):
            dd = r - 1 - j
            if dd == -1:
                msk[r, :, j * 128:(j + 1) * 128] = (kl >= ql)
            elif dd == 0:
                msk[r, :, j * 128:(j + 1) * 128] = 1.0
            elif dd == 1:
                msk[r, :, j * 128:(j + 1) * 128] = (kl <= ql)
    shared["swa_mask"] = msk
    ii = np.arange(128)
    same = (ii[:, None] // 64) == (ii[None, :] // 64)
    cmask = np.zeros((14, 128, 128), np.float32)
    cmask[5] = same & (ii[:, None] < ii[None, :])
    cmask[6] = same & (ii[:, None] > ii[None, :])
    cmask[7] = np.eye(128, dtype=np.float32)
    for lev, sz_ in enumerate((1, 2, 4, 8, 16, 32)):
        cmask[8 + lev] = ((ii[:, None] // (2 * sz_)) == (ii[None, :] // (2 * sz_))) & ((ii[:, None] // sz_) != (ii[None, :] // sz_))
    cmask[0] = same & (ii[:, None] <= ii[None, :])
    cmask[1] = same & (ii[:, None] >= ii[None, :])
    cmask[2] = same
    cmask[3, 0:64, :] = 1.0
    cmask[4, 64:128, :] = 1.0
    shared["cmask"] = cmask
    in_maps = []
    for core in range(8):
        b = core % 4
        m = dict(shared)
        m["x_ctx"] = x_prompt[core * NCTX:(core + 1) * NCTX].reshape(NCTX * TC, D)
        m["x_lat"] = x_sample[b]
        m["cvec"] = np.stack([inputs["c_ctx"], inputs["c"][b]]).astype(np.float32)
        m["ident"] = ident
        m["cache_ckv"] = np.ascontiguousarray(inputs["cache_mla_ckv"][b], dtype=np.float32)
        m["cache_kpe"] = np.ascontiguousarray(inputs["cache_mla_kpe"][b], dtype=np.float32)
        m["state_ssm"] = np.ascontiguousarray(inputs["state_ssm"][b], dtype=np.float32)
        m["state_dn"] = np.ascontiguousarray(inputs["state_dn"][b], dtype=np.float32)
        m["cache_swk"] = np.ascontiguousarray(inputs["cache_swa_k"][b], dtype=np.float32)
        m["cache_swv"] = np.ascontiguousarray(inputs["cache_swa_v"][b], dtype=np.float32)
        in_maps.append(m)
    res = run_bass_kernel_spmd(nc, in_maps, core_ids=list(range(8)))
    r = res.results
    y_prompt = np.concatenate([r[c]["y_ctx"].reshape(NCTX, TC, D) for c in range(8)], axis=0)
    y_sample = np.stack([r[b]["y_lat"] for b in range(4)], axis=0)
    def gath(name):
        return np.concatenate([np.asarray(r[c][name], dtype=np.float32) for c in range(8)], axis=0)

    new_ckv = gath("new_ckv") if "new_ckv" in r[0] else np.zeros((32, DEPTH, TC, 128), np.float32)
    new_kpe = gath("new_kpe") if "new_kpe" in r[0] else np.zeros((32, DEPTH, TC, 32), np.float32)
    new_swk = gath("new_swk") if "new_swk" in r[0] else np.zeros((32, DEPTH, TC, 2, 64), np.float32)
    new_swv = gath("new_swv") if "new_swv" in r[0] else np.zeros((32, DEPTH, TC, 2, 64), np.float32)
    new_sdn = gath("new_sdn") if "new_sdn" in r[0] else np.zeros((32, DEPTH, 2, 4, 64, 64), np.float32)
    new_ssm = gath("new_ssm") if "new_ssm" in r[0] else np.zeros((32, DEPTH, 2, 4, 64, 64), np.float32)
    outs = (y_prompt, y_sample, new_sdn, new_ckv, new_kpe, new_ssm, new_swk, new_swv)
    if dbg:
        return outs + (r,)
    return outs
```

```python
import numpy as np
import concourse.bass as bass
import concourse.mybir as mybir
from concourse.bass_utils import run_bass_kernel_spmd
from contextlib import ExitStack

F32 = mybir.dt.float32
BF16 = mybir.dt.bfloat16
AF = mybir.ActivationFunctionType
ALU = mybir.AluOpType
AX = mybir.AxisListType

D = 1024
DEPTH = 2
NCTX = 4
TC = 256
TL = 4096
TTOT = NCTX * TC + TL
LOFF = NCTX * TC
FF = 2816
EPS = 1e-6
NFM = 2496
NTM = 664
R_DNQ, R_DNK, R_DNV = 0, 256, 512
R_SSX, R_SSB, R_SSC = 768, 1024, 1152
R_MQ, R_MKV, R_MKPE = 1280, 1536, 1664
R_SWQ, R_SWQS, R_SWK, R_SWKS = 1728, 1984, 2240, 2368
C_DNZ, C_SSZ, C_BETA, C_ALPHA, C_DT, C_SWV = 0, 256, 512, 520, 528, 536


class Res:
    __slots__ = ("name", "w", "r", "t", "base", "full")

    def __init__(self, name, t=None):
        self.name = name
        self.w = {}
        self.r = {}
        self.base = {}
        self.full = None
        self.t = t

    def __getitem__(self, key):
        return self.t[key]


class Pool:
    def __init__(self, tiles):
        self.tiles = tiles
        self.i = 0

    def next(self):
        t = self.tiles[self.i]
        self.i = (self.i + 1) % len(self.tiles)
        return t


class K:
    def __init__(self, nc, es, ndma=12):
        self.nc = nc
        self.es = es
        self.engs = {"pe": nc.tensor, "act": nc.scalar, "dve": nc.vector,
                     "pool": nc.gpsimd, "sp": nc.sync}
        self.semh = {}
        self.cnt = {}
        self.waited = {e: {} for e in self.engs}
        for e in ["pe", "act", "dve", "pool"]:
            self.semh[e] = es.enter_context(nc.semaphore("s_" + e))
            self.cnt[e] = 0
        self.dq = {}
        for q in ["sp", "pool"]:
            sems = []
            for i in range(ndma):
                key = ("d", q, i)
                self.semh[key] = es.enter_context(nc.semaphore("d_%s_%d" % (q, i)))
                self.cnt[key] = 0
                sems.append(key)
            self.dq[q] = {"sems": sems, "rr": 0}
        self.uid = 0

    def tile(self, name, shape, dtype, es=None):
        self.uid += 1
        t = (es or self.es).enter_context(
            self.nc.sbuf_tensor("%s_%d" % (name, self.uid), list(shape), dtype))
        return Res(name, t)

    def ptile(self, name, shape, dtype=F32, es=None):
        self.uid += 1
        t = (es or self.es).enter_context(
            self.nc.psum_tensor("%s_%d" % (name, self.uid), list(shape), dtype))
        return Res(name, t)

    def dram(self, name, shape, dtype, kind="Internal"):
        if name in getattr(self, "ext", ()):
            kind = "ExternalOutput"
        t = self.nc.dram_tensor(name, list(shape), dtype, kind=kind)
        return Res(name, t.ap())

    def pool(self, name, shape, dtype, n, es=None, psum=False):
        return Pool([(self.ptile if psum else self.tile)("%s%d" % (name, i), shape, dtype, es)
                     for i in range(n)])

    def _wait(self, eng, need):
        for s, v in need.items():
            if self.waited[eng].get(s, 0) < v:
                self.engs[eng].wait_ge(self.semh[s], v)
                self.waited[eng][s] = v

    def _deps(self, eng, reads, writes, acc=False):
        need = {}

        def add(s, v, war=False):
            if s == eng and eng == "pe":
                return
            if need.get(s, 0) < v:
                need[s] = v

        for t in reads:
            for s, v in t.w.items():
                add(s, v)
        for t in writes:
            if acc:
                if t.full is not None:
                    add(t.full[0], t.full[1])
                for s, v in t.base.items():
                    add(s, v, True)
            else:
                b = {}
                for s, v in t.w.items():
                    add(s, v)
                    b[s] = max(b.get(s, 0), v)
                for s, v in t.r.items():
                    add(s, v, True)
                    b[s] = max(b.get(s, 0), v)
                t.base = b
        self._wait(eng, need)

    def _mark(self, key, val, reads, writes, acc=False):
        for t in reads:
            if t.r.get(key, 0) < val:
                t.r[key] = val
        for t in writes:
            if acc:
                if t.w.get(key, 0) < val:
                    t.w[key] = val
            else:
                t.w = {key: val}
                t.r = {}
                t.full = (key, val)

    def op(self, eng, fn, reads=(), writes=(), inc=True, acc=False):
        self._deps(eng, reads, writes, acc)
        ins = fn(self.engs[eng])
        if inc:
            self.cnt[eng] += 1
            ins.then_inc(self.semh[eng], 1)
            val = self.cnt[eng]
        else:
            val = self.cnt[eng] + 1
        self._mark(eng, val, reads, writes, acc)
        return ins

    def dma(self, q, out, in_, reads=(), writes=(), acc=False, **kw):
        d = self.dq[q]
        key = d["sems"][d["rr"]]
        d["rr"] = (d["rr"] + 1) % len(d["sems"])
        cur = self.cnt[key]
        if cur > 0:
            self._wait(q, {key: cur})
        self._deps(q, reads, writes, acc)
        ins = self.engs[q].dma_start(out=out, in_=in_, **kw)
        ins.then_inc(self.semh[key], 16)
        self.cnt[key] = cur + 16
        self._mark(key, cur + 16, reads, writes, acc)

    def barrier(self):
        need = {s: v for s, v in self.cnt.items() if v > 0}
        for e in self.engs:
            self._wait(e, dict(need))


def build_program(debug=None):
    debug = debug or {}
    nc = bass.Bass("TRN2", target_bir_lowering=False)

    def din(name, shape):
        return nc.dram_tensor(name, list(shape), F32, kind="ExternalInput").ap()

    def dout(name, shape):
        return nc.dram_tensor(name, list(shape), F32, kind="ExternalOutput").ap()

    I = {}
    I["x_ctx"] = din("x_ctx", [NCTX * TC, D])
    I["x_lat"] = din("x_lat", [TL, D])
    I["cvec"] = din("cvec", [2, D])
    I["w_ada"] = din("w_ada", [DEPTH, D, 6 * D])
    I["b_ada"] = din("b_ada", [DEPTH, 6 * D])
    I["norm1_w"] = din("norm1_w", [DEPTH, D])
    I["norm2_w"] = din("norm2_w", [DEPTH, D])
    I["final_norm_w"] = din("final_norm_w", [D])
    I["w_in"] = din("w_in", [DEPTH, D, 2744])
    I["w_out"] = din("w_out", [DEPTH, D, D])
    I["w_gate_up"] = din("w_gate_up", [DEPTH, D, 2 * FF])
    I["w_down"] = din("w_down", [DEPTH, FF, D])
    I["ident"] = din("ident", [128, 128])
    I["mla_q_norm_w"] = din("mla_q_norm_w", [DEPTH, 256])
    I["mla_w_uq"] = din("mla_w_uq", [DEPTH, 256, 384])
    I["mla_kv_norm_w"] = din("mla_kv_norm_w", [DEPTH, 128])
    I["mla_w_ukv"] = din("mla_w_ukv", [DEPTH, 128, 512])
    I["cache_ckv"] = din("cache_ckv", [DEPTH, 256, 128])
    I["cache_kpe"] = din("cache_kpe", [DEPTH, 256, 32])
    I["rope_m"] = din("rope_m", [2, 32, TL])
    I["rope_s"] = din("rope_s", [2, 64, TL])
    I["swa_mask"] = din("swa_mask", [6, 128, 512])
    I["cmask"] = din("cmask", [14, 128, 128])
    I["dn_conv_w"] = din("dn_conv_w", [DEPTH, 3, 768])
    I["dn_a_log"] = din("dn_a_log", [DEPTH, 2, 4])
    I["dn_dt_bias"] = din("dn_dt_bias", [DEPTH, 2, 4])
    I["dn_norm_w"] = din("dn_norm_w", [DEPTH, 64])
    I["state_dn"] = din("state_dn", [DEPTH, 2, 4, 64, 64])
    I["ssm_conv_w"] = din("ssm_conv_w", [DEPTH, 3, 512])
    I["ssm_conv_b"] = din("ssm_conv_b", [DEPTH, 512])
    I["ssm_a_log"] = din("ssm_a_log", [DEPTH, 2, 4])
    I["ssm_dt_bias"] = din("ssm_dt_bias", [DEPTH, 2, 4])
    I["ssm_d"] = din("ssm_d", [DEPTH, 4])
    I["ssm_norm_w"] = din("ssm_norm_w", [DEPTH, 256])
    I["state_ssm"] = din("state_ssm", [DEPTH, 2, 4, 64, 64])
    I["swa_sinks"] = din("swa_sinks", [DEPTH, 4])
    I["cache_swk"] = din("cache_swk", [DEPTH, 256, 2, 64])
    I["cache_swv"] = din("cache_swv", [DEPTH, 256, 2, 64])
    O = {}
    O["y_ctx"] = dout("y_ctx", [NCTX * TC, D])
    O["y_lat"] = dout("y_lat", [TL, D])
    O["new_ckv"] = dout("new_ckv", [NCTX, DEPTH, TC, 128])
    O["new_kpe"] = dout("new_kpe", [NCTX, DEPTH, TC, 32])
    O["new_ssm"] = dout("new_ssm", [NCTX, DEPTH, 2, 4, 64, 64])
    O["new_sdn"] = dout("new_sdn", [NCTX, DEPTH, 2, 4, 64, 64])
    O["new_swk"] = dout("new_swk", [NCTX, DEPTH, TC, 2, 64])
    O["new_swv"] = dout("new_swv", [NCTX, DEPTH, TC, 2, 64])

    with ExitStack() as es:
        k = K(nc, es)
        k.ext = set(debug.get("ext", ()))

        def dump(name, res, ap, shape, dtype=F32):
            if name in debug.get("dump", ()):
                d = nc.dram_tensor("dbg_" + name, list(shape), dtype, kind="ExternalOutput").ap()
                k.dma("sp", d, ap, reads=[res])
        X = [k.dram("xs%d" % i, [D, TTOT], F32) for i in range(DEPTH + 1)]
        XA = k.dram("xa", [D, TTOT], F32)
        XB = k.dram("xb", [D, TTOT], F32)
        PFM = [k.dram("pfm%d" % l, [NFM, TTOT], BF16) for l in range(DEPTH)]
        PTM = [k.dram("ptm%d" % l, [TTOT, NTM], F32) for l in range(DEPTH)]
        MIX = [k.dram("mix%d" % l, [D, TTOT], BF16) for l in range(DEPTH)]

        ident = k.tile("ident", [128, 128], F32)
        k.dma("sp", ident[:], I["ident"][:, :], writes=[ident])
        ones_bf = k.tile("ones_bf", [128, 128], BF16)
        k.op("dve", lambda e: e.memset(ones_bf[:], 1.0), writes=[ones_bf])
        ones_f = k.tile("ones_f", [128, 64], F32)
        k.op("dve", lambda e: e.memset(ones_f[:], 1.0), writes=[ones_f])
        epsb = k.tile("epsb", [128, 1], F32)
        k.op("dve", lambda e: e.memset(epsb[:], EPS), writes=[epsb])
        ps = k.pool("ps", [128, 512], F32, 6, psum=True)
        pacc = k.pool("pacc", [128, 512], F32, 2, psum=True)
        psd = [Pool(ps.tiles[0:4]), Pool(ps.tiles[4:6] + pacc.tiles[0:2])]
        ps8 = Pool(ps.tiles + pacc.tiles)
        mod = [k.tile("mod%d" % l, [128, 6, 8, 2], F32) for l in range(DEPTH)]
        fnw = k.tile("fnw", [128, 8], F32)
        k.dma("sp", fnw[:], I["final_norm_w"].rearrange("(c p) -> p c", p=128), writes=[fnw],
              allow_slow_non_contiguous=True)

        def pipelined(t0s, load):
            nxt = load(t0s[0])
            for i, t0 in enumerate(t0s):
                cur = nxt
                if i + 1 < len(t0s):
                    nxt = load(t0s[i + 1])
                yield t0, cur

        def pipelined2(t0s, load, prep):
            cur = None
            for t0, ld in pipelined(t0s, load):
                pr = prep(t0, ld)
                if cur is not None:
                    yield cur
                cur = (t0, ld, pr)
            if cur is not None:
                yield cur

        T0S = list(range(0, TTOT, 512))

        def kind_of_tile(tok0):
            return 0 if tok0 < LOFF else 1

        with ExitStack() as ph:
            xin = k.pool("xin", [128, D], F32, 2, ph)
            stg = k.pool("stg", [128, 8, 512], F32, 2, ph)
            for t0 in range(0, TTOT, 512):
                st = stg.next()
                for b in range(4):
                    tok = t0 + b * 128
                    xi = xin.next()
                    src = (I["x_ctx"][tok:tok + 128, :] if tok < LOFF
                           else I["x_lat"][tok - LOFF:tok - LOFF + 128, :])
                    k.dma("sp", xi[:], src, writes=[xi])
                    for c in range(8):
                        p = ps.next()
                        k.op("pe", lambda e: e.transpose(p[:, 0:128], xi[:, c * 128:(c + 1) * 128], ident[:]),
                             reads=[xi, ident], writes=[p])
                        eng = "act" if c % 2 else "dve"
                        if eng == "act":
                            k.op("act", lambda e: e.copy(out=st[:, c, b * 128:(b + 1) * 128], in_=p[:, 0:128]),
                                 reads=[p], writes=[st], acc=not (b == 0 and c == 0))
                        else:
                            k.op("dve", lambda e: e.tensor_copy(out=st[:, c, b * 128:(b + 1) * 128], in_=p[:, 0:128]),
                                 reads=[p], writes=[st], acc=not (b == 0 and c == 0))
                k.dma("pool", X[0][:, t0:t0 + 512].rearrange("(c p) t -> p c t", p=128), st[:],
                      reads=[st], writes=[X[0]], acc=True)
            k.barrier()

        def rms_stats(ph_tiles, xt, ntok, sq, rstd):
            k.op("act", lambda e: e.activation(out=sq[:, :, 0:ntok], in_=xt[:, :, 0:ntok], func=AF.Square),
                 reads=[xt], writes=[sq])
            p = ps.next()
            for c in range(8):
                k.op("pe", lambda e: e.matmul(p[:, 0:ntok], lhsT=ones_bf[:], rhs=sq[:, c, 0:ntok],
                                              start=(c == 0), stop=(c == 7)),
                     reads=[ones_bf, sq], writes=[p], inc=(c == 7))
            k.op("act", lambda e: e.activation(out=rstd[:, 0:ntok], in_=p[:, 0:ntok], func=AF.Ln,
                                               scale=1.0 / D, bias=epsb[:, 0:1]),
                 reads=[p, epsb], writes=[rstd])
            k.op("act", lambda e: e.activation(out=rstd[:, 0:ntok], in_=rstd[:, 0:ntok], func=AF.Exp, scale=-0.5),
                 reads=[rstd], writes=[rstd])

        def mod_norm(xt, ntok, rstd, tmpp, hb, modt, ia, ib, kind):
            for c in range(8):
                tmp = tmpp.next()
                k.op("dve", lambda e: e.tensor_tensor(out=tmp[:, 0:ntok], in0=xt[:, c, 0:ntok],
                                                      in1=rstd[:, 0:ntok], op=ALU.mult),
                     reads=[xt, rstd], writes=[tmp])
                k.op("act", lambda e: e.activation(out=hb[:, c, 0:ntok], in_=tmp[:, 0:ntok], func=AF.Identity,
                                                   scale=modt[:, ia, c, kind:kind + 1],
                                                   bias=modt[:, ib, c, kind:kind + 1]),
                     reads=[tmp, modt], writes=[hb], acc=(c > 0))


        SEQS = [(i * TC, TC, False, i) for i in range(NCTX)] + [(LOFF, TL, True, 0)]
        MLA_SCALE = 96 ** -0.5
        SWA_SCALE = 64 ** -0.5

        def evac(i, out, in_, reads, writes, acc=False):
            if i % 2:
                k.op("act", lambda e: e.copy(out=out, in_=in_), reads=reads, writes=writes, acc=acc)
            else:
                k.op("dve", lambda e: e.tensor_copy(out=out, in_=in_), reads=reads, writes=writes, acc=acc)

        def rstd_from_ps(p, n, rstd, dim):
            k.op("act", lambda e: e.activation(out=rstd[:, 0:n], in_=p[:, 0:n], func=AF.Ln,
                                               scale=1.0 / dim, bias=epsb[:, 0:1]),
                 reads=[p, epsb], writes=[rstd])
            k.op("act", lambda e: e.activation(out=rstd[:, 0:n], in_=rstd[:, 0:n], func=AF.Exp, scale=-0.5),
                 reads=[rstd], writes=[rstd])

        def attn_core(kT, vt, h_v, qT, NQ, NKB, scale, ptp, masks=None, sink=None, kb_list=None):
            po = pacc.next()
            blocks = kb_list if kb_list is not None else [(kb, None) for kb in range(NKB)]
            n = len(blocks)
            LOOK = 3
            pSs = [None] * n

            def issue_s(i):
                kb = blocks[i][0]
                pS = ps.next()
                k.op("pe", lambda e: e.matmul(pS[:, 0:NQ], lhsT=kT[:, kb * 128:(kb + 1) * 128], rhs=qT[:, 0:NQ],
                                              start=True, stop=True), reads=[kT, qT], writes=[pS])
                pSs[i] = pS

            first = True
            if sink is not None:
                e64, srow = sink
                k.op("pe", lambda e: e.matmul(po[0:65, 0:NQ], lhsT=e64[0:1, 0:65], rhs=srow[0:1, 0:NQ],
                                              start=True, stop=False), reads=[e64, srow], writes=[po])
                first = False
            for i in range(min(LOOK, n)):
                issue_s(i)
            for bi, (kb, mk) in enumerate(blocks):
                if bi + LOOK < n:
                    issue_s(bi + LOOK)
                pS = pSs[bi]
                pt = ptp.next()
                k.op("act", lambda e: e.activation(out=pt[:, 0:NQ], in_=pS[:, 0:NQ], func=AF.Exp, scale=scale),
                     reads=[pS], writes=[pt])
                if mk is not None:
                    k.op("dve", lambda e: e.tensor_tensor(out=pt[:, 0:NQ], in0=pt[:, 0:NQ], in1=mk[:, 0:NQ], op=ALU.mult),
                         reads=[pt, mk], writes=[pt])
                last = (bi == n - 1)
                k.op("pe", lambda e: e.matmul(po[0:65, 0:NQ], lhsT=vt[:, kb, h_v, :], rhs=pt[:, 0:NQ],
                                              start=first, stop=last), reads=[vt, pt], writes=[po])
                first = False
            return po

        def attn_finish(po, NQ, rowbuf, bcs, ostg_p, dst_ap, dst_res):
            k.op("act", lambda e: e.activation(out=rowbuf[64:65, 0:NQ], in_=po[64:65, 0:NQ], func=AF.Ln),
                 reads=[po], writes=[rowbuf])
            k.op("act", lambda e: e.activation(out=rowbuf[64:65, 0:NQ], in_=rowbuf[64:65, 0:NQ], func=AF.Exp, scale=-1.0),
                 reads=[rowbuf], writes=[rowbuf])
            pb = ps.next()
            k.op("pe", lambda e: e.matmul(pb[0:64, 0:NQ], lhsT=ones_f[64:65, 0:64], rhs=rowbuf[64:65, 0:NQ],
                                          start=True, stop=True), reads=[ones_f, rowbuf], writes=[pb])
            k.op("act", lambda e: e.copy(out=bcs[0:64, 0:NQ], in_=pb[0:64, 0:NQ]), reads=[pb], writes=[bcs])
            og = ostg_p.next()
            k.op("dve", lambda e: e.tensor_tensor(out=og[0:64, 0:NQ], in0=po[0:64, 0:NQ], in1=bcs[0:64, 0:NQ],
                                                  op=ALU.mult), reads=[po, bcs], writes=[og])
            k.dma("pool", dst_ap, og[0:64, 0:NQ], reads=[og], writes=[dst_res], acc=True)

        def mla_phase(l):
            with ExitStack() as ph:
                NKMAX = TL + 256
                wuq = k.tile("wuq", [128, 2, 4, 128], BF16, ph)
                wuqs = k.tile("wuqs", [128, 2, 4, 64], BF16, ph)
                wkk = k.tile("wkk", [128, 4, 128], BF16, ph)
                wkv = k.tile("wkv", [128, 256], BF16, ph)
                k.op("dve", lambda e: e.memset(wuq[:], 0.0), writes=[wuq])
                k.op("dve", lambda e: e.memset(wuqs[:], 0.0), writes=[wuqs])
                k.op("dve", lambda e: e.memset(wkk[:], 0.0), writes=[wkk])
                uq = I["mla_w_uq"][l]
                ukv = I["mla_w_ukv"][l]
                for h in range(4):
                    def ld(dst, src):
                        k.dma("pool", dst, src.rearrange("(c p) n -> p c n", p=128), writes=[wuq, wuqs], acc=True)
                    ld(wuq[:, :, h, 64:128], uq[:, 96 * h:96 * h + 64])
                    ld(wuq[:, :, h, 32:64], uq[:, 96 * h + 64:96 * h + 96])
                    ld(wuqs[:, :, h, 32:48], uq[:, 96 * h + 80:96 * h + 96])
                    ld(wuqs[:, :, h, 48:64], uq[:, 96 * h + 64:96 * h + 80])
                    k.dma("pool", wkk[:, h, 64:128], ukv[:, 128 * h:128 * h + 64], writes=[wkk], acc=True)
                    k.dma("pool", wkv[:, 64 * h:64 * h + 64], ukv[:, 128 * h + 64:128 * h + 128], writes=[wkv], acc=True)
                qnw = k.tile("qnw", [128, 2], F32, ph)
                k.dma("sp", qnw[:], I["mla_q_norm_w"][l].rearrange("(c p) -> p c", p=128), writes=[qnw],
                      allow_slow_non_contiguous=True)
                kvnw = k.tile("kvnw", [128, 1], F32, ph)
                k.dma("sp", kvnw[:], I["mla_kv_norm_w"][l].rearrange("(c p) -> p c", p=128), writes=[kvnw],
                      allow_slow_non_contiguous=True)
                CM = k.tile("CM", [64, TL], F32, ph)
                SM = k.tile("SM", [64, TL], F32, ph)
                k.dma("sp", CM[32:64, :], I["rope_m"][0], writes=[CM])
                k.dma("sp", SM[32:64, :], I["rope_m"][1], writes=[SM])
                ckvT = k.tile("ckvT", [128, NKMAX], BF16, ph)
                kpeT = k.tile("kpeT", [64, NKMAX], BF16, ph)
                kTm = [k.tile("kTm%d" % h, [128, NKMAX], BF16, ph) for h in range(4)]
                for h in range(4):
                    k.op("pool", lambda e: e.memset(kTm[h][0:32, :], 0.0), writes=[kTm[h]])
                    k.op("pool", lambda e: e.memset(kTm[h][0:1, :], 1.0), writes=[kTm[h]])
                vm = k.tile("vm", [128, NKMAX // 128, 4, 65], BF16, ph)
                k.op("pool", lambda e: e.memset(vm[:], 1.0), writes=[vm])
                kmx = k.tile("kmx", [1, 4, 16], F32, ph)
                nkmax = k.tile("nkmax", [1, 4], F32, ph)
                kvp = k.pool("kvp", [128, 512], BF16, 2, ph)
                sqp = k.pool("sqm", [128, 2, 512], BF16, 2, ph)
                for t_ in sqp.tiles:
                    k.op("dve", lambda e: e.memset(t_[:], 0.0), writes=[t_])
                sqb = k.tile("sqb", [128, 512], BF16, ph)
                k.op("dve", lambda e: e.memset(sqb[:], 0.0), writes=[sqb])
                rsp = k.pool("rsm", [128, 512], F32, 2, ph)
                f32p = k.pool("f32m", [128, 512], F32, 3, ph)
                kxp = k.pool("kxp", [64, 2, 512], BF16, 2, ph)
                qlp = k.pool("qlp", [128, 2, 512], BF16, 2, ph)
                qnp = k.pool("qnp", [128, 2, 512], BF16, 2, ph)
                qTp = k.pool("qTp", [128, 512], BF16, 6, ph)
                rowp = k.pool("rowpm", [1, 512], F32, 3, ph)
                for t_ in qTp.tiles:
                    k.op("dve", lambda e: e.memset(t_[:], 0.0), writes=[t_])
                ptp = k.pool("ptp", [128, 512], BF16, 6, ph)
                rowbuf = k.tile("rowbuf", [128, 512], F32, ph)
                bcs = k.tile("bcs", [64, 512], F32, ph)
                ogp = k.pool("ogp", [64, 512], BF16, 3, ph)
                tkp = k.pool("tkp", [128, 2, 128], F32, 2, ph)
                otp = k.pool("otp", [128, 128], F32, 2, ph)

                for (off, T, lat, si) in SEQS:
                    k.barrier()
                    TT = min(512, T)
                    koff = 256 if lat else 0
                    NK = T + koff
                    NKB = NK // 128
                    if lat:
                        ck = tkp.next()
                        k.dma("sp", ck[:], I["cache_ckv"][l].rearrange("(b p) f -> p b f", p=128), writes=[ck])
                        for b in range(2):
                            p = ps.next()
                            k.op("pe", lambda e: e.transpose(p[:, 0:128], ck[:, b, :], ident[:]),
                                 reads=[ck, ident], writes=[p])
                            evac(b, ckvT[:, b * 128:(b + 1) * 128], p[:, 0:128], [p], [ckvT], acc=True)
                        kp = tkp.next()
                        k.op("dve", lambda e: e.memset(kp[:], 0.0), writes=[kp])
                        k.dma("sp", kp[:, :, 32:64], I["cache_kpe"][l].rearrange("(b p) f -> p b f", p=128),
                              writes=[kp], acc=True)
                        for b in range(2):
                            p = ps.next()
                            k.op("pe", lambda e: e.transpose(p[:, 0:128], kp[:, b, :], ident[:]),
                                 reads=[kp, ident], writes=[p])
                            evac(b, kpeT[32:64, b * 128:(b + 1) * 128], p[32:64, 0:128], [p], [kpeT], acc=True)
                    for ti, t0 in enumerate(range(0, T, TT)):
                        g0 = off + t0
                        kv = kvp.next()
                        k.dma("sp", kv[:, 0:TT], PFM[l][R_MKV:R_MKV + 128, g0:g0 + TT], reads=[PFM[l]], writes=[kv])
                        sq = sqp.next()
                        k.op("act", lambda e: e.activation(out=sq[:, 0, 0:TT], in_=kv[:, 0:TT], func=AF.Square),
                             reads=[kv], writes=[sq])
                        p = ps.next()
                        k.op("pe", lambda e: e.matmul(p[:, 0:TT], lhsT=ones_bf[:], rhs=sq[:, 0, 0:TT], start=True, stop=True),
                             reads=[ones_bf, sq], writes=[p])
                        rstd = rsp.next()
                        rstd_from_ps(p, TT, rstd, 128)
                        cf = f32p.next()
                        k.op("dve", lambda e: e.scalar_tensor_tensor(out=cf[:, 0:TT], in0=kv[:, 0:TT], scalar=kvnw[:, 0:1],
                                                                     in1=rstd[:, 0:TT], op0=ALU.mult, op1=ALU.mult),
                             reads=[kv, kvnw, rstd], writes=[cf])
                        k.op("act", lambda e: e.copy(out=ckvT[:, koff + t0:koff + t0 + TT], in_=cf[:, 0:TT]),
                             reads=[cf], writes=[ckvT], acc=True)
                        kx = kxp.next()
                        k.dma("sp", kx[32:64, 0, 0:TT], PFM[l][R_MKPE:R_MKPE + 32, g0:g0 + TT], reads=[PFM[l]], writes=[kx])
                        k.dma("sp", kx[32:64, 1, 0:TT], PFM[l][R_MKPE + 32:R_MKPE + 64, g0:g0 + TT], reads=[PFM[l]],
                              writes=[kx], acc=True)
                        if lat:
                            t1 = f32p.next()
                            t2 = f32p.next()
                            k.op("dve", lambda e: e.tensor_tensor(out=t1[32:64, 0:TT], in0=kx[32:64, 0, 0:TT],
                                                                  in1=CM[32:64, t0:t0 + TT], op=ALU.mult),
                                 reads=[kx, CM], writes=[t1])
                            k.op("pool", lambda e: e.tensor_tensor(out=t2[32:64, 0:TT], in0=kx[32:64, 1, 0:TT],
                                                                   in1=SM[32:64, t0:t0 + TT], op=ALU.mult),
                                 reads=[kx, SM], writes=[t2])
                            k.op("dve", lambda e: e.tensor_tensor(out=kpeT[32:64, koff + t0:koff + t0 + TT],
                                                                  in0=t1[32:64, 0:TT], in1=t2[32:64, 0:TT], op=ALU.add),
                                 reads=[t1, t2], writes=[kpeT], acc=True)
                        else:
                            k.op("dve", lambda e: e.tensor_copy(out=kpeT[32:64, t0:t0 + TT], in_=kx[32:64, 0, 0:TT]),
                                 reads=[kx], writes=[kpeT], acc=True)
                            kf = f32p.next()
                            k.op("dve", lambda e: e.memset(kf[:, 0:TT], 0.0), writes=[kf])
                            k.op("act", lambda e: e.copy(out=kf[32:64, 0:TT], in_=kx[32:64, 0, 0:TT]),
                                 reads=[kx], writes=[kf])
                            for b in range(TT // 128):
                                p = ps.next()
                                k.op("pe", lambda e: e.transpose(p[:, 0:128], cf[:, b * 128:(b + 1) * 128], ident[:]),
                                     reads=[cf, ident], writes=[p])
                                ot = otp.next()
                                evac(b, ot[:, :], p[:, 0:128], [p], [ot])
                                k.dma("pool", O["new_ckv"][si, l, t0 + b * 128:t0 + (b + 1) * 128, :], ot[:, :], reads=[ot])
                                p = ps.next()
                                k.op("pe", lambda e: e.transpose(p[:, 0:128], kf[:, b * 128:(b + 1) * 128], ident[:]),
                                     reads=[kf, ident], writes=[p])
                                ot = otp.next()
                                evac(b + 1, ot[:, 0:32], p[:, 32:64], [p], [ot])
                                k.dma("pool", O["new_kpe"][si, l, t0 + b * 128:t0 + (b + 1) * 128, :], ot[:, 0:32], reads=[ot])
                    ntile = (NK + 511) // 512
                    for ti in range(ntile):
                        c0 = ti * 512
                        n = min(512, NK - c0)
                        for h in range(4):
                            p = ps.next()
                            k.op("pe", lambda e: e.matmul(p[:, 0:n], lhsT=wkk[:, h, :], rhs=ckvT[:, c0:c0 + n],
                                                          start=True, stop=True), reads=[wkk, ckvT], writes=[p])
                            evac(h, kTm[h][64:128, c0:c0 + n], p[64:128, 0:n], [p], [kTm[h]], acc=True)
                            k.op("pool", lambda e: e.tensor_copy(out=kTm[h][32:64, c0:c0 + n], in_=kpeT[32:64, c0:c0 + n]),
                                 reads=[kpeT], writes=[kTm[h]], acc=True)
                            for (a0, a1) in ((32, 64), (64, 128)):
                                k.op("act", lambda e: e.activation(out=sqb[a0:a1, 0:n], in_=kTm[h][a0:a1, c0:c0 + n],
                                                                   func=AF.Square), reads=[kTm[h]], writes=[sqb], acc=(a0 == 64))
                            p2 = ps.next()
                            k.op("pe", lambda e: e.matmul(p2[0:1, 0:n], lhsT=ones_bf[:, 0:1], rhs=sqb[:, 0:n],
                                                          start=True, stop=True), reads=[ones_bf, sqb], writes=[p2])
                            k.op("dve", lambda e: e.reduce_max(out=kmx[0:1, h, ti:ti + 1], in_=p2[0:1, 0:n], axis=AX.X),
                                 reads=[p2], writes=[kmx], acc=True)
                    for kb in range(NKB):
                        p = ps.next()
                        k.op("pe", lambda e: e.matmul(p[:, 0:256], lhsT=ckvT[:, kb * 128:(kb + 1) * 128], rhs=wkv[:, :],
                                                      start=True, stop=True), reads=[ckvT, wkv], writes=[p])
                        evac(kb, vm[:, kb, :, 0:64], p[:, 0:256].rearrange("p (h d) -> p h d", h=4), [p], [vm], acc=True)
                    k.op("dve", lambda e: e.reduce_max(out=nkmax[0:1, :], in_=kmx[0:1, :, 0:ntile], axis=AX.X),
                         reads=[kmx], writes=[nkmax])
                    k.op("act", lambda e: e.activation(out=nkmax[0:1, :], in_=nkmax[0:1, :], func=AF.Sqrt),
                         reads=[nkmax], writes=[nkmax])
                    k.op("dve", lambda e: e.tensor_scalar(out=nkmax[0:1, :], in0=nkmax[0:1, :], scalar1=-1.0, scalar2=None,
                                                          op0=ALU.mult), reads=[nkmax], writes=[nkmax])
                    for t0 in range(0, T, TT):
                        g0 = off + t0
                        ql = qlp.next()
                        k.dma("sp", ql[:, :, 0:TT], PFM[l][R_MQ:R_MQ + 256, g0:g0 + TT].rearrange("(c p) t -> p c t", p=128),
                              reads=[PFM[l]], writes=[ql])
                        sq = sqp.next()
                        k.op("act", lambda e: e.activation(out=sq[:, :, 0:TT], in_=ql[:, :, 0:TT], func=AF.Square),
                             reads=[ql], writes=[sq])
                        p = ps.next()
                        for c in range(2):
                            k.op("pe", lambda e: e.matmul(p[:, 0:TT], lhsT=ones_bf[:], rhs=sq[:, c, 0:TT],
                                                          start=(c == 0), stop=(c == 1)), reads=[ones_bf, sq], writes=[p], inc=(c == 1))
                        rstd = rsp.next()
                        rstd_from_ps(p, TT, rstd, 256)
                        qn = qnp.next()
                        for c in range(2):
                            k.op("dve", lambda e: e.scalar_tensor_tensor(out=qn[:, c, 0:TT], in0=ql[:, c, 0:TT],
                                                                         scalar=qnw[:, c:c + 1], in1=rstd[:, 0:TT],
                                                                         op0=ALU.mult, op1=ALU.mult),
                                 reads=[ql, qnw, rstd], writes=[qn], acc=(c > 0))
                        qTs_h = []
                        for h in range(4):
                            p1 = ps.next()
                            for c in range(2):
                                k.op("pe", lambda e: e.matmul(p1[:, 0:TT], lhsT=wuq[:, c, h, :], rhs=qn[:, c, 0:TT],
                                                              start=(c == 0), stop=(c == 1)), reads=[wuq, qn], writes=[p1], inc=(c == 1))
                            qT = qTp.next()
                            k.op("act", lambda e: e.copy(out=qT[64:128, 0:TT], in_=p1[64:128, 0:TT]), reads=[p1], writes=[qT])
                            if lat:
                                p2 = ps.next()
                                for c in range(2):
                                    k.op("pe", lambda e: e.matmul(p2[0:64, 0:TT], lhsT=wuqs[:, c, h, :], rhs=qn[:, c, 0:TT],
                                                                  start=(c == 0), stop=(c == 1)), reads=[wuqs, qn], writes=[p2], inc=(c == 1))
                                t1 = f32p.next()
                                t2 = f32p.next()
                                k.op("dve", lambda e: e.tensor_tensor(out=t1[32:64, 0:TT], in0=p1[32:64, 0:TT],
                                                                      in1=CM[32:64, t0:t0 + TT], op=ALU.mult),
                                     reads=[p1, CM], writes=[t1])
                                k.op("dve", lambda e: e.tensor_tensor(out=t2[32:64, 0:TT], in0=p2[32:64, 0:TT],
                                                                      in1=SM[32:64, t0:t0 + TT], op=ALU.mult),
                                     reads=[p2, SM], writes=[t2])
                                k.op("pool", lambda e: e.tensor_tensor(out=qT[32:64, 0:TT], in0=t1[32:64, 0:TT],
                                                                       in1=t2[32:64, 0:TT], op=ALU.add),
                                     reads=[t1, t2], writes=[qT], acc=True)
                            else:
                                k.op("dve", lambda e: e.tensor_copy(out=qT[32:64, 0:TT], in_=p1[32:64, 0:TT]),
                                     reads=[p1], writes=[qT], acc=True)
                            for (a0, a1) in ((32, 64), (64, 128)):
                                k.op("act", lambda e: e.activation(out=sqb[a0:a1, 0:TT], in_=qT[a0:a1, 0:TT], func=AF.Square),
                                     reads=[qT], writes=[sqb], acc=(a0 == 64))
                            pn = ps.next()
                            k.op("pe", lambda e: e.matmul(pn[0:1, 0:TT], lhsT=ones_bf[:, 0:1], rhs=sqb[:, 0:TT],
                                                          start=True, stop=True), reads=[ones_bf, sqb], writes=[pn])
                            rw = rowp.next()
                            k.op("act", lambda e: e.activation(out=rw[0:1, 0:TT], in_=pn[0:1, 0:TT], func=AF.Sqrt),
                                 reads=[pn], writes=[rw])
                            k.op("dve", lambda e: e.tensor_scalar(out=qT[0:1, 0:TT], in0=rw[0:1, 0:TT],
                                                                  scalar1=nkmax[0:1, h:h + 1], scalar2=None, op0=ALU.mult),
                                 reads=[rw, nkmax], writes=[qT], acc=True)
                            qTs_h.append(qT)
                        for h in range(4):
                            po = attn_core(kTm[h], vm, h, qTs_h[h], TT, NKB, MLA_SCALE, ptp)
                            attn_finish(po, TT, rowbuf, bcs, ogp,
                                        MIX[l][256 + 64 * h:256 + 64 * h + 64, g0:g0 + TT], MIX[l])
                k.barrier()


        def swa_phase(l):
            with ExitStack() as ph:
                NKMAX = TL + 256
                CS = k.tile("CS", [128, TL], F32, ph)
                SS = k.tile("SS", [128, TL], F32, ph)
                k.dma("sp", CS[64:128, :], I["rope_s"][0], writes=[CS])
                k.dma("sp", SS[64:128, :], I["rope_s"][1], writes=[SS])
                mk = k.tile("mk", [128, 6, 512], BF16, ph)
                k.dma("pool", mk[:], I["swa_mask"].rearrange("r p q -> p r q"), writes=[mk])
                sk = k.tile("sk", [1, 4], F32, ph)
                k.dma("sp", sk[:], I["swa_sinks"][l:l + 1, :], writes=[sk])
                e64 = k.tile("e64", [1, 65], BF16, ph)
                k.op("dve", lambda e: e.memset(e64[:], 0.0), writes=[e64])
                k.op("dve", lambda e: e.memset(e64[0:1, 64:65], 1.0), writes=[e64])
                kTs = [k.tile("kTs%d" % h, [128, NKMAX], BF16, ph) for h in range(2)]
                for h in range(2):
                    k.op("pool", lambda e: e.memset(kTs[h][0:64, :], 0.0), writes=[kTs[h]])
                    k.op("pool", lambda e: e.memset(kTs[h][0:1, :], 1.0), writes=[kTs[h]])
                vs = k.tile("vs", [128, NKMAX // 128, 2, 65], BF16, ph)
                k.op("pool", lambda e: e.memset(vs[:], 1.0), writes=[vs])
                kmx = k.tile("kmx", [1, 2, 16], F32, ph)
                nkmax = k.tile("nkmax", [1, 2], F32, ph)
                sqb = k.tile("sqb", [128, 512], BF16, ph)
                k.op("dve", lambda e: e.memset(sqb[:], 0.0), writes=[sqb])
                f32p = k.pool("f32s", [128, 512], F32, 3, ph)
                kxp = k.pool("kxs", [128, 2, 512], BF16, 4, ph)
                qTp = k.pool("qTs", [128, 512], BF16, 6, ph)
                rowp = k.pool("rowps", [1, 512], F32, 3, ph)
                for t_ in qTp.tiles:
                    k.op("dve", lambda e: e.memset(t_[:], 0.0), writes=[t_])
                ptp = k.pool("pts", [128, 512], BF16, 6, ph)
                rowbuf = k.tile("rowbufs", [128, 512], F32, ph)
                srowp = k.pool("srow", [1, 512], BF16, 6, ph)
                bcs = k.tile("bcss", [64, 512], F32, ph)
                ogp = k.pool("ogs", [64, 512], BF16, 3, ph)
                ckp = k.pool("cks", [128, 128], F32, 2, ph)
                for t_ in ckp.tiles:
                    k.op("dve", lambda e: e.memset(t_[:], 0.0), writes=[t_])
                otp = k.pool("ots", [128, 128], F32, 2, ph)

                def rope(dst_ap, dst_res, x, t0, TT, lat, acc=True):
                    if lat:
                        t1 = f32p.next()
                        t2 = f32p.next()
                        k.op("dve", lambda e: e.tensor_tensor(out=t1[64:128, 0:TT], in0=x[64:128, 0, 0:TT],
                                                              in1=CS[64:128, t0:t0 + TT], op=ALU.mult),
                             reads=[x, CS], writes=[t1])
                        k.op("pool", lambda e: e.tensor_tensor(out=t2[64:128, 0:TT], in0=x[64:128, 1, 0:TT],
                                                               in1=SS[64:128, t0:t0 + TT], op=ALU.mult),
                             reads=[x, SS], writes=[t2])
                        k.op("dve", lambda e: e.tensor_tensor(out=dst_ap, in0=t1[64:128, 0:TT], in1=t2[64:128, 0:TT],
                                                              op=ALU.add), reads=[t1, t2], writes=[dst_res], acc=acc)
                    else:
                        k.op("dve", lambda e: e.tensor_copy(out=dst_ap, in_=x[64:128, 0, 0:TT]), reads=[x],
                             writes=[dst_res], acc=acc)

                for (off, T, lat, si) in SEQS:
                    k.barrier()
                    TT = min(512, T)
                    koff = 256 if lat else 0
                    NK = T + koff
                    NKB = NK // 128
                    if lat:
                        for kv in range(2):
                            for b in range(2):
                                ck = ckp.next()
                                k.dma("sp", ck[:, 64:128], I["cache_swk"][l, b * 128:(b + 1) * 128, kv, :], writes=[ck])
                                p = ps.next()
                                k.op("pe", lambda e: e.transpose(p[:, 0:128], ck[:, :], ident[:]), reads=[ck, ident], writes=[p])
                                evac(b, kTs[kv][64:128, b * 128:(b + 1) * 128], p[64:128, 0:128], [p], [kTs[kv]], acc=True)
                        for b in range(2):
                            k.dma("pool", vs[:, b, :, 0:64], I["cache_swv"][l, b * 128:(b + 1) * 128, :, :],
                                  writes=[vs], acc=True)
                    else:
                        k.dma("pool", O["new_swv"][si, l].rearrange("t k d -> t (k d)"),
                              PTM[l][off:off + T, C_SWV:C_SWV + 128], reads=[PTM[l]])
                    for b in range(T // 128):
                        k.dma("pool", vs[:, koff // 128 + b, :, 0:64],
                              PTM[l][off + b * 128:off + (b + 1) * 128, C_SWV:C_SWV + 128].rearrange("p (k d) -> p k d", k=2),
                              reads=[PTM[l]], writes=[vs], acc=True)
                    for kv in range(2):
                        for ti, t0 in enumerate(range(0, T, TT)):
                            g0 = off + t0
                            kx = kxp.next()
                            k.dma("sp", kx[64:128, 0, 0:TT], PFM[l][R_SWK + 64 * kv:R_SWK + 64 * kv + 64, g0:g0 + TT],
                                  reads=[PFM[l]], writes=[kx])
                            k.dma("sp", kx[64:128, 1, 0:TT], PFM[l][R_SWKS + 64 * kv:R_SWKS + 64 * kv + 64, g0:g0 + TT],
                                  reads=[PFM[l]], writes=[kx], acc=True)
                            rope(kTs[kv][64:128, koff + t0:koff + t0 + TT], kTs[kv], kx, t0, TT, lat)
                            if not lat:
                                kf = f32p.next()
                                k.op("dve", lambda e: e.memset(kf[0:64, 0:TT], 0.0), writes=[kf])
                                k.op("act", lambda e: e.copy(out=kf[64:128, 0:TT], in_=kx[64:128, 0, 0:TT]),
                                     reads=[kx], writes=[kf], acc=True)
                                for b in range(TT // 128):
                                    p = ps.next()
                                    k.op("pe", lambda e: e.transpose(p[:, 0:128], kf[:, b * 128:(b + 1) * 128], ident[:]),
                                         reads=[kf, ident], writes=[p])
                                    ot = otp.next()
                                    evac(b, ot[:, 0:64], p[:, 64:128], [p], [ot])
                                    k.dma("pool", O["new_swk"][si, l, t0 + b * 128:t0 + (b + 1) * 128, kv, :], ot[:, 0:64], reads=[ot])
                        ntile = (NK + 511) // 512
                        for ti in range(ntile):
                            c0 = ti * 512
                            n = min(512, NK - c0)
                            k.op("act", lambda e: e.activation(out=sqb[64:128, 0:n], in_=kTs[kv][64:128, c0:c0 + n],
                                                               func=AF.Square), reads=[kTs[kv]], writes=[sqb])
                            p2 = ps.next()
                            k.op("pe", lambda e: e.matmul(p2[0:1, 0:n], lhsT=ones_bf[:, 0:1], rhs=sqb[:, 0:n],
                                                          start=True, stop=True), reads=[ones_bf, sqb], writes=[p2])
                            k.op("dve", lambda e: e.reduce_max(out=kmx[0:1, kv, ti:ti + 1], in_=p2[0:1, 0:n], axis=AX.X),
                                 reads=[p2], writes=[kmx], acc=True)
                    ntile = (NK + 511) // 512
                    k.op("dve", lambda e: e.reduce_max(out=nkmax[0:1, :], in_=kmx[0:1, :, 0:ntile], axis=AX.X),
                         reads=[kmx], writes=[nkmax])
                    k.op("act", lambda e: e.activation(out=nkmax[0:1, :], in_=nkmax[0:1, :], func=AF.Sqrt),
                         reads=[nkmax], writes=[nkmax])
                    k.op("dve", lambda e: e.tensor_scalar(out=nkmax[0:1, :], in0=nkmax[0:1, :], scalar1=-1.0, scalar2=None,
                                                          op0=ALU.mult), reads=[nkmax], writes=[nkmax])
                    for t0 in range(0, T, TT):
                        g0 = off + t0
                        i0 = t0 // 128
                        if lat:
                            kbl = [(0, None), (1, None)]
                            for r in range(6):
                                kbo = i0 - 1 + r
                                if 0 <= kbo < T // 128:
                                    kbl.append((2 + kbo, Res("mkv", mk.t[:, r, :])))
                            for (_, m_) in kbl:
                                if m_ is not None:
                                    m_.w = mk.w
                        else:
                            kbl = [(b, None) for b in range(NKB)]
                        prep_h = []
                        for h in range(4):
                            kv = h // 2
                            qx = kxp.next()
                            k.dma("sp", qx[64:128, 0, 0:TT], PFM[l][R_SWQ + 64 * h:R_SWQ + 64 * h + 64, g0:g0 + TT],
                                  reads=[PFM[l]], writes=[qx])
                            k.dma("sp", qx[64:128, 1, 0:TT], PFM[l][R_SWQS + 64 * h:R_SWQS + 64 * h + 64, g0:g0 + TT],
                                  reads=[PFM[l]], writes=[qx], acc=True)
                            qT = qTp.next()
                            rope(qT[64:128, 0:TT], qT, qx, t0, TT, lat, acc=False)
                            k.op("act", lambda e: e.activation(out=sqb[64:128, 0:TT], in_=qT[64:128, 0:TT], func=AF.Square),
                                 reads=[qT], writes=[sqb])
                            pn = ps.next()
                            k.op("pe", lambda e: e.matmul(pn[0:1, 0:TT], lhsT=ones_bf[:, 0:1], rhs=sqb[:, 0:TT],
                                                          start=True, stop=True), reads=[ones_bf, sqb], writes=[pn])
                            rw = rowp.next()
                            k.op("act", lambda e: e.activation(out=rw[0:1, 0:TT], in_=pn[0:1, 0:TT], func=AF.Sqrt),
                                 reads=[pn], writes=[rw])
                            k.op("dve", lambda e: e.tensor_scalar(out=qT[0:1, 0:TT], in0=rw[0:1, 0:TT],
                                                                  scalar1=nkmax[0:1, kv:kv + 1], scalar2=None, op0=ALU.mult),
                                 reads=[rw, nkmax], writes=[qT], acc=True)
                            srow = srowp.next()
                            k.op("act", lambda e: e.activation(out=srow[0:1, 0:TT], in_=qT[0:1, 0:TT], func=AF.Exp,
                                                               scale=SWA_SCALE, bias=sk[0:1, h:h + 1]),
                                 reads=[qT, sk], writes=[srow])
                            prep_h.append((qT, srow))
                        for h in range(4):
                            kv = h // 2
                            qT, srow = prep_h[h]
                            po = attn_core(kTs[kv], vs, kv, qT, TT, NKB, SWA_SCALE, ptp, sink=(e64, srow), kb_list=kbl)
                            attn_finish(po, TT, rowbuf, bcs, ogp,
                                        MIX[l][768 + 64 * h:768 + 64 * h + 64, g0:g0 + TT], MIX[l])
                k.barrier()


        def ssd_phase(l):
            with ExitStack() as ph:
                cm = k.tile("cm", [128, 5, 128], F32, ph)
                k.dma("sp", cm[:], I["cmask"].rearrange("r p q -> p r q")[:, 0:5, :], writes=[cm])
                onesblk, onesA, onesB = cm[:, 2, :], cm[:, 3, :], cm[:, 4, :]
                cw = k.tile("cw", [128, 4, 3], F32, ph)
                for kk in range(3):
                    k.dma("sp", cw[:, :, kk], I["ssm_conv_w"][l, kk].rearrange("(c p) -> p c", p=128), writes=[cw],
                          acc=(kk > 0), allow_slow_non_contiguous=True)
                cb = k.tile("cb", [128, 4], F32, ph)
                k.dma("sp", cb[:], I["ssm_conv_b"][l].rearrange("(c p) -> p c", p=128), writes=[cb],
                      allow_slow_non_contiguous=True)
                dtb = k.tile("dtb", [128, 8], F32, ph)
                k.dma("sp", dtb[:], I["ssm_dt_bias"][l:l + 1].rearrange("o a b -> o (a b)").partition_broadcast(128), writes=[dtb])
                aneg = k.tile("aneg", [128, 8], F32, ph)
                k.dma("sp", aneg[:], I["ssm_a_log"][l:l + 1].rearrange("o a b -> o (a b)").partition_broadcast(128), writes=[aneg])
                k.op("act", lambda e: e.activation(out=aneg[:], in_=aneg[:], func=AF.Exp), reads=[aneg], writes=[aneg])
                k.op("dve", lambda e: e.tensor_scalar(out=aneg[:], in0=aneg[:], scalar1=-1.0, scalar2=None, op0=ALU.mult),
                     reads=[aneg], writes=[aneg])
                Dt = k.tile("Dt", [128, 4], F32, ph)
                k.dma("sp", Dt[:], I["ssm_d"][l:l + 1, :].partition_broadcast(128), writes=[Dt])
                nwt = k.tile("nwt", [128, 256], F32, ph)
                k.dma("sp", nwt[:], I["ssm_norm_w"][l:l + 1, :].partition_broadcast(128), writes=[nwt])
                onec = k.tile("onec", [128, 1], F32, ph)
                k.op("dve", lambda e: e.memset(onec[:], 1.0), writes=[onec])
                NBM = TL // 128
                BT = k.tile("BT", [128, TL], BF16, ph)
                CT = k.tile("CT", [128, TL], BF16, ph)
                x_tok = k.tile("x_tok", [128, NBM, 256], F32, ph)
                B_tok = k.tile("B_tok", [128, NBM, 128], F32, ph)
                dtr = k.tile("dtr", [128, NBM, 8], F32, ph)
                dtt = k.tile("dtt", [128, NBM, 8], F32, ph)
                at = k.tile("at", [128, NBM, 8], F32, ph)
                yacc = k.tile("yacc", [128, NBM, 256], F32, ph)
                S = [k.tile("S%d" % d, [128, 2, 64], F32, ph) for d in range(2)]
                Sb = [k.tile("Sbs%d" % d, [128, 2, 64], BF16, ph) for d in range(2)]
                xinp = k.pool("xin", [128, 4, 514], BF16, 2, ph)
                xTp = k.pool("xTs", [128, 4, 512], F32, 2, ph)
                tmpp = k.pool("tmps", [128, 512], F32, 4, ph)
                WS = []
                for d_ in range(2):
                    WS.append({"stp": k.pool("stt", [128, 24], F32, 2, ph), "exp": k.pool("exs", [128, 24], F32, 2, ph),
                               "GUp": k.pool("GU", [128, 4, 128], F32, 1, ph), "Lp": k.pool("Lp", [128, 4, 128], F32, 1, ph),
                               "L2p": k.pool("L2p", [128, 4, 128], F32, 1, ph), "scp": k.pool("scT", [128, 4, 128], BF16, 2, ph),
                               "xdp": k.pool("xdt", [128, 4, 64], BF16, 2, ph), "typ": k.pool("tmpy", [128, 4, 64], F32, 3, ph),
                               "Bdp": k.pool("Bd", [128, 4, 128], BF16, 2, ph), "ydp": k.pool("yds", [128, 256], F32, 2, ph)})
                zp = k.pool("zs", [128, 256], F32, 2, ph)
                y2p = k.pool("y2s", [128, 256], F32, 4, ph)
                ssp = k.pool("ssum", [128, 4], F32, 2, ph)
                osp = k.pool("oss", [128, 2, 128], BF16, 2, ph)
                sop = k.pool("sos", [128, 128], F32, 2, ph)

                for (off, T, lat, SEQT) in ((0, NCTX * TC, False, TC), (LOFF, TL, True, TL)):
                    k.barrier()
                    NB = T // 128
                    BPS = SEQT // 128
                    TT = min(512, SEQT)
                    for t0 in range(0, T, TT):
                        g0 = off + t0
                        xin = xinp.next()
                        first = (t0 % SEQT == 0)
                        lastt = ((t0 + TT) % SEQT == 0)
                        lo = g0 if first else g0 - 1
                        hi = g0 + TT if lastt else g0 + TT + 1
                        c_lo = 1 if first else 0
                        k.dma("sp", xin[:, :, c_lo:c_lo + (hi - lo)],
                              PFM[l][R_SSX:R_SSX + 512, lo:hi].rearrange("(c p) t -> p c t", p=128),
                              reads=[PFM[l]], writes=[xin])
                        if first:
                            k.op("dve", lambda e: e.memset(xin[:, :, 0:1], 0.0), writes=[xin], acc=True)
                        if lastt:
                            k.op("dve", lambda e: e.memset(xin[:, :, TT + 1:TT + 2], 0.0), writes=[xin], acc=True)
                        xT = xTp.next()
                        for c in range(4):
                            ta = tmpp.next()
                            tb = tmpp.next()
                            k.op("dve", lambda e: e.tensor_scalar(out=ta[:, 0:TT], in0=xin[:, c, 0:TT], scalar1=cw[:, c, 0:1],
                                                                  scalar2=None, op0=ALU.mult), reads=[xin, cw], writes=[ta])
                            k.op("dve", lambda e: e.scalar_tensor_tensor(out=tb[:, 0:TT], in0=xin[:, c, 1:TT + 1], scalar=cw[:, c, 1:2],
                                                                         in1=ta[:, 0:TT], op0=ALU.mult, op1=ALU.add),
                                 reads=[xin, cw, ta], writes=[tb])
                            k.op("dve", lambda e: e.scalar_tensor_tensor(out=ta[:, 0:TT], in0=xin[:, c, 2:TT + 2], scalar=cw[:, c, 2:3],
                                                                         in1=tb[:, 0:TT], op0=ALU.mult, op1=ALU.add),
                                 reads=[xin, cw, tb], writes=[ta])
                            k.op("act", lambda e: e.activation(out=xT[:, c, 0:TT], in_=ta[:, 0:TT], func=AF.Silu, bias=cb[:, c:c + 1]),
                                 reads=[ta, cb], writes=[xT], acc=(c > 0))
                            if c >= 2:
                                dres = BT if c == 2 else CT
                                k.op("pool", lambda e: e.tensor_copy(out=dres[:, t0:t0 + TT], in_=xT[:, c, 0:TT]), reads=[xT], writes=[dres], acc=True)
                        for b in range(TT // 128):
                            blk = t0 // 128 + b
                            for c in range(2):
                                p = ps.next()
                                k.op("pe", lambda e: e.transpose(p[:, 0:128], xT[:, c, b * 128:(b + 1) * 128], ident[:]),
                                     reads=[xT, ident], writes=[p])
                                evac(c, x_tok[:, blk, c * 128:(c + 1) * 128], p[:, 0:128], [p], [x_tok], acc=True)
                            p = ps.next()
                            k.op("pe", lambda e: e.transpose(p[:, 0:128], xT[:, 2, b * 128:(b + 1) * 128], ident[:]),
                                 reads=[xT, ident], writes=[p])
                            evac(1, B_tok[:, blk, :], p[:, 0:128], [p], [B_tok], acc=True)
                    if debug.get("ssd_stop", 9) <= 1:
                        continue
                    for b0 in range(0, NB, 2):
                        k.dma("sp", dtr[:, b0:b0 + 2, :],
                              PTM[l][off + b0 * 128:off + (b0 + 2) * 128, C_DT:C_DT + 8].rearrange("(b p) j -> p b j", p=128),
                              reads=[PTM[l]], writes=[dtr], acc=(b0 > 0))
                    if debug.get("ssd_stop", 9) <= 2:
                        continue
                    k.op("dve", lambda e: e.tensor_tensor(out=dtt[:, 0:NB, :], in0=dtr[:, 0:NB, :],
                                                          in1=dtb[:].unsqueeze(1).to_broadcast([128, NB, 8]), op=ALU.add),
                         reads=[dtr, dtb], writes=[dtt])
                    k.op("act", lambda e: e.activation(out=dtr[:, 0:NB, :], in_=dtt[:, 0:NB, :], func=AF.Exp), reads=[dtt], writes=[dtr])
                    k.op("act", lambda e: e.activation(out=dtt[:, 0:NB, :], in_=dtr[:, 0:NB, :], func=AF.Ln, bias=onec[:, 0:1]),
                         reads=[dtr, onec], writes=[dtt])
                    k.op("dve", lambda e: e.tensor_tensor(out=at[:, 0:NB, :], in0=dtt[:, 0:NB, :],
                                                          in1=aneg[:].unsqueeze(1).to_broadcast([128, NB, 8]), op=ALU.mult),
                         reads=[dtt, aneg], writes=[at])
                    for d in range(2):
                        if lat:
                            stin = sop.next()
                            for g in range(2):
                                for hh in range(2):
                                    k.dma("sp", stin[hh * 64:(hh + 1) * 64, g * 64:(g + 1) * 64], I["state_ssm"][l, d, 2 * g + hh, :, :],
                                          writes=[stin], acc=not (g == 0 and hh == 0))
                            p = ps.next()
                            k.op("pe", lambda e: e.transpose(p[:, 0:128], stin[:, :], ident[:]), reads=[stin, ident], writes=[p])
                            k.op("dve", lambda e: e.tensor_copy(out=S[d][:].rearrange("p a b -> p (a b)"), in_=p[:, 0:128]),
                                 reads=[p], writes=[S[d]])
                            k.op("act", lambda e: e.copy(out=Sb[d][:], in_=S[d][:]), reads=[S[d]], writes=[Sb[d]])
                    if debug.get("ssd_stop", 9) <= 3:
                        continue
                    ywritten = set()

                    def ssd_dir_gen(d):
                        stp, exp_, GUp, Lp, L2p, scp, xdp, typ, Bdp, ydp = (WS[d][n_] for n_ in ("stp", "exp", "GUp", "Lp", "L2p", "scp", "xdp", "typ", "Bdp", "ydp"))
                        ps = psd[d]
                        U = cm[:, d, :]
                        order = list(range(NB)) if d == 0 else list(range(NB - 1, -1, -1))
                        halves = (0, 1) if d == 0 else (1, 0)
                        for blk in order:
                            tok0 = blk * 128
                            seq_first = (blk % BPS == 0) if d == 0 else (blk % BPS == BPS - 1)
                            seq_last = (blk % BPS == BPS - 1) if d == 0 else (blk % BPS == 0)
                            if seq_first and not lat:
                                k.op("dve", lambda e: e.memset(S[d][:], 0.0), writes=[S[d]])
                                k.op("act", lambda e: e.copy(out=Sb[d][:], in_=S[d][:]), reads=[S[d]], writes=[Sb[d]])
                            a_blk = at[:, blk, d * 4:(d + 1) * 4]
                            pc = ps.next()
                            for j, lh in enumerate((U, onesA, onesB)):
                                k.op("pe", lambda e: e.matmul(pc[:, 4 * j:4 * j + 4], lhsT=lh, rhs=a_blk, start=True, stop=True),
                                     reads=[cm, at], writes=[pc], inc=(j == 2))
                            st = stp.next()
                            k.op("act", lambda e: e.copy(out=st[:, 0:12], in_=pc[:, 0:12]), reads=[pc], writes=[st])
                            k.op("dve", lambda e: e.tensor_tensor(out=st[0:64, 16:20], in0=st[0:64, 4:8], in1=st[0:64, 0:4], op=ALU.subtract),
                                 reads=[st], writes=[st])
                            k.op("dve", lambda e: e.tensor_tensor(out=st[64:128, 16:20], in0=st[64:128, 8:12], in1=st[64:128, 0:4], op=ALU.subtract),
                                 reads=[st], writes=[st])
                            ex = exp_.next()
                            k.op("act", lambda e: e.activation(out=ex[:, 0:12], in_=st[:, 0:12], func=AF.Exp), reads=[st], writes=[ex])
                            k.op("act", lambda e: e.activation(out=ex[:, 16:20], in_=st[:, 16:20], func=AF.Exp), reads=[st], writes=[ex], acc=True)
                            if debug.get("scan_stop", 9) <= 1:
                                continue
                            yield
                            GU = GUp.next()
                            for h in range(4):
                                k.op("dve", lambda e: e.tensor_scalar(out=GU[:, h, :], in0=U, scalar1=a_blk[:, h:h + 1], scalar2=None, op0=ALU.mult),
                                     reads=[cm, at], writes=[GU], acc=(h > 0))
                            pa = ps.next()
                            k.op("pe", lambda e: e.matmul(pa[:, :], lhsT=onesblk, rhs=GU[:].rearrange("p h i -> p (h i)"), start=True, stop=True),
                                 reads=[cm, GU], writes=[pa])
                            L = Lp.next()
                            for h in range(4):
                                k.op("dve", lambda e: e.tensor_scalar(out=L[:, h, :], in0=pa[:, h * 128:(h + 1) * 128], scalar1=st[:, h:h + 1],
                                                                      scalar2=0.0, op0=ALU.subtract, op1=ALU.min),
                                     reads=[pa, st], writes=[L], acc=(h > 0))
                            L2 = L2p.next()
                            k.op("act", lambda e: e.activation(out=L2[:], in_=L[:], func=AF.Exp), reads=[L], writes=[L2])
                            k.op("pool", lambda e: e.tensor_tensor(out=L[:], in0=L2[:], in1=U.unsqueeze(1).to_broadcast([128, 4, 128]), op=ALU.mult),
                                 reads=[L2, cm], writes=[L])
                            if debug.get("scan_stop", 9) <= 2:
                                continue
                            yield
                            pcbs = [ps.next(), ps.next()]
                            for g in range(2):
                                k.op("pe", lambda e: e.matmul(pcbs[g][:, 0:128], lhsT=BT[g * 64:(g + 1) * 64, tok0:tok0 + 128],
                                                              rhs=CT[g * 64:(g + 1) * 64, tok0:tok0 + 128], start=True, stop=True),
                                     reads=[BT, CT], writes=[pcbs[g]])
                            scT = scp.next()
                            for g in range(2):
                                k.op("dve", lambda e: e.tensor_tensor(out=scT[:, 2 * g:2 * g + 2, :],
                                                                      in0=pcbs[g][:, 0:128].unsqueeze(1).to_broadcast([128, 2, 128]),
                                                                      in1=L[:, 2 * g:2 * g + 2, :], op=ALU.mult),
                                     reads=[pcbs[g], L], writes=[scT], acc=(g > 0))
                            xdt = xdp.next()
                            k.op("dve", lambda e: e.tensor_tensor(out=xdt[:], in0=x_tok[:, blk, :].rearrange("p (h d) -> p h d", h=4),
                                                                  in1=dtt[:, blk, d * 4:(d + 1) * 4].unsqueeze(2).to_broadcast([128, 4, 64]), op=ALU.mult),
                                 reads=[x_tok, dtt], writes=[xdt])
                            yield
                            pyd = ps.next()
                            for h in range(4):
                                k.op("pe", lambda e: e.matmul(pyd[:, h * 64:(h + 1) * 64], lhsT=scT[:, h, :], rhs=xdt[:, h, :], start=True, stop=True),
                                     reads=[scT, xdt], writes=[pyd], inc=(h == 3))
                            yds = ydp.next()
                            k.op("act", lambda e: e.copy(out=yds[:], in_=pyd[:, 0:256]), reads=[pyd], writes=[yds])
                            if debug.get("scan_stop", 9) <= 3:
                                continue
                            for half in halves:
                                hb = half * 64
                                ec = 4 if half == 0 else 8
                                yield
                                pyos = [ps.next(), ps.next()]
                                for h in range(4):
                                    g, hh = h // 2, h % 2
                                    k.op("pe", lambda e: e.matmul(pyos[g][:, hh * 64:(hh + 1) * 64], lhsT=CT[g * 64:(g + 1) * 64, tok0:tok0 + 128],
                                                                  rhs=Sb[d][g * 64:(g + 1) * 64, hh, :], start=True, stop=True),
                                         reads=[CT, Sb[d]], writes=[pyos[g]])
                                ty = typ.next()
                                for g in range(2):
                                    k.op("dve", lambda e: e.tensor_tensor(out=ty[hb:hb + 64, 2 * g:2 * g + 2, :],
                                                                          in0=pyos[g][hb:hb + 64, 0:128].rearrange("p (h d) -> p h d", h=2),
                                                                          in1=ex[hb:hb + 64, 2 * g:2 * g + 2].unsqueeze(2).to_broadcast([64, 2, 64]), op=ALU.mult),
                                         reads=[pyos[g], ex], writes=[ty], acc=(g > 0))
                                if (blk, half) not in ywritten:
                                    ywritten.add((blk, half))
                                    k.op("dve", lambda e: e.tensor_tensor(out=yacc[hb:hb + 64, blk, :], in0=ty[hb:hb + 64, :, :].rearrange("p h d -> p (h d)"),
                                                                          in1=yds[hb:hb + 64, :], op=ALU.add),
                                         reads=[ty, yds], writes=[yacc], acc=True)
                                else:
                                    ty2 = typ.next()
                                    k.op("dve", lambda e: e.tensor_tensor(out=ty2[hb:hb + 64, :, :].rearrange("p h d -> p (h d)"),
                                                                          in0=ty[hb:hb + 64, :, :].rearrange("p h d -> p (h d)"),
                                                                          in1=yds[hb:hb + 64, :], op=ALU.add),
                                         reads=[ty, yds], writes=[ty2])
                                    k.op("pool", lambda e: e.tensor_tensor(out=yacc[hb:hb + 64, blk, :], in0=yacc[hb:hb + 64, blk, :],
                                                                           in1=ty2[hb:hb + 64, :, :].rearrange("p h d -> p (h d)"), op=ALU.add),
                                         reads=[yacc, ty2], writes=[yacc])
                                if debug.get("scan_stop", 9) <= 4:
                                    continue
                                yield
                                Bd = Bdp.next()
                                for h in range(4):
                                    k.op("dve", lambda e: e.tensor_scalar(out=Bd[hb:hb + 64, h, :], in0=B_tok[hb:hb + 64, blk, :],
                                                                          scalar1=ex[hb:hb + 64, 16 + h:17 + h], scalar2=None, op0=ALU.mult),
                                         reads=[B_tok, ex], writes=[Bd], acc=(h > 0))
                                pst = ps.next()
                                for h in range(4):
                                    k.op("pe", lambda e: e.matmul(pst[:, h * 64:(h + 1) * 64], lhsT=Bd[hb:hb + 64, h, :], rhs=xdt[hb:hb + 64, h, :],
                                                                  start=True, stop=True), reads=[Bd, xdt], writes=[pst], inc=(h == 3))
                                for h in range(4):
                                    g, hh = h // 2, h % 2
                                    k.op("dve", lambda e: e.scalar_tensor_tensor(out=S[d][g * 64:(g + 1) * 64, hh, :], in0=S[d][g * 64:(g + 1) * 64, hh, :],
                                                                                 scalar=ex[g * 64:(g + 1) * 64, ec + h:ec + h + 1],
                                                                                 in1=pst[g * 64:(g + 1) * 64, h * 64:(h + 1) * 64],
                                                                                 op0=ALU.mult, op1=ALU.add),
                                         reads=[S[d], ex, pst], writes=[S[d]])
                                k.op("act", lambda e: e.copy(out=Sb[d][:], in_=S[d][:]), reads=[S[d]], writes=[Sb[d]])
                            if seq_last and not lat:
                                p = ps.next()
                                k.op("pe", lambda e: e.transpose(p[:, 0:128], S[d][:].rearrange("p a b -> p (a b)"), ident[:]),
                                     reads=[S[d], ident], writes=[p])
                                so = sop.next()
                                k.op("dve", lambda e: e.tensor_copy(out=so[:, :], in_=p[:, 0:128]), reads=[p], writes=[so])
                                for g in range(2):
                                    for hh in range(2):
                                        k.dma("pool", O["new_ssm"][blk // BPS, l, d, 2 * g + hh, :, :], so[hh * 64:(hh + 1) * 64, g * 64:(g + 1) * 64], reads=[so])

                    gens = [ssd_dir_gen(0), ssd_dir_gen(1)]
                    while gens:
                        for g_ in list(gens):
                            try:
                                next(g_)
                            except StopIteration:
                                gens.remove(g_)
                    if debug.get("ssd_stop", 9) <= 4:
                        continue
                    for blk in range(NB):
                        tok0 = blk * 128
                        z = zp.next()
                        k.dma("sp", z[:], PTM[l][off + tok0:off + tok0 + 128, C_SSZ:C_SSZ + 256], reads=[PTM[l]], writes=[z])
                        t1 = y2p.next()
                        k.op("dve", lambda e: e.tensor_tensor(out=t1[:].rearrange("p (h d) -> p h d", h=4),
                                                              in0=x_tok[:, blk, :].rearrange("p (h d) -> p h d", h=4),
                                                              in1=Dt[:, 0:4].unsqueeze(2).to_broadcast([128, 4, 64]), op=ALU.mult),
                             reads=[x_tok, Dt], writes=[t1])
                        t2 = y2p.next()
                        k.op("pool", lambda e: e.tensor_tensor(out=t2[:], in0=t1[:], in1=yacc[:, blk, :], op=ALU.add), reads=[t1, yacc], writes=[t2])
                        sz = y2p.next()
                        k.op("act", lambda e: e.activation(out=sz[:], in_=z[:], func=AF.Silu), reads=[z], writes=[sz])
                        y2 = y2p.next()
                        k.op("dve", lambda e: e.tensor_tensor(out=y2[:], in0=t2[:], in1=sz[:], op=ALU.mult), reads=[t2, sz], writes=[y2])
                        ssum = ssp.next()
                        k.op("dve", lambda e: e.memset(ssum[:], 0.0), writes=[ssum])
                        for g in range(2):
                            k.op("act", lambda e: e.activation(out=t1[:, g * 128:(g + 1) * 128], in_=y2[:, g * 128:(g + 1) * 128], func=AF.Square,
                                                               accum_out=ssum[:, g:g + 1]), reads=[y2], writes=[t1, ssum])
                        k.op("act", lambda e: e.activation(out=ssum[:, 2:4], in_=ssum[:, 0:2], func=AF.Ln, scale=1.0 / 128, bias=epsb[:, 0:1]),
                             reads=[ssum, epsb], writes=[ssum])
                        k.op("act", lambda e: e.activation(out=ssum[:, 0:2], in_=ssum[:, 2:4], func=AF.Exp, scale=-0.5), reads=[ssum], writes=[ssum])
                        y3 = sz
                        for g in range(2):
                            k.op("dve", lambda e: e.scalar_tensor_tensor(out=y3[:, g * 128:(g + 1) * 128], in0=y2[:, g * 128:(g + 1) * 128],
                                                                         scalar=ssum[:, g:g + 1], in1=nwt[:, g * 128:(g + 1) * 128],
                                                                         op0=ALU.mult, op1=ALU.mult), reads=[y2, ssum, nwt], writes=[y3])
                        os_ = osp.next()
                        for c in range(2):
                            p = ps.next()
                            k.op("pe", lambda e: e.transpose(p[:, 0:128], y3[:, c * 128:(c + 1) * 128], ident[:]), reads=[y3, ident], writes=[p])
                            evac(c, os_[:, c, :], p[:, 0:128], [p], [os_], acc=(c > 0))
                        k.dma("pool", MIX[l][512:768, off + tok0:off + tok0 + 128].rearrange("(c p) t -> p c t", p=128), os_[:],
                              reads=[os_], writes=[MIX[l]], acc=True)
                k.barrier()


        def dn_phase(l):
            with ExitStack() as ph:
                cm = k.tile("cmd", [128, 14, 128], F32, ph)
                k.dma("sp", cm[:, 0:7, :], I["cmask"].rearrange("r p q -> p r q")[:, 0:7, :], writes=[cm])
                k.dma("sp", cm[:, 7:14, :], I["cmask"].rearrange("r p q -> p r q")[:, 7:14, :], writes=[cm], acc=True)
                onesblk, onesA, onesB, identm = cm[:, 2, :], cm[:, 3, :], cm[:, 4, :], cm[:, 7, :]
                cw = k.tile("cwd", [128, 6, 3], F32, ph)
                for kk in range(3):
                    k.dma("sp", cw[:, :, kk], I["dn_conv_w"][l, kk].rearrange("(c p) -> p c", p=128), writes=[cw],
                          acc=(kk > 0), allow_slow_non_contiguous=True)
                dtb = k.tile("dtbd", [128, 8], F32, ph)
                k.dma("sp", dtb[:], I["dn_dt_bias"][l:l + 1].rearrange("o a b -> o (a b)").partition_broadcast(128), writes=[dtb])
                aneg = k.tile("anegd", [128, 8], F32, ph)
                k.dma("sp", aneg[:], I["dn_a_log"][l:l + 1].rearrange("o a b -> o (a b)").partition_broadcast(128), writes=[aneg])
                k.op("act", lambda e: e.activation(out=aneg[:], in_=aneg[:], func=AF.Exp), reads=[aneg], writes=[aneg])
                k.op("dve", lambda e: e.tensor_scalar(out=aneg[:], in0=aneg[:], scalar1=-1.0, scalar2=None, op0=ALU.mult),
                     reads=[aneg], writes=[aneg])
                nw1 = k.tile("nw1", [128, 64], F32, ph)
                k.dma("sp", nw1[:], I["dn_norm_w"][l:l + 1, :].partition_broadcast(128), writes=[nw1])
                onec = k.tile("onecd", [128, 1], F32, ph)
                k.op("dve", lambda e: e.memset(onec[:], 1.0), writes=[onec])
                NBM = TL // 128
                qT = k.tile("qTd", [128, 2, TL], BF16, ph)
                kT = k.tile("kTd", [128, 2, TL], BF16, ph)
                k_tok = k.tile("k_tok", [128, NBM, 256], BF16, ph)
                v_tok = k.tile("v_tok", [128, NBM, 256], BF16, ph)
                yacc = k.tile("yaccd", [128, NBM, 256], F32, ph)
                braw = k.tile("braw", [128, NBM, 16], F32, ph)
                btmp = k.tile("btmp", [128, NBM, 16], F32, ph)
                lnb = k.tile("lnb", [128, NBM, 8], F32, ph)
                gt = k.tile("gt", [128, NBM, 8], F32, ph)
                S = [k.tile("Sd%d" % d, [128, 2, 64], F32, ph) for d in range(2)]
                Sb = [k.tile("Sbd%d" % d, [128, 2, 64], BF16, ph) for d in range(2)]

                def h4(ap):
                    return ap.rearrange("p (c r) x -> p c r x", c=2)

                for (off, T, lat, SEQT) in ((0, NCTX * TC, False, TC), (LOFF, TL, True, TL)):
                    k.barrier()
                    NB = T // 128
                    BPS = SEQT // 128
                    TT = min(512, SEQT)
                    with ExitStack() as pre:
                        xinp = k.pool("xind", [128, 6, 514], BF16, 2, pre)
                        cqp = k.pool("cq", [128, 6, 512], F32, 1, pre)
                        tmpp = k.pool("tmpd", [128, 512], F32, 4, pre)
                        knp = k.pool("kn", [128, 2, 512], F32, 1, pre)
                        for t0 in range(0, T, TT):
                            g0 = off + t0
                            xin = xinp.next()
                            first = (t0 % SEQT == 0)
                            lastt = ((t0 + TT) % SEQT == 0)
                            lo = g0 if first else g0 - 1
                            hi = g0 + TT if lastt else g0 + TT + 1
                            c_lo = 1 if first else 0
                            for half3 in range(2):
                                k.dma("sp", xin[:, 3 * half3:3 * half3 + 3, c_lo:c_lo + (hi - lo)],
                                      PFM[l][R_DNQ + 384 * half3:R_DNQ + 384 * half3 + 384, lo:hi].rearrange("(c p) t -> p c t", p=128),
                                      reads=[PFM[l]], writes=[xin], acc=(half3 > 0))
                            if first:
                                k.op("dve", lambda e: e.memset(xin[:, :, 0:1], 0.0), writes=[xin], acc=True)
                            if lastt:
                                k.op("dve", lambda e: e.memset(xin[:, :, TT + 1:TT + 2], 0.0), writes=[xin], acc=True)
                            cq = cqp.next()
                            for c in range(6):
                                ta = tmpp.next()
                                tb = tmpp.next()
                                eng = "dve" if c % 2 == 0 else "pool"
                                k.op("dve", lambda e: e.tensor_scalar(out=ta[:, 0:TT], in0=xin[:, c, 0:TT], scalar1=cw[:, c, 0:1],
                                                                      scalar2=None, op0=ALU.mult), reads=[xin, cw], writes=[ta])
                                k.op("dve", lambda e: e.scalar_tensor_tensor(out=tb[:, 0:TT], in0=xin[:, c, 1:TT + 1], scalar=cw[:, c, 1:2],
                                                                             in1=ta[:, 0:TT], op0=ALU.mult, op1=ALU.add),
                                     reads=[xin, cw, ta], writes=[tb])
                                k.op("dve", lambda e: e.scalar_tensor_tensor(out=ta[:, 0:TT], in0=xin[:, c, 2:TT + 2], scalar=cw[:, c, 2:3],
                                                                             in1=tb[:, 0:TT], op0=ALU.mult, op1=ALU.add),
                                     reads=[xin, cw, tb], writes=[ta])
                                k.op("act", lambda e: e.activation(out=cq[:, c, 0:TT], in_=ta[:, 0:TT], func=AF.Silu),
                                     reads=[ta], writes=[cq], acc=(c > 0))
                            kn = knp.next()
                            for c in range(4):
                                sq = tmpp.next()
                                k.op("act", lambda e: e.activation(out=sq[:, 0:TT], in_=cq[:, c, 0:TT], func=AF.Square), reads=[cq], writes=[sq])
                                p = ps.next()
                                k.op("pe", lambda e: e.matmul(p[:, 0:TT], lhsT=onesblk, rhs=sq[:, 0:TT], start=True, stop=True),
                                     reads=[cm, sq], writes=[p])
                                rs = tmpp.next()
                                k.op("act", lambda e: e.activation(out=rs[:, 0:TT], in_=p[:, 0:TT], func=AF.Ln, bias=epsb[:, 0:1]),
                                     reads=[p, epsb], writes=[rs])
                                rs2 = tmpp.next()
                                k.op("act", lambda e: e.activation(out=rs2[:, 0:TT], in_=rs[:, 0:TT], func=AF.Exp, scale=-0.5),
                                     reads=[rs], writes=[rs2])
                                if c < 2:
                                    k.op("dve", lambda e: e.scalar_tensor_tensor(out=qT[:, c, t0:t0 + TT], in0=cq[:, c, 0:TT], scalar=0.125,
                                                                                 in1=rs2[:, 0:TT], op0=ALU.mult, op1=ALU.mult),
                                         reads=[cq, rs2], writes=[qT], acc=True)
                                else:
                                    k.op("dve", lambda e: e.tensor_tensor(out=kn[:, c - 2, 0:TT], in0=cq[:, c, 0:TT], in1=rs2[:, 0:TT], op=ALU.mult),
                                         reads=[cq, rs2], writes=[kn], acc=(c > 2))
                                    k.op("act", lambda e: e.copy(out=kT[:, c - 2, t0:t0 + TT], in_=kn[:, c - 2, 0:TT]), reads=[kn], writes=[kT], acc=True)
                            for b in range(TT // 128):
                                blk = t0 // 128 + b
                                for c in range(2):
                                    p = ps.next()
                                    k.op("pe", lambda e: e.transpose(p[:, 0:128], kn[:, c, b * 128:(b + 1) * 128], ident[:]),
                                         reads=[kn, ident], writes=[p])
                                    evac(c, k_tok[:, blk, c * 128:(c + 1) * 128], p[:, 0:128], [p], [k_tok], acc=True)
                                    p = ps.next()
                                    k.op("pe", lambda e: e.transpose(p[:, 0:128], cq[:, 4 + c, b * 128:(b + 1) * 128], ident[:]),
                                         reads=[cq, ident], writes=[p])
                                    evac(c + 1, v_tok[:, blk, c * 128:(c + 1) * 128], p[:, 0:128], [p], [v_tok], acc=True)
                        k.barrier()
                    for b0 in range(0, NB, 2):
                        k.dma("sp", braw[:, b0:b0 + 2, :],
                              PTM[l][off + b0 * 128:off + (b0 + 2) * 128, C_BETA:C_BETA + 16].rearrange("(b p) j -> p b j", p=128),
                              reads=[PTM[l]], writes=[braw], acc=(b0 > 0))
                    k.op("act", lambda e: e.activation(out=btmp[:, 0:NB, 0:8], in_=braw[:, 0:NB, 0:8], func=AF.Exp, scale=-1.0),
                         reads=[braw], writes=[btmp])
                    k.op("act", lambda e: e.activation(out=lnb[:, 0:NB, :], in_=btmp[:, 0:NB, 0:8], func=AF.Ln, bias=onec[:, 0:1]),
                         reads=[btmp, onec], writes=[lnb])
                    k.op("dve", lambda e: e.tensor_scalar(out=lnb[:, 0:NB, :], in0=lnb[:, 0:NB, :], scalar1=-1.0, scalar2=None, op0=ALU.mult),
                         reads=[lnb], writes=[lnb])
                    k.op("dve", lambda e: e.tensor_tensor(out=btmp[:, 0:NB, 8:16], in0=braw[:, 0:NB, 8:16],
                                                          in1=dtb[:].unsqueeze(1).to_broadcast([128, NB, 8]), op=ALU.add),
                         reads=[braw, dtb], writes=[btmp])
                    k.op("act", lambda e: e.activation(out=braw[:, 0:NB, 8:16], in_=btmp[:, 0:NB, 8:16], func=AF.Exp), reads=[btmp], writes=[braw])
                    k.op("act", lambda e: e.activation(out=btmp[:, 0:NB, 8:16], in_=braw[:, 0:NB, 8:16], func=AF.Ln, bias=onec[:, 0:1]),
                         reads=[braw, onec], writes=[btmp])
                    k.op("dve", lambda e: e.tensor_tensor(out=gt[:, 0:NB, :], in0=btmp[:, 0:NB, 8:16],
                                                          in1=aneg[:].unsqueeze(1).to_broadcast([128, NB, 8]), op=ALU.mult),
                         reads=[btmp, aneg], writes=[gt])
                    for d in range(2):
                        if lat:
                            for h in range(4):
                                c_, par = h // 2, h % 2
                                k.dma("sp", S[d][par * 64:(par + 1) * 64, c_, :], I["state_dn"][l, d, h, :, :], writes=[S[d]], acc=(h > 0))
                            k.op("act", lambda e: e.copy(out=Sb[d][:], in_=S[d][:]), reads=[S[d]], writes=[Sb[d]])
                    if debug.get("dn_stop", 9) <= 1:
                        continue
                    with ExitStack() as sc:
                        ywritten = set()
                        WK = []
                        for d_ in range(2):
                            W_ = {"stp": k.pool("std", [128, 24], F32, 2, sc), "exp": k.pool("exd", [128, 24], F32, 4, sc), "w4": {}, "w64": {}}
                            for nm in ("GU", "GUb", "L", "E1", "E2", "E3", "A", "AT", "tm"):
                                W_["w4"][nm] = k.tile("w4" + nm, [128, 4, 128], F32, sc)
                            W_["Xp"] = k.pool("Xp", [128, 4, 128], BF16, 2, sc)
                            W_["XTp"] = k.pool("XTp", [128, 4, 128], BF16, 2, sc)
                            for nm in ("As", "ATs", "Ps", "Ps2"):
                                W_["w4"][nm] = k.tile("w4b" + nm, [128, 4, 128], BF16, sc)
                            for nm in ("ty", "ty2"):
                                W_["w64"][nm] = k.tile("w64" + nm, [128, 4, 64], F32, sc)
                            for nm in ("vb", "kbg", "vn"):
                                W_["w64"][nm] = k.tile("w64" + nm, [128, 4, 64], BF16, sc)
                            W_["slots"] = [{"QK": k.tile("sQK", [128, 4, 128], BF16, sc), "wT": k.tile("swT", [128, 4, 128], BF16, sc),
                                            "u": k.tile("su", [128, 4, 64], F32, sc), "kdec": k.tile("skd", [128, 4, 64], BF16, sc)} for _ in range(2)]
                            WK.append(W_)

                        def dn_intra_gen(d):
                            stp, exp_, w4, w64, Xp, XTp = (WK[d][n_] for n_ in ("stp", "exp", "w4", "w64", "Xp", "XTp"))
                            ps = ps8
                            cnt = 0
                            U = cm[:, d, :]
                            m_incl = cm[:, d, :]
                            m_at = cm[:, 5 + d, :]
                            m_a = cm[:, 6 - d, :]
                            order = list(range(NB)) if d == 0 else list(range(NB - 1, -1, -1))
                            halves = (0, 1) if d == 0 else (1, 0)

                            def bc4(m):
                                return m.unsqueeze(1).to_broadcast([128, 4, 128])

                            for blk in order:
                                while busy[d] >= 2:
                                    yield
                                busy[d] += 1
                                slot = WK[d]["slots"][cnt % 2]
                                cnt += 1
                                tok0 = blk * 128
                                g_blk = gt[:, blk, d * 4:(d + 1) * 4]
                                lb_blk = lnb[:, blk, d * 4:(d + 1) * 4]
                                pc = ps.next()
                                for j, lh in enumerate((U, onesA, onesB)):
                                    k.op("pe", lambda e: e.matmul(pc[:, 4 * j:4 * j + 4], lhsT=lh, rhs=g_blk, start=True, stop=True),
                                         reads=[cm, gt], writes=[pc], inc=(j == 2))
                                st = stp.next()
                                k.op("act", lambda e: e.copy(out=st[:, 0:12], in_=pc[:, 0:12]), reads=[pc], writes=[st])
                                k.op("dve", lambda e: e.tensor_tensor(out=st[:, 12:16], in0=st[:, 0:4], in1=lb_blk, op=ALU.add),
                                     reads=[st, lnb], writes=[st])
                                k.op("dve", lambda e: e.tensor_tensor(out=st[0:64, 16:20], in0=st[0:64, 4:8], in1=st[0:64, 0:4], op=ALU.subtract),
                                     reads=[st], writes=[st])
                                k.op("dve", lambda e: e.tensor_tensor(out=st[64:128, 16:20], in0=st[64:128, 8:12], in1=st[64:128, 0:4], op=ALU.subtract),
                                     reads=[st], writes=[st])
                                ex = exp_.next()
                                k.op("act", lambda e: e.activation(out=ex[:, 0:20], in_=st[:, 0:20], func=AF.Exp), reads=[st], writes=[ex])
                                k.op("act", lambda e: e.activation(out=ex[:, 20:24], in_=lb_blk, func=AF.Exp), reads=[lnb], writes=[ex], acc=True)
                                yield
                                GU, GUb, L = w4["GU"], w4["GUb"], w4["L"]
                                for h in range(4):
                                    k.op("dve", lambda e: e.tensor_scalar(out=GU[:, h, :], in0=U, scalar1=g_blk[:, h:h + 1], scalar2=None, op0=ALU.mult),
                                         reads=[cm, gt], writes=[GU], acc=(h > 0))
                                for h in range(4):
                                    k.op("dve", lambda e: e.scalar_tensor_tensor(out=GUb[:, h, :], in0=identm, scalar=lb_blk[:, h:h + 1],
                                                                                 in1=GU[:, h, :], op0=ALU.mult, op1=ALU.add),
                                         reads=[cm, lnb, GU], writes=[GUb], acc=(h > 0))
                                pa1 = ps.next()
                                k.op("pe", lambda e: e.matmul(pa1[:, :], lhsT=onesblk, rhs=GU[:].rearrange("p h i -> p (h i)"), start=True, stop=True),
                                     reads=[cm, GU], writes=[pa1])
                                pa2 = ps.next()
                                k.op("pe", lambda e: e.matmul(pa2[:, :], lhsT=onesblk, rhs=GUb[:].rearrange("p h i -> p (h i)"), start=True, stop=True),
                                     reads=[cm, GUb], writes=[pa2])
                                for h in range(4):
                                    k.op("dve", lambda e: e.tensor_scalar(out=L[:, h, :], in0=pa1[:, h * 128:(h + 1) * 128], scalar1=st[:, h:h + 1],
                                                                          scalar2=0.0, op0=ALU.subtract, op1=ALU.min),
                                         reads=[pa1, st], writes=[L], acc=(h > 0))
                                k.op("act", lambda e: e.activation(out=w4["tm"][:], in_=L[:], func=AF.Exp), reads=[L], writes=[w4["tm"]])
                                k.op("pool", lambda e: e.tensor_tensor(out=w4["E3"][:], in0=w4["tm"][:], in1=bc4(m_incl), op=ALU.mult),
                                     reads=[w4["tm"], cm], writes=[w4["E3"]])
                                for h in range(4):
                                    k.op("dve", lambda e: e.tensor_scalar(out=L[:, h, :], in0=pa1[:, h * 128:(h + 1) * 128], scalar1=st[:, 12 + h:13 + h],
                                                                          scalar2=0.0, op0=ALU.subtract, op1=ALU.max),
                                         reads=[pa1, st], writes=[L], acc=(h > 0))
                                k.op("act", lambda e: e.activation(out=w4["tm"][:], in_=L[:], func=AF.Exp, scale=-1.0), reads=[L], writes=[w4["tm"]])
                                k.op("pool", lambda e: e.tensor_tensor(out=w4["E1"][:], in0=w4["tm"][:], in1=bc4(m_a), op=ALU.mult),
                                     reads=[w4["tm"], cm], writes=[w4["E1"]])
                                for h in range(4):
                                    k.op("dve", lambda e: e.tensor_scalar(out=L[:, h, :], in0=pa2[:, h * 128:(h + 1) * 128], scalar1=st[:, h:h + 1],
                                                                          scalar2=0.0, op0=ALU.subtract, op1=ALU.min),
                                         reads=[pa2, st], writes=[L], acc=(h > 0))
                                k.op("act", lambda e: e.activation(out=w4["tm"][:], in_=L[:], func=AF.Exp), reads=[L], writes=[w4["tm"]])
                                k.op("pool", lambda e: e.tensor_tensor(out=w4["E2"][:], in0=w4["tm"][:], in1=bc4(m_at), op=ALU.mult),
                                     reads=[w4["tm"], cm], writes=[w4["E2"]])
                                yield
                                pk = [ps.next(), ps.next()]
                                for c_ in range(2):
                                    for par in range(2):
                                        k.op("pe", lambda e: e.matmul(pk[par][:, c_ * 128:(c_ + 1) * 128],
                                                                      lhsT=kT[par * 64:(par + 1) * 64, c_, tok0 + (off - off):tok0 + 128],
                                                                      rhs=kT[par * 64:(par + 1) * 64, c_, tok0:tok0 + 128], start=True, stop=True),
                                             reads=[kT], writes=[pk[par]])
                                A, AT, QK = w4["A"], w4["AT"], slot["QK"]
                                for par in range(2):
                                    k.op("dve", lambda e: e.tensor_tensor(out=h4(A[:])[:, :, par, :], in0=pk[par][:, 0:256].rearrange("p (c i) -> p c i", c=2),
                                                                          in1=h4(w4["E1"][:])[:, :, par, :], op=ALU.mult),
                                         reads=[pk[par], w4["E1"]], writes=[A], acc=(par > 0))
                                    k.op("dve", lambda e: e.tensor_tensor(out=h4(AT[:])[:, :, par, :], in0=pk[par][:, 0:256].rearrange("p (c i) -> p c i", c=2),
                                                                          in1=h4(w4["E2"][:])[:, :, par, :], op=ALU.mult),
                                         reads=[pk[par], w4["E2"]], writes=[AT], acc=(par > 0))
                                pq = [ps.next(), ps.next()]
                                for c_ in range(2):
                                    for par in range(2):
                                        k.op("pe", lambda e: e.matmul(pq[par][:, c_ * 128:(c_ + 1) * 128],
                                                                      lhsT=kT[par * 64:(par + 1) * 64, c_, tok0:tok0 + 128],
                                                                      rhs=qT[par * 64:(par + 1) * 64, c_, tok0:tok0 + 128], start=True, stop=True),
                                             reads=[kT, qT], writes=[pq[par]])
                                for par in range(2):
                                    k.op("dve", lambda e: e.tensor_tensor(out=h4(QK[:])[:, :, par, :], in0=pq[par][:, 0:256].rearrange("p (c i) -> p c i", c=2),
                                                                          in1=h4(w4["E3"][:])[:, :, par, :], op=ALU.mult),
                                         reads=[pq[par], w4["E3"]], writes=[QK], acc=(par > 0))
                                yield
                                X = Xp.next()
                                XT = XTp.next()
                                tm = w4["tm"]
                                k.op("pool", lambda e: e.tensor_tensor(out=tm[:], in0=A[:], in1=bc4(cm[:, 8, :]), op=ALU.mult), reads=[A, cm], writes=[tm])
                                k.op("dve", lambda e: e.scalar_tensor_tensor(out=X[:], in0=tm[:], scalar=-1.0, in1=bc4(identm), op0=ALU.mult, op1=ALU.add),
                                     reads=[tm, cm], writes=[X])
                                k.op("pool", lambda e: e.tensor_tensor(out=L[:], in0=AT[:], in1=bc4(cm[:, 8, :]), op=ALU.mult), reads=[AT, cm], writes=[L])
                                k.op("dve", lambda e: e.scalar_tensor_tensor(out=XT[:], in0=L[:], scalar=-1.0, in1=bc4(identm), op0=ALU.mult, op1=ALU.add),
                                     reads=[L, cm], writes=[XT])
                                As, ATs, Ps, Ps2 = w4["As"], w4["ATs"], w4["Ps"], w4["Ps2"]
                                for lev in range(5):
                                    ms = cm[:, 9 + lev, :]
                                    k.op("pool", lambda e: e.tensor_tensor(out=As[:], in0=A[:], in1=bc4(ms), op=ALU.mult), reads=[A, cm], writes=[As])
                                    k.op("pool", lambda e: e.tensor_tensor(out=ATs[:], in0=AT[:], in1=bc4(ms), op=ALU.mult), reads=[AT, cm], writes=[ATs])
                                    pP = ps.next()
                                    for h in range(4):
                                        k.op("pe", lambda e: e.matmul(pP[:, h * 128:(h + 1) * 128], lhsT=ATs[:, h, :], rhs=X[:, h, :], start=True, stop=True),
                                             reads=[ATs, X], writes=[pP], inc=(h == 3))
                                    k.op("act", lambda e: e.copy(out=Ps[:].rearrange("p h i -> p (h i)"), in_=pP[:, :]), reads=[pP], writes=[Ps])
                                    pP2 = ps.next()
                                    for h in range(4):
                                        k.op("pe", lambda e: e.matmul(pP2[:, h * 128:(h + 1) * 128], lhsT=As[:, h, :], rhs=XT[:, h, :], start=True, stop=True),
                                             reads=[As, XT], writes=[pP2], inc=(h == 3))
                                    k.op("act", lambda e: e.copy(out=Ps2[:].rearrange("p h i -> p (h i)"), in_=pP2[:, :]), reads=[pP2], writes=[Ps2])
                                    yield
                                    pX = ps.next()
                                    for h in range(4):
                                        k.op("pe", lambda e: e.matmul(pX[:, h * 128:(h + 1) * 128], lhsT=XT[:, h, :], rhs=Ps[:, h, :], start=True, stop=True),
                                             reads=[XT, Ps], writes=[pX], inc=(h == 3))
                                    pXT = ps.next()
                                    for h in range(4):
                                        k.op("pe", lambda e: e.matmul(pXT[:, h * 128:(h + 1) * 128], lhsT=X[:, h, :], rhs=Ps2[:, h, :], start=True, stop=True),
                                             reads=[X, Ps2], writes=[pXT], inc=(h == 3))
                                    Xn = Xp.next()
                                    XTn = XTp.next()
                                    k.op("dve", lambda e: e.tensor_tensor(out=Xn[:].rearrange("p h i -> p (h i)"), in0=X[:].rearrange("p h i -> p (h i)"),
                                                                          in1=pX[:, :], op=ALU.subtract), reads=[X, pX], writes=[Xn])
                                    k.op("dve", lambda e: e.tensor_tensor(out=XTn[:].rearrange("p h i -> p (h i)"), in0=XT[:].rearrange("p h i -> p (h i)"),
                                                                          in1=pXT[:, :], op=ALU.subtract), reads=[XT, pXT], writes=[XTn])
                                    X, XT = Xn, XTn
                                    yield
                                yield
                                vb, kbg = w64["vb"], w64["kbg"]
                                kdec, u_sb, wT = slot["kdec"], slot["u"], slot["wT"]

                                def bc64(ap):
                                    return ap.unsqueeze(2).to_broadcast([128, 4, 64])

                                k.op("dve", lambda e: e.tensor_tensor(out=vb[:], in0=v_tok[:, blk, :].rearrange("p (h d) -> p h d", h=4), in1=bc64(ex[:, 20:24]), op=ALU.mult),
                                     reads=[v_tok, ex], writes=[vb])
                                k.op("pool", lambda e: e.tensor_tensor(out=kbg[:], in0=k_tok[:, blk, :].rearrange("p (h d) -> p h d", h=4), in1=bc64(ex[:, 12:16]), op=ALU.mult),
                                     reads=[k_tok, ex], writes=[kbg])
                                k.op("pool", lambda e: e.tensor_tensor(out=kdec[:], in0=k_tok[:, blk, :].rearrange("p (h d) -> p h d", h=4), in1=bc64(ex[:, 16:20]), op=ALU.mult),
                                     reads=[k_tok, ex], writes=[kdec])
                                pu = ps.next()
                                for h in range(4):
                                    k.op("pe", lambda e: e.matmul(pu[:, h * 64:(h + 1) * 64], lhsT=XT[:, h, :], rhs=vb[:, h, :], start=True, stop=True),
                                         reads=[XT, vb], writes=[pu], inc=(h == 3))
                                k.op("act", lambda e: e.copy(out=u_sb[:].rearrange("p h d -> p (h d)"), in_=pu[:, 0:256]), reads=[pu], writes=[u_sb])
                                pwT = ps.next()
                                for h in range(4):
                                    c_ = h // 2
                                    k.op("pe", lambda e: e.matmul(pwT[:, h * 128:(h + 1) * 128], lhsT=kbg[:, 2 * c_:2 * c_ + 2, :].rearrange("p r d -> p (r d)"),
                                                                  rhs=XT[:, h, :], start=True, stop=True),
                                         reads=[kbg, XT], writes=[pwT], inc=(h == 3))
                                for h in range(4):
                                    par = h % 2
                                    evac(h, wT[par * 64:(par + 1) * 64, h, :], pwT[par * 64:(par + 1) * 64, h * 128:(h + 1) * 128], [pwT], [wT], acc=(h > 0))
                                tasks[d].append((blk, ex, slot))
                                yield
                            done[d] = True

                        def dn_rec_gen(d):
                            w64 = WK[d]["w64"]
                            ps = ps8
                            halves = (0, 1) if d == 0 else (1, 0)
                            vn, ty, ty2 = w64["vn"], w64["ty"], w64["ty2"]
                            while True:
                                if not tasks[d]:
                                    if done[d]:
                                        break
                                    yield
                                    continue
                                blk, ex, slot = tasks[d].pop(0)
                                QK, kdec, u_sb, wT = slot["QK"], slot["kdec"], slot["u"], slot["wT"]
                                tok0 = blk * 128
                                seq_first = (blk % BPS == 0) if d == 0 else (blk % BPS == BPS - 1)
                                seq_last = (blk % BPS == BPS - 1) if d == 0 else (blk % BPS == 0)
                                if seq_first and not lat:
                                    k.op("dve", lambda e: e.memset(S[d][:], 0.0), writes=[S[d]])
                                    k.op("act", lambda e: e.copy(out=Sb[d][:], in_=S[d][:]), reads=[S[d]], writes=[Sb[d]])
                                for half in halves:
                                    hb = half * 64
                                    ec = 4 if half == 0 else 8
                                    pw = [ps.next(), ps.next()]
                                    for h in range(4):
                                        c_, par = h // 2, h % 2
                                        k.op("pe", lambda e: e.matmul(pw[par][:, c_ * 64:(c_ + 1) * 64], lhsT=wT[par * 64:(par + 1) * 64, h, :],
                                                                      rhs=Sb[d][par * 64:(par + 1) * 64, c_, :], start=True, stop=True),
                                             reads=[wT, Sb[d]], writes=[pw[par]])
                                    for par in range(2):
                                        k.op("dve", lambda e: e.tensor_tensor(out=vn[:].rearrange("p (c r) d -> p c r d", c=2)[:, :, par, :],
                                                                              in0=u_sb[:].rearrange("p (c r) d -> p c r d", c=2)[:, :, par, :],
                                                                              in1=pw[par][:, 0:128].rearrange("p (c d) -> p c d", c=2), op=ALU.subtract),
                                             reads=[u_sb, pw[par]], writes=[vn], acc=(par > 0))
                                    yield
                                    pqs = [ps.next(), ps.next()]
                                    for h in range(4):
                                        c_, par = h // 2, h % 2
                                        k.op("pe", lambda e: e.matmul(pqs[par][:, c_ * 64:(c_ + 1) * 64], lhsT=qT[par * 64:(par + 1) * 64, c_, tok0:tok0 + 128],
                                                                      rhs=Sb[d][par * 64:(par + 1) * 64, c_, :], start=True, stop=True),
                                             reads=[qT, Sb[d]], writes=[pqs[par]])
                                    pqk = ps.next()
                                    for h in range(4):
                                        k.op("pe", lambda e: e.matmul(pqk[:, h * 64:(h + 1) * 64], lhsT=QK[:, h, :], rhs=vn[:, h, :], start=True, stop=True),
                                             reads=[QK, vn], writes=[pqk], inc=(h == 3))
                                    for par in range(2):
                                        k.op("dve", lambda e: e.tensor_tensor(out=ty[hb:hb + 64].rearrange("p (c r) d -> p c r d", c=2)[:, :, par, :],
                                                                              in0=pqs[par][hb:hb + 64, 0:128].rearrange("p (c d) -> p c d", c=2),
                                                                              in1=ex[hb:hb + 64, 0:4].rearrange("p (c r) -> p c r", c=2)[:, :, par].unsqueeze(2).to_broadcast([64, 2, 64]),
                                                                              op=ALU.mult),
                                             reads=[pqs[par], ex], writes=[ty], acc=(par > 0))
                                    if (blk, half) not in ywritten:
                                        ywritten.add((blk, half))
                                        k.op("dve", lambda e: e.tensor_tensor(out=yacc[hb:hb + 64, blk, :], in0=ty[hb:hb + 64].rearrange("p h d -> p (h d)"),
                                                                              in1=pqk[hb:hb + 64, 0:256], op=ALU.add),
                                             reads=[ty, pqk], writes=[yacc], acc=True)
                                    else:
                                        k.op("dve", lambda e: e.tensor_tensor(out=ty2[hb:hb + 64].rearrange("p h d -> p (h d)"),
                                                                              in0=ty[hb:hb + 64].rearrange("p h d -> p (h d)"),
                                                                              in1=pqk[hb:hb + 64, 0:256], op=ALU.add),
                                             reads=[ty, pqk], writes=[ty2])
                                        k.op("pool", lambda e: e.tensor_tensor(out=yacc[hb:hb + 64, blk, :], in0=yacc[hb:hb + 64, blk, :],
                                                                               in1=ty2[hb:hb + 64].rearrange("p h d -> p (h d)"), op=ALU.add),
                                             reads=[yacc, ty2], writes=[yacc])
                                    yield
                                    pst = ps.next()
                                    for h in range(4):
                                        c_ = h // 2
                                        k.op("pe", lambda e: e.matmul(pst[:, h * 64:(h + 1) * 64],
                                                                      lhsT=kdec[hb:hb + 64, 2 * c_:2 * c_ + 2, :].rearrange("p r d -> p (r d)"),
                                                                      rhs=vn[hb:hb + 64, h, :], start=True, stop=True),
                                             reads=[kdec, vn], writes=[pst], inc=(h == 3))
                                    for h in range(4):
                                        c_, par = h // 2, h % 2
                                        k.op("dve", lambda e: e.scalar_tensor_tensor(out=S[d][par * 64:(par + 1) * 64, c_, :], in0=S[d][par * 64:(par + 1) * 64, c_, :],
                                                                                     scalar=ex[par * 64:(par + 1) * 64, ec + h:ec + h + 1],
                                                                                     in1=pst[par * 64:(par + 1) * 64, h * 64:(h + 1) * 64],
                                                                                     op0=ALU.mult, op1=ALU.add),
                                             reads=[S[d], ex, pst], writes=[S[d]])
                                    k.op("act", lambda e: e.copy(out=Sb[d][:], in_=S[d][:]), reads=[S[d]], writes=[Sb[d]])
                                    yield
                                if seq_last and not lat:
                                    for h in range(4):
                                        c_, par = h // 2, h % 2
                                        k.dma("pool", O["new_sdn"][blk // BPS, l, d, h, :, :], S[d][par * 64:(par + 1) * 64, c_, :], reads=[S[d]])
                                busy[d] -= 1

                        tasks = [[], []]
                        busy = [0, 0]
                        done = [False, False]
                        gens = [dn_intra_gen(0), dn_intra_gen(1), dn_rec_gen(0), dn_rec_gen(1)]
                        while gens:
                            for g_ in list(gens):
                                try:
                                    next(g_)
                                except StopIteration:
                                    gens.remove(g_)
                        k.barrier()
                    if debug.get("dn_stop", 9) <= 3:
                        continue
                    with ExitStack() as fin:
                        zp = k.pool("zd", [128, 256], F32, 2, fin)
                        y2p = k.pool("y2d", [128, 256], F32, 4, fin)
                        ssp = k.pool("ssumd", [128, 8], F32, 2, fin)
                        osp = k.pool("osd", [128, 2, 128], BF16, 2, fin)
                        for blk in range(NB):
                            tok0 = blk * 128
                            z = zp.next()
                            k.dma("sp", z[:], PTM[l][off + tok0:off + tok0 + 128, C_DNZ:C_DNZ + 256], reads=[PTM[l]], writes=[z])
                            ssum = ssp.next()
                            k.op("dve", lambda e: e.memset(ssum[:], 0.0), writes=[ssum])
                            t1 = y2p.next()
                            for h in range(4):
                                k.op("act", lambda e: e.activation(out=t1[:, h * 64:(h + 1) * 64], in_=yacc[:, blk, h * 64:(h + 1) * 64], func=AF.Square,
                                                                   accum_out=ssum[:, h:h + 1]), reads=[yacc], writes=[t1, ssum])
                            k.op("act", lambda e: e.activation(out=ssum[:, 4:8], in_=ssum[:, 0:4], func=AF.Ln, scale=1.0 / 64, bias=epsb[:, 0:1]),
                                 reads=[ssum, epsb], writes=[ssum])
                            k.op("act", lambda e: e.activation(out=ssum[:, 0:4], in_=ssum[:, 4:8], func=AF.Exp, scale=-0.5), reads=[ssum], writes=[ssum])
                            t2 = y2p.next()
                            k.op("dve", lambda e: e.tensor_tensor(out=t2[:].rearrange("p (h d) -> p h d", h=4),
                                                                  in0=yacc[:, blk, :].rearrange("p (h d) -> p h d", h=4),
                                                                  in1=ssum[:, 0:4].unsqueeze(2).to_broadcast([128, 4, 64]), op=ALU.mult),
                                 reads=[yacc, ssum], writes=[t2])
                            t3 = y2p.next()
                            k.op("pool", lambda e: e.tensor_tensor(out=t3[:].rearrange("p (h d) -> p h d", h=4),
                                                                   in0=t2[:].rearrange("p (h d) -> p h d", h=4),
                                                                   in1=nw1[:].unsqueeze(1).to_broadcast([128, 4, 64]), op=ALU.mult),
                                 reads=[t2, nw1], writes=[t3])
                            sz = y2p.next()
                            k.op("act", lambda e: e.activation(out=sz[:], in_=z[:], func=AF.Silu), reads=[z], writes=[sz])
                            k.op("dve", lambda e: e.tensor_tensor(out=t1[:], in0=t3[:], in1=sz[:], op=ALU.mult), reads=[t3, sz], writes=[t1])
                            os_ = osp.next()
                            for c in range(2):
                                p = ps.next()
                                k.op("pe", lambda e: e.transpose(p[:, 0:128], t1[:, c * 128:(c + 1) * 128], ident[:]), reads=[t1, ident], writes=[p])
                                evac(c, os_[:, c, :], p[:, 0:128], [p], [os_], acc=(c > 0))
                            k.dma("pool", MIX[l][0:256, off + tok0:off + tok0 + 128].rearrange("(c p) t -> p c t", p=128), os_[:],
                                  reads=[os_], writes=[MIX[l]], acc=True)
                        k.barrier()
                k.barrier()

        for l in range(DEPTH):
            with ExitStack() as ph:
                cs = k.tile("cs", [128, 8, 2], F32, ph)
                for kind in range(2):
                    k.dma("sp", cs[:, :, kind],
                          I["cvec"][kind].rearrange("(c p) -> p c", p=128),
                          writes=[cs], acc=(kind > 0), allow_slow_non_contiguous=True)
                k.op("act", lambda e: e.activation(out=cs[:], in_=cs[:], func=AF.Silu), reads=[cs], writes=[cs])
                bad = k.tile("bad", [128, 48], F32, ph)
                k.dma("sp", bad[:], I["b_ada"][l].rearrange("(j p) -> p j", p=128),
                      writes=[bad], allow_slow_non_contiguous=True)
                nw = k.tile("nw", [128, 2, 8], F32, ph)
                k.dma("sp", nw[:, 0, :], I["norm1_w"][l].rearrange("(c p) -> p c", p=128),
                      writes=[nw], allow_slow_non_contiguous=True)
                k.dma("sp", nw[:, 1, :], I["norm2_w"][l].rearrange("(c p) -> p c", p=128),
                      writes=[nw], acc=True, allow_slow_non_contiguous=True)
                ada = k.tile("ada", [128, 48, 2], F32, ph)
                wap = k.pool("wap", [128, 8, 512], F32, 2, ph)
                for pc in range(12):
                    wa = wap.next()
                    k.dma("sp" if pc % 2 == 0 else "pool", wa[:],
                          I["w_ada"][l][:, pc * 512:(pc + 1) * 512].rearrange("(c p) n -> p c n", p=128),
                          writes=[wa])
                    for jj in range(4):
                        j = pc * 4 + jj
                        p = ps.next()
                        for c in range(8):
                            k.op("pe", lambda e: e.matmul(p[:, 0:2], lhsT=wa[:, c, jj * 128:(jj + 1) * 128],
                                                          rhs=cs[:, c, :], start=(c == 0), stop=(c == 7)),
                                 reads=[wa, cs], writes=[p], inc=(c == 7))
                        k.op("dve", lambda e: e.tensor_scalar(out=ada[:, j, :], in0=p[:, 0:2],
                                                              scalar1=bad[:, j:j + 1], scalar2=None, op0=ALU.add),
                             reads=[p, bad], writes=[ada], acc=(j > 0))
                m = mod[l]
                for (dst, srcj, nwi) in ((0, 8, 0), (3, 32, 1)):
                    for kind in range(2):
                        k.op("dve", lambda e: e.scalar_tensor_tensor(
                            out=m[:, dst, :, kind], in0=ada[:, srcj:srcj + 8, kind], scalar=1.0,
                            in1=nw[:, nwi, :], op0=ALU.add, op1=ALU.mult),
                            reads=[ada, nw], writes=[m], acc=True)
                for (dst, srcj) in ((1, 0), (2, 16), (4, 24), (5, 40)):
                    k.op("dve", lambda e: e.tensor_copy(out=m[:, dst, :, :], in_=ada[:, srcj:srcj + 8, :]),
                         reads=[ada], writes=[m], acc=True)
                dump("mod%d" % l, m, m[:].rearrange("p a c k -> p (a c k)"), [128, 96])
                dump("ada%d" % l, ada, ada[:].rearrange("p a k -> p (a k)"), [128, 96])
                k.barrier()

            with ExitStack() as ph:
                wfm = k.tile("wfm", [128, 8, NFM], BF16, ph)
                wtm = k.tile("wtm", [128, 8, NTM], BF16, ph)
                win = I["w_in"][l]

                def wload(dst, d0, s0, n, first=False):
                    k.dma("pool", dst[:, :, d0:d0 + n], win[:, s0:s0 + n].rearrange("(c p) n -> p c n", p=128),
                          writes=[dst], acc=True)

                wload(wfm, R_DNQ, 0, 768)
                wload(wfm, R_SSX, 1456 + 256, 512)
                wload(wfm, R_MQ, 1040, 384)
                wload(wfm, R_MKPE, 1424, 32)
                wload(wfm, R_MKPE + 32, 1424 + 16, 16)
                wload(wfm, R_MKPE + 48, 1424, 16)
                wload(wfm, R_SWQ, 2232, 256)
                for h in range(4):
                    wload(wfm, R_SWQS + h * 64, 2232 + h * 64 + 32, 32)
                    wload(wfm, R_SWQS + h * 64 + 32, 2232 + h * 64, 32)
                wload(wfm, R_SWK, 2488, 128)
                for h in range(2):
                    wload(wfm, R_SWKS + h * 64, 2488 + h * 64 + 32, 32)
                    wload(wfm, R_SWKS + h * 64 + 32, 2488 + h * 64, 32)
                wload(wtm, C_DNZ, 768, 256)
                wload(wtm, C_SSZ, 1456, 256)
                wload(wtm, C_BETA, 1024, 16)
                wload(wtm, C_DT, 2224, 8)
                wload(wtm, C_SWV, 2616, 128)

                xtp = k.pool("xt", [128, 8, 512], F32, 2, ph)
                sqp = k.pool("sq", [128, 8, 512], BF16, 1, ph)
                hbp = k.pool("hb", [128, 8, 512], BF16, 2, ph)
                rsp = k.pool("rstd", [128, 512], F32, 2, ph)
                tmpp = k.pool("tmp", [128, 512], F32, 3, ph)
                fstg = k.pool("fstg", [128, 512], BF16, 4, ph)
                tstg = k.pool("tstg", [128, NTM], F32, 2, ph)
                fm_chunks = [(r, 128) for r in range(0, R_MKPE, 128)] + [(R_MKPE, 64)] + \
                            [(r, 128) for r in range(R_SWQ, NFM, 128)]
                def loadA(t0):
                    xt = xtp.next()
                    k.dma("sp", xt[:], X[l][:, t0:t0 + 512].rearrange("(c p) t -> p c t", p=128),
                          reads=[X[l]], writes=[xt])
                    return xt

                def prepA(t0, xt):
                    sq = sqp.next()
                    rstd = rsp.next()
                    rms_stats(None, xt, 512, sq, rstd)
                    hb = hbp.next()
                    mod_norm(xt, 512, rstd, tmpp, hb, mod[l], 0, 1, kind_of_tile(t0))
                    return hb

                for t0, xt, hb in pipelined2(T0S, loadA, prepA):
                    kind = kind_of_tile(t0)
                    for ci, (r0, n) in enumerate(fm_chunks):
                        p = ps.next()
                        for c in range(8):
                            k.op("pe", lambda e: e.matmul(p[0:n, :], lhsT=wfm[:, c, r0:r0 + n], rhs=hb[:, c, :],
                                                          start=(c == 0), stop=(c == 7)),
                                 reads=[wfm, hb], writes=[p], inc=(c == 7))
                        fs = fstg.next()
                        if ci % 2:
                            k.op("act", lambda e: e.copy(out=fs[0:n, :], in_=p[0:n, :]), reads=[p], writes=[fs])
                        else:
                            k.op("dve", lambda e: e.tensor_copy(out=fs[0:n, :], in_=p[0:n, :]), reads=[p], writes=[fs])
                        k.dma("sp", PFM[l][r0:r0 + n, t0:t0 + 512], fs[0:n, :], reads=[fs], writes=[PFM[l]], acc=True)
                    for b in range(4):
                        ts_ = tstg.next()
                        for g, (c0, n) in enumerate(((0, 512), (512, NTM - 512))):
                            p = ps.next()
                            for c in range(8):
                                k.op("pe", lambda e: e.matmul(p[:, 0:n], lhsT=hb[:, c, b * 128:(b + 1) * 128],
                                                              rhs=wtm[:, c, c0:c0 + n], start=(c == 0), stop=(c == 7)),
                                     reads=[wtm, hb], writes=[p], inc=(c == 7))
                            if g == 0:
                                k.op("act", lambda e: e.copy(out=ts_[:, c0:c0 + n], in_=p[:, 0:n]),
                                     reads=[p], writes=[ts_])
                            else:
                                k.op("dve", lambda e: e.tensor_copy(out=ts_[:, c0:c0 + n], in_=p[:, 0:n]),
                                     reads=[p], writes=[ts_], acc=True)
                        k.dma("pool", PTM[l][t0 + b * 128:t0 + (b + 1) * 128, :], ts_[:], reads=[ts_],
                              writes=[PTM[l]], acc=True)
                k.barrier()

            if debug.get("zero_mix"):
                with ExitStack() as ph:
                    z = k.tile("z", [128, 8, 512], BF16, ph)
                    k.op("dve", lambda e: e.memset(z[:], 0.0), writes=[z])
                    for t0 in range(0, TTOT, 512):
                        k.dma("sp", MIX[l][:, t0:t0 + 512].rearrange("(c p) t -> p c t", p=128), z[:],
                              reads=[z], writes=[MIX[l]], acc=True)
                    k.barrier()

            if not debug.get("skip_mla"):
                mla_phase(l)
            if not debug.get("skip_swa"):
                swa_phase(l)
            if not debug.get("skip_ssd"):
                ssd_phase(l)
            if not debug.get("skip_dn"):
                dn_phase(l)

            with ExitStack() as ph:
                wo = k.tile("wo", [128, 8, D], BF16, ph)
                k.dma("pool", wo[:], I["w_out"][l].rearrange("(c p) n -> p c n", p=128), writes=[wo])
                xtp = k.pool("xt", [128, 8, 512], F32, 2, ph)
                mxp = k.pool("mx", [128, 8, 512], BF16, 2, ph)
                def loadC1(t0):
                    xt = xtp.next()
                    mx = mxp.next()
                    k.dma("sp", xt[:], X[l][:, t0:t0 + 512].rearrange("(c p) t -> p c t", p=128),
                          reads=[X[l]], writes=[xt])
                    k.dma("sp", mx[:], MIX[l][:, t0:t0 + 512].rearrange("(c p) t -> p c t", p=128),
                          reads=[MIX[l]], writes=[mx])
                    return xt, mx

                for t0, (xt, mx) in pipelined(T0S, loadC1):
                    kind = kind_of_tile(t0)
                    for co in range(8):
                        p = ps.next()
                        for c in range(8):
                            k.op("pe", lambda e: e.matmul(p[:, :], lhsT=wo[:, c, co * 128:(co + 1) * 128],
                                                          rhs=mx[:, c, :], start=(c == 0), stop=(c == 7)),
                                 reads=[wo, mx], writes=[p], inc=(c == 7))
                        k.op("dve", lambda e: e.scalar_tensor_tensor(
                            out=xt[:, co, :], in0=p[:, :], scalar=mod[l][:, 2, co, kind:kind + 1],
                            in1=xt[:, co, :], op0=ALU.mult, op1=ALU.add),
                            reads=[p, mod[l], xt], writes=[xt])
                    k.dma("pool", XA[:, t0:t0 + 512].rearrange("(c p) t -> p c t", p=128), xt[:],
                          reads=[xt], writes=[XA], acc=True)
                k.barrier()

            HJ = 11
            for half in range(2):
                src2 = XA if half == 0 else XB
                dst2 = XB if half == 0 else X[l + 1]
                last = (half == 1 and l == DEPTH - 1)
                with ExitStack() as ph:
                    wg = k.tile("wg", [128, 8, 2, HJ * 128], BF16, ph)
                    wd = k.tile("wd", [128, HJ, D], BF16, ph)
                    j0 = half * HJ * 128
                    for gu in range(2):
                        k.dma("pool", wg[:, :, gu, :],
                              I["w_gate_up"][l][:, gu * FF + j0:gu * FF + j0 + HJ * 128].rearrange(
                                  "(c p) n -> p c n", p=128), writes=[wg], acc=True)
                    k.dma("pool", wd[:], I["w_down"][l][j0:j0 + HJ * 128, :].rearrange("(j p) n -> p j n", p=128),
                          writes=[wd])
                    xtp = k.pool("xt", [128, 8, 512], F32, 3 if half == 0 else 2, ph)
                    x2p = k.pool("x2", [128, 8, 512], F32, 3, ph) if half == 1 else None
                    sqp = k.pool("sq", [128, 8, 512], BF16, 1, ph)
                    hbp = k.pool("hb", [128, 8, 512], BF16, 2, ph)
                    rsp = k.pool("rstd", [128, 512], F32, 2, ph)
                    tmpp = k.pool("tmp", [128, 512], F32, 3, ph)
                    acp = k.pool("act", [128, HJ, 512], BF16, 1, ph)
                    ostg = k.pool("ostg", [128, D], F32, 2, ph) if last else None
                    def loadF(t0):
                        xt = xtp.next()
                        k.dma("sp", xt[:], XA[:, t0:t0 + 512].rearrange("(c p) t -> p c t", p=128),
                              reads=[XA], writes=[xt])
                        if half == 1:
                            x2 = x2p.next()
                            k.dma("sp", x2[:], XB[:, t0:t0 + 512].rearrange("(c p) t -> p c t", p=128),
                                  reads=[XB], writes=[x2])
                        else:
                            x2 = xt
                        return xt, x2

                    def prepF(t0, ld):
                        sq = sqp.next()
                        rstd = rsp.next()
                        rms_stats(None, ld[0], 512, sq, rstd)
                        hb = hbp.next()
                        mod_norm(ld[0], 512, rstd, tmpp, hb, mod[l], 3, 4, kind_of_tile(t0))
                        return hb

                    for t0, (xt, x2), hb in pipelined2(T0S, loadF, prepF):
                        kind = kind_of_tile(t0)
                        sq = sqp.tiles[0]
                        ac = acp.next()
                        for j in range(HJ):
                            pg = ps.next()
                            pu = ps.next()
                            for gu, pp in ((0, pg), (1, pu)):
                                for c in range(8):
                                    k.op("pe", lambda e: e.matmul(pp[:, :], lhsT=wg[:, c, gu, j * 128:(j + 1) * 128],
                                                                  rhs=hb[:, c, :], start=(c == 0), stop=(c == 7)),
                                         reads=[wg, hb], writes=[pp], inc=(c == 7))
                            tmp = tmpp.next()
                            k.op("act", lambda e: e.activation(out=tmp[:], in_=pg[:], func=AF.Silu),
                                 reads=[pg], writes=[tmp])
                            k.op("dve", lambda e: e.tensor_tensor(out=ac[:, j, :], in0=tmp[:], in1=pu[:], op=ALU.mult),
                                 reads=[tmp, pu], writes=[ac], acc=(j > 0))
                        for co in range(8):
                            p = ps.next()
                            for j in range(HJ):
                                k.op("pe", lambda e: e.matmul(p[:, :], lhsT=wd[:, j, co * 128:(co + 1) * 128],
                                                              rhs=ac[:, j, :], start=(j == 0), stop=(j == HJ - 1)),
                                     reads=[wd, ac], writes=[p], inc=(j == HJ - 1))
                            k.op("dve", lambda e: e.scalar_tensor_tensor(
                                out=x2[:, co, :], in0=p[:, :], scalar=mod[l][:, 5, co, kind:kind + 1],
                                in1=x2[:, co, :], op0=ALU.mult, op1=ALU.add),
                                reads=[p, mod[l], x2], writes=[x2])
                        if not last:
                            k.dma("pool", dst2[:, t0:t0 + 512].rearrange("(c p) t -> p c t", p=128), x2[:],
                                  reads=[x2], writes=[dst2], acc=True)
                        else:
                            rstd = rsp.next()
                            rms_stats(None, x2, 512, sq, rstd)
                            for c in range(8):
                                k.op("dve", lambda e: e.scalar_tensor_tensor(
                                    out=x2[:, c, :], in0=x2[:, c, :], scalar=fnw[:, c:c + 1], in1=rstd[:, :],
                                    op0=ALU.mult, op1=ALU.mult), reads=[x2, fnw, rstd], writes=[x2])
                            for b in range(4):
                                os_ = ostg.next()
                                for c in range(8):
                                    p = ps.next()
                                    k.op("pe", lambda e: e.transpose(p[:, 0:128], x2[:, c, b * 128:(b + 1) * 128], ident[:]),
                                         reads=[x2, ident], writes=[p])
                                    if c % 2:
                                        k.op("act", lambda e: e.copy(out=os_[:, c * 128:(c + 1) * 128], in_=p[:, 0:128]),
                                             reads=[p], writes=[os_], acc=(c > 0))
                                    else:
                                        k.op("dve", lambda e: e.tensor_copy(out=os_[:, c * 128:(c + 1) * 128], in_=p[:, 0:128]),
                                             reads=[p], writes=[os_], acc=(c > 0))
                                tok = t0 + b * 128
                                dst = (O["y_ctx"][tok:tok + 128, :] if tok < LOFF
                                       else O["y_lat"][tok - LOFF:tok - LOFF + 128, :])
                                k.dma("pool", dst, os_[:], reads=[os_])
                    k.barrier()
        k.barrier()
    return nc


_CACHE = {}


def _rope_tables():
    out = {}
    pos = np.arange(TL)
    row_ids = (pos // 64).astype(np.float32)
    col_ids = (pos % 64).astype(np.float32)
    for name, rot in (("rope_m", 32), ("rope_s", 64)):
        nf = rot // 4
        inv = (10000.0 ** (-np.arange(nf, dtype=np.float32) / nf)).astype(np.float32)
        ang = np.concatenate([row_ids[:, None] * inv, col_ids[:, None] * inv], axis=-1).astype(np.float32)
        c = np.cos(ang).astype(np.float32).T
        sn = np.sin(ang).astype(np.float32).T
        out[name] = np.ascontiguousarray(np.stack([np.concatenate([c, c], 0), np.concatenate([-sn, sn], 0)]))
    return out


def kernel(**inputs):
    x_prompt = np.ascontiguousarray(inputs["x_prompt"], dtype=np.float32)
    x_sample = np.ascontiguousarray(inputs["x_sample"], dtype=np.float32)
    dbg = inputs.pop("_debug", None) if "_debug" in inputs else None
    if "nc" not in _CACHE or dbg:
        _CACHE["nc"] = build_program(dbg)
    nc = _CACHE["nc"]
    ident = np.eye(128, dtype=np.float32)
    shared = {}
    for name in ["w_ada", "b_ada", "norm1_w", "norm2_w", "final_norm_w", "w_in", "w_out",
                 "w_gate_up", "w_down"]:
        shared[name] = np.ascontiguousarray(inputs[name], dtype=np.float32)
    for name in ["mla_q_norm_w", "mla_w_uq", "mla_kv_norm_w", "mla_w_ukv", "swa_sinks",
                 "dn_conv_w", "dn_a_log", "dn_dt_bias", "dn_norm_w", "ssm_conv_w", "ssm_conv_b", "ssm_a_log", "ssm_dt_bias", "ssm_d", "ssm_norm_w"]:
        shared[name] = np.ascontiguousarray(inputs[name], dtype=np.float32)
    shared.update(_rope_tables())
    kl = np.arange(128)[:, None]
    ql = np.arange(128)[None, :]
    msk = np.zeros((6, 128, 512), np.float32)
    for r in range(6):
        for j in range(4):
            dd = r - 1 - j
            if dd == -1:
                msk[r, :, j * 128:(j + 1) * 128] = (kl >= ql)
            elif dd == 0:
                msk[r, :, j * 128:(j + 1) * 128] = 1.0
            elif dd == 1:
                msk[r, :, j * 128:(j + 1) * 128] = (kl <= ql)
    shared["swa_mask"] = msk
    ii = np.arange(128)
    same = (ii[:, None] // 64) == (ii[None, :] // 64)
    cmask = np.zeros((14, 128, 128), np.float32)
    cmask[5] = same & (ii[:, None] < ii[None, :])
    cmask[6] = same & (ii[:, None] > ii[None, :])
    cmask[7] = np.eye(128, dtype=np.float32)
    for lev, sz_ in enumerate((1, 2, 4, 8, 16, 32)):
        cmask[8 + lev] = ((ii[:, None] // (2 * sz_)) == (ii[None, :] // (2 * sz_))) & ((ii[:, None] // sz_) != (ii[None, :] // sz_))
    cmask[0] = same & (ii[:, None] <= ii[None, :])
    cmask[1] = same & (ii[:, None] >= ii[None, :])
    cmask[2] = same
    cmask[3, 0:64, :] = 1.0
    cmask[4, 64:128, :] = 1.0
    shared["cmask"] = cmask
    in_maps = []
    for core in range(8):
        b = core % 4
        m = dict(shared)
        m["x_ctx"] = x_prompt[core * NCTX:(core + 1) * NCTX].reshape(NCTX * TC, D)
        m["x_lat"] = x_sample[b]
        m["cvec"] = np.stack([inputs["c_ctx"], inputs["c"][b]]).astype(np.float32)
        m["ident"] = ident
        m["cache_ckv"] = np.ascontiguousarray(inputs["cache_mla_ckv"][b], dtype=np.float32)
        m["cache_kpe"] = np.ascontiguousarray(inputs["cache_mla_kpe"][b], dtype=np.float32)
        m["state_ssm"] = np.ascontiguousarray(inputs["state_ssm"][b], dtype=np.float32)
        m["state_dn"] = np.ascontiguousarray(inputs["state_dn"][b], dtype=np.float32)
        m["cache_swk"] = np.ascontiguousarray(inputs["cache_swa_k"][b], dtype=np.float32)
        m["cache_swv"] = np.ascontiguousarray(inputs["cache_swa_v"][b], dtype=np.float32)
        in_maps.append(m)
    res = run_bass_kernel_spmd(nc, in_maps, core_ids=list(range(8)))
    r = res.results
    y_prompt = np.concatenate([r[c]["y_ctx"].reshape(NCTX, TC, D) for c in range(8)], axis=0)
    y_sample = np.stack([r[b]["y_lat"] for b in range(4)], axis=0)
    def gath(name):
        return np.concatenate([np.asarray(r[c][name], dtype=np.float32) for c in range(8)], axis=0)

    new_ckv = gath("new_ckv") if "new_ckv" in r[0] else np.zeros((32, DEPTH, TC, 128), np.float32)
    new_kpe = gath("new_kpe") if "new_kpe" in r[0] else np.zeros((32, DEPTH, TC, 32), np.float32)
    new_swk = gath("new_swk") if "new_swk" in r[0] else np.zeros((32, DEPTH, TC, 2, 64), np.float32)
    new_swv = gath("new_swv") if "new_swv" in r[0] else np.zeros((32, DEPTH, TC, 2, 64), np.float32)
    new_sdn = gath("new_sdn") if "new_sdn" in r[0] else np.zeros((32, DEPTH, 2, 4, 64, 64), np.float32)
    new_ssm = gath("new_ssm") if "new_ssm" in r[0] else np.zeros((32, DEPTH, 2, 4, 64, 64), np.float32)
    outs = (y_prompt, y_sample, new_sdn, new_ckv, new_kpe, new_ssm, new_swk, new_swv)
    if dbg:
        return outs + (r,)
    return outs
```
